# Optimizing a Trainium2 kernel written in Bass

```python
import math
import jax, jax.numpy as jnp
from jax import lax
import numpy as np

D_MODEL = 1024
BATCH = 16
SEQ = 4096
DEPTH = 2

ATTN_WIDTH = D_MODEL // 2
ATTN_HEAD_DIM = 64
N_ATTN_HEADS = ATTN_WIDTH // ATTN_HEAD_DIM
Q_BLOCK = 128
SSM_WIDTH = D_MODEL // 4
SSM_GROUP_CH = 16
SSM_GROUPS = SSM_WIDTH // SSM_GROUP_CH
SSM_STATE = 64
POOL_WIDTH = D_MODEL // 4
POOL_WINDOWS = (2, 4, 8, 16)
POOL_GROUPS = len(POOL_WINDOWS)
POOL_CH = POOL_WIDTH // POOL_GROUPS
N_BRANCHES = 3
D_FF = 4 * D_MODEL
NORM_EPS = 1e-6
IN_SPLITS = (ATTN_WIDTH, ATTN_WIDTH, ATTN_WIDTH, N_ATTN_HEADS, SSM_WIDTH, POOL_WIDTH, N_BRANCHES * D_MODEL)
IN_COLS = sum(IN_SPLITS)
IN_OFFSETS = tuple(int(o) for o in np.cumsum(IN_SPLITS)[:-1])

kernel_name = 'hybrid_fox_s5_pool_gated_trunk'


def _rms_norm(x, gain):
    xf = x.astype(jnp.float32)
    y = xf * lax.rsqrt(jnp.mean(xf * xf, axis=-1, keepdims=True) + NORM_EPS)
    return y * gain.astype(jnp.float32)


def _forgetting_attention(q, k, v, f_logit, q_gain, k_gain):
    b, s, h, dh = q.shape
    q = _rms_norm(q, q_gain) * (dh ** -0.5)
    k = _rms_norm(k, k_gain)
    v = v.astype(jnp.float32)
    cum = jnp.cumsum(jax.nn.log_sigmoid(f_logit.astype(jnp.float32)), axis=1)
    cum = cum.transpose(0, 2, 1)
    n_blk = s // Q_BLOCK
    q_blocks = q.reshape(b, n_blk, Q_BLOCK, h, dh).transpose(1, 0, 2, 3, 4)
    c_blocks = cum.reshape(b, h, n_blk, Q_BLOCK).transpose(2, 0, 1, 3)
    k_pos = jnp.arange(s)

    def one_block(args):
        blk, qb, cb = args
        q_pos = blk * Q_BLOCK + jnp.arange(Q_BLOCK)
        logits = jnp.einsum('bqhd,bkhd->bhqk', qb, k) + (cb[..., :, None] - cum[..., None, :])
        logits = jnp.where(k_pos[None, :] <= q_pos[:, None], logits, -jnp.inf)
        p = jax.nn.softmax(logits, axis=-1)
        return jnp.einsum('bhqk,bkhd->bqhd', p, v)

    out = lax.map(one_block, (jnp.arange(n_blk), q_blocks, c_blocks))
    return out.transpose(1, 0, 2, 3, 4).reshape(b, s, h * dh)


def _s5_mixer(u, a_re, a_im, log_dt, b_re, b_im, c_re, c_im, d, w_glu):
    bsz, s, _ = u.shape
    f32 = jnp.float32
    uf = u.astype(f32)
    ug = uf.reshape(bsz, s, SSM_GROUPS, SSM_GROUP_CH)
    a_re, a_im = a_re.astype(f32), a_im.astype(f32)
    b_re, b_im = b_re.astype(f32), b_im.astype(f32)
    c_re, c_im = c_re.astype(f32), c_im.astype(f32)
    dt = jnp.exp(log_dt.astype(f32))[:, None]
    mag = jnp.exp(a_re * dt)
    ab_re = mag * jnp.cos(a_im * dt)
    ab_im = mag * jnp.sin(a_im * dt)
    den = a_re * a_re + a_im * a_im
    nr, ni = ab_re - 1.0, ab_im
    coef_re = ((nr * a_re + ni * a_im) / den)[..., None]
    coef_im = ((ni * a_re - nr * a_im) / den)[..., None]
    bb_re = coef_re * b_re - coef_im * b_im
    bb_im = coef_re * b_im + coef_im * b_re
    bu_re = jnp.einsum('gnc,bsgc->bsgn', bb_re, ug)
    bu_im = jnp.einsum('gnc,bsgc->bsgn', bb_im, ug)
    a_seq_re = jnp.broadcast_to(ab_re, (1, s, SSM_GROUPS, SSM_STATE))
    a_seq_im = jnp.broadcast_to(ab_im, (1, s, SSM_GROUPS, SSM_STATE))

    def combine(e1, e2):
        a1r, a1i, b1r, b1i = e1
        a2r, a2i, b2r, b2i = e2
        return (a1r * a2r - a1i * a2i,
                a1r * a2i + a1i * a2r,
                a2r * b1r - a2i * b1i + b2r,
                a2r * b1i + a2i * b1r + b2i)

    _, _, x_re, x_im = lax.associative_scan(combine, (a_seq_re, a_seq_im, bu_re, bu_im), axis=1)
    y = jnp.einsum('gcn,bsgn->bsgc', c_re, x_re) - jnp.einsum('gcn,bsgn->bsgc', c_im, x_im)
    y = y.reshape(bsz, s, SSM_WIDTH) + d.astype(f32) * uf
    y = jax.nn.gelu(y)
    g_a, g_b = jnp.split(y @ w_glu.astype(f32), 2, axis=-1)
    return g_a * jax.nn.sigmoid(g_b)


def _multiscale_pool(p, pool_w, pool_scale):
    bsz, s, _ = p.shape
    pf = p.astype(jnp.float32)
    positions = jnp.arange(1, s + 1, dtype=jnp.float32)
    outs = []
    for g, w in enumerate(POOL_WINDOWS):
        pg = pf[..., g * POOL_CH:(g + 1) * POOL_CH]
        cs = jnp.cumsum(pg, axis=1)
        lagged = jnp.pad(cs, ((0, 0), (w, 0), (0, 0)))[:, :s]
        mean = (cs - lagged) / jnp.minimum(positions, w)[None, :, None]
        outs.append(mean - pg)
    pooled = jnp.stack(outs, axis=2)
    mixed = jnp.einsum('bsgc,gcd->bsgd', pooled, pool_w.astype(jnp.float32))
    return mixed.reshape(bsz, s, POOL_WIDTH) * pool_scale.astype(jnp.float32)


def setup_inputs(seed: int = 0) -> dict:
    key = jax.random.key(seed)
    ks = jax.random.split(key, 32)
    f32 = jnp.float32
    L = DEPTH
    nrm = lambda k, shape, scale: jax.random.normal(k, shape, f32) * scale
    a_im_base = math.pi * jnp.arange(SSM_STATE, dtype=f32)
    return {
        'x': nrm(ks[0], (BATCH, SEQ, D_MODEL), 1.0),
        'mix_norm': 1.0 + nrm(ks[1], (L, D_MODEL), 0.02),
        'w_in': nrm(ks[2], (L, D_MODEL, IN_COLS), D_MODEL ** -0.5),
        'b_forget': jax.random.uniform(ks[3], (L, N_ATTN_HEADS), f32, 1.0, 5.0),
        'q_norm': 1.0 + nrm(ks[4], (L, ATTN_HEAD_DIM), 0.02),
        'k_norm': 1.0 + nrm(ks[5], (L, ATTN_HEAD_DIM), 0.02),
        'ssm_a_re': -0.5 + nrm(ks[6], (L, SSM_GROUPS, SSM_STATE), 0.01),
        'ssm_a_im': a_im_base + nrm(ks[7], (L, SSM_GROUPS, SSM_STATE), 0.01),
        'ssm_log_dt': jax.random.uniform(ks[8], (L, SSM_GROUPS), f32, math.log(1e-3), math.log(1e-1)),
        'ssm_b_re': nrm(ks[9], (L, SSM_GROUPS, SSM_STATE, SSM_GROUP_CH), (2 * SSM_GROUP_CH) ** -0.5),
        'ssm_b_im': nrm(ks[10], (L, SSM_GROUPS, SSM_STATE, SSM_GROUP_CH), (2 * SSM_GROUP_CH) ** -0.5),
        'ssm_c_re': nrm(ks[11], (L, SSM_GROUPS, SSM_GROUP_CH, SSM_STATE), SSM_STATE ** -0.5),
        'ssm_c_im': nrm(ks[12], (L, SSM_GROUPS, SSM_GROUP_CH, SSM_STATE), SSM_STATE ** -0.5),
        'ssm_d': nrm(ks[13], (L, SSM_WIDTH), 1.0),
        'w_glu': nrm(ks[14], (L, SSM_WIDTH, 2 * SSM_WIDTH), SSM_WIDTH ** -0.5),
        'pool_w': nrm(ks[15], (L, POOL_GROUPS, POOL_CH, POOL_CH), POOL_CH ** -0.5),
        'pool_scale': 1.0 + nrm(ks[16], (L, POOL_WIDTH), 0.02),
        'w_br_attn': nrm(ks[17], (L, ATTN_WIDTH, D_MODEL), ATTN_WIDTH ** -0.5),
        'w_br_ssm': nrm(ks[18], (L, SSM_WIDTH, D_MODEL), SSM_WIDTH ** -0.5),
        'w_br_pool': nrm(ks[19], (L, POOL_WIDTH, D_MODEL), POOL_WIDTH ** -0.5),
        'b_gate': nrm(ks[20], (L, N_BRANCHES * D_MODEL), 0.01),
        'w_out': nrm(ks[21], (L, D_MODEL, D_MODEL), D_MODEL ** -0.5),
        'mlp_norm': 1.0 + nrm(ks[22], (L, D_MODEL), 0.02),
        'w_up': nrm(ks[23], (L, D_MODEL, D_FF), D_MODEL ** -0.5),
        'w_down': nrm(ks[24], (L, D_FF, D_MODEL), D_FF ** -0.5),
    }


def reference(x, mix_norm, w_in, b_forget, q_norm, k_norm, ssm_a_re, ssm_a_im, ssm_log_dt,
              ssm_b_re, ssm_b_im, ssm_c_re, ssm_c_im, ssm_d, w_glu, pool_w, pool_scale,
              w_br_attn, w_br_ssm, w_br_pool, b_gate, w_out, mlp_norm, w_up, w_down):
    bsz, s, _ = x.shape
    dt = x.dtype
    for l in range(DEPTH):
        h = _rms_norm(x, mix_norm[l]).astype(dt)
        z = h @ w_in[l]
        q, k, v, f_logit, u_ssm, u_pool, gate_logit = jnp.split(z, IN_OFFSETS, axis=-1)
        heads = (bsz, s, N_ATTN_HEADS, ATTN_HEAD_DIM)
        y_attn = _forgetting_attention(q.reshape(heads), k.reshape(heads), v.reshape(heads),
                                       f_logit + b_forget[l], q_norm[l], k_norm[l]).astype(dt)
        y_ssm = _s5_mixer(u_ssm, ssm_a_re[l], ssm_a_im[l], ssm_log_dt[l], ssm_b_re[l], ssm_b_im[l],
                          ssm_c_re[l], ssm_c_im[l], ssm_d[l], w_glu[l]).astype(dt)
        y_pool = _multiscale_pool(u_pool, pool_w[l], pool_scale[l]).astype(dt)
        gates = jax.nn.sigmoid(gate_logit.astype(jnp.float32) + b_gate[l].astype(jnp.float32))
        gates = gates.reshape(bsz, s, N_BRANCHES, D_MODEL)
        merged = (gates[:, :, 0] * (y_attn @ w_br_attn[l])
                  + gates[:, :, 1] * (y_ssm @ w_br_ssm[l])
                  + gates[:, :, 2] * (y_pool @ w_br_pool[l]))
        x = (x + merged.astype(dt) @ w_out[l]).astype(dt)
        h = _rms_norm(x, mlp_norm[l]).astype(dt)
        act = jnp.square(jax.nn.relu(h @ w_up[l]))
        x = (x + act @ w_down[l]).astype(dt)
    return x
```

```python
import contextlib
import math
import numpy as np
import concourse.bass as bass
import concourse.mybir as mybir
from concourse.bass_utils import run_bass_kernel_spmd

F32 = mybir.dt.float32
BF16 = mybir.dt.bfloat16
ALU = mybir.AluOpType
AF = mybir.ActivationFunctionType
AX = mybir.AxisListType

D = 1024
DEPTH = 2
DFF = 4096
NH = 8
DH = 64
INC = 5128
EPS = 1e-6
NCORES = 8

ENGS = ("pe", "act", "dve", "pool", "sp")
N_DMA_SEMS = 12
SAME_ENGINE_SYNC = True


class Buf:
    __slots__ = ("name", "w", "r")

    def __init__(self, name):
        self.name = name
        self.w = None
        self.r = {}


class Op:
    __slots__ = ("eng", "idx", "fn", "deps", "dma", "needs_inc", "sem", "semval")

    def __init__(self, eng, idx, fn, deps, dma):
        self.eng, self.idx, self.fn, self.deps, self.dma = eng, idx, fn, deps, dma
        self.needs_inc = False
        self.sem = None
        self.semval = None


class Sched:
    def __init__(self, nc, es, same_engine_sync=SAME_ENGINE_SYNC):
        self.nc = nc
        self.q = {e: [] for e in ENGS}
        self.ndma = {e: 0 for e in ENGS}
        self.cnt = {e: 0 for e in ENGS}
        self.same_engine_sync = same_engine_sync
        self.bufs = {}
        self.esem = {e: es.enter_context(nc.semaphore("es_" + e)) for e in ENGS if e != "sp"}
        self.dsem = {}
        for e in ("sp", "act", "pool"):
            for j in range(N_DMA_SEMS):
                self.dsem[(e, j)] = es.enter_context(nc.semaphore(f"ds_{e}_{j}"))
        self.total = {e: 0 for e in ENGS}
        self.nwaits = {e: 0 for e in ENGS}

    def buf(self, name):
        b = self.bufs.get(name)
        if b is None:
            b = self.bufs[name] = Buf(name)
        return b

    def _b(self, x):
        return x if isinstance(x, Buf) else self.buf(x)

    def op(self, eng, fn, reads=(), writes=(), dma=False):
        reads = [self._b(x) for x in reads]
        writes = [self._b(x) for x in writes]
        deps = {}
        for b in reads:
            if b.w is not None:
                deps[id(b.w)] = b.w
        for b in writes:
            if b.w is not None:
                deps[id(b.w)] = b.w
            for o in b.r.values():
                deps[id(o)] = o
        o = Op(eng, len(self.q[eng]), fn, list(deps.values()), dma)
        if dma:
            i = self.ndma[eng]
            self.ndma[eng] += 1
            o.sem = (eng, i % N_DMA_SEMS)
            o.semval = 16 * (i // N_DMA_SEMS + 1)
        self.q[eng].append(o)
        for b in reads:
            key = ("dma", eng, o.idx) if dma else eng
            b.r[key] = o
        for b in writes:
            b.w = o
            b.r = {}
        for d in o.deps:
            if not d.dma:
                d.needs_inc = True
        return o

    def dma(self, eng, out, in_, reads=(), writes=(), **kw):
        return self.op(eng, lambda e: e.dma_start(out=out, in_=in_, **kw), reads, writes, dma=True)

    def emit(self):
        nc = self.nc
        for e in ENGS:
            c = self.cnt[e]
            for o in self.q[e]:
                if not o.dma and o.needs_inc:
                    c += 1
                    o.sem = e
                    o.semval = c
            self.cnt[e] = c

        def replay(e, engobj):
            known = {}
            for o in self.q[e]:
                waits = {}
                for d in o.deps:
                    if d.eng == e and not d.dma:
                        if e == "pe" or (not self.same_engine_sync and e != "pool"):
                            continue
                    k = d.sem
                    if waits.get(k, 0) < d.semval:
                        waits[k] = d.semval
                if o.dma and o.semval > 16:
                    k = o.sem
                    waits[k] = max(waits.get(k, 0), o.semval - 16)
                for k, v in waits.items():
                    if known.get(k, 0) < v:
                        known[k] = v
                        s = self.dsem[k] if isinstance(k, tuple) else self.esem[k]
                        engobj.wait_ge(s, v)
                        self.nwaits[e] += 1
                ins = o.fn(engobj)
                if o.dma:
                    ins.then_inc(self.dsem[o.sem], 16)
                elif o.needs_inc:
                    ins.then_inc(self.esem[e], 1)
            if self.ndma[e]:
                n = self.ndma[e]
                for j in range(min(N_DMA_SEMS, n)):
                    last = ((n - 1 - j) // N_DMA_SEMS) + 1
                    if known.get((e, j), 0) < 16 * last:
                        engobj.wait_ge(self.dsem[(e, j)], 16 * last)

        with nc.Block() as block:
            @block.sync
            def _(e):
                replay("sp", e)

            @block.tensor
            def _(e):
                replay("pe", e)

            @block.scalar
            def _(e):
                replay("act", e)

            @block.vector
            def _(e):
                replay("dve", e)

            @block.gpsimd
            def _(e):
                replay("pool", e)

        for e in ENGS:
            self.total[e] += len(self.q[e])
            self.q[e] = []
        for b in self.bufs.values():
            b.w = None
            b.r = {}


class Ctx:
    def __init__(self, nc, es):
        self.nc = nc
        self.es = es
        self.s = Sched(nc, es)
        self.pes = None
        self.uid = 0

    def begin(self):
        self.pes = contextlib.ExitStack()
        self.pes.__enter__()

    def end(self):
        self.s.emit()
        self.pes.close()
        self.pes = None

    def dbg(self, name, ap, reads):
        if not getattr(self, "debug", False):
            return
        d = self.nc.dram_tensor("dbg_" + name, list(ap.shape), ap.dtype, kind="ExternalOutput").ap()
        self.s.dma("sp", d, ap, reads=reads)

    def sb(self, name, shape, dtype):
        self.uid += 1
        return self.pes.enter_context(self.nc.sbuf_tensor(f"{name}_{self.uid}", list(shape), dtype))

    def ps(self, name, shape=(128, 512), dtype=F32):
        self.uid += 1
        return self.pes.enter_context(self.nc.psum_tensor(f"{name}_{self.uid}", list(shape), dtype))


def _cast_engine(i):
    return ("dve", "pool", "act")[i % 3]


def _cast(s, eng, out, in_, reads, writes, scalar=None):
    if scalar is None:
        if eng == "act":
            return s.op("act", lambda e: e.copy(out=out, in_=in_), reads, writes)
        return s.op(eng, lambda e: e.tensor_copy(out=out, in_=in_), reads, writes)
    if eng == "act":
        return s.op("act", lambda e: e.activation(out=out, in_=in_, func=AF.Copy, scale=scalar), reads, writes)
    return s.op(eng, lambda e: e.tensor_scalar(out=out, in0=in_, scalar1=scalar, scalar2=None, op0=ALU.mult),
                reads, writes)


def make_identity(c, ident):
    s = c.s
    s.op("pool", lambda e: e.memset(ident[:], 1.0), writes=["ident"])
    s.op("pool", lambda e: e.affine_select(out=ident[:], in_=ident[:], pattern=[[-1, 128]],
                                           compare_op=ALU.is_equal, fill=0.0, base=0, channel_multiplier=1),
         reads=["ident"], writes=["ident"])


def phase_mlp(c, l, T, x_d, w_up, w_down, mlp_norm, xout_d=None):
    nc, s = c.nc, c.s
    if xout_d is None:
        xout_d = x_d
    c.begin()
    TT = 256
    NT = T // TT
    wup = c.sb("wup", [128, 8, DFF], BF16)
    wdn = c.sb("wdn", [128, 32, D], BF16)
    stg = [c.sb(f"stg{i}", [128, 1024], F32) for i in range(2)]
    gcol = c.sb("gcol", [128, 8], F32)
    ident = c.sb("ident", [128, 128], BF16)
    xt = [c.sb(f"xt{i}", [128, 2, D], F32) for i in range(3)]
    hb = c.sb("hb", [128, 2, D], BF16)
    hT = [c.sb(f"hT{i}", [128, 8, TT], BF16) for i in range(2)]
    actT = c.sb("actT", [128, 32, TT], BF16)
    rl = [c.sb(f"rl{i}", [128, TT], F32) for i in range(2)]
    ss = c.sb("ss", [128, 2], F32)
    rstd = c.sb("rstd", [128, 2], F32)
    junk = c.sb("junk", [128, 2, D], F32)
    pt = [c.ps(f"pt{i}") for i in range(2)]
    pu = [c.ps(f"pu{i}") for i in range(3)]
    pd = [c.ps(f"pd{i}") for i in range(3)]

    make_identity(c, ident)
    s.dma("sp", gcol[:], mlp_norm[l].rearrange("(kc p) -> p kc", p=128), writes=["gcol"],
          allow_slow_non_contiguous=True)
    n = 0
    for kc in range(8):
        for qq in range(4):
            st = stg[n % 2]
            s.dma("sp" if n % 2 == 0 else "act", st[:], w_up[l, kc * 128:(kc + 1) * 128, qq * 1024:(qq + 1) * 1024],
                  writes=[f"stg{n % 2}"])
            _cast(s, _cast_engine(n), wup[:, kc, qq * 1024:(qq + 1) * 1024], st[:],
                  [f"stg{n % 2}", "gcol"], [f"wup{kc}_{qq}"], scalar=gcol[:, kc:kc + 1])
            n += 1
    for fc in range(32):
        st = stg[n % 2]
        s.dma("sp" if n % 2 == 0 else "act", st[:], w_down[l, fc * 128:(fc + 1) * 128, :], writes=[f"stg{n % 2}"])
        _cast(s, _cast_engine(n), wdn[:, fc, :], st[:], [f"stg{n % 2}"], [f"wdn{fc}"])
        n += 1

    def load(i):
        s.dma("sp", xt[i % 3][:], x_d[i * TT:(i + 1) * TT, :].rearrange("(a p) d -> p a d", p=128),
              writes=[f"xt{i % 3}"])

    def front(i):
        X, xb = xt[i % 3], f"xt{i % 3}"
        HT, htn = hT[i % 2], f"hT{i % 2}"
        for a in range(2):
            ACTF(s, junk[:, a, :], X[:, a, :], AF.Square, [xb], [f"junk{a}"])
        s.op("dve", lambda e: e.tensor_reduce(out=ss[:], in_=junk[:], axis=AX.X, op=ALU.add),
             reads=["junk0", "junk1"], writes=["ss"])
        ACTF(s, rstd[:], ss[:], AF.Sqrt, ["ss"], ["rstd"], scale=1.0 / D, bias=EPS)
        s.op("dve", lambda e: e.reciprocal(out=rstd[:], in_=rstd[:]), reads=["rstd"], writes=["rstd"])
        for a in range(2):
            TS(s, "dve" if a == 0 else "pool", hb[:, a, :], X[:, a, :], rstd[:, a:a + 1], ALU.mult, [xb, "rstd"],
               [f"hb{a}"])
        for a in range(2):
            pview = pt[a][:].bitcast(BF16)
            for kc in range(8):
                TR(s, pview[:, kc * 128:(kc + 1) * 128], hb[:, a, kc * 128:(kc + 1) * 128], ident[:],
                   [f"hb{a}", "ident"], [f"pt{a}"])
            CP(s, "dve" if a == 0 else "act", HT[:, :, a * 128:(a + 1) * 128],
               pview.rearrange("p (k t) -> p k t", k=8), [f"pt{a}"], [htn + f"_{a}"])

    def up(i):
        HT, htn = hT[i % 2], f"hT{i % 2}"
        for fc in range(32):
            Pb, pb = pu[fc % 3], f"pu{fc % 3}"
            for kc in range(8):
                MM(s, Pb[:, 0:TT], wup[:, kc, fc * 128:(fc + 1) * 128], HT[:, kc, :],
                   [f"wup{kc}_{fc // 8}", htn + "_0", htn + "_1"], [pb], start=(kc == 0), stop=(kc == 7))
            R, rb = rl[fc % 2], f"rl{fc % 2}"
            ACTF(s, R[:], Pb[:, 0:TT], AF.Relu, [pb], [rb])
            TT_(s, "pool", actT[:, fc, :], R[:], R[:], ALU.mult, [rb], ["actT"])

    def down(i):
        X, xb = xt[i % 3], f"xt{i % 3}"
        for a in range(2):
            for db in range(2):
                jj = a * 2 + db
                Pb, pb = pd[jj % 3], f"pd{jj % 3}"
                for fc in range(32):
                    MM(s, Pb[:], actT[:, fc, a * 128:(a + 1) * 128], wdn[:, fc, db * 512:(db + 1) * 512],
                       ["actT", f"wdn{fc}"], [pb], start=(fc == 0), stop=(fc == 31))
                TT_(s, "dve", X[:, a, db * 512:(db + 1) * 512], Pb[:], X[:, a, db * 512:(db + 1) * 512], ALU.add,
                    [pb, xb], [xb])
        s.dma("pool", xout_d[i * TT:(i + 1) * TT, :].rearrange("(a p) d -> p a d", p=128), X[:], reads=[xb])

    load(0)
    if NT > 1:
        load(1)
    front(0)
    for i in range(NT):
        if i + 2 < NT:
            load(i + 2)
        up(i)
        if i + 1 < NT:
            front(i + 1)
        down(i)
    c.end()


def build_test_mlp(T):
    nc = bass.Bass("TRN2", target_bir_lowering=False)
    x = nc.dram_tensor("x", [T, D], F32, kind="ExternalInput").ap()
    w_up = nc.dram_tensor("w_up", [DEPTH, D, DFF], F32, kind="ExternalInput").ap()
    w_down = nc.dram_tensor("w_down", [DEPTH, DFF, D], F32, kind="ExternalInput").ap()
    mlp_norm = nc.dram_tensor("mlp_norm", [DEPTH, D], F32, kind="ExternalInput").ap()
    y = nc.dram_tensor("y", [T, D], F32, kind="ExternalOutput").ap()
    with contextlib.ExitStack() as es:
        c = Ctx(nc, es)
        c.debug = True
        phase_mlp(c, 0, T, x, w_up, w_down, mlp_norm, xout_d=y)
        print("ops", c.s.total, "waits", c.s.nwaits)
    return nc


def rms_rstd(c, X, xb, junk, ss, rstd, width, tag):
    s = c.s
    s.op("act", lambda e: e.activation(out=junk, in_=X, func=AF.Square), reads=[xb], writes=["junk" + tag])
    s.op("dve", lambda e: e.tensor_reduce(out=ss, in_=junk, axis=AX.X, op=ALU.add),
         reads=["junk" + tag], writes=["ss" + tag])
    s.op("act", lambda e: e.activation(out=rstd, in_=ss, func=AF.Sqrt, scale=1.0 / width, bias=EPS),
         reads=["ss" + tag], writes=["rstd" + tag])
    s.op("dve", lambda e: e.reciprocal(out=rstd, in_=rstd), reads=["rstd" + tag], writes=["rstd" + tag])


def make_tri(c, tri, name, dtype_one=1.0):
    s = c.s
    s.op("pool", lambda e: e.memset(tri[:], 1.0), writes=[name])
    s.op("pool", lambda e: e.affine_select(out=tri[:], in_=tri[:], pattern=[[1, 128]],
                                           compare_op=ALU.is_ge, fill=0.0, base=0, channel_multiplier=-1),
         reads=[name], writes=[name])


def phase_a(c, l, NSEQ, S, x_d, P, G, scr, do_ssm=True, do_pool=True):
    nc, s = c.nc, c.s
    NJ = S // 128
    c.begin()
    NA = 2056
    win = c.sb("win", [128, 8, NA], BF16)
    stg = [c.sb(f"stg{i}", [128, NA], F32) for i in range(2)]
    gcol = c.sb("gcol", [128, 8], F32)
    ident = c.sb("ident", [128, 128], BF16)
    trif = c.sb("trif", [128, 128], F32)
    onesf = c.sb("onesf", [128, 128], F32)
    xt = [c.sb(f"xt{i}", [128, D], F32) for i in range(3)]
    junk = c.sb("junk", [128, D], F32)
    ss = c.sb("ss", [128, 1], F32)
    rstd = c.sb("rstd", [128, 1], F32)
    hb = c.sb("hb", [128, D], BF16)
    hT = [c.sb(f"hT{i}", [128, 8, 128], BF16) for i in range(2)]
    qkg = c.sb("qkg", [128, 2, 8, DH], F32)
    bfg = c.sb("bfg", [128, 8], F32)
    sq = [c.sb(f"sq{i}", [128, 512], F32) for i in range(2)]
    qe = [c.sb(f"qe{i}", [128, 512], F32) for i in range(2)]
    ssq = [c.sb(f"ssq{i}", [128, 8], F32) for i in range(2)]
    rq = [c.sb(f"rq{i}", [128, 8], F32) for i in range(2)]
    qn = [c.sb(f"qn{i}", [128, 8, DH], F32) for i in range(2)]
    qa = [c.sb(f"qa{i}", [128, 8, 70], BF16) for i in range(2)]
    r1 = c.sb("r1", [128, 8], F32)
    r2 = c.sb("r2", [128, 8], F32)
    qTs = [[c.sb(f"qTs{i}_{b}", [128, 8, 128], BF16) for b in range(2)] for i in range(2)]
    vst = [c.sb(f"vst{b}", [128, 8, 65], BF16) for b in range(2)]
    ust = [c.sb(f"ust{b}", [128, 512], F32) for b in range(2)]
    fl = c.sb("fl", [128, 8], F32)
    sp_ = c.sb("sp_", [128, 8], F32)
    carry = c.sb("carry", [128, 8], F32)
    cumall = G["cumall"]
    pt = c.ps("pt")
    pq = [c.ps(f"pq{i}") for i in range(2)]
    pv = c.ps("pv")
    pf = c.ps("pf")
    pp = c.ps("pp")
    ptq = [c.ps(f"ptq{i}") for i in range(2)]

    make_identity(c, ident)
    make_tri(c, trif, "trif")
    s.op("pool", lambda e: e.memset(onesf[:], 1.0), writes=["onesf"])
    for b in range(2):
        s.op("pool", lambda e, b=b: e.memset(vst[b][:], 1.0), writes=[f"vst{b}"])
    s.dma("sp", gcol[:], P["mix_norm"][l].rearrange("(kc p) -> p kc", p=128), writes=["gcol"],
          allow_slow_non_contiguous=True)
    s.dma("sp", qkg[:, 0, 0, :], P["q_norm"][l:l + 1, :].partition_broadcast(128), writes=["qkg"])
    s.dma("sp", qkg[:, 1, 0, :], P["k_norm"][l:l + 1, :].partition_broadcast(128), writes=["qkg"])
    s.dma("sp", bfg[:], P["b_forget"][l:l + 1, :].partition_broadcast(128), writes=["bfg"])
    s.op("dve", lambda e: e.tensor_scalar(out=qkg[:, 0, 0, :], in0=qkg[:, 0, 0, :], scalar1=DH ** -0.5, scalar2=None,
                                          op0=ALU.mult), reads=["qkg"], writes=["qkg"])
    for h in range(1, 8):
        s.op("dve", lambda e, h=h: e.tensor_copy(out=qkg[:, :, h, :], in_=qkg[:, :, 0, :]),
             reads=["qkg"], writes=["qkg"])
    for kc in range(8):
        st = stg[kc % 2]
        s.dma("sp" if kc % 2 == 0 else "act", st[:], P["w_in"][l, kc * 128:(kc + 1) * 128, 0:NA],
              writes=[f"stg{kc % 2}"])
        _cast(s, _cast_engine(kc), win[:, kc, :], st[:], [f"stg{kc % 2}", "gcol"], [f"win{kc}"],
              scalar=gcol[:, kc:kc + 1])

    def load(n):
        s.dma("sp", xt[n % 3][:], x_d[n * 128:(n + 1) * 128, :], writes=[f"xt{n % 3}"])

    NCH = NSEQ * NJ
    ptv = pt[:].bitcast(BF16)
    pc = pf[:, 272:288]
    for i in range(2):
        s.op("pool", lambda e, i=i: e.memset(qa[i][:], 1.0), writes=[f"qg{i}", f"qaug{i}"])

    def front(n):
        X, xb = xt[n % 3], f"xt{n % 3}"
        HT, htn = hT[n % 2], f"hT{n % 2}"
        ACTF(s, junk[:], X[:], AF.Square, [xb], ["junk"])
        s.op("dve", lambda e: e.tensor_reduce(out=ss[:], in_=junk[:], axis=AX.X, op=ALU.add), reads=["junk"],
             writes=["ss"])
        ACTF(s, rstd[:], ss[:], AF.Ln, ["ss"], ["rstd"], scale=1.0 / D, bias=EPS)
        ACTF(s, rstd[:], rstd[:], AF.Exp, ["rstd"], ["rstd"], scale=-0.5)
        TS(s, "dve", hb[:], X[:], rstd[:, 0:1], ALU.mult, [xb, "rstd"], ["hb"])
        for kc in range(8):
            TR(s, ptv[:, kc * 128:(kc + 1) * 128], hb[:, kc * 128:(kc + 1) * 128], ident[:], ["hb", "ident"], ["pt"])
        CP(s, "act", HT[:], ptv.rearrange("p (k t) -> p k t", k=8), ["pt"], [htn])

    def proj_all(n):
        HT, htn = hT[n % 2], f"hT{n % 2}"

        def proj(Pt, pname, c0, c1):
            for kc in range(8):
                MM(s, Pt, HT[:, kc, :], win[:, kc, c0:c1], [htn, f"win{kc}"], [pname], start=(kc == 0), stop=(kc == 7))
        proj(pq[0][:], "pq0", 0, 512)
        proj(pq[1][:], "pq1", 512, 1024)
        proj(pv[:], "pv", 1024, 1536)
        proj(pf[:, 0:264], "pf", 1536, 1800)
        proj(pp[:, 0:256], "pp", 1800, 2056)

    def post(n):
        q_, j = divmod(n, NJ)
        CP(s, "act", qe[0][:], pq[0][:], ["pq0"], ["qe0"])
        CP(s, "dve", qe[1][:], pq[1][:], ["pq1"], ["qe1"])
        VS = vst[n % 2]
        CP(s, "act", VS[:, :, 0:64], pv[:].rearrange("p (h d) -> p h d", h=8), ["pv"], [f"vst{n % 2}"])
        US = ust[n % 2]
        CP(s, "dve", US[:, 0:256], pf[:, 8:264], ["pf"], [f"ust{n % 2}"])
        TT_(s, "dve", fl[:], pf[:, 0:8], bfg[:], ALU.add, ["pf", "bfg"], ["fl"])
        CP(s, "act", US[:, 256:512], pp[:, 0:256], ["pp"], [f"ust{n % 2}"])
        ACTF(s, fl[:], fl[:], AF.Exp, ["fl"], ["fl"], scale=-1.0)
        ACTF(s, sp_[:], fl[:], AF.Ln, ["fl"], ["sp_"], scale=1.0, bias=1.0)
        if j == 0:
            s.op("pool", lambda e: e.memset(carry[:], 0.0), writes=["carry"])
        MM(s, pc[:, 0:8], trif[:], sp_[:], ["trif", "sp_"], ["pf"])
        MM(s, pc[:, 8:16], onesf[:], sp_[:], ["onesf", "sp_"], ["pf"])
        cum = cumall[:, q_, j, :]
        TT_(s, "dve", cum, carry[:], pc[:, 0:8], ALU.subtract, ["carry", "pf"], ["cumall"])
        TT_(s, "dve", carry[:], carry[:], pc[:, 8:16], ALU.subtract, ["carry", "pf"], ["carry"])
        CP(s, "dve", qa[0][:, :, 67], cum, ["cumall"], ["qaug0"])
        TT_(s, "dve", r1[:], cum, qa[0][:, :, 67], ALU.subtract, ["cumall", "qaug0"], ["r1"])
        CP(s, "dve", qa[0][:, :, 68], r1[:], ["r1"], ["qaug0"])
        TT_(s, "dve", r2[:], r1[:], qa[0][:, :, 68], ALU.subtract, ["r1", "qaug0"], ["r2"])
        CP(s, "dve", qa[0][:, :, 69], r2[:], ["r2"], ["qaug0"])
        TS(s, "dve", qa[1][:, :, 64:67], qa[0][:, :, 67:70], -1.0, ALU.mult, ["qaug0"], ["qaug1"])
        for i in range(2):
            PQ = qe[i]
            ACTF(s, sq[i][:], PQ[:], AF.Square, [f"qe{i}"], [f"sq{i}"])
            s.op("dve", lambda e, i=i: e.tensor_reduce(out=ssq[i][:], in_=sq[i][:].rearrange("p (h d) -> p h d", h=8),
                                                       axis=AX.X, op=ALU.add),
                 reads=[f"sq{i}"], writes=[f"ssq{i}"])
            ACTF(s, rq[i][:], ssq[i][:], AF.Ln, [f"ssq{i}"], [f"rq{i}"], scale=1.0 / DH, bias=EPS)
            ACTF(s, rq[i][:], rq[i][:], AF.Exp, [f"rq{i}"], [f"rq{i}"], scale=-0.5)
            TT_(s, "dve", qn[i][:], PQ[:].rearrange("p (h d) -> p h d", h=8),
                rq[i][:].unsqueeze(2).broadcast_to([128, 8, DH]), ALU.mult, [f"qe{i}", f"rq{i}"], [f"qn{i}"])
            TT_(s, "pool", qa[i][:, :, 0:64], qn[i][:], qkg[:, i, :, :], ALU.mult, [f"qn{i}", "qkg"], [f"qg{i}"])
            pvw = ptq[i][:].bitcast(BF16)
            for h in range(8):
                TR(s, pvw[0:70, h * 128:(h + 1) * 128], qa[i][:, h, :], ident[:], [f"qg{i}", f"qaug{i}", "ident"],
                   [f"ptq{i}"])
            QT = qTs[i][n % 2]
            CP(s, "act" if i == 0 else "dve", QT[0:70, :, :], pvw[0:70, :].rearrange("p (h t) -> p h t", h=8),
               [f"ptq{i}"], [f"qTs{i}_{n % 2}"])
            dst = scr["qT" if i == 0 else "kT"]
            s.dma("pool", dst[q_, :, :, j * 128:(j + 1) * 128].rearrange("h p t -> p h t"), QT[0:70, :, :],
                  reads=[f"qTs{i}_{n % 2}"])
        s.dma("pool", scr["v"][q_, j * 128:(j + 1) * 128, :, :], VS[:], reads=[f"vst{n % 2}"])
        s.dma("pool", scr["u"][q_, j * 128:(j + 1) * 128, :], US[:], reads=[f"ust{n % 2}"])

    load(0)
    if NCH > 1:
        load(1)
    for t in range(NCH + 1):
        if t + 2 < NCH:
            load(t + 2)
        if t >= 1:
            proj_all(t - 1)
        if t < NCH:
            front(t)
        if t >= 1:
            post(t - 1)
    c.end()


def phase_b(c, NSEQ, S, G, scr):
    nc, s = c.nc, c.s
    NJ = S // 128
    NI = NJ // 4
    c.begin()
    tri = c.sb("tri", [128, 128], BF16)
    vall = [c.sb(f"vall{i}", [128, NJ, 8, 65], BF16) for i in range(2)]
    qT = [c.sb(f"qT{i}", [128, S], BF16) for i in range(2)]
    kT = [c.sb(f"kT{i}", [128, S], BF16) for i in range(2)]
    NPT = 4
    pT = [c.sb(f"pT{i}", [128, 512], BF16) for i in range(NPT)]
    ybuf = [c.sb(f"ybuf{i}", [128, NJ, 64], BF16) for i in range(2)]
    rc = [c.sb(f"rc{i}", [128, 1], F32) for i in range(4)]
    NPS = 4
    pS = [c.ps(f"pS{i}") for i in range(NPS)]
    pO = [c.ps(f"pO{i}") for i in range(4)]
    make_tri(c, tri, "tri")
    LOOK = 3
    blocks = []
    for q_ in range(NSEQ):
        for h in range(8):
            for i4 in range(NI):
                for j in range(4 * i4 + 4):
                    blocks.append((q_, h, i4, j))

    def stage1(bi):
        q_, h, i4, j = blocks[bi]
        g = q_ * 8 + h
        QT, KT = qT[g % 2], kT[g % 2]
        if h == 0 and i4 == 0 and j == 0:
            s.dma("sp", vall[q_ % 2][:], scr["v"][q_].rearrange("(j p) h d -> p j h d", p=128), writes=[f"vall{q_ % 2}"])
        if i4 == 0 and j == 0:
            s.dma("sp", QT[0:70, :], scr["qT"][q_, h], writes=[f"qT{g % 2}"])
            s.dma("act", KT[0:70, :], scr["kT"][q_, h], writes=[f"kT{g % 2}"])
        jj = j - 4 * i4
        c0 = 128 * max(jj, 0)
        PS, psn = pS[bi % NPS], f"pS{bi % NPS}"
        PT, ptn = pT[bi % NPT], f"pT{bi % NPT}"
        MM(s, PS[:, c0:512], KT[0:70, j * 128:(j + 1) * 128], QT[0:70, i4 * 512 + c0:(i4 + 1) * 512],
           [f"qT{g % 2}", f"kT{g % 2}"], [psn])
        ACTF(s, PT[:, c0:512], PS[:, c0:512], AF.Exp, [psn], [ptn])
        if jj >= 0:
            TT_(s, "pool", PT[:, c0:c0 + 128], PT[:, c0:c0 + 128], tri[:], ALU.mult, [ptn, "tri"], [ptn])

    def stage2(bi):
        q_, h, i4, j = blocks[bi]
        g = q_ * 8 + h
        PT, ptn = pT[bi % NPT], f"pT{bi % NPT}"
        YB = ybuf[g % 2]
        jj = j - 4 * i4
        for tt in range(max(jj, 0), 4):
            last = (j == 4 * i4 + tt)
            MM(s, pO[tt][:, 0:65], PT[:, tt * 128:(tt + 1) * 128], vall[q_ % 2][:, j, h, :], [ptn, f"vall{q_ % 2}"],
               [f"pO{tt}"], start=(j == 0), stop=last)
            if last:
                s.op("dve", lambda e, tt=tt: e.reciprocal(out=rc[tt][:], in_=pO[tt][:, 64:65]), reads=[f"pO{tt}"],
                     writes=[f"rc{tt}"])
                TS(s, "dve", YB[:, 4 * i4 + tt, :], pO[tt][:, 0:64], rc[tt][:, 0:1], ALU.mult, [f"pO{tt}", f"rc{tt}"],
                   [f"ybuf{g % 2}"])
        if i4 == NI - 1 and j == NJ - 1:
            s.dma("pool", scr["yattn"][q_, :, h * 64:(h + 1) * 64].rearrange("(j p) c -> p j c", p=128), YB[:],
                  reads=[f"ybuf{g % 2}"])

    nb = len(blocks)
    for t in range(nb + LOOK):
        if t < nb:
            stage1(t)
        if t >= LOOK:
            stage2(t - LOOK)
    c.end()


def make_scratch(nc, NSEQ, S, debug_out=False):
    kind = {"kind": "ExternalOutput"} if debug_out else {}
    scr = {}
    scr["qT"] = nc.dram_tensor("scr_qT", [NSEQ, 8, 70, S], BF16).ap()
    scr["kT"] = nc.dram_tensor("scr_kT", [NSEQ, 8, 70, S], BF16).ap()
    scr["v"] = nc.dram_tensor("scr_v", [NSEQ, S, 8, 65], BF16).ap()
    scr["cend"] = nc.dram_tensor("scr_cend", [1, NSEQ, S // 128, 8], F32).ap()
    scr["u"] = nc.dram_tensor("scr_u", [NSEQ, S, 512], F32).ap()
    scr["yattn"] = nc.dram_tensor("scr_yattn", [NSEQ, S, 512], BF16, **kind).ap()
    scr["yssmT"] = nc.dram_tensor("scr_yssmT", [NSEQ, 256, S], BF16, **kind).ap()
    scr["ypoolT"] = nc.dram_tensor("scr_ypoolT", [NSEQ, 256, S], BF16, **kind).ap()
    return scr


PARAM_SHAPES = {
    "mix_norm": [DEPTH, D], "w_in": [DEPTH, D, INC], "b_forget": [DEPTH, NH], "q_norm": [DEPTH, DH],
    "k_norm": [DEPTH, DH], "ssm_a_re": [DEPTH, 16, 64], "ssm_a_im": [DEPTH, 16, 64], "ssm_log_dt": [DEPTH, 16],
    "ssm_b_re": [DEPTH, 16, 64, 16], "ssm_b_im": [DEPTH, 16, 64, 16], "ssm_c_re": [DEPTH, 16, 16, 64],
    "ssm_c_im": [DEPTH, 16, 16, 64], "ssm_d": [DEPTH, 256], "w_glu": [DEPTH, 256, 512],
    "pool_w": [DEPTH, 4, 64, 64], "pool_scale": [DEPTH, 256], "w_br_attn": [DEPTH, 512, D],
    "w_br_ssm": [DEPTH, 256, D], "w_br_pool": [DEPTH, 256, D], "b_gate": [DEPTH, 3 * D],
    "w_out": [DEPTH, D, D], "mlp_norm": [DEPTH, D], "w_up": [DEPTH, D, DFF], "w_down": [DEPTH, DFF, D],
}


def declare_params(nc):
    return {k: nc.dram_tensor(k, shp, F32, kind="ExternalInput").ap() for k, shp in PARAM_SHAPES.items()}


def build_test_ab(NSEQ, S, l=0):
    nc = bass.Bass("TRN2", target_bir_lowering=False)
    x = nc.dram_tensor("x", [NSEQ * S, D], F32, kind="ExternalInput").ap()
    P = declare_params(nc)
    scr = make_scratch(nc, NSEQ, S, debug_out=True)
    with contextlib.ExitStack() as es:
        c = Ctx(nc, es)
        G = {"cumall": es.enter_context(nc.sbuf_tensor("cumall", [128, NSEQ, S // 128, 8], F32))}
        phase_a(c, l, NSEQ, S, x, P, G, scr, do_ssm=False, do_pool=False)
        phase_b(c, NSEQ, S, G, scr)
        print("ops", c.s.total, "waits", c.s.nwaits)
    return nc


def TT_(s, eng, out, in0, in1, op, reads, writes):
    return s.op(eng, lambda e: e.tensor_tensor(out=out, in0=in0, in1=in1, op=op), reads, writes)


def TS(s, eng, out, in0, scalar1, op0, reads, writes, scalar2=None, op1=None):
    if op1 is None:
        return s.op(eng, lambda e: e.tensor_scalar(out=out, in0=in0, scalar1=scalar1, scalar2=None, op0=op0),
                    reads, writes)
    return s.op(eng, lambda e: e.tensor_scalar(out=out, in0=in0, scalar1=scalar1, scalar2=scalar2, op0=op0, op1=op1),
                reads, writes)


def ACTF(s, out, in_, func, reads, writes, scale=1.0, bias=None):
    if bias is None:
        return s.op("act", lambda e: e.activation(out=out, in_=in_, func=func, scale=scale), reads, writes)
    return s.op("act", lambda e: e.activation(out=out, in_=in_, func=func, scale=scale, bias=bias), reads, writes)


def CP(s, eng, out, in_, reads, writes):
    if eng == "act":
        return s.op("act", lambda e: e.copy(out=out, in_=in_), reads, writes)
    return s.op(eng, lambda e: e.tensor_copy(out=out, in_=in_), reads, writes)


def MM(s, out, lhsT, rhs, reads, writes, start=True, stop=True):
    return s.op("pe", lambda e: e.matmul(out, lhsT=lhsT, rhs=rhs, start=start, stop=stop), reads, writes)


def TR(s, out, in_, ident, reads, writes):
    return s.op("pe", lambda e: e.transpose(out=out, in_=in_, identity=ident), reads, writes)


TWO_PI = 2.0 * math.pi
MAGIC = 12582912.0


def sincos(s, eng, ang, o_sin, o_cos, t1, t2, rd, tag):
    n1, n2 = "sc1" + tag, "sc2" + tag
    for which, o in (("s", o_sin), ("c", o_cos)):
        if which == "s":
            TS(s, eng, t1, ang, 1.0 / TWO_PI, ALU.mult, rd, [n1])
        else:
            TS(s, eng, t1, ang, 1.0 / TWO_PI, ALU.mult, rd, [n1], scalar2=0.25, op1=ALU.add)
        TS(s, eng, t2, t1, MAGIC, ALU.add, [n1], [n2])
        TS(s, eng, t2, t2, MAGIC, ALU.subtract, [n2], [n2])
        TT_(s, eng, t2, t1, t2, ALU.subtract, [n1, n2], [n2])
        ACTF(s, o, t2, AF.Sin, [n2], [("sin" if which == "s" else "cos") + tag], scale=TWO_PI * (1 - 1e-6))


POOL_WINDOWS = (2, 4, 8, 16)


def phase_a2(c, l, NSEQ, S, P, scr):
    nc, s = c.nc, c.s
    NJ = S // 128
    c.begin()
    I32 = mybir.dt.int32
    ident = c.sb("ident", [128, 128], BF16)
    identf = c.sb("identf", [128, 128], F32)
    tribf = c.sb("tribf", [128, 128], BF16)
    iot_i = c.sb("iot_i", [128, 128], I32)
    iot = c.sb("iot", [128, 128], F32)
    pcol_i = c.sb("pcol_i", [128, 1], I32)
    pcol = c.sb("pcol", [128, 1], F32)
    npcol = c.sb("npcol", [128, 1], F32)
    are = c.sb("are", [128, 1024], F32)
    aim = c.sb("aim", [128, 1024], F32)
    ldt = c.sb("ldt", [128, 16], F32)
    dtr = c.sb("dtr", [128, 1024], F32)
    t1 = c.sb("t1", [128, 1024], F32)
    t2 = c.sb("t2", [128, 1024], F32)
    t3 = c.sb("t3", [128, 1024], F32)
    t4 = c.sb("t4", [128, 1024], F32)
    t5 = c.sb("t5", [128, 1024], F32)
    t6 = c.sb("t6", [128, 1024], F32)
    cfr = c.sb("cfr", [128, 1024], F32)
    cfi = c.sb("cfi", [128, 1024], F32)
    Tr = c.sb("Tr", [128, 1024], F32)
    Ti = c.sb("Ti", [128, 1024], F32)
    TAr = c.sb("TAr", [128, 8, 128], F32)
    TAi = c.sb("TAi", [128, 8, 128], F32)
    acol = c.sb("acol", [128, 3, 8], F32)
    BX = c.sb("BX", [128, 8, 2, 128], F32)
    BXb = c.sb("BXb", [128, 8, 2, 128], BF16)
    BT = c.sb("BT", [128, 8, 2, 128], BF16)
    CN = c.sb("CN", [32, 8, 2, 128], F32)
    CNb = c.sb("CNb", [32, 8, 2, 128], BF16)
    CX = c.sb("CX", [128, 8, 2, 32], BF16)
    drb = c.sb("drb", [128, 256], F32)
    wg_st = c.sb("wg_st", [128, 2, 512], F32)
    wglu = c.sb("wglu", [128, 2, 512], BF16)
    pw_st = c.sb("pw_st", [128, 2, 64], F32)
    PW = c.sb("PW", [128, 2, 64], BF16)
    pscol = c.sb("pscol", [128, 2], F32)
    MT = c.sb("MT", [128, 12, 128], BF16)
    mtmp = c.sb("mtmp", [128, 128], F32)
    mrat = c.sb("mrat", [128, 128], F32)
    ut = [c.sb(f"ut{i}", [128, 512], F32) for i in range(3)]
    ub = [c.sb(f"ub{i}", [128, 512], BF16) for i in range(3)]
    uT = c.sb("uT", [128, 2, 128], BF16)
    m1 = c.sb("m1", [128, 4, 128], F32)
    m2 = c.sb("m2", [128, 4, 128], F32)
    m3 = c.sb("m3", [128, 4, 128], F32)
    m4 = c.sb("m4", [128, 4, 128], F32)
    Wt = c.sb("Wt", [128, 8, 2, 128], BF16)
    Pr = c.sb("Pr", [128, 8, 128], F32)
    Pi = c.sb("Pi", [128, 8, 128], F32)
    Xt = c.sb("Xt", [128, 8, 2, 128], BF16)
    car = c.sb("car", [128, 2, 8], F32)
    sn = c.sb("sn", [128, 2, 4], F32)
    sm = c.sb("sm", [128, 4, 4], F32)
    du = c.sb("du", [128, 256], F32)
    yv = c.sb("yv", [128, 256], F32)
    y2 = c.sb("y2", [128, 256], F32)
    sg = c.sb("sg", [128, 256], F32)
    gy = c.sb("gy", [128, 256], BF16)
    gyT = c.sb("gyT", [128, 2, 128], BF16)
    sgb = c.sb("sgb", [128, 2, 128], F32)
    ysT = [c.sb(f"ysT{i}", [128, 2, 128], BF16) for i in range(2)]
    plT = c.sb("plT", [128, 2, 128], BF16)
    ypT = [c.sb(f"ypT{i}", [128, 2, 128], BF16) for i in range(2)]
    pbu = c.ps("pbu")
    ppf = c.ps("ppf", [128, 2048])
    pym = c.ps("pym")
    pglu = c.ps("pglu")
    ppl = c.ps("ppl")
    py = pym
    w1 = c.sb("w1", [128, 2, 128], F32)
    w2 = c.sb("w2", [128, 2, 128], F32)
    w3 = c.sb("w3", [128, 2, 128], F32)
    w4 = c.sb("w4", [128, 2, 128], F32)

    make_identity(c, ident)
    make_tri(c, tribf, "tribf")
    s.op("pool", lambda e: e.iota(iot_i[:], pattern=[[1, 128]], base=0, channel_multiplier=0), writes=["iot_i"])
    CP(s, "dve", iot[:], iot_i[:], ["iot_i"], ["iot"])
    s.op("pool", lambda e: e.iota(pcol_i[:], pattern=[[0, 1]], base=0, channel_multiplier=1), writes=["pcol_i"])
    CP(s, "dve", pcol[:], pcol_i[:], ["pcol_i"], ["pcol"])
    TS(s, "dve", npcol[:], pcol[:], -1.0, ALU.mult, ["pcol"], ["npcol"])
    CP(s, "dve", identf[:], ident[:], ["ident"], ["identf"])

    s.dma("sp", are[:], P["ssm_a_re"][l:l + 1].rearrange("o g n -> o (g n)").partition_broadcast(128), writes=["are"])
    s.dma("act", aim[:], P["ssm_a_im"][l:l + 1].rearrange("o g n -> o (g n)").partition_broadcast(128), writes=["aim"])
    s.dma("sp", ldt[:], P["ssm_log_dt"][l:l + 1, :].partition_broadcast(128), writes=["ldt"])
    s.dma("sp", acol[:, 0, :], P["ssm_a_re"][l].rearrange("(k gl) n -> (gl n) k", gl=2), writes=["acol"],
          allow_slow_non_contiguous=True)
    s.dma("sp", acol[:, 1, :], P["ssm_a_im"][l].rearrange("(k gl) n -> (gl n) k", gl=2), writes=["acol"],
          allow_slow_non_contiguous=True)
    for gl in range(2):
        s.dma("sp", acol[64 * gl:64 * gl + 64, 2, :],
              P["ssm_log_dt"][l:l + 1, :].rearrange("o (k gl) -> o k gl", gl=2)[:, :, gl].partition_broadcast(64),
              writes=["acol"], allow_slow_non_contiguous=True)
    s.dma("sp", drb[:], P["ssm_d"][l:l + 1, :].partition_broadcast(128), writes=["drb"])
    s.dma("act", wg_st[:], P["w_glu"][l].rearrange("(kc p) g -> p kc g", p=128), writes=["wg_st"])
    CP(s, "pool", wglu[:], wg_st[:], ["wg_st"], ["wglu"])
    for g in range(4):
        s.dma("sp", pw_st[64 * (g % 2):64 * (g % 2) + 64, g // 2, :], P["pool_w"][l, g], writes=["pw_st"])
    CP(s, "pool", PW[:], pw_st[:], ["pw_st"], ["PW"])
    s.dma("sp", pscol[:], P["pool_scale"][l].rearrange("(kc p) -> p kc", p=128), writes=["pscol"],
          allow_slow_non_contiguous=True)
    s.op("pool", lambda e: e.memset(BX[:], 0.0), writes=["BX"])
    s.op("pool", lambda e: e.memset(CN[:], 0.0), writes=["CN"])
    nd = 0
    for k in range(8):
        for gl in range(2):
            g = 2 * k + gl
            c0 = 32 * (k % 4) + 16 * gl
            for part, nm in ((0, "ssm_b_re"), (1, "ssm_b_im")):
                s.dma("sp" if nd % 2 == 0 else "act", BX[64 * gl:64 * gl + 64, k, part, c0:c0 + 16], P[nm][l, g],
                      reads=[], writes=["BX"])
                nd += 1
            for part, nm in ((0, "ssm_c_re"), (1, "ssm_c_im")):
                s.dma("sp" if nd % 2 == 0 else "act", CN[16 * gl:16 * gl + 16, k, part, 64 * gl:64 * gl + 64],
                      P[nm][l, g], reads=[], writes=["CN"])
                nd += 1
    CP(s, "dve", BXb[:], BX[:], ["BX"], ["BXb"])
    CP(s, "dve", CNb[:], CN[:], ["CN"], ["CNb"])
    pmv = ppf[:, 0:512].bitcast(BF16)
    for k in range(8):
        for part in range(2):
            i = (k * 2 + part) % 8
            TR(s, pmv[:, i * 128:(i + 1) * 128], BXb[:, k, part, :], ident[:], ["BXb", "ident"], ["ppf"])
        if k % 4 == 3:
            k0 = k - 3
            CP(s, "dve", BT[:, k0:k0 + 4, :, :], pmv.rearrange("p (k a m) -> p k a m", k=4, a=2), ["ppf"], ["BT"])
    for k in range(8):
        for part in range(2):
            i = k * 2 + part
            TR(s, pmv[:, i * 32:(i + 1) * 32], CNb[:, k, part, :], ident[0:32, 0:32], ["CNb", "ident"], ["ppf"])
    cxv = pmv[:, 0:512].rearrange("p (k a m) -> p k a m", k=8, a=2)
    CP(s, "dve", CX[:, :, 0, :], cxv[:, :, 0, :], ["ppf"], ["CX"])
    TS(s, "dve", CX[:, :, 1, :], cxv[:, :, 1, :], -1.0, ALU.mult, ["ppf"], ["CX"])

    ACTF(s, ldt[:], ldt[:], AF.Exp, ["ldt"], ["ldt"])
    CP(s, "dve", dtr[:].rearrange("p (g n) -> p g n", g=16), ldt[:].unsqueeze(2).broadcast_to([128, 16, 64]),
       ["ldt"], ["dtr"])
    ardt, wr = t5, t6
    TT_(s, "dve", ardt[:], are[:], dtr[:], ALU.mult, ["are", "dtr"], ["ardt"])
    TT_(s, "dve", wr[:], aim[:], dtr[:], ALU.mult, ["aim", "dtr"], ["wr"])
    sincos(s, "dve", wr[:], t3[:], t4[:], t1[:], t2[:], ["wr"], "0")
    ACTF(s, t1[:], ardt[:], AF.Exp, ["ardt", "sc10"], ["mag1"])
    TT_(s, "dve", t4[:], t4[:], t1[:], ALU.mult, ["cos0", "mag1"], ["cos0"])
    TT_(s, "dve", t3[:], t3[:], t1[:], ALU.mult, ["sin0", "mag1"], ["sin0"])
    TS(s, "dve", t4[:], t4[:], -1.0, ALU.add, ["cos0"], ["cos0"])
    TT_(s, "dve", t1[:], are[:], are[:], ALU.mult, ["are", "mag1", "sin0"], ["mag1"])
    TT_(s, "dve", t2[:], aim[:], aim[:], ALU.mult, ["aim", "sc20"], ["sc20"])
    TT_(s, "dve", t1[:], t1[:], t2[:], ALU.add, ["mag1", "sc20"], ["mag1"])
    s.op("dve", lambda e: e.reciprocal(out=t1[:], in_=t1[:]), reads=["mag1"], writes=["mag1"])
    TT_(s, "dve", cfr[:], t4[:], are[:], ALU.mult, ["cos0", "are"], ["cfr"])
    TT_(s, "dve", t2[:], t3[:], aim[:], ALU.mult, ["sin0", "aim", "sc20"], ["sc20"])
    TT_(s, "dve", cfr[:], cfr[:], t2[:], ALU.add, ["cfr", "sc20"], ["cfr"])
    TT_(s, "dve", cfr[:], cfr[:], t1[:], ALU.mult, ["cfr", "mag1"], ["cfr"])
    TT_(s, "dve", cfi[:], t3[:], are[:], ALU.mult, ["sin0", "are"], ["cfi"])
    TT_(s, "dve", t2[:], t4[:], aim[:], ALU.mult, ["cos0", "aim", "sc20", "cfr"], ["sc20"])
    TT_(s, "dve", cfi[:], cfi[:], t2[:], ALU.subtract, ["cfi", "sc20"], ["cfi"])
    TT_(s, "dve", cfi[:], cfi[:], t1[:], ALU.mult, ["cfi", "mag1"], ["cfi"])
    TS(s, "dve", dtr[:], wr[:], pcol[:, 0:1], ALU.mult, ["wr", "pcol", "dtr"], ["ang"])
    sincos(s, "dve", dtr[:], t3[:], t4[:], t1[:], t2[:], ["ang", "cfi", "cfr"], "1")
    s.op("act", lambda e: e.activation(out=t1[:], in_=ardt[:], func=AF.Exp, scale=npcol[:, 0:1]),
         reads=["ardt", "npcol", "sc11", "cos1"], writes=["mag2"])
    TT_(s, "dve", t4[:], t4[:], t1[:], ALU.mult, ["cos1", "mag2"], ["cos1"])
    TT_(s, "dve", t3[:], t3[:], t1[:], ALU.mult, ["sin1", "mag2"], ["sin1"])
    TT_(s, "dve", Tr[:], t4[:], cfr[:], ALU.mult, ["cos1", "cfr"], ["Tr"])
    TT_(s, "dve", t2[:], t3[:], cfi[:], ALU.mult, ["sin1", "cfi", "sc21"], ["sc21"])
    TT_(s, "dve", Tr[:], Tr[:], t2[:], ALU.add, ["Tr", "sc21"], ["Tr"])
    TT_(s, "dve", Ti[:], t4[:], cfi[:], ALU.mult, ["cos1", "cfi"], ["Ti"])
    TT_(s, "dve", t2[:], t3[:], cfr[:], ALU.mult, ["sin1", "cfr", "Tr"], ["sc21"])
    TT_(s, "dve", Ti[:], Ti[:], t2[:], ALU.subtract, ["Ti", "sc21"], ["Ti"])
    ACTF(s, acol[:, 2, :], acol[:, 2, :], AF.Exp, ["acol"], ["acol"])
    TT_(s, "dve", acol[:, 0, :], acol[:, 0, :], acol[:, 2, :], ALU.mult, ["acol"], ["acol"])
    TT_(s, "dve", acol[:, 1, :], acol[:, 1, :], acol[:, 2, :], ALU.mult, ["acol"], ["acol"])
    angf = t5[:].rearrange("p (k t) -> p k t", k=8)
    magf = t6[:].rearrange("p (k t) -> p k t", k=8)
    for k in range(8):
        TS(s, "dve", angf[:, k, :], iot[:], acol[:, 1, k:k + 1], ALU.mult, ["iot", "acol", "ardt", "Tr", "Ti"], ["angf"])
        s.op("act", lambda e, k=k: e.activation(out=magf[:, k, :], in_=iot[:], func=AF.Exp, scale=acol[:, 0, k:k + 1]),
             reads=["iot", "acol", "wr", "ang", "Tr", "Ti"], writes=["magf"])
    sincos(s, "dve", t5[:], t3[:], t4[:], t1[:], t2[:], ["angf", "Tr", "Ti"], "2")
    TT_(s, "dve", TAr[:].rearrange("p k t -> p (k t)"), t4[:], t6[:], ALU.mult, ["cos2", "magf"], ["TAr"])
    TT_(s, "dve", TAi[:].rearrange("p k t -> p (k t)"), t3[:], t6[:], ALU.mult, ["sin2", "magf"], ["TAi"])

    for g, w in enumerate(POOL_WINDOWS):
        s.op("pool", lambda e, w=w: e.memset(mtmp[:], 1.0 / w), reads=["mtmp", "mrat", "MT"], writes=["mtmp"])
        s.op("pool", lambda e: e.affine_select(out=mtmp[:], in_=mtmp[:], pattern=[[1, 128]], compare_op=ALU.is_ge,
                                               fill=0.0, base=0, channel_multiplier=-1),
             reads=["mtmp"], writes=["mtmp"])
        s.op("pool", lambda e, w=w: e.affine_select(out=mtmp[:], in_=mtmp[:], pattern=[[-1, 128]],
                                                    compare_op=ALU.is_ge, fill=0.0, base=w - 1, channel_multiplier=1),
             reads=["mtmp"], writes=["mtmp"])
        TT_(s, "pool", MT[:, g * 3 + 0, :], mtmp[:], identf[:], ALU.subtract, ["mtmp", "identf"], ["MT"])
        TS(s, "dve", mrat[:], iot[:], 1.0, ALU.add, ["iot"], ["mrat"])
        s.op("dve", lambda e: e.reciprocal(out=mrat[:], in_=mrat[:]), reads=["mrat"], writes=["mrat"])
        TS(s, "dve", mrat[:], mrat[:], float(w), ALU.mult, ["mrat"], ["mrat"], scalar2=1.0, op1=ALU.max)
        TT_(s, "dve", mrat[:], mrat[:], mtmp[:], ALU.mult, ["mrat", "mtmp"], ["mrat"])
        TT_(s, "dve", MT[:, g * 3 + 2, :], mrat[:], identf[:], ALU.subtract, ["mrat", "identf"], ["MT"])
        s.op("pool", lambda e, w=w: e.memset(mtmp[:], 1.0 / w), reads=["mtmp", "mrat", "MT"], writes=["mtmp"])
        s.op("pool", lambda e, w=w: e.affine_select(out=mtmp[:], in_=mtmp[:], pattern=[[-1, 128]],
                                                    compare_op=ALU.is_ge, fill=0.0, base=-(129 - w),
                                                    channel_multiplier=1),
             reads=["mtmp"], writes=["mtmp"])
        CP(s, "pool", MT[:, g * 3 + 1, :], mtmp[:], ["mtmp"], ["MT"])

    NCH = NSEQ * NJ
    pmq = pym[:, 256:512].bitcast(BF16)

    def load(n):
        q_, j = divmod(n, NJ)
        s.dma("sp", ut[n % 3][:], scr["u"][q_, j * 128:(j + 1) * 128, :], writes=[f"ut{n % 3}"])

    def S1(n):
        U, UB = ut[n % 3], ub[n % 3]
        un, ubn = f"ut{n % 3}", f"ub{n % 3}"
        CP(s, "act", UB[:], U[:], [un], [ubn])
        for kc in range(2):
            TR(s, pmq[:, kc * 128:(kc + 1) * 128], UB[:, kc * 128:(kc + 1) * 128], ident[:], [ubn, "ident"], ["pym"])
        CP(s, "dve", uT[:], pmq[:, 0:256].rearrange("p (k t) -> p k t", k=2), ["pym"], ["uT"])
        for qt in range(4):
            k0 = qt * 2
            for kk in range(2):
                k = k0 + kk
                MM(s, pbu[:, kk * 256:(kk + 1) * 256], uT[:, k // 4, :], BT[:, k, :, :].rearrange("p a m -> p (a m)"),
                   ["uT", "BT"], ["pbu"])
            buv = pbu[:].rearrange("p (k a m) -> p k a m", k=2, a=2)
            trv = Tr[:, k0 * 128:(k0 + 2) * 128].rearrange("p (k m) -> p k m", k=2)
            tiv = Ti[:, k0 * 128:(k0 + 2) * 128].rearrange("p (k m) -> p k m", k=2)
            yield
            TT_(s, "dve", w1[:], buv[:, :, 0, :], trv, ALU.mult, ["pbu", "Tr"], ["w1"])
            TT_(s, "dve", w2[:], buv[:, :, 1, :], tiv, ALU.mult, ["pbu", "Ti"], ["w2"])
            yield
            TT_(s, "pool", Wt[:, k0:k0 + 2, 0, :], w1[:], w2[:], ALU.subtract, ["w1", "w2"], [f"Wt{qt}"])
            TT_(s, "dve", w3[:], buv[:, :, 1, :], trv, ALU.mult, ["pbu", "Tr"], ["w3"])
            TT_(s, "dve", w4[:], buv[:, :, 0, :], tiv, ALU.mult, ["pbu", "Ti"], ["w4"])
            yield
            TT_(s, "pool", Wt[:, k0:k0 + 2, 1, :], w3[:], w4[:], ALU.add, ["w3", "w4"], [f"Wt{qt}"])
            yield
            for kk in range(2):
                for part in range(2):
                    i = (k0 + kk) * 2 + part
                    MM(s, ppf[:, i * 128:(i + 1) * 128], Wt[:, k0 + kk, part, :], tribf[:], [f"Wt{qt}", "tribf"],
                       [f"ppf{qt // 2}"])

    def XP(n):
        q_, j = divmod(n, NJ)
        if j == 0:
            s.op("pool", lambda e: e.memset(car[:], 0.0), writes=["car"])
        for hf in range(2):
            k0 = hf * 4
            pfv = ppf[:, hf * 1024:(hf + 1) * 1024].rearrange("p (k a t) -> p k a t", k=4, a=2)
            pfn = f"ppf{hf}"
            TT_(s, "dve", Pr[:, k0:k0 + 4, :], pfv[:, :, 0, :],
                car[:, 0, k0:k0 + 4].unsqueeze(2).broadcast_to([128, 4, 128]), ALU.add, [pfn, "car"], [f"Pr{hf}"])
            TT_(s, "dve", Pi[:, k0:k0 + 4, :], pfv[:, :, 1, :],
                car[:, 1, k0:k0 + 4].unsqueeze(2).broadcast_to([128, 4, 128]), ALU.add, [pfn, "car"], [f"Pi{hf}"])
        yield
        for hf in range(2):
            k0 = hf * 4
            prn, pin = f"Pr{hf}", f"Pi{hf}"
            PR, PI = Pr[:, k0:k0 + 4, :], Pi[:, k0:k0 + 4, :]
            tar, tai = TAr[:, k0:k0 + 4, :], TAi[:, k0:k0 + 4, :]
            TT_(s, "dve", m1[:], tar, PR, ALU.mult, ["TAr", prn], ["m1"])
            TT_(s, "pool", m2[:], tai, PI, ALU.mult, ["TAi", pin], ["m2"])
            yield
            TT_(s, "dve", m3[:], tar, PI, ALU.mult, ["TAr", pin], ["m3"])
            TT_(s, "pool", m4[:], tai, PR, ALU.mult, ["TAi", prn], ["m4"])
            yield
            TT_(s, "pool", Xt[:, k0:k0 + 4, 0, :], m1[:], m2[:], ALU.subtract, ["m1", "m2"], [f"Xt{hf}"])
            TT_(s, "dve", sn[:, 0, :], m1[:, :, 127], m2[:, :, 127], ALU.subtract, ["m1", "m2"], ["sn"])
            yield
            TT_(s, "dve", Xt[:, k0:k0 + 4, 1, :], m3[:], m4[:], ALU.add, ["m3", "m4"], [f"Xt{hf}"])
            TT_(s, "dve", sn[:, 1, :], m3[:, :, 127], m4[:, :, 127], ALU.add, ["m3", "m4"], ["sn"])
            yield
            a1r, a1i = TAr[:, k0:k0 + 4, 1], TAi[:, k0:k0 + 4, 1]
            TT_(s, "dve", sm[:, 0, :], a1r, sn[:, 0, :], ALU.mult, ["TAr", "sn"], ["sm"])
            TT_(s, "dve", sm[:, 1, :], a1i, sn[:, 1, :], ALU.mult, ["TAi", "sn"], ["sm"])
            TT_(s, "dve", sm[:, 2, :], a1r, sn[:, 1, :], ALU.mult, ["TAr", "sn"], ["sm"])
            TT_(s, "dve", sm[:, 3, :], a1i, sn[:, 0, :], ALU.mult, ["TAi", "sn"], ["sm"])
            yield
            TT_(s, "dve", car[:, 0, k0:k0 + 4], sm[:, 0, :], sm[:, 1, :], ALU.subtract, ["sm"], ["car"])
            TT_(s, "dve", car[:, 1, k0:k0 + 4], sm[:, 2, :], sm[:, 3, :], ALU.add, ["sm"], ["car"])
            yield

    def REST(n):
        q_, j = divmod(n, NJ)
        U, UB = ut[n % 3], ub[n % 3]
        un, ubn = f"ut{n % 3}", f"ub{n % 3}"
        UBP, ubpn = ub[(n - 1) % 3], f"ub{(n - 1) % 3}"
        for k in range(8):
            for part in range(2):
                MM(s, py[:, 32 * k:32 * k + 32], Xt[:, k, part, :], CX[:, k, part, :], [f"Xt{k // 4}", "CX"], ["pym"],
                   start=(part == 0), stop=(part == 1))
        yield
        TT_(s, "pool", du[:], U[:, 0:256], drb[:], ALU.mult, [un, "drb"], ["du"])
        yield
        TT_(s, "dve", yv[:], py[:, 0:256], du[:], ALU.add, ["pym", "du"], ["yv"])
        yield
        TT_(s, "pool", y2[:], yv[:], yv[:], ALU.mult, ["yv"], ["y2"])
        yield
        TS(s, "pool", y2[:], y2[:], 0.044715, ALU.mult, ["y2"], ["y2"], scalar2=1.0, op1=ALU.add)
        yield
        TT_(s, "pool", y2[:], y2[:], yv[:], ALU.mult, ["y2", "yv"], ["y2"])
        ACTF(s, sg[:], y2[:], AF.Sigmoid, ["y2"], ["sg"], scale=1.5957691216057308)
        yield
        TT_(s, "dve", gy[:], yv[:], sg[:], ALU.mult, ["yv", "sg"], ["gy"])
        yield
        for kc in range(2):
            TR(s, pmq[:, 256 + kc * 128:256 + (kc + 1) * 128], gy[:, kc * 128:(kc + 1) * 128], ident[:],
               ["gy", "ident"], ["pym"])
        CP(s, "act", gyT[:], pmq[:, 256:512].rearrange("p (k t) -> p k t", k=2), ["pym"], ["gyT"])
        for gc in range(4):
            for kc in range(2):
                MM(s, pglu[:, gc * 128:(gc + 1) * 128], wglu[:, kc, gc * 128:(gc + 1) * 128], gyT[:, kc, :],
                   ["wglu", "gyT"], ["pglu"], start=(kc == 0), stop=(kc == 1))
        yield
        ACTF(s, sgb[:], pglu[:, 256:512].rearrange("p (k t) -> p k t", k=2), AF.Sigmoid, ["pglu"], ["sgb"])
        yield
        YS = ysT[n % 2]
        TT_(s, "dve", YS[:], pglu[:, 0:256].rearrange("p (k t) -> p k t", k=2), sgb[:], ALU.mult, ["pglu", "sgb"],
            [f"ysT{n % 2}"])
        s.dma("pool", scr["yssmT"][q_, :, j * 128:(j + 1) * 128].rearrange("(kc p) t -> p kc t", p=128), YS[:],
              reads=[f"ysT{n % 2}"])
        yield
        for g in range(4):
            o = ppl[64 * (g % 2):64 * (g % 2) + 64, (g // 2) * 128:(g // 2 + 1) * 128]
            if j == 0:
                MM(s, o, UB[:, 256 + 64 * g:256 + 64 * g + 64], MT[:, g * 3 + 2, :], [ubn, "MT"], ["ppl"])
            else:
                MM(s, o, UB[:, 256 + 64 * g:256 + 64 * g + 64], MT[:, g * 3 + 0, :], [ubn, "MT"], ["ppl"],
                   start=True, stop=False)
                MM(s, o, UBP[:, 256 + 64 * g:256 + 64 * g + 64], MT[:, g * 3 + 1, :], [ubpn, "MT"], ["ppl"],
                   start=False, stop=True)
        CP(s, "act", plT[:], ppl[:, 0:256].rearrange("p (k t) -> p k t", k=2), ["ppl"], ["plT"])
        for g in range(4):
            pb = 64 * (g % 2)
            MM(s, ppl[pb:pb + 64, 256 + (g // 2) * 128:256 + (g // 2 + 1) * 128], PW[pb:pb + 64, g // 2, :],
               plT[pb:pb + 64, g // 2, :], ["PW", "plT"], ["ppl"])
        yield
        YP = ypT[n % 2]
        for kc in range(2):
            TS(s, "dve", YP[:, kc, :], ppl[:, 256 + kc * 128:256 + (kc + 1) * 128], pscol[:, kc:kc + 1], ALU.mult,
               ["ppl", "pscol"], [f"ypT{n % 2}"])
        s.dma("pool", scr["ypoolT"][q_, :, j * 128:(j + 1) * 128].rearrange("(kc p) t -> p kc t", p=128), YP[:],
              reads=[f"ypT{n % 2}"])

    load(0)
    if NCH > 1:
        load(1)
    def chain(*gs):
        for g in gs:
            yield from g

    for t in range(-1, NCH):
        if 0 <= t + 2 < NCH and t + 2 >= 2:
            load(t + 2)
        gens = []
        if t >= 0:
            gens.append(chain(XP(t), REST(t)))
        if t + 1 < NCH:
            gens.append(S1(t + 1))
        while gens:
            for g in list(gens):
                try:
                    next(g)
                except StopIteration:
                    gens.remove(g)
    c.end()


def build_test_a2(NSEQ, S, l=0):
    nc = bass.Bass("TRN2", target_bir_lowering=False)
    u = nc.dram_tensor("u_in", [NSEQ, S, 512], F32, kind="ExternalInput").ap()
    P = declare_params(nc)
    scr = make_scratch(nc, NSEQ, S, debug_out=True)
    scr["u"] = u
    with contextlib.ExitStack() as es:
        c = Ctx(nc, es)
        phase_a2(c, l, NSEQ, S, P, scr)
        print("ops", c.s.total, "waits", c.s.nwaits)
    return nc


def phase_c(c, l, NSEQ, S, x_d, xout_d, P, scr):
    nc, s = c.nc, c.s
    NJ = S // 128
    c.begin()
    wg = c.sb("wg", [128, 8, 3 * D], BF16)
    wba = c.sb("wba", [128, 4, D], BF16)
    wbs = c.sb("wbs", [128, 2, D], BF16)
    wbp = c.sb("wbp", [128, 2, D], BF16)
    wout = c.sb("wout", [128, 8, D], BF16)
    bg = c.sb("bg", [128, 3 * D], F32)
    stg = [c.sb(f"stg{i}", [128, 3 * D], F32) for i in range(2)]
    gcol = c.sb("gcol", [128, 8], F32)
    ident = c.sb("ident", [128, 128], BF16)
    xt = [c.sb(f"xt{i}", [128, D], F32) for i in range(4)]
    junk = c.sb("junk", [128, D], F32)
    ss = c.sb("ss", [128, 1], F32)
    rstd = c.sb("rstd", [128, 1], F32)
    hb = c.sb("hb", [128, D], BF16)
    hT = [c.sb(f"hT{i}", [128, 8, 128], BF16) for i in range(2)]
    gates = c.sb("gates", [128, 3 * D], F32)
    ya = [c.sb(f"ya{i}", [128, 512], BF16) for i in range(2)]
    yaT = c.sb("yaT", [128, 4, 128], BF16)
    ysT = [c.sb(f"ysT{i}", [128, 2, 128], BF16) for i in range(2)]
    ypT = [c.sb(f"ypT{i}", [128, 2, 128], BF16) for i in range(2)]
    macc = [c.sb(f"macc{i}", [128, 512], F32) for i in range(2)]
    mtmp = [c.sb(f"mtmp{i}", [128, 512], F32) for i in range(2)]
    mrg = [c.sb(f"mrg{i}", [128, D], BF16) for i in range(2)]
    mT = c.sb("mT", [128, 8, 128], BF16)
    pt = c.ps("pt")
    pg = [c.ps(f"pg{i}") for i in range(2)]
    pbr = [c.ps(f"pbr{i}") for i in range(3)]
    po = [c.ps(f"po{i}") for i in range(2)]

    make_identity(c, ident)
    s.dma("sp", gcol[:], P["mix_norm"][l].rearrange("(kc p) -> p kc", p=128), writes=["gcol"],
          allow_slow_non_contiguous=True)
    s.dma("act", bg[:], P["b_gate"][l:l + 1, :].partition_broadcast(128), writes=["bg"])
    n = 0
    for kc in range(8):
        st = stg[n % 2]
        s.dma("sp" if n % 2 == 0 else "act", st[:], P["w_in"][l, kc * 128:(kc + 1) * 128, 2056:INC],
              writes=[f"stg{n % 2}"])
        _cast(s, _cast_engine(n), wg[:, kc, :], st[:], [f"stg{n % 2}", "gcol"], [f"wg{kc}"], scalar=gcol[:, kc:kc + 1])
        n += 1
    for (wt, nm, nk) in ((wba, "w_br_attn", 4), (wbs, "w_br_ssm", 2), (wbp, "w_br_pool", 2), (wout, "w_out", 8)):
        for k0 in range(0, nk, 2):
            st = stg[n % 2]
            s.dma("sp" if n % 2 == 0 else "act", st[:, 0:2 * D].rearrange("p (a d) -> p a d", a=2),
                  P[nm][l, k0 * 128:(k0 + 2) * 128, :].rearrange("(a p) d -> p a d", p=128), writes=[f"stg{n % 2}"])
            _cast(s, _cast_engine(n), wt[:, k0:k0 + 2, :], st[:, 0:2 * D].rearrange("p (a d) -> p a d", a=2),
                  [f"stg{n % 2}"], [nm])
            n += 1

    NCH = NSEQ * NJ
    ptv = pt[:].bitcast(BF16)

    def load(n):
        q_, j = divmod(n, NJ)
        b = n % 2
        s.dma("sp", xt[n % 4][:], x_d[n * 128:(n + 1) * 128, :], writes=[f"xt{n % 4}"])

    def load_y(n):
        q_, j = divmod(n, NJ)
        b = n % 2
        s.dma("sp", ya[b][:], scr["yattn"][q_, j * 128:(j + 1) * 128, :], writes=[f"ya{b}"])
        s.dma("sp", ysT[b][:], scr["yssmT"][q_, :, j * 128:(j + 1) * 128].rearrange("(kc p) t -> p kc t", p=128),
              writes=[f"ysT{b}"])
        s.dma("sp", ypT[b][:], scr["ypoolT"][q_, :, j * 128:(j + 1) * 128].rearrange("(kc p) t -> p kc t", p=128),
              writes=[f"ypT{b}"])

    def s1(n):
        X, xb = xt[n % 4], f"xt{n % 4}"
        HT, htn = hT[n % 2], f"hT{n % 2}"
        rms_rstd(c, X[:], xb, junk[:], ss[:], rstd[:], D, "")
        TS(s, "dve", hb[:], X[:], rstd[:, 0:1], ALU.mult, [xb, "rstd"], ["hb"])
        for kc in range(8):
            TR(s, ptv[:, kc * 128:(kc + 1) * 128], hb[:, kc * 128:(kc + 1) * 128], ident[:], ["hb", "ident"], ["pt"])
        CP(s, "act", HT[:], ptv.rearrange("p (k t) -> p k t", k=8), ["pt"], [htn])

    def s2_gates(n):
        HT, htn = hT[n % 2], f"hT{n % 2}"
        for gb in range(6):
            PG, pgn = pg[gb % 2], f"pg{gb % 2}"
            for kc in range(8):
                MM(s, PG[:], HT[:, kc, :], wg[:, kc, gb * 512:(gb + 1) * 512], [htn, f"wg{kc}"], [pgn],
                   start=(kc == 0), stop=(kc == 7))
            gsl = gates[:, gb * 512:(gb + 1) * 512]
            TT_(s, "dve", gsl, PG[:], bg[:, gb * 512:(gb + 1) * 512], ALU.add, [pgn, "bg"], [f"gate{gb}"])
            ACTF(s, gsl, gsl, AF.Sigmoid, [f"gate{gb}"], [f"gate{gb}"])

    def s2_ya(n):
        b = n % 2
        for kc in range(4):
            TR(s, ptv[:, kc * 128:(kc + 1) * 128], ya[b][:, kc * 128:(kc + 1) * 128], ident[:], [f"ya{b}", "ident"],
               ["pt"])
        CP(s, "dve", yaT[:], ptv[:, 0:512].rearrange("p (k t) -> p k t", k=4), ["pt"], ["yaT"])

    def s2_branch(n, dbs):
        b = n % 2
        MR = mrg[b]
        for db in dbs:
            cs = slice(db * 512, (db + 1) * 512)
            for kc in range(4):
                MM(s, pbr[0][:], yaT[:, kc, :], wba[:, kc, cs], ["yaT", "w_br_attn"], ["pbr0"], start=(kc == 0),
                   stop=(kc == 3))
            for kc in range(2):
                MM(s, pbr[1][:], ysT[b][:, kc, :], wbs[:, kc, cs], [f"ysT{b}", "w_br_ssm"], ["pbr1"], start=(kc == 0),
                   stop=(kc == 1))
            for kc in range(2):
                MM(s, pbr[2][:], ypT[b][:, kc, :], wbp[:, kc, cs], [f"ypT{b}", "w_br_pool"], ["pbr2"], start=(kc == 0),
                   stop=(kc == 1))
            MA, TM = macc[db], mtmp[db]
            TT_(s, "dve", MA[:], pbr[0][:], gates[:, db * 512:(db + 1) * 512], ALU.mult, ["pbr0", f"gate{db}"],
                [f"macc{db}"])
            TT_(s, "dve", TM[:], pbr[1][:], gates[:, D + db * 512:D + (db + 1) * 512], ALU.mult,
                ["pbr1", f"gate{2 + db}"], [f"mtmp{db}"])
            TT_(s, "pool", MA[:], MA[:], TM[:], ALU.add, [f"macc{db}", f"mtmp{db}"], [f"macc{db}"])
            TT_(s, "dve", TM[:], pbr[2][:], gates[:, 2 * D + db * 512:2 * D + (db + 1) * 512], ALU.mult,
                ["pbr2", f"gate{4 + db}"], [f"mtmp{db}"])
            TT_(s, "pool", MR[:, cs], MA[:], TM[:], ALU.add, [f"macc{db}", f"mtmp{db}"], [f"mrg{b}_{db}"])

    def s3(n):
        b = n % 2
        MR = mrg[b]
        X, xb = xt[n % 4], f"xt{n % 4}"
        for kc in range(8):
            TR(s, ptv[:, kc * 128:(kc + 1) * 128], MR[:, kc * 128:(kc + 1) * 128], ident[:],
               [f"mrg{b}_{kc // 4}", "ident"], ["pt"])
        CP(s, "act", mT[:], ptv.rearrange("p (k t) -> p k t", k=8), ["pt"], ["mT"])
        for db in range(2):
            cs = slice(db * 512, (db + 1) * 512)
            for kc in range(8):
                MM(s, po[db][:], mT[:, kc, :], wout[:, kc, cs], ["mT", "w_out"], [f"po{db}"], start=(kc == 0),
                   stop=(kc == 7))
            TT_(s, "dve", X[:, cs], po[db][:], X[:, cs], ALU.add, [f"po{db}", xb], [xb])
        s.dma("pool", xout_d[n * 128:(n + 1) * 128, :], X[:], reads=[xb])

    load(0)
    load_y(0)
    if NCH > 1:
        load(1)
        load_y(1)
    s1(0)
    for t in range(NCH + 1):
        if t + 2 < NCH:
            load(t + 2)
        if t < NCH:
            s2_gates(t)
        if t + 1 < NCH:
            s1(t + 1)
        if t < NCH:
            s2_ya(t)
            s2_branch(t, [0])
        if t >= 1:
            s3(t - 1)
        if t < NCH:
            s2_branch(t, [1])
        if t + 2 < NCH:
            load_y(t + 2)
    c.end()


def build_full(NSEQ, S, depth=DEPTH):
    T = NSEQ * S
    nc = bass.Bass("TRN2", target_bir_lowering=False)
    x = nc.dram_tensor("x", [T, D], F32, kind="ExternalInput").ap()
    y = nc.dram_tensor("y", [T, D], F32, kind="ExternalOutput").ap()
    P = declare_params(nc)
    scr = make_scratch(nc, NSEQ, S)
    xa = nc.dram_tensor("scr_xa", [T, D], F32).ap()
    with contextlib.ExitStack() as es:
        c = Ctx(nc, es)
        G = {"cumall": es.enter_context(nc.sbuf_tensor("cumall", [128, NSEQ, S // 128, 8], F32))}
        for l in range(depth):
            xin = x if l == 0 else xa
            phase_a(c, l, NSEQ, S, xin, P, G, scr)
            phase_a2(c, l, NSEQ, S, P, scr)
            phase_b(c, NSEQ, S, G, scr)
            phase_c(c, l, NSEQ, S, xin, xa, P, scr)
            phase_mlp(c, l, T, xa, P["w_up"], P["w_down"], P["mlp_norm"], xout_d=(y if l == depth - 1 else xa))
        print("ops", c.s.total, "waits", c.s.nwaits, flush=True)
    return nc


_NC_CACHE = {}


def kernel(**inputs):
    x = np.ascontiguousarray(np.asarray(inputs["x"], dtype=np.float32))
    B, S, _ = x.shape
    NSEQ = B // NCORES
    key = (NSEQ, S)
    if key not in _NC_CACHE:
        _NC_CACHE[key] = build_full(NSEQ, S)
    nc = _NC_CACHE[key]
    params = {k: np.ascontiguousarray(np.asarray(inputs[k], dtype=np.float32)) for k in PARAM_SHAPES}
    in_maps = []
    for cid in range(NCORES):
        m = {"x": x[cid * NSEQ:(cid + 1) * NSEQ].reshape(NSEQ * S, D)}
        m.update(params)
        in_maps.append(m)
    res = run_bass_kernel_spmd(nc, in_maps, core_ids=list(range(NCORES)))
    out = np.empty((B, S, D), dtype=np.float32)
    for cid in range(NCORES):
        out[cid * NSEQ:(cid + 1) * NSEQ] = np.asarray(res.results[cid]["y"]).reshape(NSEQ, S, D)
    return out
```

```python
import contextlib
import math
import numpy as np
import concourse.bass as bass
import concourse.mybir as mybir
from concourse.bass_utils import run_bass_kernel_spmd

F32 = mybir.dt.float32
BF16 = mybir.dt.bfloat16
ALU = mybir.AluOpType
AF = mybir.ActivationFunctionType
AX = mybir.AxisListType

D = 1024
DEPTH = 2
DFF = 4096
NH = 8
DH = 64
INC = 5128
EPS = 1e-6
NCORES = 8

ENGS = ("pe", "act", "dve", "pool", "sp")
N_DMA_SEMS = 12
SAME_ENGINE_SYNC = True


class Buf:
    __slots__ = ("name", "w", "r")

    def __init__(self, name):
        self.name = name
        self.w = None
        self.r = {}


class Op:
    __slots__ = ("eng", "idx", "fn", "deps", "dma", "needs_inc", "sem", "semval")

    def __init__(self, eng, idx, fn, deps, dma):
        self.eng, self.idx, self.fn, self.deps, self.dma = eng, idx, fn, deps, dma
        self.needs_inc = False
        self.sem = None
        self.semval = None


class Sched:
    def __init__(self, nc, es, same_engine_sync=SAME_ENGINE_SYNC):
        self.nc = nc
        self.q = {e: [] for e in ENGS}
        self.ndma = {e: 0 for e in ENGS}
        self.cnt = {e: 0 for e in ENGS}
        self.same_engine_sync = same_engine_sync
        self.bufs = {}
        self.esem = {e: es.enter_context(nc.semaphore("es_" + e)) for e in ENGS if e != "sp"}
        self.dsem = {}
        for e in ("sp", "act", "pool"):
            for j in range(N_DMA_SEMS):
                self.dsem[(e, j)] = es.enter_context(nc.semaphore(f"ds_{e}_{j}"))
        self.total = {e: 0 for e in ENGS}
        self.nwaits = {e: 0 for e in ENGS}

    def buf(self, name):
        b = self.bufs.get(name)
        if b is None:
            b = self.bufs[name] = Buf(name)
        return b

    def _b(self, x):
        return x if isinstance(x, Buf) else self.buf(x)

    def op(self, eng, fn, reads=(), writes=(), dma=False):
        reads = [self._b(x) for x in reads]
        writes = [self._b(x) for x in writes]
        deps = {}
        for b in reads:
            if b.w is not None:
                deps[id(b.w)] = b.w
        for b in writes:
            if b.w is not None:
                deps[id(b.w)] = b.w
            for o in b.r.values():
                deps[id(o)] = o
        o = Op(eng, len(self.q[eng]), fn, list(deps.values()), dma)
        if dma:
            i = self.ndma[eng]
            self.ndma[eng] += 1
            o.sem = (eng, i % N_DMA_SEMS)
            o.semval = 16 * (i // N_DMA_SEMS + 1)
        self.q[eng].append(o)
        for b in reads:
            key = ("dma", eng, o.idx) if dma else eng
            b.r[key] = o
        for b in writes:
            b.w = o
            b.r = {}
        for d in o.deps:
            if not d.dma:
                d.needs_inc = True
        return o

    def dma(self, eng, out, in_, reads=(), writes=(), **kw):
        return self.op(eng, lambda e: e.dma_start(out=out, in_=in_, **kw), reads, writes, dma=True)

    def emit(self):
        nc = self.nc
        for e in ENGS:
            c = self.cnt[e]
            for o in self.q[e]:
                if not o.dma and o.needs_inc:
                    c += 1
                    o.sem = e
                    o.semval = c
            self.cnt[e] = c

        def replay(e, engobj):
            known = {}
            for o in self.q[e]:
                waits = {}
                for d in o.deps:
                    if d.eng == e and not d.dma:
                        if e == "pe" or (not self.same_engine_sync and e != "pool"):
                            continue
                    k = d.sem
                    if waits.get(k, 0) < d.semval:
                        waits[k] = d.semval
                if o.dma and o.semval > 16:
                    k = o.sem
                    waits[k] = max(waits.get(k, 0), o.semval - 16)
                for k, v in waits.items():
                    if known.get(k, 0) < v:
                        known[k] = v
                        s = self.dsem[k] if isinstance(k, tuple) else self.esem[k]
                        engobj.wait_ge(s, v)
                        self.nwaits[e] += 1
                ins = o.fn(engobj)
                if o.dma:
                    ins.then_inc(self.dsem[o.sem], 16)
                elif o.needs_inc:
                    ins.then_inc(self.esem[e], 1)
            if self.ndma[e]:
                n = self.ndma[e]
                for j in range(min(N_DMA_SEMS, n)):
                    last = ((n - 1 - j) // N_DMA_SEMS) + 1
                    if known.get((e, j), 0) < 16 * last:
                        engobj.wait_ge(self.dsem[(e, j)], 16 * last)

        with nc.Block() as block:
            @block.sync
            def _(e):
                replay("sp", e)

            @block.tensor
            def _(e):
                replay("pe", e)

            @block.scalar
            def _(e):
                replay("act", e)

            @block.vector
            def _(e):
                replay("dve", e)

            @block.gpsimd
            def _(e):
                replay("pool", e)

        for e in ENGS:
            self.total[e] += len(self.q[e])
            self.q[e] = []
        for b in self.bufs.values():
            b.w = None
            b.r = {}


class Ctx:
    def __init__(self, nc, es):
        self.nc = nc
        self.es = es
        self.s = Sched(nc, es)
        self.pes = None
        self.uid = 0

    def begin(self):
        self.pes = contextlib.ExitStack()
        self.pes.__enter__()

    def end(self):
        self.s.emit()
        self.pes.close()
        self.pes = None

    def dbg(self, name, ap, reads):
        if not getattr(self, "debug", False):
            return
        d = self.nc.dram_tensor("dbg_" + name, list(ap.shape), ap.dtype, kind="ExternalOutput").ap()
        self.s.dma("sp", d, ap, reads=reads)

    def sb(self, name, shape, dtype):
        self.uid += 1
        return self.pes.enter_context(self.nc.sbuf_tensor(f"{name}_{self.uid}", list(shape), dtype))

    def ps(self, name, shape=(128, 512), dtype=F32):
        self.uid += 1
        return self.pes.enter_context(self.nc.psum_tensor(f"{name}_{self.uid}", list(shape), dtype))


def _cast_engine(i):
    return ("dve", "pool", "act")[i % 3]


def _cast(s, eng, out, in_, reads, writes, scalar=None):
    if scalar is None:
        if eng == "act":
            return s.op("act", lambda e: e.copy(out=out, in_=in_), reads, writes)
        return s.op(eng, lambda e: e.tensor_copy(out=out, in_=in_), reads, writes)
    if eng == "act":
        return s.op("act", lambda e: e.activation(out=out, in_=in_, func=AF.Copy, scale=scalar), reads, writes)
    return s.op(eng, lambda e: e.tensor_scalar(out=out, in0=in_, scalar1=scalar, scalar2=None, op0=ALU.mult),
                reads, writes)


def make_identity(c, ident):
    s = c.s
    s.op("pool", lambda e: e.memset(ident[:], 1.0), writes=["ident"])
    s.op("pool", lambda e: e.affine_select(out=ident[:], in_=ident[:], pattern=[[-1, 128]],
                                           compare_op=ALU.is_equal, fill=0.0, base=0, channel_multiplier=1),
         reads=["ident"], writes=["ident"])


def phase_mlp(c, l, T, x_d, w_up, w_down, mlp_norm, xout_d=None):
    nc, s = c.nc, c.s
    if xout_d is None:
        xout_d = x_d
    c.begin()
    TT = 256
    NT = T // TT
    wup = c.sb("wup", [128, 8, DFF], BF16)
    wdn = c.sb("wdn", [128, 32, D], BF16)
    stg = [c.sb(f"stg{i}", [128, 1024], F32) for i in range(2)]
    gcol = c.sb("gcol", [128, 8], F32)
    ident = c.sb("ident", [128, 128], BF16)
    xt = [c.sb(f"xt{i}", [128, 2, D], F32) for i in range(3)]
    hb = c.sb("hb", [128, 2, D], BF16)
    hT = [c.sb(f"hT{i}", [128, 8, TT], BF16) for i in range(2)]
    actT = c.sb("actT", [128, 32, TT], BF16)
    rl = [c.sb(f"rl{i}", [128, TT], F32) for i in range(2)]
    ss = c.sb("ss", [128, 2], F32)
    rstd = c.sb("rstd", [128, 2], F32)
    junk = c.sb("junk", [128, 2, D], F32)
    pt = [c.ps(f"pt{i}") for i in range(2)]
    pu = [c.ps(f"pu{i}") for i in range(3)]
    pd = [c.ps(f"pd{i}") for i in range(3)]

    make_identity(c, ident)
    s.dma("sp", gcol[:], mlp_norm[l].rearrange("(kc p) -> p kc", p=128), writes=["gcol"],
          allow_slow_non_contiguous=True)
    n = 0
    for kc in range(8):
        for qq in range(4):
            st = stg[n % 2]
            s.dma("sp" if n % 2 == 0 else "act", st[:], w_up[l, kc * 128:(kc + 1) * 128, qq * 1024:(qq + 1) * 1024],
                  writes=[f"stg{n % 2}"])
            _cast(s, _cast_engine(n), wup[:, kc, qq * 1024:(qq + 1) * 1024], st[:],
                  [f"stg{n % 2}", "gcol"], [f"wup{kc}_{qq}"], scalar=gcol[:, kc:kc + 1])
            n += 1
    for fc in range(32):
        st = stg[n % 2]
        s.dma("sp" if n % 2 == 0 else "act", st[:], w_down[l, fc * 128:(fc + 1) * 128, :], writes=[f"stg{n % 2}"])
        _cast(s, _cast_engine(n), wdn[:, fc, :], st[:], [f"stg{n % 2}"], [f"wdn{fc}"])
        n += 1

    def load(i):
        s.dma("sp", xt[i % 3][:], x_d[i * TT:(i + 1) * TT, :].rearrange("(a p) d -> p a d", p=128),
              writes=[f"xt{i % 3}"])

    def front(i):
        X, xb = xt[i % 3], f"xt{i % 3}"
        HT, htn = hT[i % 2], f"hT{i % 2}"
        for a in range(2):
            ACTF(s, junk[:, a, :], X[:, a, :], AF.Square, [xb], [f"junk{a}"])
        s.op("dve", lambda e: e.tensor_reduce(out=ss[:], in_=junk[:], axis=AX.X, op=ALU.add),
             reads=["junk0", "junk1"], writes=["ss"])
        ACTF(s, rstd[:], ss[:], AF.Sqrt, ["ss"], ["rstd"], scale=1.0 / D, bias=EPS)
        s.op("dve", lambda e: e.reciprocal(out=rstd[:], in_=rstd[:]), reads=["rstd"], writes=["rstd"])
        for a in range(2):
            TS(s, "dve" if a == 0 else "pool", hb[:, a, :], X[:, a, :], rstd[:, a:a + 1], ALU.mult, [xb, "rstd"],
               [f"hb{a}"])
        for a in range(2):
            pview = pt[a][:].bitcast(BF16)
            for kc in range(8):
                TR(s, pview[:, kc * 128:(kc + 1) * 128], hb[:, a, kc * 128:(kc + 1) * 128], ident[:],
                   [f"hb{a}", "ident"], [f"pt{a}"])
            CP(s, "dve" if a == 0 else "act", HT[:, :, a * 128:(a + 1) * 128],
               pview.rearrange("p (k t) -> p k t", k=8), [f"pt{a}"], [htn + f"_{a}"])

    def up(i):
        HT, htn = hT[i % 2], f"hT{i % 2}"
        for fc in range(32):
            Pb, pb = pu[fc % 3], f"pu{fc % 3}"
            for kc in range(8):
                MM(s, Pb[:, 0:TT], wup[:, kc, fc * 128:(fc + 1) * 128], HT[:, kc, :],
                   [f"wup{kc}_{fc // 8}", htn + "_0", htn + "_1"], [pb], start=(kc == 0), stop=(kc == 7))
            R, rb = rl[fc % 2], f"rl{fc % 2}"
            ACTF(s, R[:], Pb[:, 0:TT], AF.Relu, [pb], [rb])
            TT_(s, "pool", actT[:, fc, :], R[:], R[:], ALU.mult, [rb], ["actT"])

    def down(i):
        X, xb = xt[i % 3], f"xt{i % 3}"
        for a in range(2):
            for db in range(2):
                jj = a * 2 + db
                Pb, pb = pd[jj % 3], f"pd{jj % 3}"
                for fc in range(32):
                    MM(s, Pb[:], actT[:, fc, a * 128:(a + 1) * 128], wdn[:, fc, db * 512:(db + 1) * 512],
                       ["actT", f"wdn{fc}"], [pb], start=(fc == 0), stop=(fc == 31))
                TT_(s, "dve", X[:, a, db * 512:(db + 1) * 512], Pb[:], X[:, a, db * 512:(db + 1) * 512], ALU.add,
                    [pb, xb], [xb])
        s.dma("pool", xout_d[i * TT:(i + 1) * TT, :].rearrange("(a p) d -> p a d", p=128), X[:], reads=[xb])

    load(0)
    if NT > 1:
        load(1)
    front(0)
    for i in range(NT):
        if i + 2 < NT:
            load(i + 2)
        up(i)
        if i + 1 < NT:
            front(i + 1)
        down(i)
    c.end()


def build_test_mlp(T):
    nc = bass.Bass("TRN2", target_bir_lowering=False)
    x = nc.dram_tensor("x", [T, D], F32, kind="ExternalInput").ap()
    w_up = nc.dram_tensor("w_up", [DEPTH, D, DFF], F32, kind="ExternalInput").ap()
    w_down = nc.dram_tensor("w_down", [DEPTH, DFF, D], F32, kind="ExternalInput").ap()
    mlp_norm = nc.dram_tensor("mlp_norm", [DEPTH, D], F32, kind="ExternalInput").ap()
    y = nc.dram_tensor("y", [T, D], F32, kind="ExternalOutput").ap()
    with contextlib.ExitStack() as es:
        c = Ctx(nc, es)
        c.debug = True
        phase_mlp(c, 0, T, x, w_up, w_down, mlp_norm, xout_d=y)
        print("ops", c.s.total, "waits", c.s.nwaits)
    return nc


def rms_rstd(c, X, xb, junk, ss, rstd, width, tag):
    s = c.s
    s.op("act", lambda e: e.activation(out=junk, in_=X, func=AF.Square), reads=[xb], writes=["junk" + tag])
    s.op("dve", lambda e: e.tensor_reduce(out=ss, in_=junk, axis=AX.X, op=ALU.add),
         reads=["junk" + tag], writes=["ss" + tag])
    s.op("act", lambda e: e.activation(out=rstd, in_=ss, func=AF.Sqrt, scale=1.0 / width, bias=EPS),
         reads=["ss" + tag], writes=["rstd" + tag])
    s.op("dve", lambda e: e.reciprocal(out=rstd, in_=rstd), reads=["rstd" + tag], writes=["rstd" + tag])


def make_tri(c, tri, name, dtype_one=1.0):
    s = c.s
    s.op("pool", lambda e: e.memset(tri[:], 1.0), writes=[name])
    s.op("pool", lambda e: e.affine_select(out=tri[:], in_=tri[:], pattern=[[1, 128]],
                                           compare_op=ALU.is_ge, fill=0.0, base=0, channel_multiplier=-1),
         reads=[name], writes=[name])


def phase_a(c, l, NSEQ, S, x_d, P, G, scr, do_ssm=True, do_pool=True):
    nc, s = c.nc, c.s
    NJ = S // 128
    c.begin()
    NA = 2056
    win = c.sb("win", [128, 8, NA], BF16)
    stg = [c.sb(f"stg{i}", [128, NA], F32) for i in range(2)]
    gcol = c.sb("gcol", [128, 8], F32)
    ident = c.sb("ident", [128, 128], BF16)
    trif = c.sb("trif", [128, 128], F32)
    onesf = c.sb("onesf", [128, 128], F32)
    xt = [c.sb(f"xt{i}", [128, D], F32) for i in range(3)]
    junk = c.sb("junk", [128, D], F32)
    ss = c.sb("ss", [128, 1], F32)
    rstd = c.sb("rstd", [128, 1], F32)
    hb = c.sb("hb", [128, D], BF16)
    hT = [c.sb(f"hT{i}", [128, 8, 128], BF16) for i in range(2)]
    qkg = c.sb("qkg", [128, 2, 8, DH], F32)
    bfg = c.sb("bfg", [128, 8], F32)
    sq = [c.sb(f"sq{i}", [128, 512], F32) for i in range(2)]
    qe = [[c.sb(f"qe{i}_{b}", [128, 512], F32) for b in range(2)] for i in range(2)]
    ssq = [c.sb(f"ssq{i}", [128, 8], F32) for i in range(2)]
    rq = [c.sb(f"rq{i}", [128, 8], F32) for i in range(2)]
    qn = [c.sb(f"qn{i}", [128, 8, DH], F32) for i in range(2)]
    qa = [c.sb(f"qa{i}", [128, 8, 70], BF16) for i in range(2)]
    r1 = c.sb("r1", [128, 8], F32)
    r2 = c.sb("r2", [128, 8], F32)
    qTs = [[c.sb(f"qTs{i}_{b}", [128, 8, 128], BF16) for b in range(2)] for i in range(2)]
    vst = [c.sb(f"vst{b}", [128, 8, 65], BF16) for b in range(2)]
    ust = [c.sb(f"ust{b}", [128, 512], F32) for b in range(2)]
    fls = [c.sb(f"fl{b}", [128, 8], F32) for b in range(2)]
    sp_ = c.sb("sp_", [128, 8], F32)
    carry = c.sb("carry", [128, 8], F32)
    cumall = G["cumall"]
    pt = c.ps("pt")
    pq = [c.ps(f"pq{i}") for i in range(2)]
    pv = c.ps("pv")
    pf = c.ps("pf")
    pp = c.ps("pp")
    ptq = [c.ps(f"ptq{i}") for i in range(2)]

    make_identity(c, ident)
    make_tri(c, trif, "trif")
    s.op("pool", lambda e: e.memset(onesf[:], 1.0), writes=["onesf"])
    for b in range(2):
        s.op("pool", lambda e, b=b: e.memset(vst[b][:], 1.0), writes=[f"vst{b}"])
    s.dma("sp", gcol[:], P["mix_norm"][l].rearrange("(kc p) -> p kc", p=128), writes=["gcol"],
          allow_slow_non_contiguous=True)
    s.dma("sp", qkg[:, 0, 0, :], P["q_norm"][l:l + 1, :].partition_broadcast(128), writes=["qkg"])
    s.dma("sp", qkg[:, 1, 0, :], P["k_norm"][l:l + 1, :].partition_broadcast(128), writes=["qkg"])
    s.dma("sp", bfg[:], P["b_forget"][l:l + 1, :].partition_broadcast(128), writes=["bfg"])
    s.op("dve", lambda e: e.tensor_scalar(out=qkg[:, 0, 0, :], in0=qkg[:, 0, 0, :], scalar1=DH ** -0.5, scalar2=None,
                                          op0=ALU.mult), reads=["qkg"], writes=["qkg"])
    for h in range(1, 8):
        s.op("dve", lambda e, h=h: e.tensor_copy(out=qkg[:, :, h, :], in_=qkg[:, :, 0, :]),
             reads=["qkg"], writes=["qkg"])
    for kc in range(8):
        st = stg[kc % 2]
        s.dma("sp" if kc % 2 == 0 else "act", st[:], P["w_in"][l, kc * 128:(kc + 1) * 128, 0:NA],
              writes=[f"stg{kc % 2}"])
        _cast(s, _cast_engine(kc), win[:, kc, :], st[:], [f"stg{kc % 2}", "gcol"], [f"win{kc}"],
              scalar=gcol[:, kc:kc + 1])

    def load(n):
        s.dma("sp", xt[n % 3][:], x_d[n * 128:(n + 1) * 128, :], writes=[f"xt{n % 3}"])

    NCH = NSEQ * NJ
    ptv = pt[:].bitcast(BF16)
    pc = pf[:, 272:288]
    for i in range(2):
        s.op("pool", lambda e, i=i: e.memset(qa[i][:], 1.0), writes=[f"qg{i}", f"qaug{i}"])

    def front_elem(n):
        X, xb = xt[n % 3], f"xt{n % 3}"
        ACTF(s, junk[:], X[:], AF.Square, [xb], ["junk"])
        s.op("dve", lambda e: e.tensor_reduce(out=ss[:], in_=junk[:], axis=AX.X, op=ALU.add), reads=["junk"],
             writes=["ss"])
        ACTF(s, rstd[:], ss[:], AF.Ln, ["ss"], ["rstd"], scale=1.0 / D, bias=EPS)
        ACTF(s, rstd[:], rstd[:], AF.Exp, ["rstd"], ["rstd"], scale=-0.5)
        TS(s, "dve", hb[:], X[:], rstd[:, 0:1], ALU.mult, [xb, "rstd"], ["hb"])

    def front_tr(n):
        HT, htn = hT[n % 2], f"hT{n % 2}"
        for kc in range(8):
            TR(s, ptv[:, kc * 128:(kc + 1) * 128], hb[:, kc * 128:(kc + 1) * 128], ident[:], ["hb", "ident"], ["pt"])
        CP(s, "act", HT[:], ptv.rearrange("p (k t) -> p k t", k=8), ["pt"], [htn])

    def proj_all(n):
        HT, htn = hT[n % 2], f"hT{n % 2}"

        def proj(Pt, pname, c0, c1):
            for kc in range(8):
                MM(s, Pt, HT[:, kc, :], win[:, kc, c0:c1], [htn, f"win{kc}"], [pname], start=(kc == 0), stop=(kc == 7))
        proj(pq[0][:], "pq0", 0, 512)
        proj(pq[1][:], "pq1", 512, 1024)
        proj(pv[:], "pv", 1024, 1536)
        proj(pf[:, 0:264], "pf", 1536, 1800)
        proj(pp[:, 0:256], "pp", 1800, 2056)

    def evac(n):
        b = n % 2
        CP(s, "act", qe[0][b][:], pq[0][:], ["pq0"], [f"qe0_{b}"])
        CP(s, "dve", qe[1][b][:], pq[1][:], ["pq1"], [f"qe1_{b}"])
        VS = vst[b]
        CP(s, "act", VS[:, :, 0:64], pv[:].rearrange("p (h d) -> p h d", h=8), ["pv"], [f"vst{b}"])
        US = ust[b]
        CP(s, "dve", US[:, 0:256], pf[:, 8:264], ["pf"], [f"ust{b}"])
        TT_(s, "dve", fls[b][:], pf[:, 0:8], bfg[:], ALU.add, ["pf", "bfg"], [f"fl{b}"])
        CP(s, "act", US[:, 256:512], pp[:, 0:256], ["pp"], [f"ust{b}"])

    def forget(n):
        q_, j = divmod(n, NJ)
        b = n % 2
        fl = fls[b]
        ACTF(s, fl[:], fl[:], AF.Exp, [f"fl{b}"], [f"fl{b}"], scale=-1.0)
        ACTF(s, sp_[:], fl[:], AF.Ln, [f"fl{b}"], ["sp_"], scale=1.0, bias=1.0)
        if j == 0:
            s.op("pool", lambda e: e.memset(carry[:], 0.0), writes=["carry"])
        MM(s, pc[:, 0:8], trif[:], sp_[:], ["trif", "sp_"], ["pf"])
        MM(s, pc[:, 8:16], onesf[:], sp_[:], ["onesf", "sp_"], ["pf"])
        cum = cumall[:, q_, j, :]
        TT_(s, "dve", cum, carry[:], pc[:, 0:8], ALU.subtract, ["carry", "pf"], ["cumall"])
        TT_(s, "dve", carry[:], carry[:], pc[:, 8:16], ALU.subtract, ["carry", "pf"], ["carry"])

    def post(n):
        q_, j = divmod(n, NJ)
        b = n % 2
        fl = fls[b]
        VS, US = vst[b], ust[b]
        cum = cumall[:, q_, j, :]
        CP(s, "dve", qa[0][:, :, 67], cum, ["cumall"], ["qaug0"])
        yield
        TT_(s, "dve", r1[:], cum, qa[0][:, :, 67], ALU.subtract, ["cumall", "qaug0"], ["r1"])
        yield
        CP(s, "dve", qa[0][:, :, 68], r1[:], ["r1"], ["qaug0"])
        yield
        TT_(s, "dve", r2[:], r1[:], qa[0][:, :, 68], ALU.subtract, ["r1", "qaug0"], ["r2"])
        yield
        CP(s, "dve", qa[0][:, :, 69], r2[:], ["r2"], ["qaug0"])
        yield
        TS(s, "dve", qa[1][:, :, 64:67], qa[0][:, :, 67:70], -1.0, ALU.mult, ["qaug0"], ["qaug1"])
        yield
        for i in range(2):
            PQ = qe[i][b]
            ACTF(s, sq[i][:], PQ[:], AF.Square, [f"qe{i}_{b}"], [f"sq{i}"])
            yield
            s.op("dve", lambda e, i=i: e.tensor_reduce(out=ssq[i][:], in_=sq[i][:].rearrange("p (h d) -> p h d", h=8),
                                                       axis=AX.X, op=ALU.add),
                 reads=[f"sq{i}"], writes=[f"ssq{i}"])
            yield
            ACTF(s, rq[i][:], ssq[i][:], AF.Ln, [f"ssq{i}"], [f"rq{i}"], scale=1.0 / DH, bias=EPS)
            yield
            ACTF(s, rq[i][:], rq[i][:], AF.Exp, [f"rq{i}"], [f"rq{i}"], scale=-0.5)
            yield
            TT_(s, "dve", qn[i][:], PQ[:].rearrange("p (h d) -> p h d", h=8),
                rq[i][:].unsqueeze(2).broadcast_to([128, 8, DH]), ALU.mult, [f"qe{i}_{b}", f"rq{i}"], [f"qn{i}"])
            yield
            TT_(s, "pool", qa[i][:, :, 0:64], qn[i][:], qkg[:, i, :, :], ALU.mult, [f"qn{i}", "qkg"], [f"qg{i}"])
            yield
            pvw = ptq[i][:].bitcast(BF16)
            for h in range(8):
                TR(s, pvw[0:70, h * 128:(h + 1) * 128], qa[i][:, h, :], ident[:], [f"qg{i}", f"qaug{i}", "ident"],
                   [f"ptq{i}"])
            QT = qTs[i][n % 2]
            CP(s, "act" if i == 0 else "dve", QT[0:70, :, :], pvw[0:70, :].rearrange("p (h t) -> p h t", h=8),
               [f"ptq{i}"], [f"qTs{i}_{n % 2}"])
            yield
            dst = scr["qT" if i == 0 else "kT"]
            s.dma("pool", dst[q_, :, :, j * 128:(j + 1) * 128].rearrange("h p t -> p h t"), QT[0:70, :, :],
                  reads=[f"qTs{i}_{n % 2}"])
            yield
        s.dma("pool", scr["v"][q_, j * 128:(j + 1) * 128, :, :], VS[:], reads=[f"vst{n % 2}"])
        yield
        s.dma("pool", scr["u"][q_, j * 128:(j + 1) * 128, :], US[:], reads=[f"ust{n % 2}"])
        yield

    load(0)
    if NCH > 1:
        load(1)
    def main_stream(t):
        if t + 2 < NCH:
            load(t + 2)
        if t < NCH:
            front_elem(t)
        yield
        if 1 <= t <= NCH:
            HT, htn = hT[(t - 1) % 2], f"hT{(t - 1) % 2}"
            for (Pt, pname, c0, c1) in ((pq[0][:], "pq0", 0, 512), (pq[1][:], "pq1", 512, 1024),
                                        (pv[:], "pv", 1024, 1536), (pf[:, 0:264], "pf", 1536, 1800),
                                        (pp[:, 0:256], "pp", 1800, 2056)):
                for kc in range(8):
                    MM(s, Pt, HT[:, kc, :], win[:, kc, c0:c1], [htn, f"win{kc}"], [pname], start=(kc == 0),
                       stop=(kc == 7))
                    if kc % 2 == 1:
                        yield
        if t < NCH:
            front_tr(t)
        yield
        if 1 <= t <= NCH:
            evac(t - 1)
            forget(t - 1)
        yield

    for t in range(NCH + 2):
        gens = [main_stream(t)]
        if t >= 2:
            gens.append(post(t - 2))
        while gens:
            for g in list(gens):
                try:
                    next(g)
                except StopIteration:
                    gens.remove(g)
    c.end()


def phase_b(c, NSEQ, S, G, scr):
    nc, s = c.nc, c.s
    NJ = S // 128
    NI = NJ // 4
    c.begin()
    tri = c.sb("tri", [128, 128], BF16)
    vall = [c.sb(f"vall{i}", [128, NJ, 8, 65], BF16) for i in range(2)]
    qT = [c.sb(f"qT{i}", [128, S], BF16) for i in range(2)]
    kT = [c.sb(f"kT{i}", [128, S], BF16) for i in range(2)]
    NPT = 4
    pT = [c.sb(f"pT{i}", [128, 512], BF16) for i in range(NPT)]
    ybuf = [c.sb(f"ybuf{i}", [128, NJ, 64], BF16) for i in range(2)]
    rc = [c.sb(f"rc{i}", [128, 1], F32) for i in range(4)]
    NPS = 4
    pS = [c.ps(f"pS{i}") for i in range(NPS)]
    pO = [c.ps(f"pO{i}") for i in range(4)]
    make_tri(c, tri, "tri")
    LOOK = 3
    blocks = []
    for q_ in range(NSEQ):
        for h in range(8):
            for i4 in range(NI):
                for j in range(4 * i4 + 4):
                    blocks.append((q_, h, i4, j))

    def stage1(bi):
        q_, h, i4, j = blocks[bi]
        g = q_ * 8 + h
        QT, KT = qT[g % 2], kT[g % 2]
        if h == 0 and i4 == 0 and j == 0:
            s.dma("sp", vall[q_ % 2][:], scr["v"][q_].rearrange("(j p) h d -> p j h d", p=128), writes=[f"vall{q_ % 2}"])
        if i4 == 0 and j == 0:
            s.dma("sp", QT[0:70, :], scr["qT"][q_, h], writes=[f"qT{g % 2}"])
            s.dma("act", KT[0:70, :], scr["kT"][q_, h], writes=[f"kT{g % 2}"])
        jj = j - 4 * i4
        c0 = 128 * max(jj, 0)
        PS, psn = pS[bi % NPS], f"pS{bi % NPS}"
        PT, ptn = pT[bi % NPT], f"pT{bi % NPT}"
        MM(s, PS[:, c0:512], KT[0:70, j * 128:(j + 1) * 128], QT[0:70, i4 * 512 + c0:(i4 + 1) * 512],
           [f"qT{g % 2}", f"kT{g % 2}"], [psn])
        ACTF(s, PT[:, c0:512], PS[:, c0:512], AF.Exp, [psn], [ptn])
        if jj >= 0:
            TT_(s, "pool", PT[:, c0:c0 + 128], PT[:, c0:c0 + 128], tri[:], ALU.mult, [ptn, "tri"], [ptn])

    def stage2(bi):
        q_, h, i4, j = blocks[bi]
        g = q_ * 8 + h
        PT, ptn = pT[bi % NPT], f"pT{bi % NPT}"
        YB = ybuf[g % 2]
        jj = j - 4 * i4
        for tt in range(max(jj, 0), 4):
            last = (j == 4 * i4 + tt)
            MM(s, pO[tt][:, 0:65], PT[:, tt * 128:(tt + 1) * 128], vall[q_ % 2][:, j, h, :], [ptn, f"vall{q_ % 2}"],
               [f"pO{tt}"], start=(j == 0), stop=last)
            if last:
                s.op("dve", lambda e, tt=tt: e.reciprocal(out=rc[tt][:], in_=pO[tt][:, 64:65]), reads=[f"pO{tt}"],
                     writes=[f"rc{tt}"])
                TS(s, "dve", YB[:, 4 * i4 + tt, :], pO[tt][:, 0:64], rc[tt][:, 0:1], ALU.mult, [f"pO{tt}", f"rc{tt}"],
                   [f"ybuf{g % 2}"])
        if i4 == NI - 1 and j == NJ - 1:
            s.dma("pool", scr["yattn"][q_, :, h * 64:(h + 1) * 64].rearrange("(j p) c -> p j c", p=128), YB[:],
                  reads=[f"ybuf{g % 2}"])

    nb = len(blocks)
    for t in range(nb + LOOK):
        if t < nb:
            stage1(t)
        if t >= LOOK:
            stage2(t - LOOK)
    c.end()


def make_scratch(nc, NSEQ, S, debug_out=False):
    kind = {"kind": "ExternalOutput"} if debug_out else {}
    scr = {}
    scr["qT"] = nc.dram_tensor("scr_qT", [NSEQ, 8, 70, S], BF16).ap()
    scr["kT"] = nc.dram_tensor("scr_kT", [NSEQ, 8, 70, S], BF16).ap()
    scr["v"] = nc.dram_tensor("scr_v", [NSEQ, S, 8, 65], BF16).ap()
    scr["cend"] = nc.dram_tensor("scr_cend", [1, NSEQ, S // 128, 8], F32).ap()
    scr["u"] = nc.dram_tensor("scr_u", [NSEQ, S, 512], F32).ap()
    scr["yattn"] = nc.dram_tensor("scr_yattn", [NSEQ, S, 512], BF16, **kind).ap()
    scr["yssmT"] = nc.dram_tensor("scr_yssmT", [NSEQ, 256, S], BF16, **kind).ap()
    scr["ypoolT"] = nc.dram_tensor("scr_ypoolT", [NSEQ, 256, S], BF16, **kind).ap()
    return scr


PARAM_SHAPES = {
    "mix_norm": [DEPTH, D], "w_in": [DEPTH, D, INC], "b_forget": [DEPTH, NH], "q_norm": [DEPTH, DH],
    "k_norm": [DEPTH, DH], "ssm_a_re": [DEPTH, 16, 64], "ssm_a_im": [DEPTH, 16, 64], "ssm_log_dt": [DEPTH, 16],
    "ssm_b_re": [DEPTH, 16, 64, 16], "ssm_b_im": [DEPTH, 16, 64, 16], "ssm_c_re": [DEPTH, 16, 16, 64],
    "ssm_c_im": [DEPTH, 16, 16, 64], "ssm_d": [DEPTH, 256], "w_glu": [DEPTH, 256, 512],
    "pool_w": [DEPTH, 4, 64, 64], "pool_scale": [DEPTH, 256], "w_br_attn": [DEPTH, 512, D],
    "w_br_ssm": [DEPTH, 256, D], "w_br_pool": [DEPTH, 256, D], "b_gate": [DEPTH, 3 * D],
    "w_out": [DEPTH, D, D], "mlp_norm": [DEPTH, D], "w_up": [DEPTH, D, DFF], "w_down": [DEPTH, DFF, D],
}


def declare_params(nc):
    return {k: nc.dram_tensor(k, shp, F32, kind="ExternalInput").ap() for k, shp in PARAM_SHAPES.items()}


def build_test_ab(NSEQ, S, l=0):
    nc = bass.Bass("TRN2", target_bir_lowering=False)
    x = nc.dram_tensor("x", [NSEQ * S, D], F32, kind="ExternalInput").ap()
    P = declare_params(nc)
    scr = make_scratch(nc, NSEQ, S, debug_out=True)
    with contextlib.ExitStack() as es:
        c = Ctx(nc, es)
        G = {"cumall": es.enter_context(nc.sbuf_tensor("cumall", [128, NSEQ, S // 128, 8], F32))}
        phase_a(c, l, NSEQ, S, x, P, G, scr, do_ssm=False, do_pool=False)
        phase_b(c, NSEQ, S, G, scr)
        print("ops", c.s.total, "waits", c.s.nwaits)
    return nc


def TT_(s, eng, out, in0, in1, op, reads, writes):
    return s.op(eng, lambda e: e.tensor_tensor(out=out, in0=in0, in1=in1, op=op), reads, writes)


def TS(s, eng, out, in0, scalar1, op0, reads, writes, scalar2=None, op1=None):
    if op1 is None:
        return s.op(eng, lambda e: e.tensor_scalar(out=out, in0=in0, scalar1=scalar1, scalar2=None, op0=op0),
                    reads, writes)
    return s.op(eng, lambda e: e.tensor_scalar(out=out, in0=in0, scalar1=scalar1, scalar2=scalar2, op0=op0, op1=op1),
                reads, writes)


def ACTF(s, out, in_, func, reads, writes, scale=1.0, bias=None):
    if bias is None:
        return s.op("act", lambda e: e.activation(out=out, in_=in_, func=func, scale=scale), reads, writes)
    return s.op("act", lambda e: e.activation(out=out, in_=in_, func=func, scale=scale, bias=bias), reads, writes)


def CP(s, eng, out, in_, reads, writes):
    if eng == "act":
        return s.op("act", lambda e: e.copy(out=out, in_=in_), reads, writes)
    return s.op(eng, lambda e: e.tensor_copy(out=out, in_=in_), reads, writes)


def MM(s, out, lhsT, rhs, reads, writes, start=True, stop=True):
    return s.op("pe", lambda e: e.matmul(out, lhsT=lhsT, rhs=rhs, start=start, stop=stop), reads, writes)


def TR(s, out, in_, ident, reads, writes):
    return s.op("pe", lambda e: e.transpose(out=out, in_=in_, identity=ident), reads, writes)


TWO_PI = 2.0 * math.pi
MAGIC = 12582912.0


def sincos(s, eng, ang, o_sin, o_cos, t1, t2, rd, tag):
    n1, n2 = "sc1" + tag, "sc2" + tag
    for which, o in (("s", o_sin), ("c", o_cos)):
        if which == "s":
            TS(s, eng, t1, ang, 1.0 / TWO_PI, ALU.mult, rd, [n1])
        else:
            TS(s, eng, t1, ang, 1.0 / TWO_PI, ALU.mult, rd, [n1], scalar2=0.25, op1=ALU.add)
        TS(s, eng, t2, t1, MAGIC, ALU.add, [n1], [n2])
        TS(s, eng, t2, t2, MAGIC, ALU.subtract, [n2], [n2])
        TT_(s, eng, t2, t1, t2, ALU.subtract, [n1, n2], [n2])
        ACTF(s, o, t2, AF.Sin, [n2], [("sin" if which == "s" else "cos") + tag], scale=TWO_PI * (1 - 1e-6))


POOL_WINDOWS = (2, 4, 8, 16)


def phase_a2(c, l, NSEQ, S, P, scr):
    nc, s = c.nc, c.s
    NJ = S // 128
    c.begin()
    I32 = mybir.dt.int32
    ident = c.sb("ident", [128, 128], BF16)
    identf = c.sb("identf", [128, 128], F32)
    tribf = c.sb("tribf", [128, 128], BF16)
    iot_i = c.sb("iot_i", [128, 128], I32)
    iot = c.sb("iot", [128, 128], F32)
    pcol_i = c.sb("pcol_i", [128, 1], I32)
    pcol = c.sb("pcol", [128, 1], F32)
    npcol = c.sb("npcol", [128, 1], F32)
    are = c.sb("are", [128, 1024], F32)
    aim = c.sb("aim", [128, 1024], F32)
    ldt = c.sb("ldt", [128, 16], F32)
    dtr = c.sb("dtr", [128, 1024], F32)
    t1 = c.sb("t1", [128, 1024], F32)
    t2 = c.sb("t2", [128, 1024], F32)
    t3 = c.sb("t3", [128, 1024], F32)
    t4 = c.sb("t4", [128, 1024], F32)
    t5 = c.sb("t5", [128, 1024], F32)
    t6 = c.sb("t6", [128, 1024], F32)
    cfr = c.sb("cfr", [128, 1024], F32)
    cfi = c.sb("cfi", [128, 1024], F32)
    Tr = c.sb("Tr", [128, 1024], F32)
    Ti = c.sb("Ti", [128, 1024], F32)
    TAr = c.sb("TAr", [128, 8, 128], F32)
    TAi = c.sb("TAi", [128, 8, 128], F32)
    acol = c.sb("acol", [128, 3, 8], F32)
    BX = c.sb("BX", [128, 8, 2, 128], F32)
    BXb = c.sb("BXb", [128, 8, 2, 128], BF16)
    BT = c.sb("BT", [128, 8, 2, 128], BF16)
    CN = c.sb("CN", [32, 8, 2, 128], F32)
    CNb = c.sb("CNb", [32, 8, 2, 128], BF16)
    CX = c.sb("CX", [128, 8, 2, 32], BF16)
    drb = c.sb("drb", [128, 256], F32)
    wg_st = c.sb("wg_st", [128, 2, 512], F32)
    wglu = c.sb("wglu", [128, 2, 512], BF16)
    pw_st = c.sb("pw_st", [128, 2, 64], F32)
    PW = c.sb("PW", [128, 2, 64], BF16)
    pscol = c.sb("pscol", [128, 2], F32)
    MT = c.sb("MT", [128, 12, 128], BF16)
    mtmp = c.sb("mtmp", [128, 128], F32)
    mrat = c.sb("mrat", [128, 128], F32)
    ut = [c.sb(f"ut{i}", [128, 512], F32) for i in range(3)]
    ub = [c.sb(f"ub{i}", [128, 512], BF16) for i in range(3)]
    uT = c.sb("uT", [128, 2, 128], BF16)
    m1 = c.sb("m1", [128, 4, 128], F32)
    m2 = c.sb("m2", [128, 4, 128], F32)
    m3 = c.sb("m3", [128, 4, 128], F32)
    m4 = c.sb("m4", [128, 4, 128], F32)
    Wt = c.sb("Wt", [128, 8, 2, 128], BF16)
    Pr = c.sb("Pr", [128, 8, 128], F32)
    Pi = c.sb("Pi", [128, 8, 128], F32)
    Xt = c.sb("Xt", [128, 8, 2, 128], BF16)
    car = c.sb("car", [128, 2, 8], F32)
    sn = c.sb("sn", [128, 2, 4], F32)
    sm = c.sb("sm", [128, 4, 4], F32)
    du = c.sb("du", [128, 256], F32)
    yv = c.sb("yv", [128, 256], F32)
    y2 = c.sb("y2", [128, 256], F32)
    sg = c.sb("sg", [128, 256], F32)
    gy = c.sb("gy", [128, 256], BF16)
    gyT = c.sb("gyT", [128, 2, 128], BF16)
    sgb = c.sb("sgb", [128, 2, 128], F32)
    ysT = [c.sb(f"ysT{i}", [128, 2, 128], BF16) for i in range(2)]
    plT = c.sb("plT", [128, 2, 128], BF16)
    ypT = [c.sb(f"ypT{i}", [128, 2, 128], BF16) for i in range(2)]
    pbu = c.ps("pbu")
    ppf = c.ps("ppf", [128, 2048])
    pym = c.ps("pym")
    pglu = c.ps("pglu")
    ppl = c.ps("ppl")
    py = pym
    w1 = c.sb("w1", [128, 2, 128], F32)
    w2 = c.sb("w2", [128, 2, 128], F32)
    w3 = c.sb("w3", [128, 2, 128], F32)
    w4 = c.sb("w4", [128, 2, 128], F32)

    make_identity(c, ident)
    make_tri(c, tribf, "tribf")
    s.op("pool", lambda e: e.iota(iot_i[:], pattern=[[1, 128]], base=0, channel_multiplier=0), writes=["iot_i"])
    CP(s, "dve", iot[:], iot_i[:], ["iot_i"], ["iot"])
    s.op("pool", lambda e: e.iota(pcol_i[:], pattern=[[0, 1]], base=0, channel_multiplier=1), writes=["pcol_i"])
    CP(s, "dve", pcol[:], pcol_i[:], ["pcol_i"], ["pcol"])
    TS(s, "dve", npcol[:], pcol[:], -1.0, ALU.mult, ["pcol"], ["npcol"])
    CP(s, "dve", identf[:], ident[:], ["ident"], ["identf"])

    s.dma("sp", are[:], P["ssm_a_re"][l:l + 1].rearrange("o g n -> o (g n)").partition_broadcast(128), writes=["are"])
    s.dma("act", aim[:], P["ssm_a_im"][l:l + 1].rearrange("o g n -> o (g n)").partition_broadcast(128), writes=["aim"])
    s.dma("sp", ldt[:], P["ssm_log_dt"][l:l + 1, :].partition_broadcast(128), writes=["ldt"])
    s.dma("sp", acol[:, 0, :], P["ssm_a_re"][l].rearrange("(k gl) n -> (gl n) k", gl=2), writes=["acol"],
          allow_slow_non_contiguous=True)
    s.dma("sp", acol[:, 1, :], P["ssm_a_im"][l].rearrange("(k gl) n -> (gl n) k", gl=2), writes=["acol"],
          allow_slow_non_contiguous=True)
    for gl in range(2):
        s.dma("sp", acol[64 * gl:64 * gl + 64, 2, :],
              P["ssm_log_dt"][l:l + 1, :].rearrange("o (k gl) -> o k gl", gl=2)[:, :, gl].partition_broadcast(64),
              writes=["acol"], allow_slow_non_contiguous=True)
    s.dma("sp", drb[:], P["ssm_d"][l:l + 1, :].partition_broadcast(128), writes=["drb"])
    s.dma("act", wg_st[:], P["w_glu"][l].rearrange("(kc p) g -> p kc g", p=128), writes=["wg_st"])
    CP(s, "pool", wglu[:], wg_st[:], ["wg_st"], ["wglu"])
    for g in range(4):
        s.dma("sp", pw_st[64 * (g % 2):64 * (g % 2) + 64, g // 2, :], P["pool_w"][l, g], writes=["pw_st"])
    CP(s, "pool", PW[:], pw_st[:], ["pw_st"], ["PW"])
    s.dma("sp", pscol[:], P["pool_scale"][l].rearrange("(kc p) -> p kc", p=128), writes=["pscol"],
          allow_slow_non_contiguous=True)
    s.op("pool", lambda e: e.memset(BX[:], 0.0), writes=["BX"])
    s.op("pool", lambda e: e.memset(CN[:], 0.0), writes=["CN"])
    nd = 0
    for k in range(8):
        for gl in range(2):
            g = 2 * k + gl
            c0 = 32 * (k % 4) + 16 * gl
            for part, nm in ((0, "ssm_b_re"), (1, "ssm_b_im")):
                s.dma("sp" if nd % 2 == 0 else "act", BX[64 * gl:64 * gl + 64, k, part, c0:c0 + 16], P[nm][l, g],
                      reads=[], writes=["BX"])
                nd += 1
            for part, nm in ((0, "ssm_c_re"), (1, "ssm_c_im")):
                s.dma("sp" if nd % 2 == 0 else "act", CN[16 * gl:16 * gl + 16, k, part, 64 * gl:64 * gl + 64],
                      P[nm][l, g], reads=[], writes=["CN"])
                nd += 1
    CP(s, "dve", BXb[:], BX[:], ["BX"], ["BXb"])
    CP(s, "dve", CNb[:], CN[:], ["CN"], ["CNb"])
    pmv = ppf[:, 0:512].bitcast(BF16)
    for k in range(8):
        for part in range(2):
            i = (k * 2 + part) % 8
            TR(s, pmv[:, i * 128:(i + 1) * 128], BXb[:, k, part, :], ident[:], ["BXb", "ident"], ["ppf"])
        if k % 4 == 3:
            k0 = k - 3
            CP(s, "dve", BT[:, k0:k0 + 4, :, :], pmv.rearrange("p (k a m) -> p k a m", k=4, a=2), ["ppf"], ["BT"])
    for k in range(8):
        for part in range(2):
            i = k * 2 + part
            TR(s, pmv[:, i * 32:(i + 1) * 32], CNb[:, k, part, :], ident[0:32, 0:32], ["CNb", "ident"], ["ppf"])
    cxv = pmv[:, 0:512].rearrange("p (k a m) -> p k a m", k=8, a=2)
    CP(s, "dve", CX[:, :, 0, :], cxv[:, :, 0, :], ["ppf"], ["CX"])
    TS(s, "dve", CX[:, :, 1, :], cxv[:, :, 1, :], -1.0, ALU.mult, ["ppf"], ["CX"])

    ACTF(s, ldt[:], ldt[:], AF.Exp, ["ldt"], ["ldt"])
    CP(s, "dve", dtr[:].rearrange("p (g n) -> p g n", g=16), ldt[:].unsqueeze(2).broadcast_to([128, 16, 64]),
       ["ldt"], ["dtr"])
    ardt, wr = t5, t6
    TT_(s, "dve", ardt[:], are[:], dtr[:], ALU.mult, ["are", "dtr"], ["ardt"])
    TT_(s, "dve", wr[:], aim[:], dtr[:], ALU.mult, ["aim", "dtr"], ["wr"])
    sincos(s, "dve", wr[:], t3[:], t4[:], t1[:], t2[:], ["wr"], "0")
    ACTF(s, t1[:], ardt[:], AF.Exp, ["ardt", "sc10"], ["mag1"])
    TT_(s, "dve", t4[:], t4[:], t1[:], ALU.mult, ["cos0", "mag1"], ["cos0"])
    TT_(s, "dve", t3[:], t3[:], t1[:], ALU.mult, ["sin0", "mag1"], ["sin0"])
    TS(s, "dve", t4[:], t4[:], -1.0, ALU.add, ["cos0"], ["cos0"])
    TT_(s, "dve", t1[:], are[:], are[:], ALU.mult, ["are", "mag1", "sin0"], ["mag1"])
    TT_(s, "dve", t2[:], aim[:], aim[:], ALU.mult, ["aim", "sc20"], ["sc20"])
    TT_(s, "dve", t1[:], t1[:], t2[:], ALU.add, ["mag1", "sc20"], ["mag1"])
    s.op("dve", lambda e: e.reciprocal(out=t1[:], in_=t1[:]), reads=["mag1"], writes=["mag1"])
    TT_(s, "dve", cfr[:], t4[:], are[:], ALU.mult, ["cos0", "are"], ["cfr"])
    TT_(s, "dve", t2[:], t3[:], aim[:], ALU.mult, ["sin0", "aim", "sc20"], ["sc20"])
    TT_(s, "dve", cfr[:], cfr[:], t2[:], ALU.add, ["cfr", "sc20"], ["cfr"])
    TT_(s, "dve", cfr[:], cfr[:], t1[:], ALU.mult, ["cfr", "mag1"], ["cfr"])
    TT_(s, "dve", cfi[:], t3[:], are[:], ALU.mult, ["sin0", "are"], ["cfi"])
    TT_(s, "dve", t2[:], t4[:], aim[:], ALU.mult, ["cos0", "aim", "sc20", "cfr"], ["sc20"])
    TT_(s, "dve", cfi[:], cfi[:], t2[:], ALU.subtract, ["cfi", "sc20"], ["cfi"])
    TT_(s, "dve", cfi[:], cfi[:], t1[:], ALU.mult, ["cfi", "mag1"], ["cfi"])
    TS(s, "dve", dtr[:], wr[:], pcol[:, 0:1], ALU.mult, ["wr", "pcol", "dtr"], ["ang"])
    sincos(s, "dve", dtr[:], t3[:], t4[:], t1[:], t2[:], ["ang", "cfi", "cfr"], "1")
    s.op("act", lambda e: e.activation(out=t1[:], in_=ardt[:], func=AF.Exp, scale=npcol[:, 0:1]),
         reads=["ardt", "npcol", "sc11", "cos1"], writes=["mag2"])
    TT_(s, "dve", t4[:], t4[:], t1[:], ALU.mult, ["cos1", "mag2"], ["cos1"])
    TT_(s, "dve", t3[:], t3[:], t1[:], ALU.mult, ["sin1", "mag2"], ["sin1"])
    TT_(s, "dve", Tr[:], t4[:], cfr[:], ALU.mult, ["cos1", "cfr"], ["Tr"])
    TT_(s, "dve", t2[:], t3[:], cfi[:], ALU.mult, ["sin1", "cfi", "sc21"], ["sc21"])
    TT_(s, "dve", Tr[:], Tr[:], t2[:], ALU.add, ["Tr", "sc21"], ["Tr"])
    TT_(s, "dve", Ti[:], t4[:], cfi[:], ALU.mult, ["cos1", "cfi"], ["Ti"])
    TT_(s, "dve", t2[:], t3[:], cfr[:], ALU.mult, ["sin1", "cfr", "Tr"], ["sc21"])
    TT_(s, "dve", Ti[:], Ti[:], t2[:], ALU.subtract, ["Ti", "sc21"], ["Ti"])
    ACTF(s, acol[:, 2, :], acol[:, 2, :], AF.Exp, ["acol"], ["acol"])
    TT_(s, "dve", acol[:, 0, :], acol[:, 0, :], acol[:, 2, :], ALU.mult, ["acol"], ["acol"])
    TT_(s, "dve", acol[:, 1, :], acol[:, 1, :], acol[:, 2, :], ALU.mult, ["acol"], ["acol"])
    angf = t5[:].rearrange("p (k t) -> p k t", k=8)
    magf = t6[:].rearrange("p (k t) -> p k t", k=8)
    for k in range(8):
        TS(s, "dve", angf[:, k, :], iot[:], acol[:, 1, k:k + 1], ALU.mult, ["iot", "acol", "ardt", "Tr", "Ti"], ["angf"])
        s.op("act", lambda e, k=k: e.activation(out=magf[:, k, :], in_=iot[:], func=AF.Exp, scale=acol[:, 0, k:k + 1]),
             reads=["iot", "acol", "wr", "ang", "Tr", "Ti"], writes=["magf"])
    sincos(s, "dve", t5[:], t3[:], t4[:], t1[:], t2[:], ["angf", "Tr", "Ti"], "2")
    TT_(s, "dve", TAr[:].rearrange("p k t -> p (k t)"), t4[:], t6[:], ALU.mult, ["cos2", "magf"], ["TAr"])
    TT_(s, "dve", TAi[:].rearrange("p k t -> p (k t)"), t3[:], t6[:], ALU.mult, ["sin2", "magf"], ["TAi"])

    for g, w in enumerate(POOL_WINDOWS):
        s.op("pool", lambda e, w=w: e.memset(mtmp[:], 1.0 / w), reads=["mtmp", "mrat", "MT"], writes=["mtmp"])
        s.op("pool", lambda e: e.affine_select(out=mtmp[:], in_=mtmp[:], pattern=[[1, 128]], compare_op=ALU.is_ge,
                                               fill=0.0, base=0, channel_multiplier=-1),
             reads=["mtmp"], writes=["mtmp"])
        s.op("pool", lambda e, w=w: e.affine_select(out=mtmp[:], in_=mtmp[:], pattern=[[-1, 128]],
                                                    compare_op=ALU.is_ge, fill=0.0, base=w - 1, channel_multiplier=1),
             reads=["mtmp"], writes=["mtmp"])
        TT_(s, "pool", MT[:, g * 3 + 0, :], mtmp[:], identf[:], ALU.subtract, ["mtmp", "identf"], ["MT"])
        TS(s, "dve", mrat[:], iot[:], 1.0, ALU.add, ["iot"], ["mrat"])
        s.op("dve", lambda e: e.reciprocal(out=mrat[:], in_=mrat[:]), reads=["mrat"], writes=["mrat"])
        TS(s, "dve", mrat[:], mrat[:], float(w), ALU.mult, ["mrat"], ["mrat"], scalar2=1.0, op1=ALU.max)
        TT_(s, "dve", mrat[:], mrat[:], mtmp[:], ALU.mult, ["mrat", "mtmp"], ["mrat"])
        TT_(s, "dve", MT[:, g * 3 + 2, :], mrat[:], identf[:], ALU.subtract, ["mrat", "identf"], ["MT"])
        s.op("pool", lambda e, w=w: e.memset(mtmp[:], 1.0 / w), reads=["mtmp", "mrat", "MT"], writes=["mtmp"])
        s.op("pool", lambda e, w=w: e.affine_select(out=mtmp[:], in_=mtmp[:], pattern=[[-1, 128]],
                                                    compare_op=ALU.is_ge, fill=0.0, base=-(129 - w),
                                                    channel_multiplier=1),
             reads=["mtmp"], writes=["mtmp"])
        CP(s, "pool", MT[:, g * 3 + 1, :], mtmp[:], ["mtmp"], ["MT"])

    NCH = NSEQ * NJ
    pmq = pym[:, 256:512].bitcast(BF16)

    def load(n):
        q_, j = divmod(n, NJ)
        s.dma("sp", ut[n % 3][:], scr["u"][q_, j * 128:(j + 1) * 128, :], writes=[f"ut{n % 3}"])

    def S1(n):
        U, UB = ut[n % 3], ub[n % 3]
        un, ubn = f"ut{n % 3}", f"ub{n % 3}"
        CP(s, "act", UB[:], U[:], [un], [ubn])
        for kc in range(2):
            TR(s, pmq[:, kc * 128:(kc + 1) * 128], UB[:, kc * 128:(kc + 1) * 128], ident[:], [ubn, "ident"], ["pym"])
        CP(s, "dve", uT[:], pmq[:, 0:256].rearrange("p (k t) -> p k t", k=2), ["pym"], ["uT"])
        for qt in range(4):
            k0 = qt * 2
            for kk in range(2):
                k = k0 + kk
                MM(s, pbu[:, kk * 256:(kk + 1) * 256], uT[:, k // 4, :], BT[:, k, :, :].rearrange("p a m -> p (a m)"),
                   ["uT", "BT"], ["pbu"])
            buv = pbu[:].rearrange("p (k a m) -> p k a m", k=2, a=2)
            trv = Tr[:, k0 * 128:(k0 + 2) * 128].rearrange("p (k m) -> p k m", k=2)
            tiv = Ti[:, k0 * 128:(k0 + 2) * 128].rearrange("p (k m) -> p k m", k=2)
            yield
            TT_(s, "dve", w1[:], buv[:, :, 0, :], trv, ALU.mult, ["pbu", "Tr"], ["w1"])
            TT_(s, "dve", w2[:], buv[:, :, 1, :], tiv, ALU.mult, ["pbu", "Ti"], ["w2"])
            yield
            TT_(s, "pool", Wt[:, k0:k0 + 2, 0, :], w1[:], w2[:], ALU.subtract, ["w1", "w2"], [f"Wt{qt}"])
            TT_(s, "dve", w3[:], buv[:, :, 1, :], trv, ALU.mult, ["pbu", "Tr"], ["w3"])
            TT_(s, "dve", w4[:], buv[:, :, 0, :], tiv, ALU.mult, ["pbu", "Ti"], ["w4"])
            yield
            TT_(s, "pool", Wt[:, k0:k0 + 2, 1, :], w3[:], w4[:], ALU.add, ["w3", "w4"], [f"Wt{qt}"])
            yield
            for kk in range(2):
                for part in range(2):
                    i = (k0 + kk) * 2 + part
                    MM(s, ppf[:, i * 128:(i + 1) * 128], Wt[:, k0 + kk, part, :], tribf[:], [f"Wt{qt}", "tribf"],
                       [f"ppf{qt // 2}"])

    def XP(n):
        q_, j = divmod(n, NJ)
        if j == 0:
            s.op("pool", lambda e: e.memset(car[:], 0.0), writes=["car"])
        for hf in range(2):
            k0 = hf * 4
            pfv = ppf[:, hf * 1024:(hf + 1) * 1024].rearrange("p (k a t) -> p k a t", k=4, a=2)
            pfn = f"ppf{hf}"
            TT_(s, "dve", Pr[:, k0:k0 + 4, :], pfv[:, :, 0, :],
                car[:, 0, k0:k0 + 4].unsqueeze(2).broadcast_to([128, 4, 128]), ALU.add, [pfn, "car"], [f"Pr{hf}"])
            TT_(s, "dve", Pi[:, k0:k0 + 4, :], pfv[:, :, 1, :],
                car[:, 1, k0:k0 + 4].unsqueeze(2).broadcast_to([128, 4, 128]), ALU.add, [pfn, "car"], [f"Pi{hf}"])
        yield
        for hf in range(2):
            k0 = hf * 4
            prn, pin = f"Pr{hf}", f"Pi{hf}"
            PR, PI = Pr[:, k0:k0 + 4, :], Pi[:, k0:k0 + 4, :]
            tar, tai = TAr[:, k0:k0 + 4, :], TAi[:, k0:k0 + 4, :]
            TT_(s, "dve", m1[:], tar, PR, ALU.mult, ["TAr", prn], ["m1"])
            TT_(s, "pool", m2[:], tai, PI, ALU.mult, ["TAi", pin], ["m2"])
            yield
            TT_(s, "dve", m3[:], tar, PI, ALU.mult, ["TAr", pin], ["m3"])
            TT_(s, "pool", m4[:], tai, PR, ALU.mult, ["TAi", prn], ["m4"])
            yield
            TT_(s, "pool", Xt[:, k0:k0 + 4, 0, :], m1[:], m2[:], ALU.subtract, ["m1", "m2"], [f"Xt{hf}"])
            TT_(s, "dve", sn[:, 0, :], m1[:, :, 127], m2[:, :, 127], ALU.subtract, ["m1", "m2"], ["sn"])
            yield
            TT_(s, "dve", Xt[:, k0:k0 + 4, 1, :], m3[:], m4[:], ALU.add, ["m3", "m4"], [f"Xt{hf}"])
            TT_(s, "dve", sn[:, 1, :], m3[:, :, 127], m4[:, :, 127], ALU.add, ["m3", "m4"], ["sn"])
            yield
            a1r, a1i = TAr[:, k0:k0 + 4, 1], TAi[:, k0:k0 + 4, 1]
            TT_(s, "dve", sm[:, 0, :], a1r, sn[:, 0, :], ALU.mult, ["TAr", "sn"], ["sm"])
            TT_(s, "dve", sm[:, 1, :], a1i, sn[:, 1, :], ALU.mult, ["TAi", "sn"], ["sm"])
            TT_(s, "dve", sm[:, 2, :], a1r, sn[:, 1, :], ALU.mult, ["TAr", "sn"], ["sm"])
            TT_(s, "dve", sm[:, 3, :], a1i, sn[:, 0, :], ALU.mult, ["TAi", "sn"], ["sm"])
            yield
            TT_(s, "dve", car[:, 0, k0:k0 + 4], sm[:, 0, :], sm[:, 1, :], ALU.subtract, ["sm"], ["car"])
            TT_(s, "dve", car[:, 1, k0:k0 + 4], sm[:, 2, :], sm[:, 3, :], ALU.add, ["sm"], ["car"])
            yield

    def REST(n):
        q_, j = divmod(n, NJ)
        U, UB = ut[n % 3], ub[n % 3]
        un, ubn = f"ut{n % 3}", f"ub{n % 3}"
        UBP, ubpn = ub[(n - 1) % 3], f"ub{(n - 1) % 3}"
        for k in range(8):
            for part in range(2):
                MM(s, py[:, 32 * k:32 * k + 32], Xt[:, k, part, :], CX[:, k, part, :], [f"Xt{k // 4}", "CX"], ["pym"],
                   start=(part == 0), stop=(part == 1))
        yield
        TT_(s, "pool", du[:], U[:, 0:256], drb[:], ALU.mult, [un, "drb"], ["du"])
        yield
        TT_(s, "dve", yv[:], py[:, 0:256], du[:], ALU.add, ["pym", "du"], ["yv"])
        yield
        TT_(s, "pool", y2[:], yv[:], yv[:], ALU.mult, ["yv"], ["y2"])
        yield
        TS(s, "pool", y2[:], y2[:], 0.044715, ALU.mult, ["y2"], ["y2"], scalar2=1.0, op1=ALU.add)
        yield
        TT_(s, "pool", y2[:], y2[:], yv[:], ALU.mult, ["y2", "yv"], ["y2"])
        ACTF(s, sg[:], y2[:], AF.Sigmoid, ["y2"], ["sg"], scale=1.5957691216057308)
        yield
        TT_(s, "dve", gy[:], yv[:], sg[:], ALU.mult, ["yv", "sg"], ["gy"])
        yield
        for kc in range(2):
            TR(s, pmq[:, 256 + kc * 128:256 + (kc + 1) * 128], gy[:, kc * 128:(kc + 1) * 128], ident[:],
               ["gy", "ident"], ["pym"])
        CP(s, "act", gyT[:], pmq[:, 256:512].rearrange("p (k t) -> p k t", k=2), ["pym"], ["gyT"])
        for gc in range(4):
            for kc in range(2):
                MM(s, pglu[:, gc * 128:(gc + 1) * 128], wglu[:, kc, gc * 128:(gc + 1) * 128], gyT[:, kc, :],
                   ["wglu", "gyT"], ["pglu"], start=(kc == 0), stop=(kc == 1))
        yield
        ACTF(s, sgb[:], pglu[:, 256:512].rearrange("p (k t) -> p k t", k=2), AF.Sigmoid, ["pglu"], ["sgb"])
        yield
        YS = ysT[n % 2]
        TT_(s, "dve", YS[:], pglu[:, 0:256].rearrange("p (k t) -> p k t", k=2), sgb[:], ALU.mult, ["pglu", "sgb"],
            [f"ysT{n % 2}"])
        s.dma("pool", scr["yssmT"][q_, :, j * 128:(j + 1) * 128].rearrange("(kc p) t -> p kc t", p=128), YS[:],
              reads=[f"ysT{n % 2}"])
        yield
        for g in range(4):
            o = ppl[64 * (g % 2):64 * (g % 2) + 64, (g // 2) * 128:(g // 2 + 1) * 128]
            if j == 0:
                MM(s, o, UB[:, 256 + 64 * g:256 + 64 * g + 64], MT[:, g * 3 + 2, :], [ubn, "MT"], ["ppl"])
            else:
                MM(s, o, UB[:, 256 + 64 * g:256 + 64 * g + 64], MT[:, g * 3 + 0, :], [ubn, "MT"], ["ppl"],
                   start=True, stop=False)
                MM(s, o, UBP[:, 256 + 64 * g:256 + 64 * g + 64], MT[:, g * 3 + 1, :], [ubpn, "MT"], ["ppl"],
                   start=False, stop=True)
        CP(s, "act", plT[:], ppl[:, 0:256].rearrange("p (k t) -> p k t", k=2), ["ppl"], ["plT"])
        for g in range(4):
            pb = 64 * (g % 2)
            MM(s, ppl[pb:pb + 64, 256 + (g // 2) * 128:256 + (g // 2 + 1) * 128], PW[pb:pb + 64, g // 2, :],
               plT[pb:pb + 64, g // 2, :], ["PW", "plT"], ["ppl"])
        yield
        YP = ypT[n % 2]
        for kc in range(2):
            TS(s, "dve", YP[:, kc, :], ppl[:, 256 + kc * 128:256 + (kc + 1) * 128], pscol[:, kc:kc + 1], ALU.mult,
               ["ppl", "pscol"], [f"ypT{n % 2}"])
        s.dma("pool", scr["ypoolT"][q_, :, j * 128:(j + 1) * 128].rearrange("(kc p) t -> p kc t", p=128), YP[:],
              reads=[f"ypT{n % 2}"])

    load(0)
    if NCH > 1:
        load(1)
    def chain(*gs):
        for g in gs:
            yield from g

    for t in range(-1, NCH):
        if 0 <= t + 2 < NCH and t + 2 >= 2:
            load(t + 2)
        gens = []
        if t >= 0:
            gens.append(chain(XP(t), REST(t)))
        if t + 1 < NCH:
            gens.append(S1(t + 1))
        while gens:
            for g in list(gens):
                try:
                    next(g)
                except StopIteration:
                    gens.remove(g)
    c.end()


def build_test_a2(NSEQ, S, l=0):
    nc = bass.Bass("TRN2", target_bir_lowering=False)
    u = nc.dram_tensor("u_in", [NSEQ, S, 512], F32, kind="ExternalInput").ap()
    P = declare_params(nc)
    scr = make_scratch(nc, NSEQ, S, debug_out=True)
    scr["u"] = u
    with contextlib.ExitStack() as es:
        c = Ctx(nc, es)
        phase_a2(c, l, NSEQ, S, P, scr)
        print("ops", c.s.total, "waits", c.s.nwaits)
    return nc


def phase_c(c, l, NSEQ, S, x_d, xout_d, P, scr):
    nc, s = c.nc, c.s
    NJ = S // 128
    c.begin()
    wg = c.sb("wg", [128, 8, 3 * D], BF16)
    wba = c.sb("wba", [128, 4, D], BF16)
    wbs = c.sb("wbs", [128, 2, D], BF16)
    wbp = c.sb("wbp", [128, 2, D], BF16)
    wout = c.sb("wout", [128, 8, D], BF16)
    bg = c.sb("bg", [128, 3 * D], F32)
    stg = [c.sb(f"stg{i}", [128, 3 * D], F32) for i in range(2)]
    gcol = c.sb("gcol", [128, 8], F32)
    ident = c.sb("ident", [128, 128], BF16)
    xt = [c.sb(f"xt{i}", [128, D], F32) for i in range(5)]
    junk = c.sb("junk", [128, D], F32)
    ss = c.sb("ss", [128, 1], F32)
    rstd = c.sb("rstd", [128, 1], F32)
    hb = c.sb("hb", [128, D], BF16)
    hT = [c.sb(f"hT{i}", [128, 8, 128], BF16) for i in range(2)]
    gates_b = [c.sb(f"gates{i}", [128, 3 * D], F32) for i in range(2)]
    ya = [c.sb(f"ya{i}", [128, 512], BF16) for i in range(2)]
    yaT = c.sb("yaT", [128, 4, 128], BF16)
    ysT = [c.sb(f"ysT{i}", [128, 2, 128], BF16) for i in range(2)]
    ypT = [c.sb(f"ypT{i}", [128, 2, 128], BF16) for i in range(2)]
    macc = [c.sb(f"macc{i}", [128, 512], F32) for i in range(2)]
    mtmp = [c.sb(f"mtmp{i}", [128, 512], F32) for i in range(2)]
    mrg = [c.sb(f"mrg{i}", [128, D], BF16) for i in range(2)]
    mT = c.sb("mT", [128, 8, 128], BF16)
    pt = c.ps("pt")
    pg = [c.ps(f"pg{i}") for i in range(2)]
    pbr = [c.ps(f"pbr{i}") for i in range(3)]
    po = [c.ps(f"po{i}") for i in range(2)]

    make_identity(c, ident)
    s.dma("sp", gcol[:], P["mix_norm"][l].rearrange("(kc p) -> p kc", p=128), writes=["gcol"],
          allow_slow_non_contiguous=True)
    s.dma("act", bg[:], P["b_gate"][l:l + 1, :].partition_broadcast(128), writes=["bg"])
    n = 0
    for kc in range(8):
        st = stg[n % 2]
        s.dma("sp" if n % 2 == 0 else "act", st[:], P["w_in"][l, kc * 128:(kc + 1) * 128, 2056:INC],
              writes=[f"stg{n % 2}"])
        _cast(s, _cast_engine(n), wg[:, kc, :], st[:], [f"stg{n % 2}", "gcol"], [f"wg{kc}"], scalar=gcol[:, kc:kc + 1])
        n += 1
    for (wt, nm, nk) in ((wba, "w_br_attn", 4), (wbs, "w_br_ssm", 2), (wbp, "w_br_pool", 2), (wout, "w_out", 8)):
        for k0 in range(0, nk, 2):
            st = stg[n % 2]
            s.dma("sp" if n % 2 == 0 else "act", st[:, 0:2 * D].rearrange("p (a d) -> p a d", a=2),
                  P[nm][l, k0 * 128:(k0 + 2) * 128, :].rearrange("(a p) d -> p a d", p=128), writes=[f"stg{n % 2}"])
            _cast(s, _cast_engine(n), wt[:, k0:k0 + 2, :], st[:, 0:2 * D].rearrange("p (a d) -> p a d", a=2),
                  [f"stg{n % 2}"], [nm])
            n += 1

    NCH = NSEQ * NJ
    ptv = pt[:].bitcast(BF16)

    def load(n):
        q_, j = divmod(n, NJ)
        b = n % 2
        s.dma("sp", xt[n % 5][:], x_d[n * 128:(n + 1) * 128, :], writes=[f"xt{n % 5}"])

    def load_y(n):
        q_, j = divmod(n, NJ)
        b = n % 2
        s.dma("sp", ya[b][:], scr["yattn"][q_, j * 128:(j + 1) * 128, :], writes=[f"ya{b}"])
        s.dma("sp", ysT[b][:], scr["yssmT"][q_, :, j * 128:(j + 1) * 128].rearrange("(kc p) t -> p kc t", p=128),
              writes=[f"ysT{b}"])
        s.dma("sp", ypT[b][:], scr["ypoolT"][q_, :, j * 128:(j + 1) * 128].rearrange("(kc p) t -> p kc t", p=128),
              writes=[f"ypT{b}"])

    def s1(n):
        X, xb = xt[n % 5], f"xt{n % 5}"
        HT, htn = hT[n % 2], f"hT{n % 2}"
        rms_rstd(c, X[:], xb, junk[:], ss[:], rstd[:], D, "")
        yield
        TS(s, "dve", hb[:], X[:], rstd[:, 0:1], ALU.mult, [xb, "rstd"], ["hb"])
        yield
        for kc in range(8):
            TR(s, ptv[:, kc * 128:(kc + 1) * 128], hb[:, kc * 128:(kc + 1) * 128], ident[:], ["hb", "ident"], ["pt"])
        CP(s, "act", HT[:], ptv.rearrange("p (k t) -> p k t", k=8), ["pt"], [htn])
        yield

    def s2_gates(n):
        HT, htn = hT[n % 2], f"hT{n % 2}"
        gates = gates_b[n % 2]
        gp = f"g{n % 2}_"
        for gb in range(6):
            PG, pgn = pg[gb % 2], f"pg{gb % 2}"
            for kc in range(8):
                MM(s, PG[:], HT[:, kc, :], wg[:, kc, gb * 512:(gb + 1) * 512], [htn, f"wg{kc}"], [pgn],
                   start=(kc == 0), stop=(kc == 7))
                yield
            gsl = gates[:, gb * 512:(gb + 1) * 512]
            TT_(s, "dve", gsl, PG[:], bg[:, gb * 512:(gb + 1) * 512], ALU.add, [pgn, "bg"], [gp + f"gate{gb}"])
            yield
            ACTF(s, gsl, gsl, AF.Sigmoid, [gp + f"gate{gb}"], [gp + f"gate{gb}"])
            yield

    def s2_ya(n):
        b = n % 2
        for kc in range(4):
            TR(s, ptv[:, kc * 128:(kc + 1) * 128], ya[b][:, kc * 128:(kc + 1) * 128], ident[:], [f"ya{b}", "ident"],
               ["pt"])
        CP(s, "dve", yaT[:], ptv[:, 0:512].rearrange("p (k t) -> p k t", k=4), ["pt"], ["yaT"])
        yield

    def s2_branch(n, dbs):
        b = n % 2
        MR = mrg[b]
        gates = gates_b[n % 2]
        gp = f"g{n % 2}_"
        for db in dbs:
            cs = slice(db * 512, (db + 1) * 512)
            for kc in range(4):
                MM(s, pbr[0][:], yaT[:, kc, :], wba[:, kc, cs], ["yaT", "w_br_attn"], ["pbr0"], start=(kc == 0),
                   stop=(kc == 3))
                yield
            for kc in range(2):
                MM(s, pbr[1][:], ysT[b][:, kc, :], wbs[:, kc, cs], [f"ysT{b}", "w_br_ssm"], ["pbr1"], start=(kc == 0),
                   stop=(kc == 1))
                yield
            for kc in range(2):
                MM(s, pbr[2][:], ypT[b][:, kc, :], wbp[:, kc, cs], [f"ypT{b}", "w_br_pool"], ["pbr2"], start=(kc == 0),
                   stop=(kc == 1))
                yield
            MA, TM = macc[db], mtmp[db]
            TT_(s, "dve", MA[:], pbr[0][:], gates[:, db * 512:(db + 1) * 512], ALU.mult, ["pbr0", gp + f"gate{db}"],
                [f"macc{db}"])
            yield
            TT_(s, "dve", TM[:], pbr[1][:], gates[:, D + db * 512:D + (db + 1) * 512], ALU.mult,
                ["pbr1", gp + f"gate{2 + db}"], [f"mtmp{db}"])
            yield
            TT_(s, "pool", MA[:], MA[:], TM[:], ALU.add, [f"macc{db}", f"mtmp{db}"], [f"macc{db}"])
            yield
            TT_(s, "dve", TM[:], pbr[2][:], gates[:, 2 * D + db * 512:2 * D + (db + 1) * 512], ALU.mult,
                ["pbr2", gp + f"gate{4 + db}"], [f"mtmp{db}"])
            yield
            TT_(s, "pool", MR[:, cs], MA[:], TM[:], ALU.add, [f"macc{db}", f"mtmp{db}"], [f"mrg{b}_{db}"])
            yield

    def s3(n):
        b = n % 2
        MR = mrg[b]
        X, xb = xt[n % 5], f"xt{n % 5}"
        for kc in range(8):
            TR(s, ptv[:, kc * 128:(kc + 1) * 128], MR[:, kc * 128:(kc + 1) * 128], ident[:],
               [f"mrg{b}_{kc // 4}", "ident"], ["pt"])
        CP(s, "act", mT[:], ptv.rearrange("p (k t) -> p k t", k=8), ["pt"], ["mT"])
        yield
        for db in range(2):
            cs = slice(db * 512, (db + 1) * 512)
            for kc in range(8):
                MM(s, po[db][:], mT[:, kc, :], wout[:, kc, cs], ["mT", "w_out"], [f"po{db}"], start=(kc == 0),
                   stop=(kc == 7))
                yield
            TT_(s, "dve", X[:, cs], po[db][:], X[:, cs], ALU.add, [f"po{db}", xb], [xb])
            yield
        s.dma("pool", xout_d[n * 128:(n + 1) * 128, :], X[:], reads=[xb])
        yield

    def chain(*gs):
        for g in gs:
            yield from g

    def run(gens):
        gens = list(gens)
        while gens:
            for g in list(gens):
                try:
                    next(g)
                except StopIteration:
                    gens.remove(g)

    load(0)
    load_y(0)
    if NCH > 1:
        load(1)
    run([s1(0)])
    for t in range(NCH + 2):
        if t + 2 < NCH:
            load(t + 2)
        gens = []
        if t < NCH:
            gens.append(s2_gates(t))
        if t + 1 < NCH:
            gens.append(s1(t + 1))
        if 1 <= t <= NCH:
            gens.append(chain(s2_ya(t - 1), s2_branch(t - 1, [0, 1])))
        if t >= 2:
            gens.append(s3(t - 2))
        run(gens)
        if t + 1 < NCH:
            load_y(t + 1)
    c.end()


def build_full(NSEQ, S, depth=DEPTH):
    T = NSEQ * S
    nc = bass.Bass("TRN2", target_bir_lowering=False)
    x = nc.dram_tensor("x", [T, D], F32, kind="ExternalInput").ap()
    y = nc.dram_tensor("y", [T, D], F32, kind="ExternalOutput").ap()
    P = declare_params(nc)
    scr = make_scratch(nc, NSEQ, S)
    xa = nc.dram_tensor("scr_xa", [T, D], F32).ap()
    with contextlib.ExitStack() as es:
        c = Ctx(nc, es)
        G = {"cumall": es.enter_context(nc.sbuf_tensor("cumall", [128, NSEQ, S // 128, 8], F32))}
        for l in range(depth):
            xin = x if l == 0 else xa
            phase_a(c, l, NSEQ, S, xin, P, G, scr)
            phase_a2(c, l, NSEQ, S, P, scr)
            phase_b(c, NSEQ, S, G, scr)
            phase_c(c, l, NSEQ, S, xin, xa, P, scr)
            phase_mlp(c, l, T, xa, P["w_up"], P["w_down"], P["mlp_norm"], xout_d=(y if l == depth - 1 else xa))
        print("ops", c.s.total, "waits", c.s.nwaits, flush=True)
    return nc


_NC_CACHE = {}


def kernel(**inputs):
    x = np.ascontiguousarray(np.asarray(inputs["x"], dtype=np.float32))
    B, S, _ = x.shape
    NSEQ = B // NCORES
    key = (NSEQ, S)
    if key not in _NC_CACHE:
        _NC_CACHE[key] = build_full(NSEQ, S)
    nc = _NC_CACHE[key]
    params = {k: np.ascontiguousarray(np.asarray(inputs[k], dtype=np.float32)) for k in PARAM_SHAPES}
    in_maps = []
    for cid in range(NCORES):
        m = {"x": x[cid * NSEQ:(cid + 1) * NSEQ].reshape(NSEQ * S, D)}
        m.update(params)
        in_maps.append(m)
    res = run_bass_kernel_spmd(nc, in_maps, core_ids=list(range(NCORES)))
    out = np.empty((B, S, D), dtype=np.float32)
    for cid in range(NCORES):
        out[cid * NSEQ:(cid + 1) * NSEQ] = np.asarray(res.results[cid]["y"]).reshape(NSEQ, S, D)
    return out
```

```python
import contextlib
import math
import numpy as np
import concourse.bass as bass
import concourse.mybir as mybir
from concourse.bass_utils import run_bass_kernel_spmd

F32 = mybir.dt.float32
BF16 = mybir.dt.bfloat16
ALU = mybir.AluOpType
AF = mybir.ActivationFunctionType
AX = mybir.AxisListType

D = 1024
DEPTH = 2
DFF = 4096
NH = 8
DH = 64
INC = 5128
EPS = 1e-6
NCORES = 8

ENGS = ("pe", "act", "dve", "pool", "sp")
N_DMA_SEMS = 12
SAME_ENGINE_SYNC = True


class Buf:
    __slots__ = ("name", "w", "r")

    def __init__(self, name):
        self.name = name
        self.w = None
        self.r = {}


class Op:
    __slots__ = ("eng", "idx", "fn", "deps", "dma", "needs_inc", "sem", "semval")

    def __init__(self, eng, idx, fn, deps, dma):
        self.eng, self.idx, self.fn, self.deps, self.dma = eng, idx, fn, deps, dma
        self.needs_inc = False
        self.sem = None
        self.semval = None


class Sched:
    def __init__(self, nc, es, same_engine_sync=SAME_ENGINE_SYNC):
        self.nc = nc
        self.q = {e: [] for e in ENGS}
        self.ndma = {e: 0 for e in ENGS}
        self.cnt = {e: 0 for e in ENGS}
        self.same_engine_sync = same_engine_sync
        self.bufs = {}
        self.esem = {e: es.enter_context(nc.semaphore("es_" + e)) for e in ENGS if e != "sp"}
        self.dsem = {}
        for e in ("sp", "act", "pool"):
            for j in range(N_DMA_SEMS):
                self.dsem[(e, j)] = es.enter_context(nc.semaphore(f"ds_{e}_{j}"))
        self.total = {e: 0 for e in ENGS}
        self.nwaits = {e: 0 for e in ENGS}

    def buf(self, name):
        b = self.bufs.get(name)
        if b is None:
            b = self.bufs[name] = Buf(name)
        return b

    def _b(self, x):
        return x if isinstance(x, Buf) else self.buf(x)

    def op(self, eng, fn, reads=(), writes=(), dma=False):
        reads = [self._b(x) for x in reads]
        writes = [self._b(x) for x in writes]
        deps = {}
        for b in reads:
            if b.w is not None:
                deps[id(b.w)] = b.w
        for b in writes:
            if b.w is not None:
                deps[id(b.w)] = b.w
            for o in b.r.values():
                deps[id(o)] = o
        o = Op(eng, len(self.q[eng]), fn, list(deps.values()), dma)
        if dma:
            i = self.ndma[eng]
            self.ndma[eng] += 1
            o.sem = (eng, i % N_DMA_SEMS)
            o.semval = 16 * (i // N_DMA_SEMS + 1)
        self.q[eng].append(o)
        for b in reads:
            key = ("dma", eng, o.idx) if dma else eng
            b.r[key] = o
        for b in writes:
            b.w = o
            b.r = {}
        for d in o.deps:
            if not d.dma:
                d.needs_inc = True
        return o

    def dma(self, eng, out, in_, reads=(), writes=(), **kw):
        return self.op(eng, lambda e: e.dma_start(out=out, in_=in_, **kw), reads, writes, dma=True)

    def emit(self):
        nc = self.nc
        for e in ENGS:
            c = self.cnt[e]
            for o in self.q[e]:
                if not o.dma and o.needs_inc:
                    c += 1
                    o.sem = e
                    o.semval = c
            self.cnt[e] = c

        def replay(e, engobj):
            known = {}
            for o in self.q[e]:
                waits = {}
                for d in o.deps:
                    if d.eng == e and not d.dma:
                        if e == "pe" or (not self.same_engine_sync and e != "pool"):
                            continue
                    k = d.sem
                    if waits.get(k, 0) < d.semval:
                        waits[k] = d.semval
                if o.dma and o.semval > 16:
                    k = o.sem
                    waits[k] = max(waits.get(k, 0), o.semval - 16)
                for k, v in waits.items():
                    if known.get(k, 0) < v:
                        known[k] = v
                        s = self.dsem[k] if isinstance(k, tuple) else self.esem[k]
                        engobj.wait_ge(s, v)
                        self.nwaits[e] += 1
                ins = o.fn(engobj)
                if o.dma:
                    ins.then_inc(self.dsem[o.sem], 16)
                elif o.needs_inc:
                    ins.then_inc(self.esem[e], 1)
            if self.ndma[e]:
                n = self.ndma[e]
                for j in range(min(N_DMA_SEMS, n)):
                    last = ((n - 1 - j) // N_DMA_SEMS) + 1
                    if known.get((e, j), 0) < 16 * last:
                        engobj.wait_ge(self.dsem[(e, j)], 16 * last)

        with nc.Block() as block:
            @block.sync
            def _(e):
                replay("sp", e)

            @block.tensor
            def _(e):
                replay("pe", e)

            @block.scalar
            def _(e):
                replay("act", e)

            @block.vector
            def _(e):
                replay("dve", e)

            @block.gpsimd
            def _(e):
                replay("pool", e)

        for e in ENGS:
            self.total[e] += len(self.q[e])
            self.q[e] = []
        for b in self.bufs.values():
            b.w = None
            b.r = {}


class Ctx:
    def __init__(self, nc, es):
        self.nc = nc
        self.es = es
        self.s = Sched(nc, es)
        self.pes = None
        self.uid = 0

    def begin(self):
        self.pes = contextlib.ExitStack()
        self.pes.__enter__()

    def end(self):
        self.s.emit()
        self.pes.close()
        self.pes = None

    def dbg(self, name, ap, reads):
        if not getattr(self, "debug", False):
            return
        d = self.nc.dram_tensor("dbg_" + name, list(ap.shape), ap.dtype, kind="ExternalOutput").ap()
        self.s.dma("sp", d, ap, reads=reads)

    def sb(self, name, shape, dtype):
        self.uid += 1
        return self.pes.enter_context(self.nc.sbuf_tensor(f"{name}_{self.uid}", list(shape), dtype))

    def ps(self, name, shape=(128, 512), dtype=F32):
        self.uid += 1
        return self.pes.enter_context(self.nc.psum_tensor(f"{name}_{self.uid}", list(shape), dtype))


def _cast_engine(i):
    return ("dve", "pool", "act")[i % 3]


def _cast(s, eng, out, in_, reads, writes, scalar=None):
    if scalar is None:
        if eng == "act":
            return s.op("act", lambda e: e.copy(out=out, in_=in_), reads, writes)
        return s.op(eng, lambda e: e.tensor_copy(out=out, in_=in_), reads, writes)
    if eng == "act":
        return s.op("act", lambda e: e.activation(out=out, in_=in_, func=AF.Copy, scale=scalar), reads, writes)
    return s.op(eng, lambda e: e.tensor_scalar(out=out, in0=in_, scalar1=scalar, scalar2=None, op0=ALU.mult),
                reads, writes)


def make_identity(c, ident):
    s = c.s
    s.op("pool", lambda e: e.memset(ident[:], 1.0), writes=["ident"])
    s.op("pool", lambda e: e.affine_select(out=ident[:], in_=ident[:], pattern=[[-1, 128]],
                                           compare_op=ALU.is_equal, fill=0.0, base=0, channel_multiplier=1),
         reads=["ident"], writes=["ident"])


def phase_mlp(c, l, T, x_d, w_up, w_down, mlp_norm, xout_d=None):
    nc, s = c.nc, c.s
    if xout_d is None:
        xout_d = x_d
    c.begin()
    TT = 256
    NT = T // TT
    wup = c.sb("wup", [128, 8, DFF], BF16)
    wdn = c.sb("wdn", [128, 32, D], BF16)
    NSTG = 4
    stg = [c.sb(f"stg{i}", [128, 1024], F32) for i in range(NSTG)]
    gcol = c.sb("gcol", [128, 8], F32)
    ident = c.sb("ident", [128, 128], BF16)
    xt = [c.sb(f"xt{i}", [128, 2, D], F32) for i in range(3)]
    hb = c.sb("hb", [128, 2, D], BF16)
    hT = [c.sb(f"hT{i}", [128, 8, TT], BF16) for i in range(2)]
    actT = c.sb("actT", [128, 32, TT], BF16)
    rl = [c.sb(f"rl{i}", [128, TT], F32) for i in range(2)]
    ss = c.sb("ss", [128, 2], F32)
    rstd = c.sb("rstd", [128, 2], F32)
    junk = c.sb("junk", [128, 2, D], BF16)
    pt = [c.ps(f"pt{i}") for i in range(2)]
    pu = [c.ps(f"pu{i}") for i in range(3)]
    pd = [c.ps(f"pd{i}") for i in range(3)]

    make_identity(c, ident)
    s.dma("sp", gcol[:], mlp_norm[l].rearrange("(kc p) -> p kc", p=128), writes=["gcol"],
          allow_slow_non_contiguous=True)
    n = 0
    dq = ("sp", "act", "pool")
    for kc in range(8):
        for qq in range(4):
            st, sn_ = stg[n % NSTG], f"stg{n % NSTG}"
            s.dma(dq[n % 3], st[:], w_up[l, kc * 128:(kc + 1) * 128, qq * 1024:(qq + 1) * 1024], writes=[sn_])
            _cast(s, ("dve", "act")[n % 2], wup[:, kc, qq * 1024:(qq + 1) * 1024], st[:],
                  [sn_, "gcol"], [f"wup{kc}_{qq}"], scalar=gcol[:, kc:kc + 1])
            n += 1
    for fc in range(32):
        st, sn_ = stg[n % NSTG], f"stg{n % NSTG}"
        s.dma(dq[n % 3], st[:], w_down[l, fc * 128:(fc + 1) * 128, :], writes=[sn_])
        _cast(s, ("dve", "act")[n % 2], wdn[:, fc, :], st[:], [sn_], [f"wdn{fc}"])
        n += 1

    def load(i):
        s.dma("sp", xt[i % 3][:], x_d[i * TT:(i + 1) * TT, :].rearrange("(a p) d -> p a d", p=128),
              writes=[f"xt{i % 3}"])

    def front(i):
        X, xb = xt[i % 3], f"xt{i % 3}"
        HT, htn = hT[i % 2], f"hT{i % 2}"
        for a in range(2):
            ACTF(s, junk[:, a, :], X[:, a, :], AF.Square, [xb], [f"junk{a}"])
        s.op("dve", lambda e: e.tensor_reduce(out=ss[:], in_=junk[:], axis=AX.X, op=ALU.add),
             reads=["junk0", "junk1"], writes=["ss"])
        ACTF(s, rstd[:], ss[:], AF.Sqrt, ["ss"], ["rstd"], scale=1.0 / D, bias=EPS)
        s.op("dve", lambda e: e.reciprocal(out=rstd[:], in_=rstd[:]), reads=["rstd"], writes=["rstd"])
        for a in range(2):
            TS(s, "dve" if a == 0 else "pool", hb[:, a, :], X[:, a, :], rstd[:, a:a + 1], ALU.mult, [xb, "rstd"],
               [f"hb{a}"])
        for a in range(2):
            pview = pt[a][:].bitcast(BF16)
            for kc in range(8):
                TR(s, pview[:, kc * 128:(kc + 1) * 128], hb[:, a, kc * 128:(kc + 1) * 128], ident[:],
                   [f"hb{a}", "ident"], [f"pt{a}"])
            CP(s, "dve" if a == 0 else "act", HT[:, :, a * 128:(a + 1) * 128],
               pview.rearrange("p (k t) -> p k t", k=8), [f"pt{a}"], [htn + f"_{a}"])

    def up(i):
        HT, htn = hT[i % 2], f"hT{i % 2}"
        for fc in range(32):
            Pb, pb = pu[fc % 3], f"pu{fc % 3}"
            for kc in range(8):
                MM(s, Pb[:, 0:TT], wup[:, kc, fc * 128:(fc + 1) * 128], HT[:, kc, :],
                   [f"wup{kc}_{fc // 8}", htn + "_0", htn + "_1"], [pb], start=(kc == 0), stop=(kc == 7))
            R, rb = rl[fc % 2], f"rl{fc % 2}"
            ACTF(s, R[:], Pb[:, 0:TT], AF.Relu, [pb], [rb])
            TT_(s, "pool", actT[:, fc, :], R[:], R[:], ALU.mult, [rb], ["actT"])

    def down(i):
        X, xb = xt[i % 3], f"xt{i % 3}"
        for a in range(2):
            for db in range(2):
                jj = a * 2 + db
                Pb, pb = pd[jj % 3], f"pd{jj % 3}"
                for fc in range(32):
                    MM(s, Pb[:], actT[:, fc, a * 128:(a + 1) * 128], wdn[:, fc, db * 512:(db + 1) * 512],
                       ["actT", f"wdn{fc}"], [pb], start=(fc == 0), stop=(fc == 31))
                TT_(s, "dve", X[:, a, db * 512:(db + 1) * 512], Pb[:], X[:, a, db * 512:(db + 1) * 512], ALU.add,
                    [pb, xb], [xb])
        s.dma("pool", xout_d[i * TT:(i + 1) * TT, :].rearrange("(a p) d -> p a d", p=128), X[:], reads=[xb])

    load(0)
    if NT > 1:
        load(1)
    front(0)
    for i in range(NT):
        if i + 2 < NT:
            load(i + 2)
        up(i)
        if i + 1 < NT:
            front(i + 1)
        down(i)
    c.end()


def build_test_mlp(T):
    nc = bass.Bass("TRN2", target_bir_lowering=False)
    x = nc.dram_tensor("x", [T, D], F32, kind="ExternalInput").ap()
    w_up = nc.dram_tensor("w_up", [DEPTH, D, DFF], F32, kind="ExternalInput").ap()
    w_down = nc.dram_tensor("w_down", [DEPTH, DFF, D], F32, kind="ExternalInput").ap()
    mlp_norm = nc.dram_tensor("mlp_norm", [DEPTH, D], F32, kind="ExternalInput").ap()
    y = nc.dram_tensor("y", [T, D], F32, kind="ExternalOutput").ap()
    with contextlib.ExitStack() as es:
        c = Ctx(nc, es)
        c.debug = True
        phase_mlp(c, 0, T, x, w_up, w_down, mlp_norm, xout_d=y)
        print("ops", c.s.total, "waits", c.s.nwaits)
    return nc


def rms_rstd(c, X, xb, junk, ss, rstd, width, tag):
    s = c.s
    s.op("act", lambda e: e.activation(out=junk, in_=X, func=AF.Square), reads=[xb], writes=["junk" + tag])
    s.op("dve", lambda e: e.tensor_reduce(out=ss, in_=junk, axis=AX.X, op=ALU.add),
         reads=["junk" + tag], writes=["ss" + tag])
    s.op("act", lambda e: e.activation(out=rstd, in_=ss, func=AF.Sqrt, scale=1.0 / width, bias=EPS),
         reads=["ss" + tag], writes=["rstd" + tag])
    s.op("dve", lambda e: e.reciprocal(out=rstd, in_=rstd), reads=["rstd" + tag], writes=["rstd" + tag])


def make_tri(c, tri, name, dtype_one=1.0):
    s = c.s
    s.op("pool", lambda e: e.memset(tri[:], 1.0), writes=[name])
    s.op("pool", lambda e: e.affine_select(out=tri[:], in_=tri[:], pattern=[[1, 128]],
                                           compare_op=ALU.is_ge, fill=0.0, base=0, channel_multiplier=-1),
         reads=[name], writes=[name])


def phase_a(c, l, NSEQ, S, x_d, P, G, scr, do_ssm=True, do_pool=True):
    nc, s = c.nc, c.s
    NJ = S // 128
    c.begin()
    NA = 2056
    win = c.sb("win", [128, 8, NA], BF16)
    NSTG = 4
    stg = [c.sb(f"stg{i}", [128, NA // 2], F32) for i in range(NSTG)]
    gcol = c.sb("gcol", [128, 8], F32)
    ident = c.sb("ident", [128, 128], BF16)
    trif = c.sb("trif", [128, 128], F32)
    onesf = c.sb("onesf", [128, 128], F32)
    xt = [c.sb(f"xt{i}", [128, D], F32) for i in range(3)]
    junk = c.sb("junk", [128, D], F32)
    ss = c.sb("ss", [128, 1], F32)
    rstd = c.sb("rstd", [128, 1], F32)
    hb = c.sb("hb", [128, D], BF16)
    hT = [c.sb(f"hT{i}", [128, 8, 128], BF16) for i in range(2)]
    qkg = c.sb("qkg", [128, 2, 8, DH], F32)
    bfg = c.sb("bfg", [128, 8], F32)
    sq = [c.sb(f"sq{i}", [128, 512], F32) for i in range(2)]
    qe = [[c.sb(f"qe{i}_{b}", [128, 512], F32) for b in range(2)] for i in range(2)]
    ssq = [c.sb(f"ssq{i}", [128, 8], F32) for i in range(2)]
    rq = [c.sb(f"rq{i}", [128, 8], F32) for i in range(2)]
    qn = [c.sb(f"qn{i}", [128, 8, DH], F32) for i in range(2)]
    qa = [c.sb(f"qa{i}", [128, 8, 70], BF16) for i in range(2)]
    r1 = c.sb("r1", [128, 8], F32)
    r2 = c.sb("r2", [128, 8], F32)
    qTs = [[c.sb(f"qTs{i}_{b}", [128, 8, 128], BF16) for b in range(2)] for i in range(2)]
    vst = [c.sb(f"vst{b}", [128, 8, 65], BF16) for b in range(2)]
    ust = [c.sb(f"ust{b}", [128, 512], F32) for b in range(2)]
    fls = [c.sb(f"fl{b}", [128, 8], F32) for b in range(2)]
    sp_ = c.sb("sp_", [128, 8], F32)
    carry = c.sb("carry", [128, 8], F32)
    cumall = G["cumall"]
    pt = c.ps("pt")
    pq = [c.ps(f"pq{i}") for i in range(2)]
    pv = c.ps("pv")
    pf = c.ps("pf")
    pp = c.ps("pp")
    ptq = [c.ps(f"ptq{i}") for i in range(2)]

    make_identity(c, ident)
    make_tri(c, trif, "trif")
    s.op("pool", lambda e: e.memset(onesf[:], 1.0), writes=["onesf"])
    for b in range(2):
        s.op("pool", lambda e, b=b: e.memset(vst[b][:], 1.0), writes=[f"vst{b}"])
    s.dma("sp", gcol[:], P["mix_norm"][l].rearrange("(kc p) -> p kc", p=128), writes=["gcol"],
          allow_slow_non_contiguous=True)
    s.dma("sp", qkg[:, 0, 0, :], P["q_norm"][l:l + 1, :].partition_broadcast(128), writes=["qkg"])
    s.dma("sp", qkg[:, 1, 0, :], P["k_norm"][l:l + 1, :].partition_broadcast(128), writes=["qkg"])
    s.dma("sp", bfg[:], P["b_forget"][l:l + 1, :].partition_broadcast(128), writes=["bfg"])
    s.op("dve", lambda e: e.tensor_scalar(out=qkg[:, 0, 0, :], in0=qkg[:, 0, 0, :], scalar1=DH ** -0.5, scalar2=None,
                                          op0=ALU.mult), reads=["qkg"], writes=["qkg"])
    for h in range(1, 8):
        s.op("dve", lambda e, h=h: e.tensor_copy(out=qkg[:, :, h, :], in_=qkg[:, :, 0, :]),
             reads=["qkg"], writes=["qkg"])
    nn = 0
    dq = ("sp", "act", "pool")
    HN = NA // 2
    for kc in range(8):
        for hf in range(2):
            st, sn_ = stg[nn % NSTG], f"stg{nn % NSTG}"
            s.dma(dq[nn % 3], st[:], P["w_in"][l, kc * 128:(kc + 1) * 128, hf * HN:(hf + 1) * HN], writes=[sn_])
            _cast(s, ("dve", "act")[nn % 2], win[:, kc, hf * HN:(hf + 1) * HN], st[:], [sn_, "gcol"], [f"win{kc}"],
                  scalar=gcol[:, kc:kc + 1])
            nn += 1

    def load(n):
        s.dma("sp", xt[n % 3][:], x_d[n * 128:(n + 1) * 128, :], writes=[f"xt{n % 3}"])

    NCH = NSEQ * NJ
    ptv = pt[:].bitcast(BF16)
    pc = pf[:, 272:288]
    for i in range(2):
        s.op("pool", lambda e, i=i: e.memset(qa[i][:], 1.0), writes=[f"qg{i}", f"qaug{i}"])

    def front_elem(n):
        X, xb = xt[n % 3], f"xt{n % 3}"
        ACTF(s, junk[:], X[:], AF.Square, [xb], ["junk"])
        s.op("dve", lambda e: e.tensor_reduce(out=ss[:], in_=junk[:], axis=AX.X, op=ALU.add), reads=["junk"],
             writes=["ss"])
        ACTF(s, rstd[:], ss[:], AF.Ln, ["ss"], ["rstd"], scale=1.0 / D, bias=EPS)
        ACTF(s, rstd[:], rstd[:], AF.Exp, ["rstd"], ["rstd"], scale=-0.5)
        TS(s, "dve", hb[:], X[:], rstd[:, 0:1], ALU.mult, [xb, "rstd"], ["hb"])

    def front_tr(n):
        HT, htn = hT[n % 2], f"hT{n % 2}"
        for kc in range(8):
            TR(s, ptv[:, kc * 128:(kc + 1) * 128], hb[:, kc * 128:(kc + 1) * 128], ident[:], ["hb", "ident"], ["pt"])
        CP(s, "act", HT[:], ptv.rearrange("p (k t) -> p k t", k=8), ["pt"], [htn])

    def proj_all(n):
        HT, htn = hT[n % 2], f"hT{n % 2}"

        def proj(Pt, pname, c0, c1):
            for kc in range(8):
                MM(s, Pt, HT[:, kc, :], win[:, kc, c0:c1], [htn, f"win{kc}"], [pname], start=(kc == 0), stop=(kc == 7))
        proj(pq[0][:], "pq0", 0, 512)
        proj(pq[1][:], "pq1", 512, 1024)
        proj(pv[:], "pv", 1024, 1536)
        proj(pf[:, 0:264], "pf", 1536, 1800)
        proj(pp[:, 0:256], "pp", 1800, 2056)

    def evac(n):
        b = n % 2
        CP(s, "act", qe[0][b][:], pq[0][:], ["pq0"], [f"qe0_{b}"])
        CP(s, "dve", qe[1][b][:], pq[1][:], ["pq1"], [f"qe1_{b}"])
        VS = vst[b]
        CP(s, "act", VS[:, :, 0:64], pv[:].rearrange("p (h d) -> p h d", h=8), ["pv"], [f"vst{b}"])
        US = ust[b]
        CP(s, "dve", US[:, 0:256], pf[:, 8:264], ["pf"], [f"ust{b}"])
        TT_(s, "dve", fls[b][:], pf[:, 0:8], bfg[:], ALU.add, ["pf", "bfg"], [f"fl{b}"])
        CP(s, "act", US[:, 256:512], pp[:, 0:256], ["pp"], [f"ust{b}"])

    def forget(n):
        q_, j = divmod(n, NJ)
        b = n % 2
        fl = fls[b]
        ACTF(s, fl[:], fl[:], AF.Exp, [f"fl{b}"], [f"fl{b}"], scale=-1.0)
        ACTF(s, sp_[:], fl[:], AF.Ln, [f"fl{b}"], ["sp_"], scale=1.0, bias=1.0)
        if j == 0:
            s.op("pool", lambda e: e.memset(carry[:], 0.0), writes=["carry"])
        MM(s, pc[:, 0:8], trif[:], sp_[:], ["trif", "sp_"], ["pf"])
        MM(s, pc[:, 8:16], onesf[:], sp_[:], ["onesf", "sp_"], ["pf"])
        cum = cumall[:, q_, j, :]
        TT_(s, "dve", cum, carry[:], pc[:, 0:8], ALU.subtract, ["carry", "pf"], ["cumall"])
        TT_(s, "dve", carry[:], carry[:], pc[:, 8:16], ALU.subtract, ["carry", "pf"], ["carry"])

    def post(n):
        q_, j = divmod(n, NJ)
        b = n % 2
        fl = fls[b]
        VS, US = vst[b], ust[b]
        cum = cumall[:, q_, j, :]
        CP(s, "dve", qa[0][:, :, 67], cum, ["cumall"], ["qaug0"])
        yield
        TT_(s, "dve", r1[:], cum, qa[0][:, :, 67], ALU.subtract, ["cumall", "qaug0"], ["r1"])
        yield
        CP(s, "dve", qa[0][:, :, 68], r1[:], ["r1"], ["qaug0"])
        yield
        TT_(s, "dve", r2[:], r1[:], qa[0][:, :, 68], ALU.subtract, ["r1", "qaug0"], ["r2"])
        yield
        CP(s, "dve", qa[0][:, :, 69], r2[:], ["r2"], ["qaug0"])
        yield
        TS(s, "dve", qa[1][:, :, 64:67], qa[0][:, :, 67:70], -1.0, ALU.mult, ["qaug0"], ["qaug1"])
        yield
        for i in range(2):
            PQ = qe[i][b]
            ACTF(s, sq[i][:], PQ[:], AF.Square, [f"qe{i}_{b}"], [f"sq{i}"])
            yield
            s.op("dve", lambda e, i=i: e.tensor_reduce(out=ssq[i][:], in_=sq[i][:].rearrange("p (h d) -> p h d", h=8),
                                                       axis=AX.X, op=ALU.add),
                 reads=[f"sq{i}"], writes=[f"ssq{i}"])
            yield
            ACTF(s, rq[i][:], ssq[i][:], AF.Ln, [f"ssq{i}"], [f"rq{i}"], scale=1.0 / DH, bias=EPS)
            yield
            ACTF(s, rq[i][:], rq[i][:], AF.Exp, [f"rq{i}"], [f"rq{i}"], scale=-0.5)
            yield
            TT_(s, "dve", qn[i][:], PQ[:].rearrange("p (h d) -> p h d", h=8),
                rq[i][:].unsqueeze(2).broadcast_to([128, 8, DH]), ALU.mult, [f"qe{i}_{b}", f"rq{i}"], [f"qn{i}"])
            yield
            TT_(s, "pool", qa[i][:, :, 0:64], qn[i][:], qkg[:, i, :, :], ALU.mult, [f"qn{i}", "qkg"], [f"qg{i}"])
            yield
            pvw = ptq[i][:].bitcast(BF16)
            for h in range(8):
                TR(s, pvw[0:70, h * 128:(h + 1) * 128], qa[i][:, h, :], ident[:], [f"qg{i}", f"qaug{i}", "ident"],
                   [f"ptq{i}"])
            QT = qTs[i][n % 2]
            CP(s, "act" if i == 0 else "dve", QT[0:70, :, :], pvw[0:70, :].rearrange("p (h t) -> p h t", h=8),
               [f"ptq{i}"], [f"qTs{i}_{n % 2}"])
            yield
            dst = scr["qT" if i == 0 else "kT"]
            s.dma("pool", dst[q_, :, :, j * 128:(j + 1) * 128].rearrange("h p t -> p h t"), QT[0:70, :, :],
                  reads=[f"qTs{i}_{n % 2}"])
            yield
        s.dma("pool", scr["v"][q_, j * 128:(j + 1) * 128, :, :], VS[:], reads=[f"vst{n % 2}"])
        yield
        s.dma("pool", scr["u"][q_, j * 128:(j + 1) * 128, :], US[:], reads=[f"ust{n % 2}"])
        yield

    load(0)
    if NCH > 1:
        load(1)
    def main_stream(t):
        if t + 2 < NCH:
            load(t + 2)
        if t < NCH:
            front_elem(t)
        yield
        if 1 <= t <= NCH:
            HT, htn = hT[(t - 1) % 2], f"hT{(t - 1) % 2}"
            for (Pt, pname, c0, c1) in ((pq[0][:], "pq0", 0, 512), (pq[1][:], "pq1", 512, 1024),
                                        (pv[:], "pv", 1024, 1536), (pf[:, 0:264], "pf", 1536, 1800),
                                        (pp[:, 0:256], "pp", 1800, 2056)):
                for kc in range(8):
                    MM(s, Pt, HT[:, kc, :], win[:, kc, c0:c1], [htn, f"win{kc}"], [pname], start=(kc == 0),
                       stop=(kc == 7))
                    if kc % 2 == 1:
                        yield
        if t < NCH:
            front_tr(t)
        yield
        if 1 <= t <= NCH:
            evac(t - 1)
            forget(t - 1)
        yield

    for t in range(NCH + 2):
        gens = [main_stream(t)]
        if t >= 2:
            gens.append(post(t - 2))
        while gens:
            for g in list(gens):
                try:
                    next(g)
                except StopIteration:
                    gens.remove(g)
    c.end()


def phase_b(c, NSEQ, S, G, scr):
    nc, s = c.nc, c.s
    NJ = S // 128
    NI = NJ // 4
    c.begin()
    tri = c.sb("tri", [128, 128], BF16)
    vall = [c.sb(f"vall{i}", [128, NJ, 8, 65], BF16) for i in range(2)]
    qT = [c.sb(f"qT{i}", [128, S], BF16) for i in range(2)]
    kT = [c.sb(f"kT{i}", [128, S], BF16) for i in range(2)]
    NPT = 4
    pT = [c.sb(f"pT{i}", [128, 512], BF16) for i in range(NPT)]
    ybuf = [c.sb(f"ybuf{i}", [128, NJ, 64], BF16) for i in range(2)]
    rc = [c.sb(f"rc{i}", [128, 1], F32) for i in range(4)]
    NPS = 4
    pS = [c.ps(f"pS{i}") for i in range(NPS)]
    pO = [c.ps(f"pO{i}") for i in range(4)]
    make_tri(c, tri, "tri")
    LOOK = 3
    blocks = []
    for q_ in range(NSEQ):
        for h in range(8):
            for i4 in range(NI):
                for j in range(4 * i4 + 4):
                    blocks.append((q_, h, i4, j))

    def stage1(bi):
        q_, h, i4, j = blocks[bi]
        g = q_ * 8 + h
        QT, KT = qT[g % 2], kT[g % 2]
        if h == 0 and i4 == 0 and j == 0:
            s.dma("sp", vall[q_ % 2][:], scr["v"][q_].rearrange("(j p) h d -> p j h d", p=128), writes=[f"vall{q_ % 2}"])
        if i4 == 0 and j == 0:
            s.dma("sp", QT[0:70, :], scr["qT"][q_, h], writes=[f"qT{g % 2}"])
            s.dma("act", KT[0:70, :], scr["kT"][q_, h], writes=[f"kT{g % 2}"])
        jj = j - 4 * i4
        c0 = 128 * max(jj, 0)
        PS, psn = pS[bi % NPS], f"pS{bi % NPS}"
        PT, ptn = pT[bi % NPT], f"pT{bi % NPT}"
        MM(s, PS[:, c0:512], KT[0:70, j * 128:(j + 1) * 128], QT[0:70, i4 * 512 + c0:(i4 + 1) * 512],
           [f"qT{g % 2}", f"kT{g % 2}"], [psn])
        ACTF(s, PT[:, c0:512], PS[:, c0:512], AF.Exp, [psn], [ptn])
        if jj >= 0:
            TT_(s, "pool", PT[:, c0:c0 + 128], PT[:, c0:c0 + 128], tri[:], ALU.mult, [ptn, "tri"], [ptn])

    def stage2(bi):
        q_, h, i4, j = blocks[bi]
        g = q_ * 8 + h
        PT, ptn = pT[bi % NPT], f"pT{bi % NPT}"
        YB = ybuf[g % 2]
        jj = j - 4 * i4
        for tt in range(max(jj, 0), 4):
            last = (j == 4 * i4 + tt)
            MM(s, pO[tt][:, 0:65], PT[:, tt * 128:(tt + 1) * 128], vall[q_ % 2][:, j, h, :], [ptn, f"vall{q_ % 2}"],
               [f"pO{tt}"], start=(j == 0), stop=last)
            if last:
                s.op("dve", lambda e, tt=tt: e.reciprocal(out=rc[tt][:], in_=pO[tt][:, 64:65]), reads=[f"pO{tt}"],
                     writes=[f"rc{tt}"])
                TS(s, "dve", YB[:, 4 * i4 + tt, :], pO[tt][:, 0:64], rc[tt][:, 0:1], ALU.mult, [f"pO{tt}", f"rc{tt}"],
                   [f"ybuf{g % 2}"])
        if i4 == NI - 1 and j == NJ - 1:
            s.dma("pool", scr["yattn"][q_, :, h * 64:(h + 1) * 64].rearrange("(j p) c -> p j c", p=128), YB[:],
                  reads=[f"ybuf{g % 2}"])

    nb = len(blocks)
    for t in range(nb + LOOK):
        if t < nb:
            stage1(t)
        if t >= LOOK:
            stage2(t - LOOK)
    c.end()


def make_scratch(nc, NSEQ, S, debug_out=False):
    kind = {"kind": "ExternalOutput"} if debug_out else {}
    scr = {}
    scr["qT"] = nc.dram_tensor("scr_qT", [NSEQ, 8, 70, S], BF16).ap()
    scr["kT"] = nc.dram_tensor("scr_kT", [NSEQ, 8, 70, S], BF16).ap()
    scr["v"] = nc.dram_tensor("scr_v", [NSEQ, S, 8, 65], BF16).ap()
    scr["cend"] = nc.dram_tensor("scr_cend", [1, NSEQ, S // 128, 8], F32).ap()
    scr["u"] = nc.dram_tensor("scr_u", [NSEQ, S, 512], F32).ap()
    scr["yattn"] = nc.dram_tensor("scr_yattn", [NSEQ, S, 512], BF16, **kind).ap()
    scr["yssmT"] = nc.dram_tensor("scr_yssmT", [NSEQ, 256, S], BF16, **kind).ap()
    scr["ypoolT"] = nc.dram_tensor("scr_ypoolT", [NSEQ, 256, S], BF16, **kind).ap()
    return scr


PARAM_SHAPES = {
    "mix_norm": [DEPTH, D], "w_in": [DEPTH, D, INC], "b_forget": [DEPTH, NH], "q_norm": [DEPTH, DH],
    "k_norm": [DEPTH, DH], "ssm_a_re": [DEPTH, 16, 64], "ssm_a_im": [DEPTH, 16, 64], "ssm_log_dt": [DEPTH, 16],
    "ssm_b_re": [DEPTH, 16, 64, 16], "ssm_b_im": [DEPTH, 16, 64, 16], "ssm_c_re": [DEPTH, 16, 16, 64],
    "ssm_c_im": [DEPTH, 16, 16, 64], "ssm_d": [DEPTH, 256], "w_glu": [DEPTH, 256, 512],
    "pool_w": [DEPTH, 4, 64, 64], "pool_scale": [DEPTH, 256], "w_br_attn": [DEPTH, 512, D],
    "w_br_ssm": [DEPTH, 256, D], "w_br_pool": [DEPTH, 256, D], "b_gate": [DEPTH, 3 * D],
    "w_out": [DEPTH, D, D], "mlp_norm": [DEPTH, D], "w_up": [DEPTH, D, DFF], "w_down": [DEPTH, DFF, D],
}


def declare_params(nc):
    return {k: nc.dram_tensor(k, shp, F32, kind="ExternalInput").ap() for k, shp in PARAM_SHAPES.items()}


def build_test_ab(NSEQ, S, l=0):
    nc = bass.Bass("TRN2", target_bir_lowering=False)
    x = nc.dram_tensor("x", [NSEQ * S, D], F32, kind="ExternalInput").ap()
    P = declare_params(nc)
    scr = make_scratch(nc, NSEQ, S, debug_out=True)
    with contextlib.ExitStack() as es:
        c = Ctx(nc, es)
        G = {"cumall": es.enter_context(nc.sbuf_tensor("cumall", [128, NSEQ, S // 128, 8], F32))}
        phase_a(c, l, NSEQ, S, x, P, G, scr, do_ssm=False, do_pool=False)
        phase_b(c, NSEQ, S, G, scr)
        print("ops", c.s.total, "waits", c.s.nwaits)
    return nc


def TT_(s, eng, out, in0, in1, op, reads, writes):
    return s.op(eng, lambda e: e.tensor_tensor(out=out, in0=in0, in1=in1, op=op), reads, writes)


def TS(s, eng, out, in0, scalar1, op0, reads, writes, scalar2=None, op1=None):
    if op1 is None:
        return s.op(eng, lambda e: e.tensor_scalar(out=out, in0=in0, scalar1=scalar1, scalar2=None, op0=op0),
                    reads, writes)
    return s.op(eng, lambda e: e.tensor_scalar(out=out, in0=in0, scalar1=scalar1, scalar2=scalar2, op0=op0, op1=op1),
                reads, writes)


def ACTF(s, out, in_, func, reads, writes, scale=1.0, bias=None):
    if bias is None:
        return s.op("act", lambda e: e.activation(out=out, in_=in_, func=func, scale=scale), reads, writes)
    return s.op("act", lambda e: e.activation(out=out, in_=in_, func=func, scale=scale, bias=bias), reads, writes)


def CP(s, eng, out, in_, reads, writes):
    if eng == "act":
        return s.op("act", lambda e: e.copy(out=out, in_=in_), reads, writes)
    return s.op(eng, lambda e: e.tensor_copy(out=out, in_=in_), reads, writes)


def MM(s, out, lhsT, rhs, reads, writes, start=True, stop=True):
    return s.op("pe", lambda e: e.matmul(out, lhsT=lhsT, rhs=rhs, start=start, stop=stop), reads, writes)


def TR(s, out, in_, ident, reads, writes):
    return s.op("pe", lambda e: e.transpose(out=out, in_=in_, identity=ident), reads, writes)


TWO_PI = 2.0 * math.pi
MAGIC = 12582912.0


def sincos(s, eng, ang, o_sin, o_cos, t1, t2, rd, tag):
    n1, n2 = "sc1" + tag, "sc2" + tag
    for which, o in (("s", o_sin), ("c", o_cos)):
        if which == "s":
            TS(s, eng, t1, ang, 1.0 / TWO_PI, ALU.mult, rd, [n1])
        else:
            TS(s, eng, t1, ang, 1.0 / TWO_PI, ALU.mult, rd, [n1], scalar2=0.25, op1=ALU.add)
        TS(s, eng, t2, t1, MAGIC, ALU.add, [n1], [n2])
        TS(s, eng, t2, t2, MAGIC, ALU.subtract, [n2], [n2])
        TT_(s, eng, t2, t1, t2, ALU.subtract, [n1, n2], [n2])
        ACTF(s, o, t2, AF.Sin, [n2], [("sin" if which == "s" else "cos") + tag], scale=TWO_PI * (1 - 1e-6))


POOL_WINDOWS = (2, 4, 8, 16)


def phase_a2(c, l, NSEQ, S, P, scr):
    nc, s = c.nc, c.s
    NJ = S // 128
    c.begin()
    I32 = mybir.dt.int32
    ident = c.sb("ident", [128, 128], BF16)
    identf = c.sb("identf", [128, 128], F32)
    tribf = c.sb("tribf", [128, 128], BF16)
    iot_i = c.sb("iot_i", [128, 128], I32)
    iot = c.sb("iot", [128, 128], F32)
    pcol_i = c.sb("pcol_i", [128, 1], I32)
    pcol = c.sb("pcol", [128, 1], F32)
    npcol = c.sb("npcol", [128, 1], F32)
    are = c.sb("are", [128, 1024], F32)
    aim = c.sb("aim", [128, 1024], F32)
    ldt = c.sb("ldt", [128, 16], F32)
    dtr = c.sb("dtr", [128, 1024], F32)
    t1 = c.sb("t1", [128, 1024], F32)
    t2 = c.sb("t2", [128, 1024], F32)
    t3 = c.sb("t3", [128, 1024], F32)
    t4 = c.sb("t4", [128, 1024], F32)
    t5 = c.sb("t5", [128, 1024], F32)
    t6 = c.sb("t6", [128, 1024], F32)
    cfr = c.sb("cfr", [128, 1024], F32)
    cfi = c.sb("cfi", [128, 1024], F32)
    Tr = c.sb("Tr", [128, 1024], F32)
    Ti = c.sb("Ti", [128, 1024], F32)
    TAr = c.sb("TAr", [128, 8, 128], F32)
    TAi = c.sb("TAi", [128, 8, 128], F32)
    acol = c.sb("acol", [128, 3, 8], F32)
    BX = c.sb("BX", [128, 8, 2, 128], F32)
    BXb = c.sb("BXb", [128, 8, 2, 128], BF16)
    BT = c.sb("BT", [128, 8, 2, 128], BF16)
    CN = c.sb("CN", [32, 8, 2, 128], F32)
    CNb = c.sb("CNb", [32, 8, 2, 128], BF16)
    CX = c.sb("CX", [128, 8, 2, 32], BF16)
    drb = c.sb("drb", [128, 256], F32)
    wg_st = c.sb("wg_st", [128, 2, 512], F32)
    wglu = c.sb("wglu", [128, 2, 512], BF16)
    pw_st = c.sb("pw_st", [128, 2, 64], F32)
    PW = c.sb("PW", [128, 2, 64], BF16)
    pscol = c.sb("pscol", [128, 2], F32)
    MT = c.sb("MT", [128, 12, 128], BF16)
    mtmp = c.sb("mtmp", [128, 128], F32)
    mrat = c.sb("mrat", [128, 128], F32)
    ut = [c.sb(f"ut{i}", [128, 512], F32) for i in range(3)]
    ub = [c.sb(f"ub{i}", [128, 512], BF16) for i in range(3)]
    uT = c.sb("uT", [128, 2, 128], BF16)
    m1 = c.sb("m1", [128, 4, 128], F32)
    m2 = c.sb("m2", [128, 4, 128], F32)
    m3 = c.sb("m3", [128, 4, 128], F32)
    m4 = c.sb("m4", [128, 4, 128], F32)
    Wt = c.sb("Wt", [128, 8, 2, 128], BF16)
    Pr = c.sb("Pr", [128, 8, 128], F32)
    Pi = c.sb("Pi", [128, 8, 128], F32)
    Xt = c.sb("Xt", [128, 8, 2, 128], BF16)
    car = c.sb("car", [128, 2, 8], F32)
    sn = c.sb("sn", [128, 2, 4], F32)
    sm = c.sb("sm", [128, 4, 4], F32)
    du = c.sb("du", [128, 256], F32)
    yv = c.sb("yv", [128, 256], F32)
    y2 = c.sb("y2", [128, 256], F32)
    sg = c.sb("sg", [128, 256], F32)
    gy = c.sb("gy", [128, 256], BF16)
    gyT = c.sb("gyT", [128, 2, 128], BF16)
    sgb = c.sb("sgb", [128, 2, 128], F32)
    ysT = [c.sb(f"ysT{i}", [128, 2, 128], BF16) for i in range(2)]
    plT = c.sb("plT", [128, 2, 128], BF16)
    ypT = [c.sb(f"ypT{i}", [128, 2, 128], BF16) for i in range(2)]
    pbu = c.ps("pbu")
    ppf = c.ps("ppf", [128, 2048])
    pym = c.ps("pym")
    pglu = c.ps("pglu")
    ppl = c.ps("ppl")
    py = pym
    w1 = c.sb("w1", [128, 2, 128], F32)
    w2 = c.sb("w2", [128, 2, 128], F32)
    w3 = c.sb("w3", [128, 2, 128], F32)
    w4 = c.sb("w4", [128, 2, 128], F32)

    make_identity(c, ident)
    make_tri(c, tribf, "tribf")
    s.op("pool", lambda e: e.iota(iot_i[:], pattern=[[1, 128]], base=0, channel_multiplier=0), writes=["iot_i"])
    CP(s, "dve", iot[:], iot_i[:], ["iot_i"], ["iot"])
    s.op("pool", lambda e: e.iota(pcol_i[:], pattern=[[0, 1]], base=0, channel_multiplier=1), writes=["pcol_i"])
    CP(s, "dve", pcol[:], pcol_i[:], ["pcol_i"], ["pcol"])
    TS(s, "dve", npcol[:], pcol[:], -1.0, ALU.mult, ["pcol"], ["npcol"])
    CP(s, "dve", identf[:], ident[:], ["ident"], ["identf"])

    s.dma("sp", are[:], P["ssm_a_re"][l:l + 1].rearrange("o g n -> o (g n)").partition_broadcast(128), writes=["are"])
    s.dma("act", aim[:], P["ssm_a_im"][l:l + 1].rearrange("o g n -> o (g n)").partition_broadcast(128), writes=["aim"])
    s.dma("sp", ldt[:], P["ssm_log_dt"][l:l + 1, :].partition_broadcast(128), writes=["ldt"])
    s.dma("sp", acol[:, 0, :], P["ssm_a_re"][l].rearrange("(k gl) n -> (gl n) k", gl=2), writes=["acol"],
          allow_slow_non_contiguous=True)
    s.dma("sp", acol[:, 1, :], P["ssm_a_im"][l].rearrange("(k gl) n -> (gl n) k", gl=2), writes=["acol"],
          allow_slow_non_contiguous=True)
    for gl in range(2):
        s.dma("sp", acol[64 * gl:64 * gl + 64, 2, :],
              P["ssm_log_dt"][l:l + 1, :].rearrange("o (k gl) -> o k gl", gl=2)[:, :, gl].partition_broadcast(64),
              writes=["acol"], allow_slow_non_contiguous=True)
    s.dma("sp", drb[:], P["ssm_d"][l:l + 1, :].partition_broadcast(128), writes=["drb"])
    s.dma("act", wg_st[:], P["w_glu"][l].rearrange("(kc p) g -> p kc g", p=128), writes=["wg_st"])
    CP(s, "pool", wglu[:], wg_st[:], ["wg_st"], ["wglu"])
    for g in range(4):
        s.dma("sp", pw_st[64 * (g % 2):64 * (g % 2) + 64, g // 2, :], P["pool_w"][l, g], writes=["pw_st"])
    CP(s, "pool", PW[:], pw_st[:], ["pw_st"], ["PW"])
    s.dma("sp", pscol[:], P["pool_scale"][l].rearrange("(kc p) -> p kc", p=128), writes=["pscol"],
          allow_slow_non_contiguous=True)
    s.op("pool", lambda e: e.memset(BX[:], 0.0), writes=["BX"])
    s.op("pool", lambda e: e.memset(CN[:], 0.0), writes=["CN"])
    nd = 0
    for k in range(8):
        for gl in range(2):
            g = 2 * k + gl
            c0 = 32 * (k % 4) + 16 * gl
            for part, nm in ((0, "ssm_b_re"), (1, "ssm_b_im")):
                s.dma("sp" if nd % 2 == 0 else "act", BX[64 * gl:64 * gl + 64, k, part, c0:c0 + 16], P[nm][l, g],
                      reads=[], writes=["BX"])
                nd += 1
            for part, nm in ((0, "ssm_c_re"), (1, "ssm_c_im")):
                s.dma("sp" if nd % 2 == 0 else "act", CN[16 * gl:16 * gl + 16, k, part, 64 * gl:64 * gl + 64],
                      P[nm][l, g], reads=[], writes=["CN"])
                nd += 1
    CP(s, "dve", BXb[:], BX[:], ["BX"], ["BXb"])
    CP(s, "dve", CNb[:], CN[:], ["CN"], ["CNb"])
    pmv = ppf[:, 0:512].bitcast(BF16)
    for k in range(8):
        for part in range(2):
            i = (k * 2 + part) % 8
            TR(s, pmv[:, i * 128:(i + 1) * 128], BXb[:, k, part, :], ident[:], ["BXb", "ident"], ["ppf"])
        if k % 4 == 3:
            k0 = k - 3
            CP(s, "dve", BT[:, k0:k0 + 4, :, :], pmv.rearrange("p (k a m) -> p k a m", k=4, a=2), ["ppf"], ["BT"])
    for k in range(8):
        for part in range(2):
            i = k * 2 + part
            TR(s, pmv[:, i * 32:(i + 1) * 32], CNb[:, k, part, :], ident[0:32, 0:32], ["CNb", "ident"], ["ppf"])
    cxv = pmv[:, 0:512].rearrange("p (k a m) -> p k a m", k=8, a=2)
    CP(s, "dve", CX[:, :, 0, :], cxv[:, :, 0, :], ["ppf"], ["CX"])
    TS(s, "dve", CX[:, :, 1, :], cxv[:, :, 1, :], -1.0, ALU.mult, ["ppf"], ["CX"])

    ACTF(s, ldt[:], ldt[:], AF.Exp, ["ldt"], ["ldt"])
    CP(s, "dve", dtr[:].rearrange("p (g n) -> p g n", g=16), ldt[:].unsqueeze(2).broadcast_to([128, 16, 64]),
       ["ldt"], ["dtr"])
    ardt, wr = t5, t6
    TT_(s, "dve", ardt[:], are[:], dtr[:], ALU.mult, ["are", "dtr"], ["ardt"])
    TT_(s, "dve", wr[:], aim[:], dtr[:], ALU.mult, ["aim", "dtr"], ["wr"])
    sincos(s, "dve", wr[:], t3[:], t4[:], t1[:], t2[:], ["wr"], "0")
    ACTF(s, t1[:], ardt[:], AF.Exp, ["ardt", "sc10"], ["mag1"])
    TT_(s, "dve", t4[:], t4[:], t1[:], ALU.mult, ["cos0", "mag1"], ["cos0"])
    TT_(s, "dve", t3[:], t3[:], t1[:], ALU.mult, ["sin0", "mag1"], ["sin0"])
    TS(s, "dve", t4[:], t4[:], -1.0, ALU.add, ["cos0"], ["cos0"])
    TT_(s, "dve", t1[:], are[:], are[:], ALU.mult, ["are", "mag1", "sin0"], ["mag1"])
    TT_(s, "dve", t2[:], aim[:], aim[:], ALU.mult, ["aim", "sc20"], ["sc20"])
    TT_(s, "dve", t1[:], t1[:], t2[:], ALU.add, ["mag1", "sc20"], ["mag1"])
    s.op("dve", lambda e: e.reciprocal(out=t1[:], in_=t1[:]), reads=["mag1"], writes=["mag1"])
    TT_(s, "dve", cfr[:], t4[:], are[:], ALU.mult, ["cos0", "are"], ["cfr"])
    TT_(s, "dve", t2[:], t3[:], aim[:], ALU.mult, ["sin0", "aim", "sc20"], ["sc20"])
    TT_(s, "dve", cfr[:], cfr[:], t2[:], ALU.add, ["cfr", "sc20"], ["cfr"])
    TT_(s, "dve", cfr[:], cfr[:], t1[:], ALU.mult, ["cfr", "mag1"], ["cfr"])
    TT_(s, "dve", cfi[:], t3[:], are[:], ALU.mult, ["sin0", "are"], ["cfi"])
    TT_(s, "dve", t2[:], t4[:], aim[:], ALU.mult, ["cos0", "aim", "sc20", "cfr"], ["sc20"])
    TT_(s, "dve", cfi[:], cfi[:], t2[:], ALU.subtract, ["cfi", "sc20"], ["cfi"])
    TT_(s, "dve", cfi[:], cfi[:], t1[:], ALU.mult, ["cfi", "mag1"], ["cfi"])
    TS(s, "dve", dtr[:], wr[:], pcol[:, 0:1], ALU.mult, ["wr", "pcol", "dtr"], ["ang"])
    sincos(s, "dve", dtr[:], t3[:], t4[:], t1[:], t2[:], ["ang", "cfi", "cfr"], "1")
    s.op("act", lambda e: e.activation(out=t1[:], in_=ardt[:], func=AF.Exp, scale=npcol[:, 0:1]),
         reads=["ardt", "npcol", "sc11", "cos1"], writes=["mag2"])
    TT_(s, "dve", t4[:], t4[:], t1[:], ALU.mult, ["cos1", "mag2"], ["cos1"])
    TT_(s, "dve", t3[:], t3[:], t1[:], ALU.mult, ["sin1", "mag2"], ["sin1"])
    TT_(s, "dve", Tr[:], t4[:], cfr[:], ALU.mult, ["cos1", "cfr"], ["Tr"])
    TT_(s, "dve", t2[:], t3[:], cfi[:], ALU.mult, ["sin1", "cfi", "sc21"], ["sc21"])
    TT_(s, "dve", Tr[:], Tr[:], t2[:], ALU.add, ["Tr", "sc21"], ["Tr"])
    TT_(s, "dve", Ti[:], t4[:], cfi[:], ALU.mult, ["cos1", "cfi"], ["Ti"])
    TT_(s, "dve", t2[:], t3[:], cfr[:], ALU.mult, ["sin1", "cfr", "Tr"], ["sc21"])
    TT_(s, "dve", Ti[:], Ti[:], t2[:], ALU.subtract, ["Ti", "sc21"], ["Ti"])
    ACTF(s, acol[:, 2, :], acol[:, 2, :], AF.Exp, ["acol"], ["acol"])
    TT_(s, "dve", acol[:, 0, :], acol[:, 0, :], acol[:, 2, :], ALU.mult, ["acol"], ["acol"])
    TT_(s, "dve", acol[:, 1, :], acol[:, 1, :], acol[:, 2, :], ALU.mult, ["acol"], ["acol"])
    angf = t5[:].rearrange("p (k t) -> p k t", k=8)
    magf = t6[:].rearrange("p (k t) -> p k t", k=8)
    for k in range(8):
        TS(s, "dve", angf[:, k, :], iot[:], acol[:, 1, k:k + 1], ALU.mult, ["iot", "acol", "ardt", "Tr", "Ti"], ["angf"])
        s.op("act", lambda e, k=k: e.activation(out=magf[:, k, :], in_=iot[:], func=AF.Exp, scale=acol[:, 0, k:k + 1]),
             reads=["iot", "acol", "wr", "ang", "Tr", "Ti"], writes=["magf"])
    sincos(s, "dve", t5[:], t3[:], t4[:], t1[:], t2[:], ["angf", "Tr", "Ti"], "2")
    TT_(s, "dve", TAr[:].rearrange("p k t -> p (k t)"), t4[:], t6[:], ALU.mult, ["cos2", "magf"], ["TAr"])
    TT_(s, "dve", TAi[:].rearrange("p k t -> p (k t)"), t3[:], t6[:], ALU.mult, ["sin2", "magf"], ["TAi"])

    for g, w in enumerate(POOL_WINDOWS):
        s.op("pool", lambda e, w=w: e.memset(mtmp[:], 1.0 / w), reads=["mtmp", "mrat", "MT"], writes=["mtmp"])
        s.op("pool", lambda e: e.affine_select(out=mtmp[:], in_=mtmp[:], pattern=[[1, 128]], compare_op=ALU.is_ge,
                                               fill=0.0, base=0, channel_multiplier=-1),
             reads=["mtmp"], writes=["mtmp"])
        s.op("pool", lambda e, w=w: e.affine_select(out=mtmp[:], in_=mtmp[:], pattern=[[-1, 128]],
                                                    compare_op=ALU.is_ge, fill=0.0, base=w - 1, channel_multiplier=1),
             reads=["mtmp"], writes=["mtmp"])
        TT_(s, "pool", MT[:, g * 3 + 0, :], mtmp[:], identf[:], ALU.subtract, ["mtmp", "identf"], ["MT"])
        TS(s, "dve", mrat[:], iot[:], 1.0, ALU.add, ["iot"], ["mrat"])
        s.op("dve", lambda e: e.reciprocal(out=mrat[:], in_=mrat[:]), reads=["mrat"], writes=["mrat"])
        TS(s, "dve", mrat[:], mrat[:], float(w), ALU.mult, ["mrat"], ["mrat"], scalar2=1.0, op1=ALU.max)
        TT_(s, "dve", mrat[:], mrat[:], mtmp[:], ALU.mult, ["mrat", "mtmp"], ["mrat"])
        TT_(s, "dve", MT[:, g * 3 + 2, :], mrat[:], identf[:], ALU.subtract, ["mrat", "identf"], ["MT"])
        s.op("pool", lambda e, w=w: e.memset(mtmp[:], 1.0 / w), reads=["mtmp", "mrat", "MT"], writes=["mtmp"])
        s.op("pool", lambda e, w=w: e.affine_select(out=mtmp[:], in_=mtmp[:], pattern=[[-1, 128]],
                                                    compare_op=ALU.is_ge, fill=0.0, base=-(129 - w),
                                                    channel_multiplier=1),
             reads=["mtmp"], writes=["mtmp"])
        CP(s, "pool", MT[:, g * 3 + 1, :], mtmp[:], ["mtmp"], ["MT"])

    NCH = NSEQ * NJ
    pmq = pym[:, 256:512].bitcast(BF16)

    def load(n):
        q_, j = divmod(n, NJ)
        s.dma("sp", ut[n % 3][:], scr["u"][q_, j * 128:(j + 1) * 128, :], writes=[f"ut{n % 3}"])

    def S1(n):
        U, UB = ut[n % 3], ub[n % 3]
        un, ubn = f"ut{n % 3}", f"ub{n % 3}"
        CP(s, "act", UB[:], U[:], [un], [ubn])
        for kc in range(2):
            TR(s, pmq[:, kc * 128:(kc + 1) * 128], UB[:, kc * 128:(kc + 1) * 128], ident[:], [ubn, "ident"], ["pym"])
        CP(s, "dve", uT[:], pmq[:, 0:256].rearrange("p (k t) -> p k t", k=2), ["pym"], ["uT"])
        for qt in range(4):
            k0 = qt * 2
            for kk in range(2):
                k = k0 + kk
                MM(s, pbu[:, kk * 256:(kk + 1) * 256], uT[:, k // 4, :], BT[:, k, :, :].rearrange("p a m -> p (a m)"),
                   ["uT", "BT"], ["pbu"])
            buv = pbu[:].rearrange("p (k a m) -> p k a m", k=2, a=2)
            trv = Tr[:, k0 * 128:(k0 + 2) * 128].rearrange("p (k m) -> p k m", k=2)
            tiv = Ti[:, k0 * 128:(k0 + 2) * 128].rearrange("p (k m) -> p k m", k=2)
            yield
            TT_(s, "dve", w1[:], buv[:, :, 0, :], trv, ALU.mult, ["pbu", "Tr"], ["w1"])
            TT_(s, "dve", w2[:], buv[:, :, 1, :], tiv, ALU.mult, ["pbu", "Ti"], ["w2"])
            yield
            TT_(s, "pool", Wt[:, k0:k0 + 2, 0, :], w1[:], w2[:], ALU.subtract, ["w1", "w2"], [f"Wt{qt}"])
            TT_(s, "dve", w3[:], buv[:, :, 1, :], trv, ALU.mult, ["pbu", "Tr"], ["w3"])
            TT_(s, "dve", w4[:], buv[:, :, 0, :], tiv, ALU.mult, ["pbu", "Ti"], ["w4"])
            yield
            TT_(s, "pool", Wt[:, k0:k0 + 2, 1, :], w3[:], w4[:], ALU.add, ["w3", "w4"], [f"Wt{qt}"])
            yield
            for kk in range(2):
                for part in range(2):
                    i = (k0 + kk) * 2 + part
                    MM(s, ppf[:, i * 128:(i + 1) * 128], Wt[:, k0 + kk, part, :], tribf[:], [f"Wt{qt}", "tribf"],
                       [f"ppf{qt // 2}"])

    def XP(n):
        q_, j = divmod(n, NJ)
        if j == 0:
            s.op("pool", lambda e: e.memset(car[:], 0.0), writes=["car"])
        for hf in range(2):
            k0 = hf * 4
            pfv = ppf[:, hf * 1024:(hf + 1) * 1024].rearrange("p (k a t) -> p k a t", k=4, a=2)
            pfn = f"ppf{hf}"
            TT_(s, "dve", Pr[:, k0:k0 + 4, :], pfv[:, :, 0, :],
                car[:, 0, k0:k0 + 4].unsqueeze(2).broadcast_to([128, 4, 128]), ALU.add, [pfn, "car"], [f"Pr{hf}"])
            TT_(s, "dve", Pi[:, k0:k0 + 4, :], pfv[:, :, 1, :],
                car[:, 1, k0:k0 + 4].unsqueeze(2).broadcast_to([128, 4, 128]), ALU.add, [pfn, "car"], [f"Pi{hf}"])
        yield
        for hf in range(2):
            k0 = hf * 4
            prn, pin = f"Pr{hf}", f"Pi{hf}"
            PR, PI = Pr[:, k0:k0 + 4, :], Pi[:, k0:k0 + 4, :]
            tar, tai = TAr[:, k0:k0 + 4, :], TAi[:, k0:k0 + 4, :]
            TT_(s, "dve", m1[:], tar, PR, ALU.mult, ["TAr", prn], ["m1"])
            TT_(s, "pool", m2[:], tai, PI, ALU.mult, ["TAi", pin], ["m2"])
            yield
            TT_(s, "dve", m3[:], tar, PI, ALU.mult, ["TAr", pin], ["m3"])
            TT_(s, "pool", m4[:], tai, PR, ALU.mult, ["TAi", prn], ["m4"])
            yield
            TT_(s, "pool", Xt[:, k0:k0 + 4, 0, :], m1[:], m2[:], ALU.subtract, ["m1", "m2"], [f"Xt{hf}"])
            TT_(s, "dve", sn[:, 0, :], m1[:, :, 127], m2[:, :, 127], ALU.subtract, ["m1", "m2"], ["sn"])
            yield
            TT_(s, "dve", Xt[:, k0:k0 + 4, 1, :], m3[:], m4[:], ALU.add, ["m3", "m4"], [f"Xt{hf}"])
            TT_(s, "dve", sn[:, 1, :], m3[:, :, 127], m4[:, :, 127], ALU.add, ["m3", "m4"], ["sn"])
            yield
            a1r, a1i = TAr[:, k0:k0 + 4, 1], TAi[:, k0:k0 + 4, 1]
            TT_(s, "dve", sm[:, 0, :], a1r, sn[:, 0, :], ALU.mult, ["TAr", "sn"], ["sm"])
            TT_(s, "dve", sm[:, 1, :], a1i, sn[:, 1, :], ALU.mult, ["TAi", "sn"], ["sm"])
            TT_(s, "dve", sm[:, 2, :], a1r, sn[:, 1, :], ALU.mult, ["TAr", "sn"], ["sm"])
            TT_(s, "dve", sm[:, 3, :], a1i, sn[:, 0, :], ALU.mult, ["TAi", "sn"], ["sm"])
            yield
            TT_(s, "dve", car[:, 0, k0:k0 + 4], sm[:, 0, :], sm[:, 1, :], ALU.subtract, ["sm"], ["car"])
            TT_(s, "dve", car[:, 1, k0:k0 + 4], sm[:, 2, :], sm[:, 3, :], ALU.add, ["sm"], ["car"])
            yield

    def REST(n):
        q_, j = divmod(n, NJ)
        U, UB = ut[n % 3], ub[n % 3]
        un, ubn = f"ut{n % 3}", f"ub{n % 3}"
        UBP, ubpn = ub[(n - 1) % 3], f"ub{(n - 1) % 3}"
        for k in range(8):
            for part in range(2):
                MM(s, py[:, 32 * k:32 * k + 32], Xt[:, k, part, :], CX[:, k, part, :], [f"Xt{k // 4}", "CX"], ["pym"],
                   start=(part == 0), stop=(part == 1))
        yield
        TT_(s, "pool", du[:], U[:, 0:256], drb[:], ALU.mult, [un, "drb"], ["du"])
        yield
        TT_(s, "dve", yv[:], py[:, 0:256], du[:], ALU.add, ["pym", "du"], ["yv"])
        yield
        TT_(s, "pool", y2[:], yv[:], yv[:], ALU.mult, ["yv"], ["y2"])
        yield
        TS(s, "pool", y2[:], y2[:], 0.044715, ALU.mult, ["y2"], ["y2"], scalar2=1.0, op1=ALU.add)
        yield
        TT_(s, "pool", y2[:], y2[:], yv[:], ALU.mult, ["y2", "yv"], ["y2"])
        ACTF(s, sg[:], y2[:], AF.Sigmoid, ["y2"], ["sg"], scale=1.5957691216057308)
        yield
        TT_(s, "dve", gy[:], yv[:], sg[:], ALU.mult, ["yv", "sg"], ["gy"])
        yield
        for kc in range(2):
            TR(s, pmq[:, 256 + kc * 128:256 + (kc + 1) * 128], gy[:, kc * 128:(kc + 1) * 128], ident[:],
               ["gy", "ident"], ["pym"])
        CP(s, "act", gyT[:], pmq[:, 256:512].rearrange("p (k t) -> p k t", k=2), ["pym"], ["gyT"])
        for gc in range(4):
            for kc in range(2):
                MM(s, pglu[:, gc * 128:(gc + 1) * 128], wglu[:, kc, gc * 128:(gc + 1) * 128], gyT[:, kc, :],
                   ["wglu", "gyT"], ["pglu"], start=(kc == 0), stop=(kc == 1))
        yield
        ACTF(s, sgb[:], pglu[:, 256:512].rearrange("p (k t) -> p k t", k=2), AF.Sigmoid, ["pglu"], ["sgb"])
        yield
        YS = ysT[n % 2]
        TT_(s, "dve", YS[:], pglu[:, 0:256].rearrange("p (k t) -> p k t", k=2), sgb[:], ALU.mult, ["pglu", "sgb"],
            [f"ysT{n % 2}"])
        s.dma("pool", scr["yssmT"][q_, :, j * 128:(j + 1) * 128].rearrange("(kc p) t -> p kc t", p=128), YS[:],
              reads=[f"ysT{n % 2}"])
        yield
        for g in range(4):
            o = ppl[64 * (g % 2):64 * (g % 2) + 64, (g // 2) * 128:(g // 2 + 1) * 128]
            if j == 0:
                MM(s, o, UB[:, 256 + 64 * g:256 + 64 * g + 64], MT[:, g * 3 + 2, :], [ubn, "MT"], ["ppl"])
            else:
                MM(s, o, UB[:, 256 + 64 * g:256 + 64 * g + 64], MT[:, g * 3 + 0, :], [ubn, "MT"], ["ppl"],
                   start=True, stop=False)
                MM(s, o, UBP[:, 256 + 64 * g:256 + 64 * g + 64], MT[:, g * 3 + 1, :], [ubpn, "MT"], ["ppl"],
                   start=False, stop=True)
        CP(s, "act", plT[:], ppl[:, 0:256].rearrange("p (k t) -> p k t", k=2), ["ppl"], ["plT"])
        for g in range(4):
            pb = 64 * (g % 2)
            MM(s, ppl[pb:pb + 64, 256 + (g // 2) * 128:256 + (g // 2 + 1) * 128], PW[pb:pb + 64, g // 2, :],
               plT[pb:pb + 64, g // 2, :], ["PW", "plT"], ["ppl"])
        yield
        YP = ypT[n % 2]
        for kc in range(2):
            TS(s, "dve", YP[:, kc, :], ppl[:, 256 + kc * 128:256 + (kc + 1) * 128], pscol[:, kc:kc + 1], ALU.mult,
               ["ppl", "pscol"], [f"ypT{n % 2}"])
        s.dma("pool", scr["ypoolT"][q_, :, j * 128:(j + 1) * 128].rearrange("(kc p) t -> p kc t", p=128), YP[:],
              reads=[f"ypT{n % 2}"])

    load(0)
    if NCH > 1:
        load(1)
    def chain(*gs):
        for g in gs:
            yield from g

    for t in range(-1, NCH):
        if 0 <= t + 2 < NCH and t + 2 >= 2:
            load(t + 2)
        gens = []
        if t >= 0:
            gens.append(chain(XP(t), REST(t)))
        if t + 1 < NCH:
            gens.append(S1(t + 1))
        while gens:
            for g in list(gens):
                try:
                    next(g)
                except StopIteration:
                    gens.remove(g)
    c.end()


def build_test_a2(NSEQ, S, l=0):
    nc = bass.Bass("TRN2", target_bir_lowering=False)
    u = nc.dram_tensor("u_in", [NSEQ, S, 512], F32, kind="ExternalInput").ap()
    P = declare_params(nc)
    scr = make_scratch(nc, NSEQ, S, debug_out=True)
    scr["u"] = u
    with contextlib.ExitStack() as es:
        c = Ctx(nc, es)
        phase_a2(c, l, NSEQ, S, P, scr)
        print("ops", c.s.total, "waits", c.s.nwaits)
    return nc


def phase_c(c, l, NSEQ, S, x_d, xout_d, P, scr):
    nc, s = c.nc, c.s
    NJ = S // 128
    c.begin()
    wg = c.sb("wg", [128, 8, 3 * D], BF16)
    wba = c.sb("wba", [128, 4, D], BF16)
    wbs = c.sb("wbs", [128, 2, D], BF16)
    wbp = c.sb("wbp", [128, 2, D], BF16)
    wout = c.sb("wout", [128, 8, D], BF16)
    bg = c.sb("bg", [128, 3 * D], F32)
    NSTG = 4
    stg = [c.sb(f"stg{i}", [128, 1536], F32) for i in range(NSTG)]
    gcol = c.sb("gcol", [128, 8], F32)
    ident = c.sb("ident", [128, 128], BF16)
    xt = [c.sb(f"xt{i}", [128, D], F32) for i in range(5)]
    junk = c.sb("junk", [128, D], F32)
    ss = c.sb("ss", [128, 1], F32)
    rstd = c.sb("rstd", [128, 1], F32)
    hb = c.sb("hb", [128, D], BF16)
    hT = [c.sb(f"hT{i}", [128, 8, 128], BF16) for i in range(2)]
    gates_b = [c.sb(f"gates{i}", [128, 3 * D], F32) for i in range(2)]
    ya = [c.sb(f"ya{i}", [128, 512], BF16) for i in range(2)]
    yaT = c.sb("yaT", [128, 4, 128], BF16)
    ysT = [c.sb(f"ysT{i}", [128, 2, 128], BF16) for i in range(2)]
    ypT = [c.sb(f"ypT{i}", [128, 2, 128], BF16) for i in range(2)]
    macc = [c.sb(f"macc{i}", [128, 512], F32) for i in range(2)]
    mtmp = [c.sb(f"mtmp{i}", [128, 512], F32) for i in range(2)]
    mrg = [c.sb(f"mrg{i}", [128, D], BF16) for i in range(2)]
    mT = c.sb("mT", [128, 8, 128], BF16)
    pt = c.ps("pt")
    pg = [c.ps(f"pg{i}") for i in range(2)]
    pbr = [c.ps(f"pbr{i}") for i in range(3)]
    po = [c.ps(f"po{i}") for i in range(2)]

    make_identity(c, ident)
    s.dma("sp", gcol[:], P["mix_norm"][l].rearrange("(kc p) -> p kc", p=128), writes=["gcol"],
          allow_slow_non_contiguous=True)
    s.dma("act", bg[:], P["b_gate"][l:l + 1, :].partition_broadcast(128), writes=["bg"])
    n = 0
    dq = ("sp", "act", "pool")
    for kc in range(8):
        for hf in range(2):
            st, sn_ = stg[n % NSTG], f"stg{n % NSTG}"
            s.dma(dq[n % 3], st[:], P["w_in"][l, kc * 128:(kc + 1) * 128, 2056 + hf * 1536:2056 + (hf + 1) * 1536],
                  writes=[sn_])
            _cast(s, ("dve", "act")[n % 2], wg[:, kc, hf * 1536:(hf + 1) * 1536], st[:], [sn_, "gcol"],
                  [f"wg{kc}" if hf == 0 else f"wg{kc}b"], scalar=gcol[:, kc:kc + 1])
            n += 1
    for (wt, nm, nk) in ((wba, "w_br_attn", 4), (wbs, "w_br_ssm", 2), (wbp, "w_br_pool", 2), (wout, "w_out", 8)):
        for k0 in range(nk):
            st, sn_ = stg[n % NSTG], f"stg{n % NSTG}"
            s.dma(dq[n % 3], st[:, 0:D], P[nm][l, k0 * 128:(k0 + 1) * 128, :], writes=[sn_])
            _cast(s, ("dve", "act")[n % 2], wt[:, k0, :], st[:, 0:D], [sn_], [nm])
            n += 1

    NCH = NSEQ * NJ
    ptv = pt[:].bitcast(BF16)

    def load(n):
        q_, j = divmod(n, NJ)
        b = n % 2
        s.dma("sp", xt[n % 5][:], x_d[n * 128:(n + 1) * 128, :], writes=[f"xt{n % 5}"])

    def load_y(n):
        q_, j = divmod(n, NJ)
        b = n % 2
        s.dma("sp", ya[b][:], scr["yattn"][q_, j * 128:(j + 1) * 128, :], writes=[f"ya{b}"])
        s.dma("sp", ysT[b][:], scr["yssmT"][q_, :, j * 128:(j + 1) * 128].rearrange("(kc p) t -> p kc t", p=128),
              writes=[f"ysT{b}"])
        s.dma("sp", ypT[b][:], scr["ypoolT"][q_, :, j * 128:(j + 1) * 128].rearrange("(kc p) t -> p kc t", p=128),
              writes=[f"ypT{b}"])

    def s1(n):
        X, xb = xt[n % 5], f"xt{n % 5}"
        HT, htn = hT[n % 2], f"hT{n % 2}"
        rms_rstd(c, X[:], xb, junk[:], ss[:], rstd[:], D, "")
        yield
        TS(s, "dve", hb[:], X[:], rstd[:, 0:1], ALU.mult, [xb, "rstd"], ["hb"])
        yield
        for kc in range(8):
            TR(s, ptv[:, kc * 128:(kc + 1) * 128], hb[:, kc * 128:(kc + 1) * 128], ident[:], ["hb", "ident"], ["pt"])
        CP(s, "act", HT[:], ptv.rearrange("p (k t) -> p k t", k=8), ["pt"], [htn])
        yield

    def s2_gates(n):
        HT, htn = hT[n % 2], f"hT{n % 2}"
        gates = gates_b[n % 2]
        gp = f"g{n % 2}_"
        for gb in range(6):
            PG, pgn = pg[gb % 2], f"pg{gb % 2}"
            for kc in range(8):
                MM(s, PG[:], HT[:, kc, :], wg[:, kc, gb * 512:(gb + 1) * 512], [htn, f"wg{kc}" if gb < 3 else f"wg{kc}b"], [pgn],
                   start=(kc == 0), stop=(kc == 7))
                yield
            gsl = gates[:, gb * 512:(gb + 1) * 512]
            TT_(s, "dve", gsl, PG[:], bg[:, gb * 512:(gb + 1) * 512], ALU.add, [pgn, "bg"], [gp + f"gate{gb}"])
            yield
            ACTF(s, gsl, gsl, AF.Sigmoid, [gp + f"gate{gb}"], [gp + f"gate{gb}"])
            yield

    def s2_ya(n):
        b = n % 2
        for kc in range(4):
            TR(s, ptv[:, kc * 128:(kc + 1) * 128], ya[b][:, kc * 128:(kc + 1) * 128], ident[:], [f"ya{b}", "ident"],
               ["pt"])
        CP(s, "dve", yaT[:], ptv[:, 0:512].rearrange("p (k t) -> p k t", k=4), ["pt"], ["yaT"])
        yield

    def s2_branch(n, dbs):
        b = n % 2
        MR = mrg[b]
        gates = gates_b[n % 2]
        gp = f"g{n % 2}_"
        for db in dbs:
            cs = slice(db * 512, (db + 1) * 512)
            for kc in range(4):
                MM(s, pbr[0][:], yaT[:, kc, :], wba[:, kc, cs], ["yaT", "w_br_attn"], ["pbr0"], start=(kc == 0),
                   stop=(kc == 3))
                yield
            for kc in range(2):
                MM(s, pbr[1][:], ysT[b][:, kc, :], wbs[:, kc, cs], [f"ysT{b}", "w_br_ssm"], ["pbr1"], start=(kc == 0),
                   stop=(kc == 1))
                yield
            for kc in range(2):
                MM(s, pbr[2][:], ypT[b][:, kc, :], wbp[:, kc, cs], [f"ypT{b}", "w_br_pool"], ["pbr2"], start=(kc == 0),
                   stop=(kc == 1))
                yield
            MA, TM = macc[db], mtmp[db]
            TT_(s, "dve", MA[:], pbr[0][:], gates[:, db * 512:(db + 1) * 512], ALU.mult, ["pbr0", gp + f"gate{db}"],
                [f"macc{db}"])
            yield
            TT_(s, "dve", TM[:], pbr[1][:], gates[:, D + db * 512:D + (db + 1) * 512], ALU.mult,
                ["pbr1", gp + f"gate{2 + db}"], [f"mtmp{db}"])
            yield
            TT_(s, "pool", MA[:], MA[:], TM[:], ALU.add, [f"macc{db}", f"mtmp{db}"], [f"macc{db}"])
            yield
            TT_(s, "dve", TM[:], pbr[2][:], gates[:, 2 * D + db * 512:2 * D + (db + 1) * 512], ALU.mult,
                ["pbr2", gp + f"gate{4 + db}"], [f"mtmp{db}"])
            yield
            TT_(s, "pool", MR[:, cs], MA[:], TM[:], ALU.add, [f"macc{db}", f"mtmp{db}"], [f"mrg{b}_{db}"])
            yield

    def s3(n):
        b = n % 2
        MR = mrg[b]
        X, xb = xt[n % 5], f"xt{n % 5}"
        for kc in range(8):
            TR(s, ptv[:, kc * 128:(kc + 1) * 128], MR[:, kc * 128:(kc + 1) * 128], ident[:],
               [f"mrg{b}_{kc // 4}", "ident"], ["pt"])
        CP(s, "act", mT[:], ptv.rearrange("p (k t) -> p k t", k=8), ["pt"], ["mT"])
        yield
        for db in range(2):
            cs = slice(db * 512, (db + 1) * 512)
            for kc in range(8):
                MM(s, po[db][:], mT[:, kc, :], wout[:, kc, cs], ["mT", "w_out"], [f"po{db}"], start=(kc == 0),
                   stop=(kc == 7))
                yield
            TT_(s, "dve", X[:, cs], po[db][:], X[:, cs], ALU.add, [f"po{db}", xb], [xb])
            yield
        s.dma("pool", xout_d[n * 128:(n + 1) * 128, :], X[:], reads=[xb])
        yield

    def chain(*gs):
        for g in gs:
            yield from g

    def run(gens):
        gens = list(gens)
        while gens:
            for g in list(gens):
                try:
                    next(g)
                except StopIteration:
                    gens.remove(g)

    load(0)
    load_y(0)
    if NCH > 1:
        load(1)
    run([s1(0)])
    for t in range(NCH + 2):
        if t + 2 < NCH:
            load(t + 2)
        gens = []
        if t < NCH:
            gens.append(s2_gates(t))
        if t + 1 < NCH:
            gens.append(s1(t + 1))
        if 1 <= t <= NCH:
            gens.append(chain(s2_ya(t - 1), s2_branch(t - 1, [0, 1])))
        if t >= 2:
            gens.append(s3(t - 2))
        run(gens)
        if t + 1 < NCH:
            load_y(t + 1)
    c.end()


def build_full(NSEQ, S, depth=DEPTH):
    T = NSEQ * S
    nc = bass.Bass("TRN2", target_bir_lowering=False)
    x = nc.dram_tensor("x", [T, D], F32, kind="ExternalInput").ap()
    y = nc.dram_tensor("y", [T, D], F32, kind="ExternalOutput").ap()
    P = declare_params(nc)
    scr = make_scratch(nc, NSEQ, S)
    xa = nc.dram_tensor("scr_xa", [T, D], F32).ap()
    with contextlib.ExitStack() as es:
        c = Ctx(nc, es)
        G = {"cumall": es.enter_context(nc.sbuf_tensor("cumall", [128, NSEQ, S // 128, 8], F32))}
        for l in range(depth):
            xin = x if l == 0 else xa
            phase_a(c, l, NSEQ, S, xin, P, G, scr)
            phase_a2(c, l, NSEQ, S, P, scr)
            phase_b(c, NSEQ, S, G, scr)
            phase_c(c, l, NSEQ, S, xin, xa, P, scr)
            phase_mlp(c, l, T, xa, P["w_up"], P["w_down"], P["mlp_norm"], xout_d=(y if l == depth - 1 else xa))
        print("ops", c.s.total, "waits", c.s.nwaits, flush=True)
    return nc


_NC_CACHE = {}


def kernel(**inputs):
    x = np.ascontiguousarray(np.asarray(inputs["x"], dtype=np.float32))
    B, S, _ = x.shape
    NSEQ = B // NCORES
    key = (NSEQ, S)
    if key not in _NC_CACHE:
        _NC_CACHE[key] = build_full(NSEQ, S)
    nc = _NC_CACHE[key]
    params = {k: np.ascontiguousarray(np.asarray(inputs[k], dtype=np.float32)) for k in PARAM_SHAPES}
    in_maps = []
    for cid in range(NCORES):
        m = {"x": x[cid * NSEQ:(cid + 1) * NSEQ].reshape(NSEQ * S, D)}
        m.update(params)
        in_maps.append(m)
    res = run_bass_kernel_spmd(nc, in_maps, core_ids=list(range(NCORES)))
    out = np.empty((B, S, D), dtype=np.float32)
    for cid in range(NCORES):
        out[cid * NSEQ:(cid + 1) * NSEQ] = np.asarray(res.results[cid]["y"]).reshape(NSEQ, S, D)
    return out
```

```python
import contextlib
import math
import numpy as np
import concourse.bass as bass
import concourse.mybir as mybir
from concourse.bass_utils import run_bass_kernel_spmd

F32 = mybir.dt.float32
BF16 = mybir.dt.bfloat16
ALU = mybir.AluOpType
AF = mybir.ActivationFunctionType
AX = mybir.AxisListType

D = 1024
DEPTH = 2
DFF = 4096
NH = 8
DH = 64
INC = 5128
EPS = 1e-6
NCORES = 8

ENGS = ("pe", "act", "dve", "pool", "sp")
N_DMA_SEMS = 12
SAME_ENGINE_SYNC = True


class Buf:
    __slots__ = ("name", "w", "r")

    def __init__(self, name):
        self.name = name
        self.w = None
        self.r = {}


class Op:
    __slots__ = ("eng", "idx", "fn", "deps", "dma", "needs_inc", "sem", "semval")

    def __init__(self, eng, idx, fn, deps, dma):
        self.eng, self.idx, self.fn, self.deps, self.dma = eng, idx, fn, deps, dma
        self.needs_inc = False
        self.sem = None
        self.semval = None


class Sched:
    def __init__(self, nc, es, same_engine_sync=SAME_ENGINE_SYNC):
        self.nc = nc
        self.q = {e: [] for e in ENGS}
        self.ndma = {e: 0 for e in ENGS}
        self.cnt = {e: 0 for e in ENGS}
        self.same_engine_sync = same_engine_sync
        self.bufs = {}
        self.esem = {e: es.enter_context(nc.semaphore("es_" + e)) for e in ENGS if e != "sp"}
        self.dsem = {}
        for e in ("sp", "act", "pool"):
            for j in range(N_DMA_SEMS):
                self.dsem[(e, j)] = es.enter_context(nc.semaphore(f"ds_{e}_{j}"))
        self.total = {e: 0 for e in ENGS}
        self.nwaits = {e: 0 for e in ENGS}

    def buf(self, name):
        b = self.bufs.get(name)
        if b is None:
            b = self.bufs[name] = Buf(name)
        return b

    def _b(self, x):
        return x if isinstance(x, Buf) else self.buf(x)

    def op(self, eng, fn, reads=(), writes=(), dma=False):
        reads = [self._b(x) for x in reads]
        writes = [self._b(x) for x in writes]
        deps = {}
        for b in reads:
            if b.w is not None:
                deps[id(b.w)] = b.w
        for b in writes:
            if b.w is not None:
                deps[id(b.w)] = b.w
            for o in b.r.values():
                deps[id(o)] = o
        o = Op(eng, len(self.q[eng]), fn, list(deps.values()), dma)
        if dma:
            i = self.ndma[eng]
            self.ndma[eng] += 1
            o.sem = (eng, i % N_DMA_SEMS)
            o.semval = 16 * (i // N_DMA_SEMS + 1)
        self.q[eng].append(o)
        for b in reads:
            key = ("dma", eng, o.idx) if dma else eng
            b.r[key] = o
        for b in writes:
            b.w = o
            b.r = {}
        for d in o.deps:
            if not d.dma:
                d.needs_inc = True
        return o

    def dma(self, eng, out, in_, reads=(), writes=(), **kw):
        return self.op(eng, lambda e: e.dma_start(out=out, in_=in_, **kw), reads, writes, dma=True)

    def emit(self):
        nc = self.nc
        for e in ENGS:
            c = self.cnt[e]
            for o in self.q[e]:
                if not o.dma and o.needs_inc:
                    c += 1
                    o.sem = e
                    o.semval = c
            self.cnt[e] = c

        def replay(e, engobj):
            known = {}
            for o in self.q[e]:
                waits = {}
                for d in o.deps:
                    if d.eng == e and not d.dma:
                        if e == "pe" or (not self.same_engine_sync and e != "pool"):
                            continue
                    k = d.sem
                    if waits.get(k, 0) < d.semval:
                        waits[k] = d.semval
                if o.dma and o.semval > 16:
                    k = o.sem
                    waits[k] = max(waits.get(k, 0), o.semval - 16)
                for k, v in waits.items():
                    if known.get(k, 0) < v:
                        known[k] = v
                        s = self.dsem[k] if isinstance(k, tuple) else self.esem[k]
                        engobj.wait_ge(s, v)
                        self.nwaits[e] += 1
                ins = o.fn(engobj)
                if o.dma:
                    ins.then_inc(self.dsem[o.sem], 16)
                elif o.needs_inc:
                    ins.then_inc(self.esem[e], 1)
            if self.ndma[e]:
                n = self.ndma[e]
                for j in range(min(N_DMA_SEMS, n)):
                    last = ((n - 1 - j) // N_DMA_SEMS) + 1
                    if known.get((e, j), 0) < 16 * last:
                        engobj.wait_ge(self.dsem[(e, j)], 16 * last)

        with nc.Block() as block:
            @block.sync
            def _(e):
                replay("sp", e)

            @block.tensor
            def _(e):
                replay("pe", e)

            @block.scalar
            def _(e):
                replay("act", e)

            @block.vector
            def _(e):
                replay("dve", e)

            @block.gpsimd
            def _(e):
                replay("pool", e)

        for e in ENGS:
            self.total[e] += len(self.q[e])
            self.q[e] = []
        for b in self.bufs.values():
            b.w = None
            b.r = {}


class Ctx:
    def __init__(self, nc, es):
        self.nc = nc
        self.es = es
        self.s = Sched(nc, es)
        self.pes = None
        self.uid = 0

    def begin(self):
        self.pes = contextlib.ExitStack()
        self.pes.__enter__()

    def end(self):
        self.s.emit()
        self.pes.close()
        self.pes = None

    def dbg(self, name, ap, reads):
        if not getattr(self, "debug", False):
            return
        d = self.nc.dram_tensor("dbg_" + name, list(ap.shape), ap.dtype, kind="ExternalOutput").ap()
        self.s.dma("sp", d, ap, reads=reads)

    def sb(self, name, shape, dtype):
        self.uid += 1
        return self.pes.enter_context(self.nc.sbuf_tensor(f"{name}_{self.uid}", list(shape), dtype))

    def ps(self, name, shape=(128, 512), dtype=F32):
        self.uid += 1
        return self.pes.enter_context(self.nc.psum_tensor(f"{name}_{self.uid}", list(shape), dtype))


def _cast_engine(i):
    return ("dve", "pool", "act")[i % 3]


def _cast(s, eng, out, in_, reads, writes, scalar=None):
    if scalar is None:
        if eng == "act":
            return s.op("act", lambda e: e.copy(out=out, in_=in_), reads, writes)
        return s.op(eng, lambda e: e.tensor_copy(out=out, in_=in_), reads, writes)
    if eng == "act":
        return s.op("act", lambda e: e.activation(out=out, in_=in_, func=AF.Copy, scale=scalar), reads, writes)
    return s.op(eng, lambda e: e.tensor_scalar(out=out, in0=in_, scalar1=scalar, scalar2=None, op0=ALU.mult),
                reads, writes)


def make_identity(c, ident):
    s = c.s
    s.op("pool", lambda e: e.memset(ident[:], 1.0), writes=["ident"])
    s.op("pool", lambda e: e.affine_select(out=ident[:], in_=ident[:], pattern=[[-1, 128]],
                                           compare_op=ALU.is_equal, fill=0.0, base=0, channel_multiplier=1),
         reads=["ident"], writes=["ident"])


def phase_mlp(c, l, T, x_d, w_up, w_down, mlp_norm, xout_d=None):
    nc, s = c.nc, c.s
    if xout_d is None:
        xout_d = x_d
    c.begin()
    TT = 256
    NT = T // TT
    wup = c.sb("wup", [128, 8, DFF], BF16)
    wdn = c.sb("wdn", [128, 32, D], BF16)
    NSTG = 4
    stg = [c.sb(f"stg{i}", [128, 1024], F32) for i in range(NSTG)]
    gcol = c.sb("gcol", [128, 8], F32)
    ident = c.sb("ident", [128, 128], BF16)
    xt = [c.sb(f"xt{i}", [128, 2, D], F32) for i in range(3)]
    hb = c.sb("hb", [128, 2, D], BF16)
    hT = [c.sb(f"hT{i}", [128, 8, TT], BF16) for i in range(2)]
    actT = c.sb("actT", [128, 32, TT], BF16)
    rl = [c.sb(f"rl{i}", [128, TT], F32) for i in range(2)]
    ss = c.sb("ss", [128, 2], F32)
    rstd = c.sb("rstd", [128, 2], F32)
    junk = c.sb("junk", [128, 2, D], BF16)
    pt = [c.ps(f"pt{i}") for i in range(2)]
    pu = [c.ps(f"pu{i}") for i in range(3)]
    pd = [c.ps(f"pd{i}") for i in range(3)]

    make_identity(c, ident)
    s.dma("sp", gcol[:], mlp_norm[l].rearrange("(kc p) -> p kc", p=128), writes=["gcol"],
          allow_slow_non_contiguous=True)
    n = 0
    dq = ("sp", "act", "pool")
    for kc in range(8):
        for qq in range(4):
            st, sn_ = stg[n % NSTG], f"stg{n % NSTG}"
            s.dma(dq[n % 3], st[:], w_up[l, kc * 128:(kc + 1) * 128, qq * 1024:(qq + 1) * 1024], writes=[sn_])
            _cast(s, ("dve", "act")[n % 2], wup[:, kc, qq * 1024:(qq + 1) * 1024], st[:],
                  [sn_, "gcol"], [f"wup{kc}_{qq}"], scalar=gcol[:, kc:kc + 1])
            n += 1
    for fc in range(32):
        st, sn_ = stg[n % NSTG], f"stg{n % NSTG}"
        s.dma(dq[n % 3], st[:], w_down[l, fc * 128:(fc + 1) * 128, :], writes=[sn_])
        _cast(s, ("dve", "act")[n % 2], wdn[:, fc, :], st[:], [sn_], [f"wdn{fc}"])
        n += 1

    def load(i):
        s.dma("sp", xt[i % 3][:], x_d[i * TT:(i + 1) * TT, :].rearrange("(a p) d -> p a d", p=128),
              writes=[f"xt{i % 3}"])

    def front_elem(i):
        X, xb = xt[i % 3], f"xt{i % 3}"
        for a in range(2):
            ACTF(s, junk[:, a, :], X[:, a, :], AF.Square, [xb], [f"junk{a}"])
        s.op("dve", lambda e: e.tensor_reduce(out=ss[:], in_=junk[:], axis=AX.X, op=ALU.add),
             reads=["junk0", "junk1"], writes=["ss"])
        ACTF(s, rstd[:], ss[:], AF.Sqrt, ["ss"], ["rstd"], scale=1.0 / D, bias=EPS)
        s.op("dve", lambda e: e.reciprocal(out=rstd[:], in_=rstd[:]), reads=["rstd"], writes=["rstd"])
        for a in range(2):
            TS(s, "dve", hb[:, a, :], X[:, a, :], rstd[:, a:a + 1], ALU.mult, [xb, "rstd"], [f"hb{a}"])
    def front_tr(i):
        HT, htn = hT[i % 2], f"hT{i % 2}"
        for a in range(2):
            pview = pt[a][:].bitcast(BF16)
            for kc in range(8):
                TR(s, pview[:, kc * 128:(kc + 1) * 128], hb[:, a, kc * 128:(kc + 1) * 128], ident[:],
                   [f"hb{a}", "ident"], [f"pt{a}"])
            CP(s, "dve" if a == 0 else "act", HT[:, :, a * 128:(a + 1) * 128],
               pview.rearrange("p (k t) -> p k t", k=8), [f"pt{a}"], [htn + f"_{a}"])

    def up(i):
        HT, htn = hT[i % 2], f"hT{i % 2}"
        for fc in range(32):
            Pb, pb = pu[fc % 3], f"pu{fc % 3}"
            for kc in range(8):
                MM(s, Pb[:, 0:TT], wup[:, kc, fc * 128:(fc + 1) * 128], HT[:, kc, :],
                   [f"wup{kc}_{fc // 8}", htn + "_0", htn + "_1"], [pb], start=(kc == 0), stop=(kc == 7))
            R, rb = rl[fc % 2], f"rl{fc % 2}"
            ACTF(s, R[:], Pb[:, 0:TT], AF.Relu, [pb], [rb])
            TT_(s, "pool", actT[:, fc, :], R[:], R[:], ALU.mult, [rb], ["actT"])

    def down(i):
        X, xb = xt[i % 3], f"xt{i % 3}"
        for a in range(2):
            for db in range(2):
                jj = a * 2 + db
                Pb, pb = pd[jj % 3], f"pd{jj % 3}"
                for fc in range(32):
                    MM(s, Pb[:], actT[:, fc, a * 128:(a + 1) * 128], wdn[:, fc, db * 512:(db + 1) * 512],
                       ["actT", f"wdn{fc}"], [pb], start=(fc == 0), stop=(fc == 31))
                TT_(s, "dve", X[:, a, db * 512:(db + 1) * 512], Pb[:], X[:, a, db * 512:(db + 1) * 512], ALU.add,
                    [pb, xb], [xb])
        s.dma("pool", xout_d[i * TT:(i + 1) * TT, :].rearrange("(a p) d -> p a d", p=128), X[:], reads=[xb])

    load(0)
    if NT > 1:
        load(1)
    front_elem(0)
    front_tr(0)
    for i in range(NT):
        if i + 2 < NT:
            load(i + 2)
        if i + 1 < NT:
            front_elem(i + 1)
        up(i)
        if i + 1 < NT:
            front_tr(i + 1)
        down(i)
    c.end()


def build_test_mlp(T):
    nc = bass.Bass("TRN2", target_bir_lowering=False)
    x = nc.dram_tensor("x", [T, D], F32, kind="ExternalInput").ap()
    w_up = nc.dram_tensor("w_up", [DEPTH, D, DFF], F32, kind="ExternalInput").ap()
    w_down = nc.dram_tensor("w_down", [DEPTH, DFF, D], F32, kind="ExternalInput").ap()
    mlp_norm = nc.dram_tensor("mlp_norm", [DEPTH, D], F32, kind="ExternalInput").ap()
    y = nc.dram_tensor("y", [T, D], F32, kind="ExternalOutput").ap()
    with contextlib.ExitStack() as es:
        c = Ctx(nc, es)
        c.debug = True
        phase_mlp(c, 0, T, x, w_up, w_down, mlp_norm, xout_d=y)
        print("ops", c.s.total, "waits", c.s.nwaits)
    return nc


def rms_rstd(c, X, xb, junk, ss, rstd, width, tag):
    s = c.s
    s.op("act", lambda e: e.activation(out=junk, in_=X, func=AF.Square), reads=[xb], writes=["junk" + tag])
    s.op("dve", lambda e: e.tensor_reduce(out=ss, in_=junk, axis=AX.X, op=ALU.add),
         reads=["junk" + tag], writes=["ss" + tag])
    s.op("act", lambda e: e.activation(out=rstd, in_=ss, func=AF.Sqrt, scale=1.0 / width, bias=EPS),
         reads=["ss" + tag], writes=["rstd" + tag])
    s.op("dve", lambda e: e.reciprocal(out=rstd, in_=rstd), reads=["rstd" + tag], writes=["rstd" + tag])


def make_tri(c, tri, name, dtype_one=1.0):
    s = c.s
    s.op("pool", lambda e: e.memset(tri[:], 1.0), writes=[name])
    s.op("pool", lambda e: e.affine_select(out=tri[:], in_=tri[:], pattern=[[1, 128]],
                                           compare_op=ALU.is_ge, fill=0.0, base=0, channel_multiplier=-1),
         reads=[name], writes=[name])


def phase_a(c, l, NSEQ, S, x_d, P, G, scr, do_ssm=True, do_pool=True):
    nc, s = c.nc, c.s
    NJ = S // 128
    c.begin()
    NA = 2056
    win = c.sb("win", [128, 8, NA], BF16)
    NSTG = 4
    stg = [c.sb(f"stg{i}", [128, NA // 2], F32) for i in range(NSTG)]
    gcol = c.sb("gcol", [128, 8], F32)
    ident = c.sb("ident", [128, 128], BF16)
    trif = c.sb("trif", [128, 128], F32)
    onesf = c.sb("onesf", [128, 128], F32)
    xt = [c.sb(f"xt{i}", [128, D], F32) for i in range(3)]
    junk = c.sb("junk", [128, D], F32)
    ss = c.sb("ss", [128, 1], F32)
    rstd = c.sb("rstd", [128, 1], F32)
    hb = c.sb("hb", [128, D], BF16)
    hT = [c.sb(f"hT{i}", [128, 8, 128], BF16) for i in range(2)]
    qkg = c.sb("qkg", [128, 2, 8, DH], F32)
    bfg = c.sb("bfg", [128, 8], F32)
    sq = [c.sb(f"sq{i}", [128, 512], F32) for i in range(2)]
    qe = [[c.sb(f"qe{i}_{b}", [128, 512], F32) for b in range(2)] for i in range(2)]
    ssq = [c.sb(f"ssq{i}", [128, 8], F32) for i in range(2)]
    rq = [c.sb(f"rq{i}", [128, 8], F32) for i in range(2)]
    qn = [c.sb(f"qn{i}", [128, 8, DH], F32) for i in range(2)]
    qa = [c.sb(f"qa{i}", [128, 8, 70], BF16) for i in range(2)]
    r1 = c.sb("r1", [128, 8], F32)
    r2 = c.sb("r2", [128, 8], F32)
    qTs = [[c.sb(f"qTs{i}_{b}", [128, 8, 128], BF16) for b in range(2)] for i in range(2)]
    vst = [c.sb(f"vst{b}", [128, 8, 65], BF16) for b in range(2)]
    ust = [c.sb(f"ust{b}", [128, 512], F32) for b in range(2)]
    fls = [c.sb(f"fl{b}", [128, 8], F32) for b in range(2)]
    sp_ = c.sb("sp_", [128, 8], F32)
    carry = c.sb("carry", [128, 8], F32)
    cumall = G["cumall"]
    pt = c.ps("pt")
    pq = [c.ps(f"pq{i}") for i in range(2)]
    pv = c.ps("pv")
    pf = c.ps("pf")
    pp = c.ps("pp")
    ptq = [c.ps(f"ptq{i}") for i in range(2)]

    make_identity(c, ident)
    make_tri(c, trif, "trif")
    s.op("pool", lambda e: e.memset(onesf[:], 1.0), writes=["onesf"])
    for b in range(2):
        s.op("pool", lambda e, b=b: e.memset(vst[b][:], 1.0), writes=[f"vst{b}"])
    s.dma("sp", gcol[:], P["mix_norm"][l].rearrange("(kc p) -> p kc", p=128), writes=["gcol"],
          allow_slow_non_contiguous=True)
    s.dma("sp", qkg[:, 0, 0, :], P["q_norm"][l:l + 1, :].partition_broadcast(128), writes=["qkg"])
    s.dma("sp", qkg[:, 1, 0, :], P["k_norm"][l:l + 1, :].partition_broadcast(128), writes=["qkg"])
    s.dma("sp", bfg[:], P["b_forget"][l:l + 1, :].partition_broadcast(128), writes=["bfg"])
    s.op("dve", lambda e: e.tensor_scalar(out=qkg[:, 0, 0, :], in0=qkg[:, 0, 0, :], scalar1=DH ** -0.5, scalar2=None,
                                          op0=ALU.mult), reads=["qkg"], writes=["qkg"])
    for h in range(1, 8):
        s.op("dve", lambda e, h=h: e.tensor_copy(out=qkg[:, :, h, :], in_=qkg[:, :, 0, :]),
             reads=["qkg"], writes=["qkg"])
    nn = 0
    dq = ("sp", "act", "pool")
    HN = NA // 2
    for kc in range(8):
        for hf in range(2):
            st, sn_ = stg[nn % NSTG], f"stg{nn % NSTG}"
            s.dma(dq[nn % 3], st[:], P["w_in"][l, kc * 128:(kc + 1) * 128, hf * HN:(hf + 1) * HN], writes=[sn_])
            _cast(s, ("dve", "act")[nn % 2], win[:, kc, hf * HN:(hf + 1) * HN], st[:], [sn_, "gcol"], [f"win{kc}"],
                  scalar=gcol[:, kc:kc + 1])
            nn += 1

    def load(n):
        s.dma("sp", xt[n % 3][:], x_d[n * 128:(n + 1) * 128, :], writes=[f"xt{n % 3}"])

    NCH = NSEQ * NJ
    ptv = pt[:].bitcast(BF16)
    pc = pf[:, 272:288]
    for i in range(2):
        s.op("pool", lambda e, i=i: e.memset(qa[i][:], 1.0), writes=[f"qg{i}", f"qaug{i}"])

    def front_elem(n):
        X, xb = xt[n % 3], f"xt{n % 3}"
        ACTF(s, junk[:], X[:], AF.Square, [xb], ["junk"])
        s.op("dve", lambda e: e.tensor_reduce(out=ss[:], in_=junk[:], axis=AX.X, op=ALU.add), reads=["junk"],
             writes=["ss"])
        ACTF(s, rstd[:], ss[:], AF.Ln, ["ss"], ["rstd"], scale=1.0 / D, bias=EPS)
        ACTF(s, rstd[:], rstd[:], AF.Exp, ["rstd"], ["rstd"], scale=-0.5)
        TS(s, "dve", hb[:], X[:], rstd[:, 0:1], ALU.mult, [xb, "rstd"], ["hb"])

    def front_tr(n):
        HT, htn = hT[n % 2], f"hT{n % 2}"
        for kc in range(8):
            TR(s, ptv[:, kc * 128:(kc + 1) * 128], hb[:, kc * 128:(kc + 1) * 128], ident[:], ["hb", "ident"], ["pt"])
        CP(s, "act", HT[:], ptv.rearrange("p (k t) -> p k t", k=8), ["pt"], [htn])

    def proj_all(n):
        HT, htn = hT[n % 2], f"hT{n % 2}"

        def proj(Pt, pname, c0, c1):
            for kc in range(8):
                MM(s, Pt, HT[:, kc, :], win[:, kc, c0:c1], [htn, f"win{kc}"], [pname], start=(kc == 0), stop=(kc == 7))
        proj(pq[0][:], "pq0", 0, 512)
        proj(pq[1][:], "pq1", 512, 1024)
        proj(pv[:], "pv", 1024, 1536)
        proj(pf[:, 0:264], "pf", 1536, 1800)
        proj(pp[:, 0:256], "pp", 1800, 2056)

    def evac(n):
        b = n % 2
        CP(s, "act", qe[0][b][:], pq[0][:], ["pq0"], [f"qe0_{b}"])
        CP(s, "dve", qe[1][b][:], pq[1][:], ["pq1"], [f"qe1_{b}"])
        VS = vst[b]
        CP(s, "act", VS[:, :, 0:64], pv[:].rearrange("p (h d) -> p h d", h=8), ["pv"], [f"vst{b}"])
        US = ust[b]
        CP(s, "dve", US[:, 0:256], pf[:, 8:264], ["pf"], [f"ust{b}"])
        TT_(s, "dve", fls[b][:], pf[:, 0:8], bfg[:], ALU.add, ["pf", "bfg"], [f"fl{b}"])
        CP(s, "act", US[:, 256:512], pp[:, 0:256], ["pp"], [f"ust{b}"])

    def forget(n):
        q_, j = divmod(n, NJ)
        b = n % 2
        fl = fls[b]
        ACTF(s, fl[:], fl[:], AF.Exp, [f"fl{b}"], [f"fl{b}"], scale=-1.0)
        ACTF(s, sp_[:], fl[:], AF.Ln, [f"fl{b}"], ["sp_"], scale=1.0, bias=1.0)
        if j == 0:
            s.op("pool", lambda e: e.memset(carry[:], 0.0), writes=["carry"])
        MM(s, pc[:, 0:8], trif[:], sp_[:], ["trif", "sp_"], ["pf"])
        MM(s, pc[:, 8:16], onesf[:], sp_[:], ["onesf", "sp_"], ["pf"])
        cum = cumall[:, q_, j, :]
        TT_(s, "dve", cum, carry[:], pc[:, 0:8], ALU.subtract, ["carry", "pf"], ["cumall"])
        TT_(s, "dve", carry[:], carry[:], pc[:, 8:16], ALU.subtract, ["carry", "pf"], ["carry"])

    def post(n):
        q_, j = divmod(n, NJ)
        b = n % 2
        fl = fls[b]
        VS, US = vst[b], ust[b]
        cum = cumall[:, q_, j, :]
        CP(s, "dve", qa[0][:, :, 67], cum, ["cumall"], ["qaug0"])
        yield
        TT_(s, "dve", r1[:], cum, qa[0][:, :, 67], ALU.subtract, ["cumall", "qaug0"], ["r1"])
        yield
        CP(s, "dve", qa[0][:, :, 68], r1[:], ["r1"], ["qaug0"])
        yield
        TT_(s, "dve", r2[:], r1[:], qa[0][:, :, 68], ALU.subtract, ["r1", "qaug0"], ["r2"])
        yield
        CP(s, "dve", qa[0][:, :, 69], r2[:], ["r2"], ["qaug0"])
        yield
        TS(s, "dve", qa[1][:, :, 64:67], qa[0][:, :, 67:70], -1.0, ALU.mult, ["qaug0"], ["qaug1"])
        yield
        for i in range(2):
            PQ = qe[i][b]
            ACTF(s, sq[i][:], PQ[:], AF.Square, [f"qe{i}_{b}"], [f"sq{i}"])
            yield
            s.op("dve", lambda e, i=i: e.tensor_reduce(out=ssq[i][:], in_=sq[i][:].rearrange("p (h d) -> p h d", h=8),
                                                       axis=AX.X, op=ALU.add),
                 reads=[f"sq{i}"], writes=[f"ssq{i}"])
            yield
            ACTF(s, rq[i][:], ssq[i][:], AF.Ln, [f"ssq{i}"], [f"rq{i}"], scale=1.0 / DH, bias=EPS)
            yield
            ACTF(s, rq[i][:], rq[i][:], AF.Exp, [f"rq{i}"], [f"rq{i}"], scale=-0.5)
            yield
            TT_(s, "dve", qn[i][:], PQ[:].rearrange("p (h d) -> p h d", h=8),
                rq[i][:].unsqueeze(2).broadcast_to([128, 8, DH]), ALU.mult, [f"qe{i}_{b}", f"rq{i}"], [f"qn{i}"])
            yield
            TT_(s, "dve", qa[i][:, :, 0:64], qn[i][:], qkg[:, i, :, :], ALU.mult, [f"qn{i}", "qkg"], [f"qg{i}"])
            yield
            pvw = ptq[i][:].bitcast(BF16)
            for h in range(8):
                TR(s, pvw[0:70, h * 128:(h + 1) * 128], qa[i][:, h, :], ident[:], [f"qg{i}", f"qaug{i}", "ident"],
                   [f"ptq{i}"])
            QT = qTs[i][n % 2]
            CP(s, "act" if i == 0 else "dve", QT[0:70, :, :], pvw[0:70, :].rearrange("p (h t) -> p h t", h=8),
               [f"ptq{i}"], [f"qTs{i}_{n % 2}"])
            yield
            dst = scr["qT" if i == 0 else "kT"]
            s.dma("pool", dst[q_, :, :, j * 128:(j + 1) * 128].rearrange("h p t -> p h t"), QT[0:70, :, :],
                  reads=[f"qTs{i}_{n % 2}"])
            yield
        s.dma("pool", scr["v"][q_, j * 128:(j + 1) * 128, :, :], VS[:], reads=[f"vst{n % 2}"])
        yield
        s.dma("pool", scr["u"][q_, j * 128:(j + 1) * 128, :], US[:], reads=[f"ust{n % 2}"])
        yield

    load(0)
    if NCH > 1:
        load(1)
    def main_stream(t):
        if t + 2 < NCH:
            load(t + 2)
        if t < NCH:
            front_elem(t)
        yield
        if 1 <= t <= NCH:
            HT, htn = hT[(t - 1) % 2], f"hT{(t - 1) % 2}"
            for (Pt, pname, c0, c1) in ((pq[0][:], "pq0", 0, 512), (pq[1][:], "pq1", 512, 1024),
                                        (pv[:], "pv", 1024, 1536), (pf[:, 0:264], "pf", 1536, 1800),
                                        (pp[:, 0:256], "pp", 1800, 2056)):
                for kc in range(8):
                    MM(s, Pt, HT[:, kc, :], win[:, kc, c0:c1], [htn, f"win{kc}"], [pname], start=(kc == 0),
                       stop=(kc == 7))
                    if kc % 2 == 1:
                        yield
        if t < NCH:
            front_tr(t)
        yield
        if 1 <= t <= NCH:
            evac(t - 1)
            forget(t - 1)
        yield

    for t in range(NCH + 2):
        gens = [main_stream(t)]
        if t >= 2:
            gens.append(post(t - 2))
        while gens:
            for g in list(gens):
                try:
                    next(g)
                except StopIteration:
                    gens.remove(g)
    c.end()


def phase_b(c, NSEQ, S, G, scr):
    nc, s = c.nc, c.s
    NJ = S // 128
    NI = NJ // 4
    c.begin()
    tri = c.sb("tri", [128, 128], BF16)
    vall = [c.sb(f"vall{i}", [128, NJ, 8, 65], BF16) for i in range(2)]
    qT = [c.sb(f"qT{i}", [128, S], BF16) for i in range(2)]
    kT = [c.sb(f"kT{i}", [128, S], BF16) for i in range(2)]
    NPT = 4
    pT = [c.sb(f"pT{i}", [128, 512], BF16) for i in range(NPT)]
    ybuf = [c.sb(f"ybuf{i}", [128, NJ, 64], BF16) for i in range(2)]
    rc = [c.sb(f"rc{i}", [128, 1], F32) for i in range(4)]
    NPS = 4
    pS = [c.ps(f"pS{i}") for i in range(NPS)]
    pO = [c.ps(f"pO{i}") for i in range(4)]
    make_tri(c, tri, "tri")
    LOOK = 3
    blocks = []
    for q_ in range(NSEQ):
        for h in range(8):
            for i4 in range(NI):
                for j in range(4 * i4 + 4):
                    blocks.append((q_, h, i4, j))

    def stage1(bi):
        q_, h, i4, j = blocks[bi]
        g = q_ * 8 + h
        QT, KT = qT[g % 2], kT[g % 2]
        if h == 0 and i4 == 0 and j == 0:
            s.dma("sp", vall[q_ % 2][:], scr["v"][q_].rearrange("(j p) h d -> p j h d", p=128), writes=[f"vall{q_ % 2}"])
        if i4 == 0 and j == 0:
            s.dma("sp", QT[0:70, :], scr["qT"][q_, h], writes=[f"qT{g % 2}"])
            s.dma("act", KT[0:70, :], scr["kT"][q_, h], writes=[f"kT{g % 2}"])
        jj = j - 4 * i4
        c0 = 128 * max(jj, 0)
        PS, psn = pS[bi % NPS], f"pS{bi % NPS}"
        PT, ptn = pT[bi % NPT], f"pT{bi % NPT}"
        MM(s, PS[:, c0:512], KT[0:70, j * 128:(j + 1) * 128], QT[0:70, i4 * 512 + c0:(i4 + 1) * 512],
           [f"qT{g % 2}", f"kT{g % 2}"], [psn])
        ACTF(s, PT[:, c0:512], PS[:, c0:512], AF.Exp, [psn], [ptn])
        if jj >= 0:
            TT_(s, "pool", PT[:, c0:c0 + 128], PT[:, c0:c0 + 128], tri[:], ALU.mult, [ptn, "tri"], [ptn])

    def stage2(bi):
        q_, h, i4, j = blocks[bi]
        g = q_ * 8 + h
        PT, ptn = pT[bi % NPT], f"pT{bi % NPT}"
        YB = ybuf[g % 2]
        jj = j - 4 * i4
        for tt in range(max(jj, 0), 4):
            last = (j == 4 * i4 + tt)
            MM(s, pO[tt][:, 0:65], PT[:, tt * 128:(tt + 1) * 128], vall[q_ % 2][:, j, h, :], [ptn, f"vall{q_ % 2}"],
               [f"pO{tt}"], start=(j == 0), stop=last)
            if last:
                s.op("dve", lambda e, tt=tt: e.reciprocal(out=rc[tt][:], in_=pO[tt][:, 64:65]), reads=[f"pO{tt}"],
                     writes=[f"rc{tt}"])
                TS(s, "dve", YB[:, 4 * i4 + tt, :], pO[tt][:, 0:64], rc[tt][:, 0:1], ALU.mult, [f"pO{tt}", f"rc{tt}"],
                   [f"ybuf{g % 2}"])
        if i4 == NI - 1 and j == NJ - 1:
            s.dma("pool", scr["yattn"][q_, :, h * 64:(h + 1) * 64].rearrange("(j p) c -> p j c", p=128), YB[:],
                  reads=[f"ybuf{g % 2}"])

    nb = len(blocks)
    for t in range(nb + LOOK):
        if t < nb:
            stage1(t)
        if t >= LOOK:
            stage2(t - LOOK)
    c.end()


def make_scratch(nc, NSEQ, S, debug_out=False):
    kind = {"kind": "ExternalOutput"} if debug_out else {}
    scr = {}
    scr["qT"] = nc.dram_tensor("scr_qT", [NSEQ, 8, 70, S], BF16).ap()
    scr["kT"] = nc.dram_tensor("scr_kT", [NSEQ, 8, 70, S], BF16).ap()
    scr["v"] = nc.dram_tensor("scr_v", [NSEQ, S, 8, 65], BF16).ap()
    scr["cend"] = nc.dram_tensor("scr_cend", [1, NSEQ, S // 128, 8], F32).ap()
    scr["u"] = nc.dram_tensor("scr_u", [NSEQ, S, 512], F32).ap()
    scr["yattn"] = nc.dram_tensor("scr_yattn", [NSEQ, S, 512], BF16, **kind).ap()
    scr["yssmT"] = nc.dram_tensor("scr_yssmT", [NSEQ, 256, S], BF16, **kind).ap()
    scr["ypoolT"] = nc.dram_tensor("scr_ypoolT", [NSEQ, 256, S], BF16, **kind).ap()
    return scr


PARAM_SHAPES = {
    "mix_norm": [DEPTH, D], "w_in": [DEPTH, D, INC], "b_forget": [DEPTH, NH], "q_norm": [DEPTH, DH],
    "k_norm": [DEPTH, DH], "ssm_a_re": [DEPTH, 16, 64], "ssm_a_im": [DEPTH, 16, 64], "ssm_log_dt": [DEPTH, 16],
    "ssm_b_re": [DEPTH, 16, 64, 16], "ssm_b_im": [DEPTH, 16, 64, 16], "ssm_c_re": [DEPTH, 16, 16, 64],
    "ssm_c_im": [DEPTH, 16, 16, 64], "ssm_d": [DEPTH, 256], "w_glu": [DEPTH, 256, 512],
    "pool_w": [DEPTH, 4, 64, 64], "pool_scale": [DEPTH, 256], "w_br_attn": [DEPTH, 512, D],
    "w_br_ssm": [DEPTH, 256, D], "w_br_pool": [DEPTH, 256, D], "b_gate": [DEPTH, 3 * D],
    "w_out": [DEPTH, D, D], "mlp_norm": [DEPTH, D], "w_up": [DEPTH, D, DFF], "w_down": [DEPTH, DFF, D],
}


def declare_params(nc):
    return {k: nc.dram_tensor(k, shp, F32, kind="ExternalInput").ap() for k, shp in PARAM_SHAPES.items()}


def build_test_ab(NSEQ, S, l=0):
    nc = bass.Bass("TRN2", target_bir_lowering=False)
    x = nc.dram_tensor("x", [NSEQ * S, D], F32, kind="ExternalInput").ap()
    P = declare_params(nc)
    scr = make_scratch(nc, NSEQ, S, debug_out=True)
    with contextlib.ExitStack() as es:
        c = Ctx(nc, es)
        G = {"cumall": es.enter_context(nc.sbuf_tensor("cumall", [128, NSEQ, S // 128, 8], F32))}
        phase_a(c, l, NSEQ, S, x, P, G, scr, do_ssm=False, do_pool=False)
        phase_b(c, NSEQ, S, G, scr)
        print("ops", c.s.total, "waits", c.s.nwaits)
    return nc


def TT_(s, eng, out, in0, in1, op, reads, writes):
    return s.op(eng, lambda e: e.tensor_tensor(out=out, in0=in0, in1=in1, op=op), reads, writes)


def TS(s, eng, out, in0, scalar1, op0, reads, writes, scalar2=None, op1=None):
    if op1 is None:
        return s.op(eng, lambda e: e.tensor_scalar(out=out, in0=in0, scalar1=scalar1, scalar2=None, op0=op0),
                    reads, writes)
    return s.op(eng, lambda e: e.tensor_scalar(out=out, in0=in0, scalar1=scalar1, scalar2=scalar2, op0=op0, op1=op1),
                reads, writes)


def ACTF(s, out, in_, func, reads, writes, scale=1.0, bias=None):
    if bias is None:
        return s.op("act", lambda e: e.activation(out=out, in_=in_, func=func, scale=scale), reads, writes)
    return s.op("act", lambda e: e.activation(out=out, in_=in_, func=func, scale=scale, bias=bias), reads, writes)


def CP(s, eng, out, in_, reads, writes):
    if eng == "act":
        return s.op("act", lambda e: e.copy(out=out, in_=in_), reads, writes)
    return s.op(eng, lambda e: e.tensor_copy(out=out, in_=in_), reads, writes)


def MM(s, out, lhsT, rhs, reads, writes, start=True, stop=True):
    return s.op("pe", lambda e: e.matmul(out, lhsT=lhsT, rhs=rhs, start=start, stop=stop), reads, writes)


def TR(s, out, in_, ident, reads, writes):
    return s.op("pe", lambda e: e.transpose(out=out, in_=in_, identity=ident), reads, writes)


TWO_PI = 2.0 * math.pi
MAGIC = 12582912.0


def sincos(s, eng, ang, o_sin, o_cos, t1, t2, rd, tag):
    n1, n2 = "sc1" + tag, "sc2" + tag
    for which, o in (("s", o_sin), ("c", o_cos)):
        if which == "s":
            TS(s, eng, t1, ang, 1.0 / TWO_PI, ALU.mult, rd, [n1])
        else:
            TS(s, eng, t1, ang, 1.0 / TWO_PI, ALU.mult, rd, [n1], scalar2=0.25, op1=ALU.add)
        TS(s, eng, t2, t1, MAGIC, ALU.add, [n1], [n2])
        TS(s, eng, t2, t2, MAGIC, ALU.subtract, [n2], [n2])
        TT_(s, eng, t2, t1, t2, ALU.subtract, [n1, n2], [n2])
        ACTF(s, o, t2, AF.Sin, [n2], [("sin" if which == "s" else "cos") + tag], scale=TWO_PI * (1 - 1e-6))


POOL_WINDOWS = (2, 4, 8, 16)


def phase_a2(c, l, NSEQ, S, P, scr):
    nc, s = c.nc, c.s
    NJ = S // 128
    c.begin()
    I32 = mybir.dt.int32
    ident = c.sb("ident", [128, 128], BF16)
    identf = c.sb("identf", [128, 128], F32)
    tribf = c.sb("tribf", [128, 128], BF16)
    iot_i = c.sb("iot_i", [128, 128], I32)
    iot = c.sb("iot", [128, 128], F32)
    pcol_i = c.sb("pcol_i", [128, 1], I32)
    pcol = c.sb("pcol", [128, 1], F32)
    npcol = c.sb("npcol", [128, 1], F32)
    are = c.sb("are", [128, 1024], F32)
    aim = c.sb("aim", [128, 1024], F32)
    ldt = c.sb("ldt", [128, 16], F32)
    dtr = c.sb("dtr", [128, 1024], F32)
    t1 = c.sb("t1", [128, 1024], F32)
    t2 = c.sb("t2", [128, 1024], F32)
    t3 = c.sb("t3", [128, 1024], F32)
    t4 = c.sb("t4", [128, 1024], F32)
    t5 = c.sb("t5", [128, 1024], F32)
    t6 = c.sb("t6", [128, 1024], F32)
    cfr = c.sb("cfr", [128, 1024], F32)
    cfi = c.sb("cfi", [128, 1024], F32)
    Tr = c.sb("Tr", [128, 1024], F32)
    Ti = c.sb("Ti", [128, 1024], F32)
    TAr = c.sb("TAr", [128, 8, 128], F32)
    TAi = c.sb("TAi", [128, 8, 128], F32)
    acol = c.sb("acol", [128, 3, 8], F32)
    BX = c.sb("BX", [128, 8, 2, 128], F32)
    BXb = c.sb("BXb", [128, 8, 2, 128], BF16)
    BT = c.sb("BT", [128, 8, 2, 128], BF16)
    CN = c.sb("CN", [32, 8, 2, 128], F32)
    CNb = c.sb("CNb", [32, 8, 2, 128], BF16)
    CX = c.sb("CX", [128, 8, 2, 32], BF16)
    drb = c.sb("drb", [128, 256], F32)
    wg_st = c.sb("wg_st", [128, 2, 512], F32)
    wglu = c.sb("wglu", [128, 2, 512], BF16)
    pw_st = c.sb("pw_st", [128, 2, 64], F32)
    PW = c.sb("PW", [128, 2, 64], BF16)
    pscol = c.sb("pscol", [128, 2], F32)
    MT = c.sb("MT", [128, 12, 128], BF16)
    mtmp = c.sb("mtmp", [128, 128], F32)
    mrat = c.sb("mrat", [128, 128], F32)
    ut = [c.sb(f"ut{i}", [128, 512], F32) for i in range(3)]
    ub = [c.sb(f"ub{i}", [128, 512], BF16) for i in range(3)]
    uT = c.sb("uT", [128, 2, 128], BF16)
    m1 = c.sb("m1", [128, 4, 128], F32)
    m2 = c.sb("m2", [128, 4, 128], F32)
    m3 = c.sb("m3", [128, 4, 128], F32)
    m4 = c.sb("m4", [128, 4, 128], F32)
    Wt = c.sb("Wt", [128, 8, 2, 128], BF16)
    Pr = c.sb("Pr", [128, 8, 128], F32)
    Pi = c.sb("Pi", [128, 8, 128], F32)
    Xt = c.sb("Xt", [128, 8, 2, 128], BF16)
    car = c.sb("car", [128, 2, 8], F32)
    sn = c.sb("sn", [128, 2, 4], F32)
    sm = c.sb("sm", [128, 4, 4], F32)
    du = c.sb("du", [128, 256], F32)
    yv = c.sb("yv", [128, 256], F32)
    y2 = c.sb("y2", [128, 256], F32)
    sg = c.sb("sg", [128, 256], F32)
    gy = c.sb("gy", [128, 256], BF16)
    gyT = c.sb("gyT", [128, 2, 128], BF16)
    sgb = c.sb("sgb", [128, 2, 128], F32)
    ysT = [c.sb(f"ysT{i}", [128, 2, 128], BF16) for i in range(2)]
    plT = c.sb("plT", [128, 2, 128], BF16)
    ypT = [c.sb(f"ypT{i}", [128, 2, 128], BF16) for i in range(2)]
    pbu = c.ps("pbu")
    ppf = c.ps("ppf", [128, 2048])
    pym = c.ps("pym")
    pglu = c.ps("pglu")
    ppl = c.ps("ppl")
    py = pym
    w1 = c.sb("w1", [128, 2, 128], F32)
    w2 = c.sb("w2", [128, 2, 128], F32)
    w3 = c.sb("w3", [128, 2, 128], F32)
    w4 = c.sb("w4", [128, 2, 128], F32)

    make_identity(c, ident)
    make_tri(c, tribf, "tribf")
    s.op("pool", lambda e: e.iota(iot_i[:], pattern=[[1, 128]], base=0, channel_multiplier=0), writes=["iot_i"])
    CP(s, "dve", iot[:], iot_i[:], ["iot_i"], ["iot"])
    s.op("pool", lambda e: e.iota(pcol_i[:], pattern=[[0, 1]], base=0, channel_multiplier=1), writes=["pcol_i"])
    CP(s, "dve", pcol[:], pcol_i[:], ["pcol_i"], ["pcol"])
    TS(s, "dve", npcol[:], pcol[:], -1.0, ALU.mult, ["pcol"], ["npcol"])
    CP(s, "dve", identf[:], ident[:], ["ident"], ["identf"])

    s.dma("sp", are[:], P["ssm_a_re"][l:l + 1].rearrange("o g n -> o (g n)").partition_broadcast(128), writes=["are"])
    s.dma("act", aim[:], P["ssm_a_im"][l:l + 1].rearrange("o g n -> o (g n)").partition_broadcast(128), writes=["aim"])
    s.dma("sp", ldt[:], P["ssm_log_dt"][l:l + 1, :].partition_broadcast(128), writes=["ldt"])
    s.dma("sp", acol[:, 0, :], P["ssm_a_re"][l].rearrange("(k gl) n -> (gl n) k", gl=2), writes=["acol"],
          allow_slow_non_contiguous=True)
    s.dma("sp", acol[:, 1, :], P["ssm_a_im"][l].rearrange("(k gl) n -> (gl n) k", gl=2), writes=["acol"],
          allow_slow_non_contiguous=True)
    for gl in range(2):
        s.dma("sp", acol[64 * gl:64 * gl + 64, 2, :],
              P["ssm_log_dt"][l:l + 1, :].rearrange("o (k gl) -> o k gl", gl=2)[:, :, gl].partition_broadcast(64),
              writes=["acol"], allow_slow_non_contiguous=True)
    s.dma("sp", drb[:], P["ssm_d"][l:l + 1, :].partition_broadcast(128), writes=["drb"])
    s.dma("act", wg_st[:], P["w_glu"][l].rearrange("(kc p) g -> p kc g", p=128), writes=["wg_st"])
    CP(s, "pool", wglu[:], wg_st[:], ["wg_st"], ["wglu"])
    for g in range(4):
        s.dma("sp", pw_st[64 * (g % 2):64 * (g % 2) + 64, g // 2, :], P["pool_w"][l, g], writes=["pw_st"])
    CP(s, "pool", PW[:], pw_st[:], ["pw_st"], ["PW"])
    s.dma("sp", pscol[:], P["pool_scale"][l].rearrange("(kc p) -> p kc", p=128), writes=["pscol"],
          allow_slow_non_contiguous=True)
    s.op("pool", lambda e: e.memset(BX[:], 0.0), writes=["BX"])
    s.op("pool", lambda e: e.memset(CN[:], 0.0), writes=["CN"])
    nd = 0
    for k in range(8):
        for gl in range(2):
            g = 2 * k + gl
            c0 = 32 * (k % 4) + 16 * gl
            for part, nm in ((0, "ssm_b_re"), (1, "ssm_b_im")):
                s.dma("sp" if nd % 2 == 0 else "act", BX[64 * gl:64 * gl + 64, k, part, c0:c0 + 16], P[nm][l, g],
                      reads=[], writes=["BX"])
                nd += 1
            for part, nm in ((0, "ssm_c_re"), (1, "ssm_c_im")):
                s.dma("sp" if nd % 2 == 0 else "act", CN[16 * gl:16 * gl + 16, k, part, 64 * gl:64 * gl + 64],
                      P[nm][l, g], reads=[], writes=["CN"])
                nd += 1
    CP(s, "dve", BXb[:], BX[:], ["BX"], ["BXb"])
    CP(s, "dve", CNb[:], CN[:], ["CN"], ["CNb"])
    pmv = ppf[:, 0:512].bitcast(BF16)
    for k in range(8):
        for part in range(2):
            i = (k * 2 + part) % 8
            TR(s, pmv[:, i * 128:(i + 1) * 128], BXb[:, k, part, :], ident[:], ["BXb", "ident"], ["ppf"])
        if k % 4 == 3:
            k0 = k - 3
            CP(s, "dve", BT[:, k0:k0 + 4, :, :], pmv.rearrange("p (k a m) -> p k a m", k=4, a=2), ["ppf"], ["BT"])
    for k in range(8):
        for part in range(2):
            i = k * 2 + part
            TR(s, pmv[:, i * 32:(i + 1) * 32], CNb[:, k, part, :], ident[0:32, 0:32], ["CNb", "ident"], ["ppf"])
    cxv = pmv[:, 0:512].rearrange("p (k a m) -> p k a m", k=8, a=2)
    CP(s, "dve", CX[:, :, 0, :], cxv[:, :, 0, :], ["ppf"], ["CX"])
    TS(s, "dve", CX[:, :, 1, :], cxv[:, :, 1, :], -1.0, ALU.mult, ["ppf"], ["CX"])

    ACTF(s, ldt[:], ldt[:], AF.Exp, ["ldt"], ["ldt"])
    CP(s, "dve", dtr[:].rearrange("p (g n) -> p g n", g=16), ldt[:].unsqueeze(2).broadcast_to([128, 16, 64]),
       ["ldt"], ["dtr"])
    ardt, wr = t5, t6
    TT_(s, "dve", ardt[:], are[:], dtr[:], ALU.mult, ["are", "dtr"], ["ardt"])
    TT_(s, "dve", wr[:], aim[:], dtr[:], ALU.mult, ["aim", "dtr"], ["wr"])
    sincos(s, "dve", wr[:], t3[:], t4[:], t1[:], t2[:], ["wr"], "0")
    ACTF(s, t1[:], ardt[:], AF.Exp, ["ardt", "sc10"], ["mag1"])
    TT_(s, "dve", t4[:], t4[:], t1[:], ALU.mult, ["cos0", "mag1"], ["cos0"])
    TT_(s, "dve", t3[:], t3[:], t1[:], ALU.mult, ["sin0", "mag1"], ["sin0"])
    TS(s, "dve", t4[:], t4[:], -1.0, ALU.add, ["cos0"], ["cos0"])
    TT_(s, "dve", t1[:], are[:], are[:], ALU.mult, ["are", "mag1", "sin0"], ["mag1"])
    TT_(s, "dve", t2[:], aim[:], aim[:], ALU.mult, ["aim", "sc20"], ["sc20"])
    TT_(s, "dve", t1[:], t1[:], t2[:], ALU.add, ["mag1", "sc20"], ["mag1"])
    s.op("dve", lambda e: e.reciprocal(out=t1[:], in_=t1[:]), reads=["mag1"], writes=["mag1"])
    TT_(s, "dve", cfr[:], t4[:], are[:], ALU.mult, ["cos0", "are"], ["cfr"])
    TT_(s, "dve", t2[:], t3[:], aim[:], ALU.mult, ["sin0", "aim", "sc20"], ["sc20"])
    TT_(s, "dve", cfr[:], cfr[:], t2[:], ALU.add, ["cfr", "sc20"], ["cfr"])
    TT_(s, "dve", cfr[:], cfr[:], t1[:], ALU.mult, ["cfr", "mag1"], ["cfr"])
    TT_(s, "dve", cfi[:], t3[:], are[:], ALU.mult, ["sin0", "are"], ["cfi"])
    TT_(s, "dve", t2[:], t4[:], aim[:], ALU.mult, ["cos0", "aim", "sc20", "cfr"], ["sc20"])
    TT_(s, "dve", cfi[:], cfi[:], t2[:], ALU.subtract, ["cfi", "sc20"], ["cfi"])
    TT_(s, "dve", cfi[:], cfi[:], t1[:], ALU.mult, ["cfi", "mag1"], ["cfi"])
    TS(s, "dve", dtr[:], wr[:], pcol[:, 0:1], ALU.mult, ["wr", "pcol", "dtr"], ["ang"])
    sincos(s, "dve", dtr[:], t3[:], t4[:], t1[:], t2[:], ["ang", "cfi", "cfr"], "1")
    s.op("act", lambda e: e.activation(out=t1[:], in_=ardt[:], func=AF.Exp, scale=npcol[:, 0:1]),
         reads=["ardt", "npcol", "sc11", "cos1"], writes=["mag2"])
    TT_(s, "dve", t4[:], t4[:], t1[:], ALU.mult, ["cos1", "mag2"], ["cos1"])
    TT_(s, "dve", t3[:], t3[:], t1[:], ALU.mult, ["sin1", "mag2"], ["sin1"])
    TT_(s, "dve", Tr[:], t4[:], cfr[:], ALU.mult, ["cos1", "cfr"], ["Tr"])
    TT_(s, "dve", t2[:], t3[:], cfi[:], ALU.mult, ["sin1", "cfi", "sc21"], ["sc21"])
    TT_(s, "dve", Tr[:], Tr[:], t2[:], ALU.add, ["Tr", "sc21"], ["Tr"])
    TT_(s, "dve", Ti[:], t4[:], cfi[:], ALU.mult, ["cos1", "cfi"], ["Ti"])
    TT_(s, "dve", t2[:], t3[:], cfr[:], ALU.mult, ["sin1", "cfr", "Tr"], ["sc21"])
    TT_(s, "dve", Ti[:], Ti[:], t2[:], ALU.subtract, ["Ti", "sc21"], ["Ti"])
    ACTF(s, acol[:, 2, :], acol[:, 2, :], AF.Exp, ["acol"], ["acol"])
    TT_(s, "dve", acol[:, 0, :], acol[:, 0, :], acol[:, 2, :], ALU.mult, ["acol"], ["acol"])
    TT_(s, "dve", acol[:, 1, :], acol[:, 1, :], acol[:, 2, :], ALU.mult, ["acol"], ["acol"])
    angf = t5[:].rearrange("p (k t) -> p k t", k=8)
    magf = t6[:].rearrange("p (k t) -> p k t", k=8)
    for k in range(8):
        TS(s, "dve", angf[:, k, :], iot[:], acol[:, 1, k:k + 1], ALU.mult, ["iot", "acol", "ardt", "Tr", "Ti"], ["angf"])
        s.op("act", lambda e, k=k: e.activation(out=magf[:, k, :], in_=iot[:], func=AF.Exp, scale=acol[:, 0, k:k + 1]),
             reads=["iot", "acol", "wr", "ang", "Tr", "Ti"], writes=["magf"])
    sincos(s, "dve", t5[:], t3[:], t4[:], t1[:], t2[:], ["angf", "Tr", "Ti"], "2")
    TT_(s, "dve", TAr[:].rearrange("p k t -> p (k t)"), t4[:], t6[:], ALU.mult, ["cos2", "magf"], ["TAr"])
    TT_(s, "dve", TAi[:].rearrange("p k t -> p (k t)"), t3[:], t6[:], ALU.mult, ["sin2", "magf"], ["TAi"])

    for g, w in enumerate(POOL_WINDOWS):
        s.op("pool", lambda e, w=w: e.memset(mtmp[:], 1.0 / w), reads=["mtmp", "mrat", "MT"], writes=["mtmp"])
        s.op("pool", lambda e: e.affine_select(out=mtmp[:], in_=mtmp[:], pattern=[[1, 128]], compare_op=ALU.is_ge,
                                               fill=0.0, base=0, channel_multiplier=-1),
             reads=["mtmp"], writes=["mtmp"])
        s.op("pool", lambda e, w=w: e.affine_select(out=mtmp[:], in_=mtmp[:], pattern=[[-1, 128]],
                                                    compare_op=ALU.is_ge, fill=0.0, base=w - 1, channel_multiplier=1),
             reads=["mtmp"], writes=["mtmp"])
        TT_(s, "pool", MT[:, g * 3 + 0, :], mtmp[:], identf[:], ALU.subtract, ["mtmp", "identf"], ["MT"])
        TS(s, "dve", mrat[:], iot[:], 1.0, ALU.add, ["iot"], ["mrat"])
        s.op("dve", lambda e: e.reciprocal(out=mrat[:], in_=mrat[:]), reads=["mrat"], writes=["mrat"])
        TS(s, "dve", mrat[:], mrat[:], float(w), ALU.mult, ["mrat"], ["mrat"], scalar2=1.0, op1=ALU.max)
        TT_(s, "dve", mrat[:], mrat[:], mtmp[:], ALU.mult, ["mrat", "mtmp"], ["mrat"])
        TT_(s, "dve", MT[:, g * 3 + 2, :], mrat[:], identf[:], ALU.subtract, ["mrat", "identf"], ["MT"])
        s.op("pool", lambda e, w=w: e.memset(mtmp[:], 1.0 / w), reads=["mtmp", "mrat", "MT"], writes=["mtmp"])
        s.op("pool", lambda e, w=w: e.affine_select(out=mtmp[:], in_=mtmp[:], pattern=[[-1, 128]],
                                                    compare_op=ALU.is_ge, fill=0.0, base=-(129 - w),
                                                    channel_multiplier=1),
             reads=["mtmp"], writes=["mtmp"])
        CP(s, "pool", MT[:, g * 3 + 1, :], mtmp[:], ["mtmp"], ["MT"])

    NCH = NSEQ * NJ
    pmq = pym[:, 256:512].bitcast(BF16)

    def load(n):
        q_, j = divmod(n, NJ)
        s.dma("sp", ut[n % 3][:], scr["u"][q_, j * 128:(j + 1) * 128, :], writes=[f"ut{n % 3}"])

    def S1(n):
        U, UB = ut[n % 3], ub[n % 3]
        un, ubn = f"ut{n % 3}", f"ub{n % 3}"
        CP(s, "act", UB[:], U[:], [un], [ubn])
        for kc in range(2):
            TR(s, pmq[:, kc * 128:(kc + 1) * 128], UB[:, kc * 128:(kc + 1) * 128], ident[:], [ubn, "ident"], ["pym"])
        CP(s, "dve", uT[:], pmq[:, 0:256].rearrange("p (k t) -> p k t", k=2), ["pym"], ["uT"])
        for qt in range(4):
            k0 = qt * 2
            for kk in range(2):
                k = k0 + kk
                MM(s, pbu[:, kk * 256:(kk + 1) * 256], uT[:, k // 4, :], BT[:, k, :, :].rearrange("p a m -> p (a m)"),
                   ["uT", "BT"], ["pbu"])
            buv = pbu[:].rearrange("p (k a m) -> p k a m", k=2, a=2)
            trv = Tr[:, k0 * 128:(k0 + 2) * 128].rearrange("p (k m) -> p k m", k=2)
            tiv = Ti[:, k0 * 128:(k0 + 2) * 128].rearrange("p (k m) -> p k m", k=2)
            yield
            TT_(s, "dve", w1[:], buv[:, :, 0, :], trv, ALU.mult, ["pbu", "Tr"], ["w1"])
            TT_(s, "dve", w2[:], buv[:, :, 1, :], tiv, ALU.mult, ["pbu", "Ti"], ["w2"])
            yield
            TT_(s, "pool", Wt[:, k0:k0 + 2, 0, :], w1[:], w2[:], ALU.subtract, ["w1", "w2"], [f"Wt{qt}"])
            TT_(s, "dve", w3[:], buv[:, :, 1, :], trv, ALU.mult, ["pbu", "Tr"], ["w3"])
            TT_(s, "dve", w4[:], buv[:, :, 0, :], tiv, ALU.mult, ["pbu", "Ti"], ["w4"])
            yield
            TT_(s, "pool", Wt[:, k0:k0 + 2, 1, :], w3[:], w4[:], ALU.add, ["w3", "w4"], [f"Wt{qt}"])
            yield
            for kk in range(2):
                for part in range(2):
                    i = (k0 + kk) * 2 + part
                    MM(s, ppf[:, i * 128:(i + 1) * 128], Wt[:, k0 + kk, part, :], tribf[:], [f"Wt{qt}", "tribf"],
                       [f"ppf{qt // 2}"])

    def XP(n):
        q_, j = divmod(n, NJ)
        if j == 0:
            s.op("pool", lambda e: e.memset(car[:], 0.0), writes=["car"])
        for hf in range(2):
            k0 = hf * 4
            pfv = ppf[:, hf * 1024:(hf + 1) * 1024].rearrange("p (k a t) -> p k a t", k=4, a=2)
            pfn = f"ppf{hf}"
            TT_(s, "dve", Pr[:, k0:k0 + 4, :], pfv[:, :, 0, :],
                car[:, 0, k0:k0 + 4].unsqueeze(2).broadcast_to([128, 4, 128]), ALU.add, [pfn, "car"], [f"Pr{hf}"])
            TT_(s, "dve", Pi[:, k0:k0 + 4, :], pfv[:, :, 1, :],
                car[:, 1, k0:k0 + 4].unsqueeze(2).broadcast_to([128, 4, 128]), ALU.add, [pfn, "car"], [f"Pi{hf}"])
        yield
        for hf in range(2):
            k0 = hf * 4
            prn, pin = f"Pr{hf}", f"Pi{hf}"
            PR, PI = Pr[:, k0:k0 + 4, :], Pi[:, k0:k0 + 4, :]
            tar, tai = TAr[:, k0:k0 + 4, :], TAi[:, k0:k0 + 4, :]
            TT_(s, "dve", m1[:], tar, PR, ALU.mult, ["TAr", prn], ["m1"])
            TT_(s, "pool", m2[:], tai, PI, ALU.mult, ["TAi", pin], ["m2"])
            yield
            TT_(s, "dve", m3[:], tar, PI, ALU.mult, ["TAr", pin], ["m3"])
            TT_(s, "pool", m4[:], tai, PR, ALU.mult, ["TAi", prn], ["m4"])
            yield
            TT_(s, "pool", Xt[:, k0:k0 + 4, 0, :], m1[:], m2[:], ALU.subtract, ["m1", "m2"], [f"Xt{hf}"])
            TT_(s, "dve", sn[:, 0, :], m1[:, :, 127], m2[:, :, 127], ALU.subtract, ["m1", "m2"], ["sn"])
            yield
            TT_(s, "dve", Xt[:, k0:k0 + 4, 1, :], m3[:], m4[:], ALU.add, ["m3", "m4"], [f"Xt{hf}"])
            TT_(s, "dve", sn[:, 1, :], m3[:, :, 127], m4[:, :, 127], ALU.add, ["m3", "m4"], ["sn"])
            yield
            a1r, a1i = TAr[:, k0:k0 + 4, 1], TAi[:, k0:k0 + 4, 1]
            TT_(s, "dve", sm[:, 0, :], a1r, sn[:, 0, :], ALU.mult, ["TAr", "sn"], ["sm"])
            TT_(s, "dve", sm[:, 1, :], a1i, sn[:, 1, :], ALU.mult, ["TAi", "sn"], ["sm"])
            TT_(s, "dve", sm[:, 2, :], a1r, sn[:, 1, :], ALU.mult, ["TAr", "sn"], ["sm"])
            TT_(s, "dve", sm[:, 3, :], a1i, sn[:, 0, :], ALU.mult, ["TAi", "sn"], ["sm"])
            yield
            TT_(s, "dve", car[:, 0, k0:k0 + 4], sm[:, 0, :], sm[:, 1, :], ALU.subtract, ["sm"], ["car"])
            TT_(s, "dve", car[:, 1, k0:k0 + 4], sm[:, 2, :], sm[:, 3, :], ALU.add, ["sm"], ["car"])
            yield

    def REST(n):
        q_, j = divmod(n, NJ)
        U, UB = ut[n % 3], ub[n % 3]
        un, ubn = f"ut{n % 3}", f"ub{n % 3}"
        UBP, ubpn = ub[(n - 1) % 3], f"ub{(n - 1) % 3}"
        for k in range(8):
            for part in range(2):
                MM(s, py[:, 32 * k:32 * k + 32], Xt[:, k, part, :], CX[:, k, part, :], [f"Xt{k // 4}", "CX"], ["pym"],
                   start=(part == 0), stop=(part == 1))
        yield
        TT_(s, "pool", du[:], U[:, 0:256], drb[:], ALU.mult, [un, "drb"], ["du"])
        yield
        TT_(s, "dve", yv[:], py[:, 0:256], du[:], ALU.add, ["pym", "du"], ["yv"])
        yield
        TT_(s, "pool", y2[:], yv[:], yv[:], ALU.mult, ["yv"], ["y2"])
        yield
        TS(s, "pool", y2[:], y2[:], 0.044715, ALU.mult, ["y2"], ["y2"], scalar2=1.0, op1=ALU.add)
        yield
        TT_(s, "pool", y2[:], y2[:], yv[:], ALU.mult, ["y2", "yv"], ["y2"])
        ACTF(s, sg[:], y2[:], AF.Sigmoid, ["y2"], ["sg"], scale=1.5957691216057308)
        yield
        TT_(s, "dve", gy[:], yv[:], sg[:], ALU.mult, ["yv", "sg"], ["gy"])
        yield
        for kc in range(2):
            TR(s, pmq[:, 256 + kc * 128:256 + (kc + 1) * 128], gy[:, kc * 128:(kc + 1) * 128], ident[:],
               ["gy", "ident"], ["pym"])
        CP(s, "act", gyT[:], pmq[:, 256:512].rearrange("p (k t) -> p k t", k=2), ["pym"], ["gyT"])
        for gc in range(4):
            for kc in range(2):
                MM(s, pglu[:, gc * 128:(gc + 1) * 128], wglu[:, kc, gc * 128:(gc + 1) * 128], gyT[:, kc, :],
                   ["wglu", "gyT"], ["pglu"], start=(kc == 0), stop=(kc == 1))
        yield
        ACTF(s, sgb[:], pglu[:, 256:512].rearrange("p (k t) -> p k t", k=2), AF.Sigmoid, ["pglu"], ["sgb"])
        yield
        YS = ysT[n % 2]
        TT_(s, "dve", YS[:], pglu[:, 0:256].rearrange("p (k t) -> p k t", k=2), sgb[:], ALU.mult, ["pglu", "sgb"],
            [f"ysT{n % 2}"])
        s.dma("pool", scr["yssmT"][q_, :, j * 128:(j + 1) * 128].rearrange("(kc p) t -> p kc t", p=128), YS[:],
              reads=[f"ysT{n % 2}"])
        yield
        for g in range(4):
            o = ppl[64 * (g % 2):64 * (g % 2) + 64, (g // 2) * 128:(g // 2 + 1) * 128]
            if j == 0:
                MM(s, o, UB[:, 256 + 64 * g:256 + 64 * g + 64], MT[:, g * 3 + 2, :], [ubn, "MT"], ["ppl"])
            else:
                MM(s, o, UB[:, 256 + 64 * g:256 + 64 * g + 64], MT[:, g * 3 + 0, :], [ubn, "MT"], ["ppl"],
                   start=True, stop=False)
                MM(s, o, UBP[:, 256 + 64 * g:256 + 64 * g + 64], MT[:, g * 3 + 1, :], [ubpn, "MT"], ["ppl"],
                   start=False, stop=True)
        CP(s, "act", plT[:], ppl[:, 0:256].rearrange("p (k t) -> p k t", k=2), ["ppl"], ["plT"])
        for g in range(4):
            pb = 64 * (g % 2)
            MM(s, ppl[pb:pb + 64, 256 + (g // 2) * 128:256 + (g // 2 + 1) * 128], PW[pb:pb + 64, g // 2, :],
               plT[pb:pb + 64, g // 2, :], ["PW", "plT"], ["ppl"])
        yield
        YP = ypT[n % 2]
        for kc in range(2):
            TS(s, "dve", YP[:, kc, :], ppl[:, 256 + kc * 128:256 + (kc + 1) * 128], pscol[:, kc:kc + 1], ALU.mult,
               ["ppl", "pscol"], [f"ypT{n % 2}"])
        s.dma("pool", scr["ypoolT"][q_, :, j * 128:(j + 1) * 128].rearrange("(kc p) t -> p kc t", p=128), YP[:],
              reads=[f"ypT{n % 2}"])

    load(0)
    if NCH > 1:
        load(1)
    def chain(*gs):
        for g in gs:
            yield from g

    for t in range(-1, NCH):
        if 0 <= t + 2 < NCH and t + 2 >= 2:
            load(t + 2)
        gens = []
        if t >= 0:
            gens.append(chain(XP(t), REST(t)))
        if t + 1 < NCH:
            gens.append(S1(t + 1))
        while gens:
            for g in list(gens):
                try:
                    next(g)
                except StopIteration:
                    gens.remove(g)
    c.end()


def build_test_a2(NSEQ, S, l=0):
    nc = bass.Bass("TRN2", target_bir_lowering=False)
    u = nc.dram_tensor("u_in", [NSEQ, S, 512], F32, kind="ExternalInput").ap()
    P = declare_params(nc)
    scr = make_scratch(nc, NSEQ, S, debug_out=True)
    scr["u"] = u
    with contextlib.ExitStack() as es:
        c = Ctx(nc, es)
        phase_a2(c, l, NSEQ, S, P, scr)
        print("ops", c.s.total, "waits", c.s.nwaits)
    return nc


def phase_c(c, l, NSEQ, S, x_d, xout_d, P, scr):
    nc, s = c.nc, c.s
    NJ = S // 128
    c.begin()
    wg = c.sb("wg", [128, 8, 3 * D], BF16)
    wba = c.sb("wba", [128, 4, D], BF16)
    wbs = c.sb("wbs", [128, 2, D], BF16)
    wbp = c.sb("wbp", [128, 2, D], BF16)
    wout = c.sb("wout", [128, 8, D], BF16)
    bg = c.sb("bg", [128, 3 * D], F32)
    NSTG = 4
    stg = [c.sb(f"stg{i}", [128, 1536], F32) for i in range(NSTG)]
    gcol = c.sb("gcol", [128, 8], F32)
    ident = c.sb("ident", [128, 128], BF16)
    xt = [c.sb(f"xt{i}", [128, D], F32) for i in range(5)]
    junk = c.sb("junk", [128, D], F32)
    ss = c.sb("ss", [128, 1], F32)
    rstd = c.sb("rstd", [128, 1], F32)
    hb = c.sb("hb", [128, D], BF16)
    hT = [c.sb(f"hT{i}", [128, 8, 128], BF16) for i in range(2)]
    gates_b = [c.sb(f"gates{i}", [128, 3 * D], F32) for i in range(2)]
    ya = [c.sb(f"ya{i}", [128, 512], BF16) for i in range(2)]
    yaT = c.sb("yaT", [128, 4, 128], BF16)
    ysT = [c.sb(f"ysT{i}", [128, 2, 128], BF16) for i in range(2)]
    ypT = [c.sb(f"ypT{i}", [128, 2, 128], BF16) for i in range(2)]
    macc = [c.sb(f"macc{i}", [128, 512], F32) for i in range(2)]
    mtmp = [c.sb(f"mtmp{i}", [128, 512], F32) for i in range(2)]
    mrg = [c.sb(f"mrg{i}", [128, D], BF16) for i in range(2)]
    mT = c.sb("mT", [128, 8, 128], BF16)
    pt = c.ps("pt")
    pg = [c.ps(f"pg{i}") for i in range(2)]
    pbr = [c.ps(f"pbr{i}") for i in range(3)]
    po = [c.ps(f"po{i}") for i in range(2)]

    make_identity(c, ident)
    s.dma("sp", gcol[:], P["mix_norm"][l].rearrange("(kc p) -> p kc", p=128), writes=["gcol"],
          allow_slow_non_contiguous=True)
    s.dma("act", bg[:], P["b_gate"][l:l + 1, :].partition_broadcast(128), writes=["bg"])
    n = 0
    dq = ("sp", "act", "pool")
    for kc in range(8):
        for hf in range(2):
            st, sn_ = stg[n % NSTG], f"stg{n % NSTG}"
            s.dma(dq[n % 3], st[:], P["w_in"][l, kc * 128:(kc + 1) * 128, 2056 + hf * 1536:2056 + (hf + 1) * 1536],
                  writes=[sn_])
            _cast(s, ("dve", "act")[n % 2], wg[:, kc, hf * 1536:(hf + 1) * 1536], st[:], [sn_, "gcol"],
                  [f"wg{kc}" if hf == 0 else f"wg{kc}b"], scalar=gcol[:, kc:kc + 1])
            n += 1
    for (wt, nm, nk) in ((wba, "w_br_attn", 4), (wbs, "w_br_ssm", 2), (wbp, "w_br_pool", 2), (wout, "w_out", 8)):
        for k0 in range(nk):
            st, sn_ = stg[n % NSTG], f"stg{n % NSTG}"
            s.dma(dq[n % 3], st[:, 0:D], P[nm][l, k0 * 128:(k0 + 1) * 128, :], writes=[sn_])
            _cast(s, ("dve", "act")[n % 2], wt[:, k0, :], st[:, 0:D], [sn_], [nm])
            n += 1

    NCH = NSEQ * NJ
    ptv = pt[:].bitcast(BF16)

    def load(n):
        q_, j = divmod(n, NJ)
        b = n % 2
        s.dma("sp", xt[n % 5][:], x_d[n * 128:(n + 1) * 128, :], writes=[f"xt{n % 5}"])

    def load_y(n):
        q_, j = divmod(n, NJ)
        b = n % 2
        s.dma("sp", ya[b][:], scr["yattn"][q_, j * 128:(j + 1) * 128, :], writes=[f"ya{b}"])
        s.dma("sp", ysT[b][:], scr["yssmT"][q_, :, j * 128:(j + 1) * 128].rearrange("(kc p) t -> p kc t", p=128),
              writes=[f"ysT{b}"])
        s.dma("sp", ypT[b][:], scr["ypoolT"][q_, :, j * 128:(j + 1) * 128].rearrange("(kc p) t -> p kc t", p=128),
              writes=[f"ypT{b}"])

    def s1(n):
        X, xb = xt[n % 5], f"xt{n % 5}"
        HT, htn = hT[n % 2], f"hT{n % 2}"
        rms_rstd(c, X[:], xb, junk[:], ss[:], rstd[:], D, "")
        yield
        TS(s, "dve", hb[:], X[:], rstd[:, 0:1], ALU.mult, [xb, "rstd"], ["hb"])
        yield
        for kc in range(8):
            TR(s, ptv[:, kc * 128:(kc + 1) * 128], hb[:, kc * 128:(kc + 1) * 128], ident[:], ["hb", "ident"], ["pt"])
        CP(s, "act", HT[:], ptv.rearrange("p (k t) -> p k t", k=8), ["pt"], [htn])
        yield

    def s2_gates(n):
        HT, htn = hT[n % 2], f"hT{n % 2}"
        gates = gates_b[n % 2]
        gp = f"g{n % 2}_"
        for gb in range(6):
            PG, pgn = pg[gb % 2], f"pg{gb % 2}"
            for kc in range(8):
                MM(s, PG[:], HT[:, kc, :], wg[:, kc, gb * 512:(gb + 1) * 512], [htn, f"wg{kc}" if gb < 3 else f"wg{kc}b"], [pgn],
                   start=(kc == 0), stop=(kc == 7))
                yield
            gsl = gates[:, gb * 512:(gb + 1) * 512]
            TT_(s, "dve", gsl, PG[:], bg[:, gb * 512:(gb + 1) * 512], ALU.add, [pgn, "bg"], [gp + f"gate{gb}"])
            yield
            ACTF(s, gsl, gsl, AF.Sigmoid, [gp + f"gate{gb}"], [gp + f"gate{gb}"])
            yield

    def s2_ya(n):
        b = n % 2
        for kc in range(4):
            TR(s, ptv[:, kc * 128:(kc + 1) * 128], ya[b][:, kc * 128:(kc + 1) * 128], ident[:], [f"ya{b}", "ident"],
               ["pt"])
        CP(s, "dve", yaT[:], ptv[:, 0:512].rearrange("p (k t) -> p k t", k=4), ["pt"], ["yaT"])
        yield

    def s2_branch(n, dbs):
        b = n % 2
        MR = mrg[b]
        gates = gates_b[n % 2]
        gp = f"g{n % 2}_"
        for db in dbs:
            cs = slice(db * 512, (db + 1) * 512)
            for kc in range(4):
                MM(s, pbr[0][:], yaT[:, kc, :], wba[:, kc, cs], ["yaT", "w_br_attn"], ["pbr0"], start=(kc == 0),
                   stop=(kc == 3))
                yield
            for kc in range(2):
                MM(s, pbr[1][:], ysT[b][:, kc, :], wbs[:, kc, cs], [f"ysT{b}", "w_br_ssm"], ["pbr1"], start=(kc == 0),
                   stop=(kc == 1))
                yield
            for kc in range(2):
                MM(s, pbr[2][:], ypT[b][:, kc, :], wbp[:, kc, cs], [f"ypT{b}", "w_br_pool"], ["pbr2"], start=(kc == 0),
                   stop=(kc == 1))
                yield
            MA, TM = macc[db], mtmp[db]
            TT_(s, "dve", MA[:], pbr[0][:], gates[:, db * 512:(db + 1) * 512], ALU.mult, ["pbr0", gp + f"gate{db}"],
                [f"macc{db}"])
            yield
            TT_(s, "dve", TM[:], pbr[1][:], gates[:, D + db * 512:D + (db + 1) * 512], ALU.mult,
                ["pbr1", gp + f"gate{2 + db}"], [f"mtmp{db}"])
            yield
            TT_(s, "pool", MA[:], MA[:], TM[:], ALU.add, [f"macc{db}", f"mtmp{db}"], [f"macc{db}"])
            yield
            TT_(s, "dve", TM[:], pbr[2][:], gates[:, 2 * D + db * 512:2 * D + (db + 1) * 512], ALU.mult,
                ["pbr2", gp + f"gate{4 + db}"], [f"mtmp{db}"])
            yield
            TT_(s, "pool", MR[:, cs], MA[:], TM[:], ALU.add, [f"macc{db}", f"mtmp{db}"], [f"mrg{b}_{db}"])
            yield

    def s3(n):
        b = n % 2
        MR = mrg[b]
        X, xb = xt[n % 5], f"xt{n % 5}"
        for kc in range(8):
            TR(s, ptv[:, kc * 128:(kc + 1) * 128], MR[:, kc * 128:(kc + 1) * 128], ident[:],
               [f"mrg{b}_{kc // 4}", "ident"], ["pt"])
        CP(s, "act", mT[:], ptv.rearrange("p (k t) -> p k t", k=8), ["pt"], ["mT"])
        yield
        for db in range(2):
            cs = slice(db * 512, (db + 1) * 512)
            for kc in range(8):
                MM(s, po[db][:], mT[:, kc, :], wout[:, kc, cs], ["mT", "w_out"], [f"po{db}"], start=(kc == 0),
                   stop=(kc == 7))
                yield
            TT_(s, "dve", X[:, cs], po[db][:], X[:, cs], ALU.add, [f"po{db}", xb], [xb])
            yield
        s.dma("pool", xout_d[n * 128:(n + 1) * 128, :], X[:], reads=[xb])
        yield

    def chain(*gs):
        for g in gs:
            yield from g

    def run(gens):
        gens = list(gens)
        while gens:
            for g in list(gens):
                try:
                    next(g)
                except StopIteration:
                    gens.remove(g)

    load(0)
    load_y(0)
    if NCH > 1:
        load(1)
    run([s1(0)])
    for t in range(NCH + 2):
        if t + 2 < NCH:
            load(t + 2)
        gens = []
        if t < NCH:
            gens.append(s2_gates(t))
        if t + 1 < NCH:
            gens.append(s1(t + 1))
        if 1 <= t <= NCH:
            gens.append(chain(s2_ya(t - 1), s2_branch(t - 1, [0, 1])))
        if t >= 2:
            gens.append(s3(t - 2))
        run(gens)
        if t + 1 < NCH:
            load_y(t + 1)
    c.end()


def build_full(NSEQ, S, depth=DEPTH):
    T = NSEQ * S
    nc = bass.Bass("TRN2", target_bir_lowering=False)
    x = nc.dram_tensor("x", [T, D], F32, kind="ExternalInput").ap()
    y = nc.dram_tensor("y", [T, D], F32, kind="ExternalOutput").ap()
    P = declare_params(nc)
    scr = make_scratch(nc, NSEQ, S)
    xa = nc.dram_tensor("scr_xa", [T, D], F32).ap()
    with contextlib.ExitStack() as es:
        c = Ctx(nc, es)
        G = {"cumall": es.enter_context(nc.sbuf_tensor("cumall", [128, NSEQ, S // 128, 8], F32))}
        for l in range(depth):
            xin = x if l == 0 else xa
            phase_a(c, l, NSEQ, S, xin, P, G, scr)
            phase_a2(c, l, NSEQ, S, P, scr)
            phase_b(c, NSEQ, S, G, scr)
            phase_c(c, l, NSEQ, S, xin, xa, P, scr)
            phase_mlp(c, l, T, xa, P["w_up"], P["w_down"], P["mlp_norm"], xout_d=(y if l == depth - 1 else xa))
        print("ops", c.s.total, "waits", c.s.nwaits, flush=True)
    return nc


_NC_CACHE = {}


def kernel(**inputs):
    x = np.ascontiguousarray(np.asarray(inputs["x"], dtype=np.float32))
    B, S, _ = x.shape
    NSEQ = B // NCORES
    key = (NSEQ, S)
    if key not in _NC_CACHE:
        _NC_CACHE[key] = build_full(NSEQ, S)
    nc = _NC_CACHE[key]
    params = {k: np.ascontiguousarray(np.asarray(inputs[k], dtype=np.float32)) for k in PARAM_SHAPES}
    in_maps = []
    for cid in range(NCORES):
        m = {"x": x[cid * NSEQ:(cid + 1) * NSEQ].reshape(NSEQ * S, D)}
        m.update(params)
        in_maps.append(m)
    res = run_bass_kernel_spmd(nc, in_maps, core_ids=list(range(NCORES)))
    out = np.empty((B, S, D), dtype=np.float32)
    for cid in range(NCORES):
        out[cid * NSEQ:(cid + 1) * NSEQ] = np.asarray(res.results[cid]["y"]).reshape(NSEQ, S, D)
    return out
```

```python
import contextlib
import math
import numpy as np
import concourse.bass as bass
import concourse.mybir as mybir
from concourse.bass_utils import run_bass_kernel_spmd

F32 = mybir.dt.float32
BF16 = mybir.dt.bfloat16
ALU = mybir.AluOpType
AF = mybir.ActivationFunctionType
AX = mybir.AxisListType

D = 1024
DEPTH = 2
DFF = 4096
NH = 8
DH = 64
INC = 5128
EPS = 1e-6
NCORES = 8

ENGS = ("pe", "act", "dve", "pool", "sp")
N_DMA_SEMS = 12
SAME_ENGINE_SYNC = True


class Buf:
    __slots__ = ("name", "w", "r")

    def __init__(self, name):
        self.name = name
        self.w = None
        self.r = {}


class Op:
    __slots__ = ("eng", "idx", "fn", "deps", "dma", "needs_inc", "sem", "semval")

    def __init__(self, eng, idx, fn, deps, dma):
        self.eng, self.idx, self.fn, self.deps, self.dma = eng, idx, fn, deps, dma
        self.needs_inc = False
        self.sem = None
        self.semval = None


class Sched:
    def __init__(self, nc, es, same_engine_sync=SAME_ENGINE_SYNC):
        self.nc = nc
        self.q = {e: [] for e in ENGS}
        self.ndma = {e: 0 for e in ENGS}
        self.cnt = {e: 0 for e in ENGS}
        self.same_engine_sync = same_engine_sync
        self.bufs = {}
        self.esem = {e: es.enter_context(nc.semaphore("es_" + e)) for e in ENGS if e != "sp"}
        self.dsem = {}
        for e in ("sp", "act", "pool"):
            for j in range(N_DMA_SEMS):
                self.dsem[(e, j)] = es.enter_context(nc.semaphore(f"ds_{e}_{j}"))
        self.total = {e: 0 for e in ENGS}
        self.nwaits = {e: 0 for e in ENGS}

    def buf(self, name):
        b = self.bufs.get(name)
        if b is None:
            b = self.bufs[name] = Buf(name)
        return b

    def _b(self, x):
        return x if isinstance(x, Buf) else self.buf(x)

    def op(self, eng, fn, reads=(), writes=(), dma=False):
        reads = [self._b(x) for x in reads]
        writes = [self._b(x) for x in writes]
        deps = {}
        for b in reads:
            if b.w is not None:
                deps[id(b.w)] = b.w
        for b in writes:
            if b.w is not None:
                deps[id(b.w)] = b.w
            for o in b.r.values():
                deps[id(o)] = o
        o = Op(eng, len(self.q[eng]), fn, list(deps.values()), dma)
        if dma:
            i = self.ndma[eng]
            self.ndma[eng] += 1
            o.sem = (eng, i % N_DMA_SEMS)
            o.semval = 16 * (i // N_DMA_SEMS + 1)
        self.q[eng].append(o)
        for b in reads:
            key = ("dma", eng, o.idx) if dma else eng
            b.r[key] = o
        for b in writes:
            b.w = o
            b.r = {}
        for d in o.deps:
            if not d.dma:
                d.needs_inc = True
        return o

    def dma(self, eng, out, in_, reads=(), writes=(), **kw):
        return self.op(eng, lambda e: e.dma_start(out=out, in_=in_, **kw), reads, writes, dma=True)

    def emit(self):
        nc = self.nc
        for e in ENGS:
            c = self.cnt[e]
            for o in self.q[e]:
                if not o.dma and o.needs_inc:
                    c += 1
                    o.sem = e
                    o.semval = c
            self.cnt[e] = c

        def replay(e, engobj):
            known = {}
            for o in self.q[e]:
                waits = {}
                for d in o.deps:
                    if d.eng == e and not d.dma:
                        if e == "pe" or (not self.same_engine_sync and e != "pool"):
                            continue
                    k = d.sem
                    if waits.get(k, 0) < d.semval:
                        waits[k] = d.semval
                if o.dma and o.semval > 16:
                    k = o.sem
                    waits[k] = max(waits.get(k, 0), o.semval - 16)
                for k, v in waits.items():
                    if known.get(k, 0) < v:
                        known[k] = v
                        s = self.dsem[k] if isinstance(k, tuple) else self.esem[k]
                        engobj.wait_ge(s, v)
                        self.nwaits[e] += 1
                ins = o.fn(engobj)
                if o.dma:
                    ins.then_inc(self.dsem[o.sem], 16)
                elif o.needs_inc:
                    ins.then_inc(self.esem[e], 1)
            if self.ndma[e]:
                n = self.ndma[e]
                for j in range(min(N_DMA_SEMS, n)):
                    last = ((n - 1 - j) // N_DMA_SEMS) + 1
                    if known.get((e, j), 0) < 16 * last:
                        engobj.wait_ge(self.dsem[(e, j)], 16 * last)

        with nc.Block() as block:
            @block.sync
            def _(e):
                replay("sp", e)

            @block.tensor
            def _(e):
                replay("pe", e)

            @block.scalar
            def _(e):
                replay("act", e)

            @block.vector
            def _(e):
                replay("dve", e)

            @block.gpsimd
            def _(e):
                replay("pool", e)

        for e in ENGS:
            self.total[e] += len(self.q[e])
            self.q[e] = []
        for b in self.bufs.values():
            b.w = None
            b.r = {}


class Ctx:
    def __init__(self, nc, es):
        self.nc = nc
        self.es = es
        self.s = Sched(nc, es)
        self.pes = None
        self.uid = 0

    def begin(self):
        self.pes = contextlib.ExitStack()
        self.pes.__enter__()

    def end(self):
        self.s.emit()
        self.pes.close()
        self.pes = None

    def dbg(self, name, ap, reads):
        if not getattr(self, "debug", False):
            return
        d = self.nc.dram_tensor("dbg_" + name, list(ap.shape), ap.dtype, kind="ExternalOutput").ap()
        self.s.dma("sp", d, ap, reads=reads)

    def sb(self, name, shape, dtype):
        self.uid += 1
        return self.pes.enter_context(self.nc.sbuf_tensor(f"{name}_{self.uid}", list(shape), dtype))

    def ps(self, name, shape=(128, 512), dtype=F32):
        self.uid += 1
        return self.pes.enter_context(self.nc.psum_tensor(f"{name}_{self.uid}", list(shape), dtype))


def _cast_engine(i):
    return ("dve", "pool", "act")[i % 3]


def _cast(s, eng, out, in_, reads, writes, scalar=None):
    if scalar is None:
        if eng == "act":
            return s.op("act", lambda e: e.copy(out=out, in_=in_), reads, writes)
        return s.op(eng, lambda e: e.tensor_copy(out=out, in_=in_), reads, writes)
    if eng == "act":
        return s.op("act", lambda e: e.activation(out=out, in_=in_, func=AF.Copy, scale=scalar), reads, writes)
    return s.op(eng, lambda e: e.tensor_scalar(out=out, in0=in_, scalar1=scalar, scalar2=None, op0=ALU.mult),
                reads, writes)


def make_identity(c, ident):
    s = c.s
    s.op("pool", lambda e: e.memset(ident[:], 1.0), writes=["ident"])
    s.op("pool", lambda e: e.affine_select(out=ident[:], in_=ident[:], pattern=[[-1, 128]],
                                           compare_op=ALU.is_equal, fill=0.0, base=0, channel_multiplier=1),
         reads=["ident"], writes=["ident"])


def phase_mlp(c, l, T, x_d, w_up, w_down, mlp_norm, xout_d=None):
    nc, s = c.nc, c.s
    if xout_d is None:
        xout_d = x_d
    c.begin()
    TT = 256
    NT = T // TT
    wup = c.sb("wup", [128, 8, DFF], BF16)
    wdn = c.sb("wdn", [128, 32, D], BF16)
    NSTG = 4
    stg = [c.sb(f"stg{i}", [128, 1024], F32) for i in range(NSTG)]
    gcol = c.sb("gcol", [128, 8], F32)
    ident = c.sb("ident", [128, 128], BF16)
    xt = [c.sb(f"xt{i}", [128, 2, D], F32) for i in range(3)]
    hb = c.sb("hb", [128, 2, D], BF16)
    hT = [c.sb(f"hT{i}", [128, 8, TT], BF16) for i in range(2)]
    actT = c.sb("actT", [128, 32, TT], BF16)
    rl = [c.sb(f"rl{i}", [128, TT], F32) for i in range(2)]
    ss = c.sb("ss", [128, 2], F32)
    rstd = c.sb("rstd", [128, 2], F32)
    junk = c.sb("junk", [128, 2, D], BF16)
    pt = [c.ps(f"pt{i}") for i in range(2)]
    pu = [c.ps(f"pu{i}") for i in range(3)]
    pd = [c.ps(f"pd{i}") for i in range(3)]

    make_identity(c, ident)
    s.dma("sp", gcol[:], mlp_norm[l].rearrange("(kc p) -> p kc", p=128), writes=["gcol"],
          allow_slow_non_contiguous=True)
    n = 0
    dq = ("sp", "act", "pool")
    for kc in range(8):
        for qq in range(4):
            st, sn_ = stg[n % NSTG], f"stg{n % NSTG}"
            s.dma(dq[n % 3], st[:], w_up[l, kc * 128:(kc + 1) * 128, qq * 1024:(qq + 1) * 1024], writes=[sn_])
            _cast(s, ("dve", "act")[n % 2], wup[:, kc, qq * 1024:(qq + 1) * 1024], st[:],
                  [sn_, "gcol"], [f"wup{kc}_{qq}"], scalar=gcol[:, kc:kc + 1])
            n += 1
    for fc in range(32):
        st, sn_ = stg[n % NSTG], f"stg{n % NSTG}"
        s.dma(dq[n % 3], st[:], w_down[l, fc * 128:(fc + 1) * 128, :], writes=[sn_])
        _cast(s, ("dve", "act")[n % 2], wdn[:, fc, :], st[:], [sn_], [f"wdn{fc}"])
        n += 1

    def load(i):
        s.dma("sp", xt[i % 3][:], x_d[i * TT:(i + 1) * TT, :].rearrange("(a p) d -> p a d", p=128),
              writes=[f"xt{i % 3}"])

    def front_elem(i):
        X, xb = xt[i % 3], f"xt{i % 3}"
        for a in range(2):
            ACTF(s, junk[:, a, :], X[:, a, :], AF.Square, [xb], [f"junk{a}"])
        s.op("dve", lambda e: e.tensor_reduce(out=ss[:], in_=junk[:], axis=AX.X, op=ALU.add),
             reads=["junk0", "junk1"], writes=["ss"])
        ACTF(s, rstd[:], ss[:], AF.Sqrt, ["ss"], ["rstd"], scale=1.0 / D, bias=EPS)
        s.op("dve", lambda e: e.reciprocal(out=rstd[:], in_=rstd[:]), reads=["rstd"], writes=["rstd"])
        for a in range(2):
            TS(s, "dve", hb[:, a, :], X[:, a, :], rstd[:, a:a + 1], ALU.mult, [xb, "rstd"], [f"hb{a}"])
    def front_tr(i):
        HT, htn = hT[i % 2], f"hT{i % 2}"
        for a in range(2):
            pview = pt[a][:].bitcast(BF16)
            for kc in range(8):
                TR(s, pview[:, kc * 128:(kc + 1) * 128], hb[:, a, kc * 128:(kc + 1) * 128], ident[:],
                   [f"hb{a}", "ident"], [f"pt{a}"])
            CP(s, "dve" if a == 0 else "act", HT[:, :, a * 128:(a + 1) * 128],
               pview.rearrange("p (k t) -> p k t", k=8), [f"pt{a}"], [htn + f"_{a}"])

    def up(i):
        HT, htn = hT[i % 2], f"hT{i % 2}"
        for fc in range(32):
            Pb, pb = pu[fc % 3], f"pu{fc % 3}"
            for kc in range(8):
                MM(s, Pb[:, 0:TT], wup[:, kc, fc * 128:(fc + 1) * 128], HT[:, kc, :],
                   [f"wup{kc}_{fc // 8}", htn + "_0", htn + "_1"], [pb], start=(kc == 0), stop=(kc == 7))
            R, rb = rl[fc % 2], f"rl{fc % 2}"
            ACTF(s, R[:], Pb[:, 0:TT], AF.Relu, [pb], [rb])
            TT_(s, "pool", actT[:, fc, :], R[:], R[:], ALU.mult, [rb], ["actT"])

    def down(i):
        X, xb = xt[i % 3], f"xt{i % 3}"
        for a in range(2):
            for db in range(2):
                jj = a * 2 + db
                Pb, pb = pd[jj % 3], f"pd{jj % 3}"
                for fc in range(32):
                    MM(s, Pb[:], actT[:, fc, a * 128:(a + 1) * 128], wdn[:, fc, db * 512:(db + 1) * 512],
                       ["actT", f"wdn{fc}"], [pb], start=(fc == 0), stop=(fc == 31))
                TT_(s, "dve", X[:, a, db * 512:(db + 1) * 512], Pb[:], X[:, a, db * 512:(db + 1) * 512], ALU.add,
                    [pb, xb], [xb])
        s.dma("pool", xout_d[i * TT:(i + 1) * TT, :].rearrange("(a p) d -> p a d", p=128), X[:], reads=[xb])

    load(0)
    if NT > 1:
        load(1)
    front_elem(0)
    front_tr(0)
    for i in range(NT):
        if i + 2 < NT:
            load(i + 2)
        if i + 1 < NT:
            front_elem(i + 1)
        up(i)
        if i + 1 < NT:
            front_tr(i + 1)
        down(i)
    c.end()


def build_test_mlp(T):
    nc = bass.Bass("TRN2", target_bir_lowering=False)
    x = nc.dram_tensor("x", [T, D], F32, kind="ExternalInput").ap()
    w_up = nc.dram_tensor("w_up", [DEPTH, D, DFF], F32, kind="ExternalInput").ap()
    w_down = nc.dram_tensor("w_down", [DEPTH, DFF, D], F32, kind="ExternalInput").ap()
    mlp_norm = nc.dram_tensor("mlp_norm", [DEPTH, D], F32, kind="ExternalInput").ap()
    y = nc.dram_tensor("y", [T, D], F32, kind="ExternalOutput").ap()
    with contextlib.ExitStack() as es:
        c = Ctx(nc, es)
        c.debug = True
        phase_mlp(c, 0, T, x, w_up, w_down, mlp_norm, xout_d=y)
        print("ops", c.s.total, "waits", c.s.nwaits)
    return nc


def rms_rstd(c, X, xb, junk, ss, rstd, width, tag):
    s = c.s
    s.op("act", lambda e: e.activation(out=junk, in_=X, func=AF.Square), reads=[xb], writes=["junk" + tag])
    s.op("dve", lambda e: e.tensor_reduce(out=ss, in_=junk, axis=AX.X, op=ALU.add),
         reads=["junk" + tag], writes=["ss" + tag])
    s.op("act", lambda e: e.activation(out=rstd, in_=ss, func=AF.Sqrt, scale=1.0 / width, bias=EPS),
         reads=["ss" + tag], writes=["rstd" + tag])
    s.op("dve", lambda e: e.reciprocal(out=rstd, in_=rstd), reads=["rstd" + tag], writes=["rstd" + tag])


def make_tri(c, tri, name, dtype_one=1.0):
    s = c.s
    s.op("pool", lambda e: e.memset(tri[:], 1.0), writes=[name])
    s.op("pool", lambda e: e.affine_select(out=tri[:], in_=tri[:], pattern=[[1, 128]],
                                           compare_op=ALU.is_ge, fill=0.0, base=0, channel_multiplier=-1),
         reads=[name], writes=[name])


def phase_a(c, l, NSEQ, S, x_d, P, G, scr, do_ssm=True, do_pool=True):
    nc, s = c.nc, c.s
    NJ = S // 128
    c.begin()
    NA = 2056
    win = c.sb("win", [128, 8, NA], BF16)
    NSTG = 4
    stg = [c.sb(f"stg{i}", [128, NA // 2], F32) for i in range(NSTG)]
    gcol = c.sb("gcol", [128, 8], F32)
    ident = c.sb("ident", [128, 128], BF16)
    trif = c.sb("trif", [128, 128], F32)
    onesf = c.sb("onesf", [128, 128], F32)
    xt = [c.sb(f"xt{i}", [128, D], F32) for i in range(3)]
    junk = c.sb("junk", [128, D], F32)
    ss = c.sb("ss", [128, 1], F32)
    rstd = c.sb("rstd", [128, 1], F32)
    hb = c.sb("hb", [128, D], BF16)
    hT = [c.sb(f"hT{i}", [128, 8, 128], BF16) for i in range(2)]
    qkg = c.sb("qkg", [128, 2, 8, DH], F32)
    bfg = c.sb("bfg", [128, 8], F32)
    sq = [c.sb(f"sq{i}", [128, 512], F32) for i in range(2)]
    qe = [[c.sb(f"qe{i}_{b}", [128, 512], F32) for b in range(2)] for i in range(2)]
    ssq = [c.sb(f"ssq{i}", [128, 8], F32) for i in range(2)]
    rq = [c.sb(f"rq{i}", [128, 8], F32) for i in range(2)]
    qn = [c.sb(f"qn{i}", [128, 8, DH], F32) for i in range(2)]
    qa = [[c.sb(f"qa{i}_{b}", [128, 8, 70], BF16) for b in range(2)] for i in range(2)]
    r1 = c.sb("r1", [128, 8], F32)
    r2 = c.sb("r2", [128, 8], F32)
    qTs = [[c.sb(f"qTs{i}_{b}", [128, 8, 128], BF16) for b in range(2)] for i in range(2)]
    vst = [c.sb(f"vst{b}", [128, 8, 65], BF16) for b in range(2)]
    ust = [c.sb(f"ust{b}", [128, 512], F32) for b in range(2)]
    fls = [c.sb(f"fl{b}", [128, 8], F32) for b in range(2)]
    sp_ = c.sb("sp_", [128, 8], F32)
    carry = c.sb("carry", [128, 8], F32)
    cumall = G["cumall"]
    pt = c.ps("pt")
    pq = [c.ps(f"pq{i}") for i in range(2)]
    pv = c.ps("pv")
    pf = c.ps("pf")
    pp = c.ps("pp")
    ptq = [c.ps(f"ptq{i}") for i in range(2)]

    make_identity(c, ident)
    make_tri(c, trif, "trif")
    s.op("pool", lambda e: e.memset(onesf[:], 1.0), writes=["onesf"])
    for b in range(2):
        s.op("pool", lambda e, b=b: e.memset(vst[b][:], 1.0), writes=[f"vst{b}"])
    s.dma("sp", gcol[:], P["mix_norm"][l].rearrange("(kc p) -> p kc", p=128), writes=["gcol"],
          allow_slow_non_contiguous=True)
    s.dma("sp", qkg[:, 0, 0, :], P["q_norm"][l:l + 1, :].partition_broadcast(128), writes=["qkg"])
    s.dma("sp", qkg[:, 1, 0, :], P["k_norm"][l:l + 1, :].partition_broadcast(128), writes=["qkg"])
    s.dma("sp", bfg[:], P["b_forget"][l:l + 1, :].partition_broadcast(128), writes=["bfg"])
    s.op("dve", lambda e: e.tensor_scalar(out=qkg[:, 0, 0, :], in0=qkg[:, 0, 0, :], scalar1=DH ** -0.5, scalar2=None,
                                          op0=ALU.mult), reads=["qkg"], writes=["qkg"])
    for h in range(1, 8):
        s.op("dve", lambda e, h=h: e.tensor_copy(out=qkg[:, :, h, :], in_=qkg[:, :, 0, :]),
             reads=["qkg"], writes=["qkg"])
    nn = 0
    dq = ("sp", "act", "pool")
    HN = NA // 2
    for kc in range(8):
        for hf in range(2):
            st, sn_ = stg[nn % NSTG], f"stg{nn % NSTG}"
            s.dma(dq[nn % 3], st[:], P["w_in"][l, kc * 128:(kc + 1) * 128, hf * HN:(hf + 1) * HN], writes=[sn_])
            _cast(s, ("dve", "act")[nn % 2], win[:, kc, hf * HN:(hf + 1) * HN], st[:], [sn_, "gcol"], [f"win{kc}"],
                  scalar=gcol[:, kc:kc + 1])
            nn += 1

    def load(n):
        s.dma("sp", xt[n % 3][:], x_d[n * 128:(n + 1) * 128, :], writes=[f"xt{n % 3}"])

    NCH = NSEQ * NJ
    ptv = pt[:].bitcast(BF16)
    pc = pf[:, 272:288]
    for i in range(2):
        for b in range(2):
            s.op("pool", lambda e, i=i, b=b: e.memset(qa[i][b][:], 1.0), writes=[f"qg{i}_{b}", f"qaug{i}_{b}"])

    def front_elem(n):
        X, xb = xt[n % 3], f"xt{n % 3}"
        ACTF(s, junk[:], X[:], AF.Square, [xb], ["junk"])
        s.op("dve", lambda e: e.tensor_reduce(out=ss[:], in_=junk[:], axis=AX.X, op=ALU.add), reads=["junk"],
             writes=["ss"])
        ACTF(s, rstd[:], ss[:], AF.Ln, ["ss"], ["rstd"], scale=1.0 / D, bias=EPS)
        ACTF(s, rstd[:], rstd[:], AF.Exp, ["rstd"], ["rstd"], scale=-0.5)
        TS(s, "dve", hb[:], X[:], rstd[:, 0:1], ALU.mult, [xb, "rstd"], ["hb"])

    def front_tr(n):
        HT, htn = hT[n % 2], f"hT{n % 2}"
        for kc in range(8):
            TR(s, ptv[:, kc * 128:(kc + 1) * 128], hb[:, kc * 128:(kc + 1) * 128], ident[:], ["hb", "ident"], ["pt"])
        CP(s, "act", HT[:], ptv.rearrange("p (k t) -> p k t", k=8), ["pt"], [htn])

    def proj_all(n):
        HT, htn = hT[n % 2], f"hT{n % 2}"

        def proj(Pt, pname, c0, c1):
            for kc in range(8):
                MM(s, Pt, HT[:, kc, :], win[:, kc, c0:c1], [htn, f"win{kc}"], [pname], start=(kc == 0), stop=(kc == 7))
        proj(pq[0][:], "pq0", 0, 512)
        proj(pq[1][:], "pq1", 512, 1024)
        proj(pv[:], "pv", 1024, 1536)
        proj(pf[:, 0:264], "pf", 1536, 1800)
        proj(pp[:, 0:256], "pp", 1800, 2056)

    def evac(n):
        b = n % 2
        CP(s, "act", qe[0][b][:], pq[0][:], ["pq0"], [f"qe0_{b}"])
        CP(s, "dve", qe[1][b][:], pq[1][:], ["pq1"], [f"qe1_{b}"])
        VS = vst[b]
        CP(s, "act", VS[:, :, 0:64], pv[:].rearrange("p (h d) -> p h d", h=8), ["pv"], [f"vst{b}"])
        US = ust[b]
        CP(s, "dve", US[:, 0:256], pf[:, 8:264], ["pf"], [f"ust{b}"])
        TT_(s, "dve", fls[b][:], pf[:, 0:8], bfg[:], ALU.add, ["pf", "bfg"], [f"fl{b}"])
        CP(s, "act", US[:, 256:512], pp[:, 0:256], ["pp"], [f"ust{b}"])

    def forget(n):
        q_, j = divmod(n, NJ)
        b = n % 2
        fl = fls[b]
        ACTF(s, fl[:], fl[:], AF.Exp, [f"fl{b}"], [f"fl{b}"], scale=-1.0)
        ACTF(s, sp_[:], fl[:], AF.Ln, [f"fl{b}"], ["sp_"], scale=1.0, bias=1.0)
        if j == 0:
            s.op("pool", lambda e: e.memset(carry[:], 0.0), writes=["carry"])
        MM(s, pc[:, 0:8], trif[:], sp_[:], ["trif", "sp_"], ["pf"])
        MM(s, pc[:, 8:16], onesf[:], sp_[:], ["onesf", "sp_"], ["pf"])
        cum = cumall[:, q_, j, :]
        TT_(s, "dve", cum, carry[:], pc[:, 0:8], ALU.subtract, ["carry", "pf"], ["cumall"])
        TT_(s, "dve", carry[:], carry[:], pc[:, 8:16], ALU.subtract, ["carry", "pf"], ["carry"])

    def post_a(n):
        q_, j = divmod(n, NJ)
        b = n % 2
        VS, US = vst[b], ust[b]
        cum = cumall[:, q_, j, :]
        QA, KA = qa[0][b], qa[1][b]
        CP(s, "dve", QA[:, :, 67], cum, ["cumall"], [f"qaug0_{b}"])
        yield
        TT_(s, "dve", r1[:], cum, QA[:, :, 67], ALU.subtract, ["cumall", f"qaug0_{b}"], ["r1"])
        yield
        CP(s, "dve", QA[:, :, 68], r1[:], ["r1"], [f"qaug0_{b}"])
        yield
        TT_(s, "dve", r2[:], r1[:], QA[:, :, 68], ALU.subtract, ["r1", f"qaug0_{b}"], ["r2"])
        yield
        CP(s, "dve", QA[:, :, 69], r2[:], ["r2"], [f"qaug0_{b}"])
        yield
        TS(s, "dve", KA[:, :, 64:67], QA[:, :, 67:70], -1.0, ALU.mult, [f"qaug0_{b}"], [f"qaug1_{b}"])
        yield
        for i in range(2):
            PQ = qe[i][b]
            ACTF(s, sq[i][:], PQ[:], AF.Square, [f"qe{i}_{b}"], [f"sq{i}"])
            yield
            s.op("dve", lambda e, i=i: e.tensor_reduce(out=ssq[i][:], in_=sq[i][:].rearrange("p (h d) -> p h d", h=8),
                                                       axis=AX.X, op=ALU.add),
                 reads=[f"sq{i}"], writes=[f"ssq{i}"])
            yield
            ACTF(s, rq[i][:], ssq[i][:], AF.Ln, [f"ssq{i}"], [f"rq{i}"], scale=1.0 / DH, bias=EPS)
            yield
            ACTF(s, rq[i][:], rq[i][:], AF.Exp, [f"rq{i}"], [f"rq{i}"], scale=-0.5)
            yield
            TT_(s, "dve", qn[i][:], PQ[:].rearrange("p (h d) -> p h d", h=8),
                rq[i][:].unsqueeze(2).broadcast_to([128, 8, DH]), ALU.mult, [f"qe{i}_{b}", f"rq{i}"], [f"qn{i}"])
            yield
            TT_(s, "dve", qa[i][b][:, :, 0:64], qn[i][:], qkg[:, i, :, :], ALU.mult, [f"qn{i}", "qkg"],
                [f"qg{i}_{b}"])
            yield
        s.dma("pool", scr["v"][q_, j * 128:(j + 1) * 128, :, :], VS[:], reads=[f"vst{n % 2}"])
        yield
        s.dma("pool", scr["u"][q_, j * 128:(j + 1) * 128, :], US[:], reads=[f"ust{n % 2}"])
        yield

    def post_b(n):
        q_, j = divmod(n, NJ)
        b = n % 2
        for i in range(2):
            pvw = ptq[i][:].bitcast(BF16)
            for h in range(8):
                TR(s, pvw[0:70, h * 128:(h + 1) * 128], qa[i][b][:, h, :], ident[:],
                   [f"qg{i}_{b}", f"qaug{i}_{b}", "ident"], [f"ptq{i}"])
            QT = qTs[i][n % 2]
            CP(s, "act" if i == 0 else "dve", QT[0:70, :, :], pvw[0:70, :].rearrange("p (h t) -> p h t", h=8),
               [f"ptq{i}"], [f"qTs{i}_{n % 2}"])
            yield
            dst = scr["qT" if i == 0 else "kT"]
            s.dma("pool", dst[q_, :, :, j * 128:(j + 1) * 128].rearrange("h p t -> p h t"), QT[0:70, :, :],
                  reads=[f"qTs{i}_{n % 2}"])
            yield

    load(0)
    if NCH > 1:
        load(1)
    def main_stream(t):
        if t + 2 < NCH:
            load(t + 2)
        if t < NCH:
            front_elem(t)
        yield
        if 1 <= t <= NCH:
            HT, htn = hT[(t - 1) % 2], f"hT{(t - 1) % 2}"
            for (Pt, pname, c0, c1) in ((pq[0][:], "pq0", 0, 512), (pq[1][:], "pq1", 512, 1024),
                                        (pv[:], "pv", 1024, 1536), (pf[:, 0:264], "pf", 1536, 1800),
                                        (pp[:, 0:256], "pp", 1800, 2056)):
                for kc in range(8):
                    MM(s, Pt, HT[:, kc, :], win[:, kc, c0:c1], [htn, f"win{kc}"], [pname], start=(kc == 0),
                       stop=(kc == 7))
                    if kc % 2 == 1:
                        yield
        if t < NCH:
            front_tr(t)
        yield
        if 1 <= t <= NCH:
            evac(t - 1)
            forget(t - 1)
        yield

    for t in range(NCH + 3):
        gens = [main_stream(t)]
        if 2 <= t < NCH + 2:
            gens.append(post_a(t - 2))
        if t >= 3:
            gens.append(post_b(t - 3))
        while gens:
            for g in list(gens):
                try:
                    next(g)
                except StopIteration:
                    gens.remove(g)
    c.end()


def phase_b(c, NSEQ, S, G, scr):
    nc, s = c.nc, c.s
    NJ = S // 128
    NI = NJ // 4
    c.begin()
    tri = c.sb("tri", [128, 128], BF16)
    vall = [c.sb(f"vall{i}", [128, NJ, 8, 65], BF16) for i in range(2)]
    qT = [c.sb(f"qT{i}", [128, S], BF16) for i in range(2)]
    kT = [c.sb(f"kT{i}", [128, S], BF16) for i in range(2)]
    NPT = 4
    pT = [c.sb(f"pT{i}", [128, 512], BF16) for i in range(NPT)]
    ybuf = [c.sb(f"ybuf{i}", [128, NJ, 64], BF16) for i in range(2)]
    rc = [c.sb(f"rc{i}", [128, 1], F32) for i in range(4)]
    NPS = 4
    pS = [c.ps(f"pS{i}") for i in range(NPS)]
    pO = [c.ps(f"pO{i}") for i in range(4)]
    make_tri(c, tri, "tri")
    LOOK = 3
    blocks = []
    for q_ in range(NSEQ):
        for h in range(8):
            for i4 in range(NI):
                for j in range(4 * i4 + 4):
                    blocks.append((q_, h, i4, j))

    def stage1(bi):
        q_, h, i4, j = blocks[bi]
        g = q_ * 8 + h
        QT, KT = qT[g % 2], kT[g % 2]
        if h == 0 and i4 == 0 and j == 0:
            s.dma("sp", vall[q_ % 2][:], scr["v"][q_].rearrange("(j p) h d -> p j h d", p=128), writes=[f"vall{q_ % 2}"])
        if i4 == 0 and j == 0:
            s.dma("sp", QT[0:70, :], scr["qT"][q_, h], writes=[f"qT{g % 2}"])
            s.dma("act", KT[0:70, :], scr["kT"][q_, h], writes=[f"kT{g % 2}"])
        jj = j - 4 * i4
        c0 = 128 * max(jj, 0)
        PS, psn = pS[bi % NPS], f"pS{bi % NPS}"
        PT, ptn = pT[bi % NPT], f"pT{bi % NPT}"
        MM(s, PS[:, c0:512], KT[0:70, j * 128:(j + 1) * 128], QT[0:70, i4 * 512 + c0:(i4 + 1) * 512],
           [f"qT{g % 2}", f"kT{g % 2}"], [psn])
        ACTF(s, PT[:, c0:512], PS[:, c0:512], AF.Exp, [psn], [ptn])
        if jj >= 0:
            TT_(s, "pool", PT[:, c0:c0 + 128], PT[:, c0:c0 + 128], tri[:], ALU.mult, [ptn, "tri"], [ptn])

    def stage2(bi):
        q_, h, i4, j = blocks[bi]
        g = q_ * 8 + h
        PT, ptn = pT[bi % NPT], f"pT{bi % NPT}"
        YB = ybuf[g % 2]
        jj = j - 4 * i4
        for tt in range(max(jj, 0), 4):
            last = (j == 4 * i4 + tt)
            MM(s, pO[tt][:, 0:65], PT[:, tt * 128:(tt + 1) * 128], vall[q_ % 2][:, j, h, :], [ptn, f"vall{q_ % 2}"],
               [f"pO{tt}"], start=(j == 0), stop=last)
            if last:
                s.op("dve", lambda e, tt=tt: e.reciprocal(out=rc[tt][:], in_=pO[tt][:, 64:65]), reads=[f"pO{tt}"],
                     writes=[f"rc{tt}"])
                TS(s, "dve", YB[:, 4 * i4 + tt, :], pO[tt][:, 0:64], rc[tt][:, 0:1], ALU.mult, [f"pO{tt}", f"rc{tt}"],
                   [f"ybuf{g % 2}"])
        if i4 == NI - 1 and j == NJ - 1:
            s.dma("pool", scr["yattn"][q_, :, h * 64:(h + 1) * 64].rearrange("(j p) c -> p j c", p=128), YB[:],
                  reads=[f"ybuf{g % 2}"])

    nb = len(blocks)
    for t in range(nb + LOOK):
        if t < nb:
            stage1(t)
        if t >= LOOK:
            stage2(t - LOOK)
    c.end()


def make_scratch(nc, NSEQ, S, debug_out=False):
    kind = {"kind": "ExternalOutput"} if debug_out else {}
    scr = {}
    scr["qT"] = nc.dram_tensor("scr_qT", [NSEQ, 8, 70, S], BF16).ap()
    scr["kT"] = nc.dram_tensor("scr_kT", [NSEQ, 8, 70, S], BF16).ap()
    scr["v"] = nc.dram_tensor("scr_v", [NSEQ, S, 8, 65], BF16).ap()
    scr["cend"] = nc.dram_tensor("scr_cend", [1, NSEQ, S // 128, 8], F32).ap()
    scr["u"] = nc.dram_tensor("scr_u", [NSEQ, S, 512], F32).ap()
    scr["yattn"] = nc.dram_tensor("scr_yattn", [NSEQ, S, 512], BF16, **kind).ap()
    scr["yssmT"] = nc.dram_tensor("scr_yssmT", [NSEQ, 256, S], BF16, **kind).ap()
    scr["ypoolT"] = nc.dram_tensor("scr_ypoolT", [NSEQ, 256, S], BF16, **kind).ap()
    return scr


PARAM_SHAPES = {
    "mix_norm": [DEPTH, D], "w_in": [DEPTH, D, INC], "b_forget": [DEPTH, NH], "q_norm": [DEPTH, DH],
    "k_norm": [DEPTH, DH], "ssm_a_re": [DEPTH, 16, 64], "ssm_a_im": [DEPTH, 16, 64], "ssm_log_dt": [DEPTH, 16],
    "ssm_b_re": [DEPTH, 16, 64, 16], "ssm_b_im": [DEPTH, 16, 64, 16], "ssm_c_re": [DEPTH, 16, 16, 64],
    "ssm_c_im": [DEPTH, 16, 16, 64], "ssm_d": [DEPTH, 256], "w_glu": [DEPTH, 256, 512],
    "pool_w": [DEPTH, 4, 64, 64], "pool_scale": [DEPTH, 256], "w_br_attn": [DEPTH, 512, D],
    "w_br_ssm": [DEPTH, 256, D], "w_br_pool": [DEPTH, 256, D], "b_gate": [DEPTH, 3 * D],
    "w_out": [DEPTH, D, D], "mlp_norm": [DEPTH, D], "w_up": [DEPTH, D, DFF], "w_down": [DEPTH, DFF, D],
}


def declare_params(nc):
    return {k: nc.dram_tensor(k, shp, F32, kind="ExternalInput").ap() for k, shp in PARAM_SHAPES.items()}


def build_test_ab(NSEQ, S, l=0):
    nc = bass.Bass("TRN2", target_bir_lowering=False)
    x = nc.dram_tensor("x", [NSEQ * S, D], F32, kind="ExternalInput").ap()
    P = declare_params(nc)
    scr = make_scratch(nc, NSEQ, S, debug_out=True)
    with contextlib.ExitStack() as es:
        c = Ctx(nc, es)
        G = {"cumall": es.enter_context(nc.sbuf_tensor("cumall", [128, NSEQ, S // 128, 8], F32))}
        phase_a(c, l, NSEQ, S, x, P, G, scr, do_ssm=False, do_pool=False)
        phase_b(c, NSEQ, S, G, scr)
        print("ops", c.s.total, "waits", c.s.nwaits)
    return nc


def TT_(s, eng, out, in0, in1, op, reads, writes):
    return s.op(eng, lambda e: e.tensor_tensor(out=out, in0=in0, in1=in1, op=op), reads, writes)


def TS(s, eng, out, in0, scalar1, op0, reads, writes, scalar2=None, op1=None):
    if op1 is None:
        return s.op(eng, lambda e: e.tensor_scalar(out=out, in0=in0, scalar1=scalar1, scalar2=None, op0=op0),
                    reads, writes)
    return s.op(eng, lambda e: e.tensor_scalar(out=out, in0=in0, scalar1=scalar1, scalar2=scalar2, op0=op0, op1=op1),
                reads, writes)


def ACTF(s, out, in_, func, reads, writes, scale=1.0, bias=None):
    if bias is None:
        return s.op("act", lambda e: e.activation(out=out, in_=in_, func=func, scale=scale), reads, writes)
    return s.op("act", lambda e: e.activation(out=out, in_=in_, func=func, scale=scale, bias=bias), reads, writes)


def CP(s, eng, out, in_, reads, writes):
    if eng == "act":
        return s.op("act", lambda e: e.copy(out=out, in_=in_), reads, writes)
    return s.op(eng, lambda e: e.tensor_copy(out=out, in_=in_), reads, writes)


def MM(s, out, lhsT, rhs, reads, writes, start=True, stop=True):
    return s.op("pe", lambda e: e.matmul(out, lhsT=lhsT, rhs=rhs, start=start, stop=stop), reads, writes)


def TR(s, out, in_, ident, reads, writes):
    return s.op("pe", lambda e: e.transpose(out=out, in_=in_, identity=ident), reads, writes)


TWO_PI = 2.0 * math.pi
MAGIC = 12582912.0


def sincos(s, eng, ang, o_sin, o_cos, t1, t2, rd, tag):
    n1, n2 = "sc1" + tag, "sc2" + tag
    for which, o in (("s", o_sin), ("c", o_cos)):
        if which == "s":
            TS(s, eng, t1, ang, 1.0 / TWO_PI, ALU.mult, rd, [n1])
        else:
            TS(s, eng, t1, ang, 1.0 / TWO_PI, ALU.mult, rd, [n1], scalar2=0.25, op1=ALU.add)
        TS(s, eng, t2, t1, MAGIC, ALU.add, [n1], [n2])
        TS(s, eng, t2, t2, MAGIC, ALU.subtract, [n2], [n2])
        TT_(s, eng, t2, t1, t2, ALU.subtract, [n1, n2], [n2])
        ACTF(s, o, t2, AF.Sin, [n2], [("sin" if which == "s" else "cos") + tag], scale=TWO_PI * (1 - 1e-6))


POOL_WINDOWS = (2, 4, 8, 16)


def phase_a2(c, l, NSEQ, S, P, scr):
    nc, s = c.nc, c.s
    NJ = S // 128
    c.begin()
    I32 = mybir.dt.int32
    ident = c.sb("ident", [128, 128], BF16)
    identf = c.sb("identf", [128, 128], F32)
    tribf = c.sb("tribf", [128, 128], BF16)
    iot_i = c.sb("iot_i", [128, 128], I32)
    iot = c.sb("iot", [128, 128], F32)
    pcol_i = c.sb("pcol_i", [128, 1], I32)
    pcol = c.sb("pcol", [128, 1], F32)
    npcol = c.sb("npcol", [128, 1], F32)
    are = c.sb("are", [128, 1024], F32)
    aim = c.sb("aim", [128, 1024], F32)
    ldt = c.sb("ldt", [128, 16], F32)
    dtr = c.sb("dtr", [128, 1024], F32)
    t1 = c.sb("t1", [128, 1024], F32)
    t2 = c.sb("t2", [128, 1024], F32)
    t3 = c.sb("t3", [128, 1024], F32)
    t4 = c.sb("t4", [128, 1024], F32)
    t5 = c.sb("t5", [128, 1024], F32)
    t6 = c.sb("t6", [128, 1024], F32)
    cfr = c.sb("cfr", [128, 1024], F32)
    cfi = c.sb("cfi", [128, 1024], F32)
    Tr = c.sb("Tr", [128, 1024], F32)
    Ti = c.sb("Ti", [128, 1024], F32)
    TAr = c.sb("TAr", [128, 8, 128], F32)
    TAi = c.sb("TAi", [128, 8, 128], F32)
    acol = c.sb("acol", [128, 3, 8], F32)
    BX = c.sb("BX", [128, 8, 2, 128], F32)
    BXb = c.sb("BXb", [128, 8, 2, 128], BF16)
    BT = c.sb("BT", [128, 8, 2, 128], BF16)
    CN = c.sb("CN", [32, 8, 2, 128], F32)
    CNb = c.sb("CNb", [32, 8, 2, 128], BF16)
    CX = c.sb("CX", [128, 8, 2, 32], BF16)
    drb = c.sb("drb", [128, 256], F32)
    wg_st = c.sb("wg_st", [128, 2, 512], F32)
    wglu = c.sb("wglu", [128, 2, 512], BF16)
    pw_st = c.sb("pw_st", [128, 2, 64], F32)
    PW = c.sb("PW", [128, 2, 64], BF16)
    pscol = c.sb("pscol", [128, 2], F32)
    MT = c.sb("MT", [128, 12, 128], BF16)
    mtmp = c.sb("mtmp", [128, 128], F32)
    mrat = c.sb("mrat", [128, 128], F32)
    ut = [c.sb(f"ut{i}", [128, 512], F32) for i in range(3)]
    ub = [c.sb(f"ub{i}", [128, 512], BF16) for i in range(3)]
    uT = c.sb("uT", [128, 2, 128], BF16)
    m1 = c.sb("m1", [128, 4, 128], F32)
    m2 = c.sb("m2", [128, 4, 128], F32)
    m3 = c.sb("m3", [128, 4, 128], F32)
    m4 = c.sb("m4", [128, 4, 128], F32)
    Wt = c.sb("Wt", [128, 8, 2, 128], BF16)
    Pr = c.sb("Pr", [128, 8, 128], F32)
    Pi = c.sb("Pi", [128, 8, 128], F32)
    Xt = c.sb("Xt", [128, 8, 2, 128], BF16)
    car = c.sb("car", [128, 2, 8], F32)
    sn = c.sb("sn", [128, 2, 4], F32)
    sm = c.sb("sm", [128, 4, 4], F32)
    du = c.sb("du", [128, 256], F32)
    yv = c.sb("yv", [128, 256], F32)
    y2 = c.sb("y2", [128, 256], F32)
    sg = c.sb("sg", [128, 256], F32)
    gy = c.sb("gy", [128, 256], BF16)
    gyT = c.sb("gyT", [128, 2, 128], BF16)
    sgb = c.sb("sgb", [128, 2, 128], F32)
    ysT = [c.sb(f"ysT{i}", [128, 2, 128], BF16) for i in range(2)]
    plT = c.sb("plT", [128, 2, 128], BF16)
    ypT = [c.sb(f"ypT{i}", [128, 2, 128], BF16) for i in range(2)]
    pbu = c.ps("pbu")
    ppf = c.ps("ppf", [128, 2048])
    pym = c.ps("pym")
    pglu = c.ps("pglu")
    ppl = c.ps("ppl")
    py = pym
    w1 = c.sb("w1", [128, 2, 128], F32)
    w2 = c.sb("w2", [128, 2, 128], F32)
    w3 = c.sb("w3", [128, 2, 128], F32)
    w4 = c.sb("w4", [128, 2, 128], F32)

    make_identity(c, ident)
    make_tri(c, tribf, "tribf")
    s.op("pool", lambda e: e.iota(iot_i[:], pattern=[[1, 128]], base=0, channel_multiplier=0), writes=["iot_i"])
    CP(s, "dve", iot[:], iot_i[:], ["iot_i"], ["iot"])
    s.op("pool", lambda e: e.iota(pcol_i[:], pattern=[[0, 1]], base=0, channel_multiplier=1), writes=["pcol_i"])
    CP(s, "dve", pcol[:], pcol_i[:], ["pcol_i"], ["pcol"])
    TS(s, "dve", npcol[:], pcol[:], -1.0, ALU.mult, ["pcol"], ["npcol"])
    CP(s, "dve", identf[:], ident[:], ["ident"], ["identf"])

    s.dma("sp", are[:], P["ssm_a_re"][l:l + 1].rearrange("o g n -> o (g n)").partition_broadcast(128), writes=["are"])
    s.dma("act", aim[:], P["ssm_a_im"][l:l + 1].rearrange("o g n -> o (g n)").partition_broadcast(128), writes=["aim"])
    s.dma("sp", ldt[:], P["ssm_log_dt"][l:l + 1, :].partition_broadcast(128), writes=["ldt"])
    s.dma("sp", acol[:, 0, :], P["ssm_a_re"][l].rearrange("(k gl) n -> (gl n) k", gl=2), writes=["acol"],
          allow_slow_non_contiguous=True)
    s.dma("sp", acol[:, 1, :], P["ssm_a_im"][l].rearrange("(k gl) n -> (gl n) k", gl=2), writes=["acol"],
          allow_slow_non_contiguous=True)
    for gl in range(2):
        s.dma("sp", acol[64 * gl:64 * gl + 64, 2, :],
              P["ssm_log_dt"][l:l + 1, :].rearrange("o (k gl) -> o k gl", gl=2)[:, :, gl].partition_broadcast(64),
              writes=["acol"], allow_slow_non_contiguous=True)
    s.dma("sp", drb[:], P["ssm_d"][l:l + 1, :].partition_broadcast(128), writes=["drb"])
    s.dma("act", wg_st[:], P["w_glu"][l].rearrange("(kc p) g -> p kc g", p=128), writes=["wg_st"])
    CP(s, "pool", wglu[:], wg_st[:], ["wg_st"], ["wglu"])
    for g in range(4):
        s.dma("sp", pw_st[64 * (g % 2):64 * (g % 2) + 64, g // 2, :], P["pool_w"][l, g], writes=["pw_st"])
    CP(s, "pool", PW[:], pw_st[:], ["pw_st"], ["PW"])
    s.dma("sp", pscol[:], P["pool_scale"][l].rearrange("(kc p) -> p kc", p=128), writes=["pscol"],
          allow_slow_non_contiguous=True)
    s.op("pool", lambda e: e.memset(BX[:], 0.0), writes=["BX"])
    s.op("pool", lambda e: e.memset(CN[:], 0.0), writes=["CN"])
    nd = 0
    for k in range(8):
        for gl in range(2):
            g = 2 * k + gl
            c0 = 32 * (k % 4) + 16 * gl
            for part, nm in ((0, "ssm_b_re"), (1, "ssm_b_im")):
                s.dma("sp" if nd % 2 == 0 else "act", BX[64 * gl:64 * gl + 64, k, part, c0:c0 + 16], P[nm][l, g],
                      reads=[], writes=["BX"])
                nd += 1
            for part, nm in ((0, "ssm_c_re"), (1, "ssm_c_im")):
                s.dma("sp" if nd % 2 == 0 else "act", CN[16 * gl:16 * gl + 16, k, part, 64 * gl:64 * gl + 64],
                      P[nm][l, g], reads=[], writes=["CN"])
                nd += 1
    CP(s, "dve", BXb[:], BX[:], ["BX"], ["BXb"])
    CP(s, "dve", CNb[:], CN[:], ["CN"], ["CNb"])
    pmv = ppf[:, 0:512].bitcast(BF16)
    for k in range(8):
        for part in range(2):
            i = (k * 2 + part) % 8
            TR(s, pmv[:, i * 128:(i + 1) * 128], BXb[:, k, part, :], ident[:], ["BXb", "ident"], ["ppf"])
        if k % 4 == 3:
            k0 = k - 3
            CP(s, "dve", BT[:, k0:k0 + 4, :, :], pmv.rearrange("p (k a m) -> p k a m", k=4, a=2), ["ppf"], ["BT"])
    for k in range(8):
        for part in range(2):
            i = k * 2 + part
            TR(s, pmv[:, i * 32:(i + 1) * 32], CNb[:, k, part, :], ident[0:32, 0:32], ["CNb", "ident"], ["ppf"])
    cxv = pmv[:, 0:512].rearrange("p (k a m) -> p k a m", k=8, a=2)
    CP(s, "dve", CX[:, :, 0, :], cxv[:, :, 0, :], ["ppf"], ["CX"])
    TS(s, "dve", CX[:, :, 1, :], cxv[:, :, 1, :], -1.0, ALU.mult, ["ppf"], ["CX"])

    ACTF(s, ldt[:], ldt[:], AF.Exp, ["ldt"], ["ldt"])
    CP(s, "dve", dtr[:].rearrange("p (g n) -> p g n", g=16), ldt[:].unsqueeze(2).broadcast_to([128, 16, 64]),
       ["ldt"], ["dtr"])
    ardt, wr = t5, t6
    TT_(s, "dve", ardt[:], are[:], dtr[:], ALU.mult, ["are", "dtr"], ["ardt"])
    TT_(s, "dve", wr[:], aim[:], dtr[:], ALU.mult, ["aim", "dtr"], ["wr"])
    sincos(s, "dve", wr[:], t3[:], t4[:], t1[:], t2[:], ["wr"], "0")
    ACTF(s, t1[:], ardt[:], AF.Exp, ["ardt", "sc10"], ["mag1"])
    TT_(s, "dve", t4[:], t4[:], t1[:], ALU.mult, ["cos0", "mag1"], ["cos0"])
    TT_(s, "dve", t3[:], t3[:], t1[:], ALU.mult, ["sin0", "mag1"], ["sin0"])
    TS(s, "dve", t4[:], t4[:], -1.0, ALU.add, ["cos0"], ["cos0"])
    TT_(s, "dve", t1[:], are[:], are[:], ALU.mult, ["are", "mag1", "sin0"], ["mag1"])
    TT_(s, "dve", t2[:], aim[:], aim[:], ALU.mult, ["aim", "sc20"], ["sc20"])
    TT_(s, "dve", t1[:], t1[:], t2[:], ALU.add, ["mag1", "sc20"], ["mag1"])
    s.op("dve", lambda e: e.reciprocal(out=t1[:], in_=t1[:]), reads=["mag1"], writes=["mag1"])
    TT_(s, "dve", cfr[:], t4[:], are[:], ALU.mult, ["cos0", "are"], ["cfr"])
    TT_(s, "dve", t2[:], t3[:], aim[:], ALU.mult, ["sin0", "aim", "sc20"], ["sc20"])
    TT_(s, "dve", cfr[:], cfr[:], t2[:], ALU.add, ["cfr", "sc20"], ["cfr"])
    TT_(s, "dve", cfr[:], cfr[:], t1[:], ALU.mult, ["cfr", "mag1"], ["cfr"])
    TT_(s, "dve", cfi[:], t3[:], are[:], ALU.mult, ["sin0", "are"], ["cfi"])
    TT_(s, "dve", t2[:], t4[:], aim[:], ALU.mult, ["cos0", "aim", "sc20", "cfr"], ["sc20"])
    TT_(s, "dve", cfi[:], cfi[:], t2[:], ALU.subtract, ["cfi", "sc20"], ["cfi"])
    TT_(s, "dve", cfi[:], cfi[:], t1[:], ALU.mult, ["cfi", "mag1"], ["cfi"])
    TS(s, "dve", dtr[:], wr[:], pcol[:, 0:1], ALU.mult, ["wr", "pcol", "dtr"], ["ang"])
    sincos(s, "dve", dtr[:], t3[:], t4[:], t1[:], t2[:], ["ang", "cfi", "cfr"], "1")
    s.op("act", lambda e: e.activation(out=t1[:], in_=ardt[:], func=AF.Exp, scale=npcol[:, 0:1]),
         reads=["ardt", "npcol", "sc11", "cos1"], writes=["mag2"])
    TT_(s, "dve", t4[:], t4[:], t1[:], ALU.mult, ["cos1", "mag2"], ["cos1"])
    TT_(s, "dve", t3[:], t3[:], t1[:], ALU.mult, ["sin1", "mag2"], ["sin1"])
    TT_(s, "dve", Tr[:], t4[:], cfr[:], ALU.mult, ["cos1", "cfr"], ["Tr"])
    TT_(s, "dve", t2[:], t3[:], cfi[:], ALU.mult, ["sin1", "cfi", "sc21"], ["sc21"])
    TT_(s, "dve", Tr[:], Tr[:], t2[:], ALU.add, ["Tr", "sc21"], ["Tr"])
    TT_(s, "dve", Ti[:], t4[:], cfi[:], ALU.mult, ["cos1", "cfi"], ["Ti"])
    TT_(s, "dve", t2[:], t3[:], cfr[:], ALU.mult, ["sin1", "cfr", "Tr"], ["sc21"])
    TT_(s, "dve", Ti[:], Ti[:], t2[:], ALU.subtract, ["Ti", "sc21"], ["Ti"])
    ACTF(s, acol[:, 2, :], acol[:, 2, :], AF.Exp, ["acol"], ["acol"])
    TT_(s, "dve", acol[:, 0, :], acol[:, 0, :], acol[:, 2, :], ALU.mult, ["acol"], ["acol"])
    TT_(s, "dve", acol[:, 1, :], acol[:, 1, :], acol[:, 2, :], ALU.mult, ["acol"], ["acol"])
    angf = t5[:].rearrange("p (k t) -> p k t", k=8)
    magf = t6[:].rearrange("p (k t) -> p k t", k=8)
    for k in range(8):
        TS(s, "dve", angf[:, k, :], iot[:], acol[:, 1, k:k + 1], ALU.mult, ["iot", "acol", "ardt", "Tr", "Ti"], ["angf"])
        s.op("act", lambda e, k=k: e.activation(out=magf[:, k, :], in_=iot[:], func=AF.Exp, scale=acol[:, 0, k:k + 1]),
             reads=["iot", "acol", "wr", "ang", "Tr", "Ti"], writes=["magf"])
    sincos(s, "dve", t5[:], t3[:], t4[:], t1[:], t2[:], ["angf", "Tr", "Ti"], "2")
    TT_(s, "dve", TAr[:].rearrange("p k t -> p (k t)"), t4[:], t6[:], ALU.mult, ["cos2", "magf"], ["TAr"])
    TT_(s, "dve", TAi[:].rearrange("p k t -> p (k t)"), t3[:], t6[:], ALU.mult, ["sin2", "magf"], ["TAi"])

    for g, w in enumerate(POOL_WINDOWS):
        s.op("pool", lambda e, w=w: e.memset(mtmp[:], 1.0 / w), reads=["mtmp", "mrat", "MT"], writes=["mtmp"])
        s.op("pool", lambda e: e.affine_select(out=mtmp[:], in_=mtmp[:], pattern=[[1, 128]], compare_op=ALU.is_ge,
                                               fill=0.0, base=0, channel_multiplier=-1),
             reads=["mtmp"], writes=["mtmp"])
        s.op("pool", lambda e, w=w: e.affine_select(out=mtmp[:], in_=mtmp[:], pattern=[[-1, 128]],
                                                    compare_op=ALU.is_ge, fill=0.0, base=w - 1, channel_multiplier=1),
             reads=["mtmp"], writes=["mtmp"])
        TT_(s, "pool", MT[:, g * 3 + 0, :], mtmp[:], identf[:], ALU.subtract, ["mtmp", "identf"], ["MT"])
        TS(s, "dve", mrat[:], iot[:], 1.0, ALU.add, ["iot"], ["mrat"])
        s.op("dve", lambda e: e.reciprocal(out=mrat[:], in_=mrat[:]), reads=["mrat"], writes=["mrat"])
        TS(s, "dve", mrat[:], mrat[:], float(w), ALU.mult, ["mrat"], ["mrat"], scalar2=1.0, op1=ALU.max)
        TT_(s, "dve", mrat[:], mrat[:], mtmp[:], ALU.mult, ["mrat", "mtmp"], ["mrat"])
        TT_(s, "dve", MT[:, g * 3 + 2, :], mrat[:], identf[:], ALU.subtract, ["mrat", "identf"], ["MT"])
        s.op("pool", lambda e, w=w: e.memset(mtmp[:], 1.0 / w), reads=["mtmp", "mrat", "MT"], writes=["mtmp"])
        s.op("pool", lambda e, w=w: e.affine_select(out=mtmp[:], in_=mtmp[:], pattern=[[-1, 128]],
                                                    compare_op=ALU.is_ge, fill=0.0, base=-(129 - w),
                                                    channel_multiplier=1),
             reads=["mtmp"], writes=["mtmp"])
        CP(s, "pool", MT[:, g * 3 + 1, :], mtmp[:], ["mtmp"], ["MT"])

    NCH = NSEQ * NJ
    pmq = pym[:, 256:512].bitcast(BF16)

    def load(n):
        q_, j = divmod(n, NJ)
        s.dma("sp", ut[n % 3][:], scr["u"][q_, j * 128:(j + 1) * 128, :], writes=[f"ut{n % 3}"])

    def S1(n):
        U, UB = ut[n % 3], ub[n % 3]
        un, ubn = f"ut{n % 3}", f"ub{n % 3}"
        CP(s, "act", UB[:], U[:], [un], [ubn])
        for kc in range(2):
            TR(s, pmq[:, kc * 128:(kc + 1) * 128], UB[:, kc * 128:(kc + 1) * 128], ident[:], [ubn, "ident"], ["pym"])
        CP(s, "dve", uT[:], pmq[:, 0:256].rearrange("p (k t) -> p k t", k=2), ["pym"], ["uT"])
        for qt in range(4):
            k0 = qt * 2
            for kk in range(2):
                k = k0 + kk
                MM(s, pbu[:, kk * 256:(kk + 1) * 256], uT[:, k // 4, :], BT[:, k, :, :].rearrange("p a m -> p (a m)"),
                   ["uT", "BT"], ["pbu"])
            buv = pbu[:].rearrange("p (k a m) -> p k a m", k=2, a=2)
            trv = Tr[:, k0 * 128:(k0 + 2) * 128].rearrange("p (k m) -> p k m", k=2)
            tiv = Ti[:, k0 * 128:(k0 + 2) * 128].rearrange("p (k m) -> p k m", k=2)
            yield
            TT_(s, "dve", w1[:], buv[:, :, 0, :], trv, ALU.mult, ["pbu", "Tr"], ["w1"])
            TT_(s, "dve", w2[:], buv[:, :, 1, :], tiv, ALU.mult, ["pbu", "Ti"], ["w2"])
            yield
            TT_(s, "pool", Wt[:, k0:k0 + 2, 0, :], w1[:], w2[:], ALU.subtract, ["w1", "w2"], [f"Wt{qt}"])
            TT_(s, "dve", w3[:], buv[:, :, 1, :], trv, ALU.mult, ["pbu", "Tr"], ["w3"])
            TT_(s, "dve", w4[:], buv[:, :, 0, :], tiv, ALU.mult, ["pbu", "Ti"], ["w4"])
            yield
            TT_(s, "pool", Wt[:, k0:k0 + 2, 1, :], w3[:], w4[:], ALU.add, ["w3", "w4"], [f"Wt{qt}"])
            yield
            for kk in range(2):
                for part in range(2):
                    i = (k0 + kk) * 2 + part
                    MM(s, ppf[:, i * 128:(i + 1) * 128], Wt[:, k0 + kk, part, :], tribf[:], [f"Wt{qt}", "tribf"],
                       [f"ppf{qt // 2}"])

    def XP(n):
        q_, j = divmod(n, NJ)
        if j == 0:
            s.op("pool", lambda e: e.memset(car[:], 0.0), writes=["car"])
        for hf in range(2):
            k0 = hf * 4
            pfv = ppf[:, hf * 1024:(hf + 1) * 1024].rearrange("p (k a t) -> p k a t", k=4, a=2)
            pfn = f"ppf{hf}"
            TT_(s, "dve", Pr[:, k0:k0 + 4, :], pfv[:, :, 0, :],
                car[:, 0, k0:k0 + 4].unsqueeze(2).broadcast_to([128, 4, 128]), ALU.add, [pfn, "car"], [f"Pr{hf}"])
            TT_(s, "dve", Pi[:, k0:k0 + 4, :], pfv[:, :, 1, :],
                car[:, 1, k0:k0 + 4].unsqueeze(2).broadcast_to([128, 4, 128]), ALU.add, [pfn, "car"], [f"Pi{hf}"])
        yield
        for hf in range(2):
            k0 = hf * 4
            prn, pin = f"Pr{hf}", f"Pi{hf}"
            PR, PI = Pr[:, k0:k0 + 4, :], Pi[:, k0:k0 + 4, :]
            tar, tai = TAr[:, k0:k0 + 4, :], TAi[:, k0:k0 + 4, :]
            TT_(s, "dve", m1[:], tar, PR, ALU.mult, ["TAr", prn], ["m1"])
            TT_(s, "pool", m2[:], tai, PI, ALU.mult, ["TAi", pin], ["m2"])
            yield
            TT_(s, "dve", m3[:], tar, PI, ALU.mult, ["TAr", pin], ["m3"])
            TT_(s, "pool", m4[:], tai, PR, ALU.mult, ["TAi", prn], ["m4"])
            yield
            TT_(s, "pool", Xt[:, k0:k0 + 4, 0, :], m1[:], m2[:], ALU.subtract, ["m1", "m2"], [f"Xt{hf}"])
            TT_(s, "dve", sn[:, 0, :], m1[:, :, 127], m2[:, :, 127], ALU.subtract, ["m1", "m2"], ["sn"])
            yield
            TT_(s, "dve", Xt[:, k0:k0 + 4, 1, :], m3[:], m4[:], ALU.add, ["m3", "m4"], [f"Xt{hf}"])
            TT_(s, "dve", sn[:, 1, :], m3[:, :, 127], m4[:, :, 127], ALU.add, ["m3", "m4"], ["sn"])
            yield
            a1r, a1i = TAr[:, k0:k0 + 4, 1], TAi[:, k0:k0 + 4, 1]
            TT_(s, "dve", sm[:, 0, :], a1r, sn[:, 0, :], ALU.mult, ["TAr", "sn"], ["sm"])
            TT_(s, "dve", sm[:, 1, :], a1i, sn[:, 1, :], ALU.mult, ["TAi", "sn"], ["sm"])
            TT_(s, "dve", sm[:, 2, :], a1r, sn[:, 1, :], ALU.mult, ["TAr", "sn"], ["sm"])
            TT_(s, "dve", sm[:, 3, :], a1i, sn[:, 0, :], ALU.mult, ["TAi", "sn"], ["sm"])
            yield
            TT_(s, "dve", car[:, 0, k0:k0 + 4], sm[:, 0, :], sm[:, 1, :], ALU.subtract, ["sm"], ["car"])
            TT_(s, "dve", car[:, 1, k0:k0 + 4], sm[:, 2, :], sm[:, 3, :], ALU.add, ["sm"], ["car"])
            yield

    def REST(n):
        q_, j = divmod(n, NJ)
        U, UB = ut[n % 3], ub[n % 3]
        un, ubn = f"ut{n % 3}", f"ub{n % 3}"
        UBP, ubpn = ub[(n - 1) % 3], f"ub{(n - 1) % 3}"
        for k in range(8):
            for part in range(2):
                MM(s, py[:, 32 * k:32 * k + 32], Xt[:, k, part, :], CX[:, k, part, :], [f"Xt{k // 4}", "CX"], ["pym"],
                   start=(part == 0), stop=(part == 1))
        yield
        TT_(s, "pool", du[:], U[:, 0:256], drb[:], ALU.mult, [un, "drb"], ["du"])
        yield
        TT_(s, "dve", yv[:], py[:, 0:256], du[:], ALU.add, ["pym", "du"], ["yv"])
        yield
        TT_(s, "pool", y2[:], yv[:], yv[:], ALU.mult, ["yv"], ["y2"])
        yield
        TS(s, "pool", y2[:], y2[:], 0.044715, ALU.mult, ["y2"], ["y2"], scalar2=1.0, op1=ALU.add)
        yield
        TT_(s, "pool", y2[:], y2[:], yv[:], ALU.mult, ["y2", "yv"], ["y2"])
        ACTF(s, sg[:], y2[:], AF.Sigmoid, ["y2"], ["sg"], scale=1.5957691216057308)
        yield
        TT_(s, "dve", gy[:], yv[:], sg[:], ALU.mult, ["yv", "sg"], ["gy"])
        yield
        for kc in range(2):
            TR(s, pmq[:, 256 + kc * 128:256 + (kc + 1) * 128], gy[:, kc * 128:(kc + 1) * 128], ident[:],
               ["gy", "ident"], ["pym"])
        CP(s, "act", gyT[:], pmq[:, 256:512].rearrange("p (k t) -> p k t", k=2), ["pym"], ["gyT"])
        for gc in range(4):
            for kc in range(2):
                MM(s, pglu[:, gc * 128:(gc + 1) * 128], wglu[:, kc, gc * 128:(gc + 1) * 128], gyT[:, kc, :],
                   ["wglu", "gyT"], ["pglu"], start=(kc == 0), stop=(kc == 1))
        yield
        ACTF(s, sgb[:], pglu[:, 256:512].rearrange("p (k t) -> p k t", k=2), AF.Sigmoid, ["pglu"], ["sgb"])
        yield
        YS = ysT[n % 2]
        TT_(s, "dve", YS[:], pglu[:, 0:256].rearrange("p (k t) -> p k t", k=2), sgb[:], ALU.mult, ["pglu", "sgb"],
            [f"ysT{n % 2}"])
        s.dma("pool", scr["yssmT"][q_, :, j * 128:(j + 1) * 128].rearrange("(kc p) t -> p kc t", p=128), YS[:],
              reads=[f"ysT{n % 2}"])
        yield
        for g in range(4):
            o = ppl[64 * (g % 2):64 * (g % 2) + 64, (g // 2) * 128:(g // 2 + 1) * 128]
            if j == 0:
                MM(s, o, UB[:, 256 + 64 * g:256 + 64 * g + 64], MT[:, g * 3 + 2, :], [ubn, "MT"], ["ppl"])
            else:
                MM(s, o, UB[:, 256 + 64 * g:256 + 64 * g + 64], MT[:, g * 3 + 0, :], [ubn, "MT"], ["ppl"],
                   start=True, stop=False)
                MM(s, o, UBP[:, 256 + 64 * g:256 + 64 * g + 64], MT[:, g * 3 + 1, :], [ubpn, "MT"], ["ppl"],
                   start=False, stop=True)
        CP(s, "act", plT[:], ppl[:, 0:256].rearrange("p (k t) -> p k t", k=2), ["ppl"], ["plT"])
        for g in range(4):
            pb = 64 * (g % 2)
            MM(s, ppl[pb:pb + 64, 256 + (g // 2) * 128:256 + (g // 2 + 1) * 128], PW[pb:pb + 64, g // 2, :],
               plT[pb:pb + 64, g // 2, :], ["PW", "plT"], ["ppl"])
        yield
        YP = ypT[n % 2]
        for kc in range(2):
            TS(s, "dve", YP[:, kc, :], ppl[:, 256 + kc * 128:256 + (kc + 1) * 128], pscol[:, kc:kc + 1], ALU.mult,
               ["ppl", "pscol"], [f"ypT{n % 2}"])
        s.dma("pool", scr["ypoolT"][q_, :, j * 128:(j + 1) * 128].rearrange("(kc p) t -> p kc t", p=128), YP[:],
              reads=[f"ypT{n % 2}"])

    load(0)
    if NCH > 1:
        load(1)
    def chain(*gs):
        for g in gs:
            yield from g

    for t in range(-1, NCH):
        if 0 <= t + 2 < NCH and t + 2 >= 2:
            load(t + 2)
        gens = []
        if t >= 0:
            gens.append(chain(XP(t), REST(t)))
        if t + 1 < NCH:
            gens.append(S1(t + 1))
        while gens:
            for g in list(gens):
                try:
                    next(g)
                except StopIteration:
                    gens.remove(g)
    c.end()


def build_test_a2(NSEQ, S, l=0):
    nc = bass.Bass("TRN2", target_bir_lowering=False)
    u = nc.dram_tensor("u_in", [NSEQ, S, 512], F32, kind="ExternalInput").ap()
    P = declare_params(nc)
    scr = make_scratch(nc, NSEQ, S, debug_out=True)
    scr["u"] = u
    with contextlib.ExitStack() as es:
        c = Ctx(nc, es)
        phase_a2(c, l, NSEQ, S, P, scr)
        print("ops", c.s.total, "waits", c.s.nwaits)
    return nc


def phase_c(c, l, NSEQ, S, x_d, xout_d, P, scr):
    nc, s = c.nc, c.s
    NJ = S // 128
    c.begin()
    wg = c.sb("wg", [128, 8, 3 * D], BF16)
    wba = c.sb("wba", [128, 4, D], BF16)
    wbs = c.sb("wbs", [128, 2, D], BF16)
    wbp = c.sb("wbp", [128, 2, D], BF16)
    wout = c.sb("wout", [128, 8, D], BF16)
    bg = c.sb("bg", [128, 3 * D], F32)
    NSTG = 4
    stg = [c.sb(f"stg{i}", [128, 1536], F32) for i in range(NSTG)]
    gcol = c.sb("gcol", [128, 8], F32)
    ident = c.sb("ident", [128, 128], BF16)
    xt = [c.sb(f"xt{i}", [128, D], F32) for i in range(6)]
    junk = c.sb("junk", [128, D], F32)
    ss = c.sb("ss", [128, 1], F32)
    rstd = c.sb("rstd", [128, 1], F32)
    hb = c.sb("hb", [128, D], BF16)
    hT = [c.sb(f"hT{i}", [128, 8, 128], BF16) for i in range(2)]
    gates_b = [c.sb(f"gates{i}", [128, 3 * D], F32) for i in range(2)]
    ya = [c.sb(f"ya{i}", [128, 512], BF16) for i in range(2)]
    yaT_b = [c.sb(f"yaT{i}", [128, 4, 128], BF16) for i in range(2)]
    ysT = [c.sb(f"ysT{i}", [128, 2, 128], BF16) for i in range(2)]
    ypT = [c.sb(f"ypT{i}", [128, 2, 128], BF16) for i in range(2)]
    macc = [c.sb(f"macc{i}", [128, 512], F32) for i in range(2)]
    mtmp = [c.sb(f"mtmp{i}", [128, 512], F32) for i in range(2)]
    mrg = [c.sb(f"mrg{i}", [128, D], BF16) for i in range(2)]
    mT_b = [c.sb(f"mT{i}", [128, 8, 128], BF16) for i in range(2)]
    pt = c.ps("pt")
    pg = [c.ps(f"pg{i}") for i in range(2)]
    pbr = [c.ps(f"pbr{i}") for i in range(3)]
    po = [c.ps(f"po{i}") for i in range(2)]

    make_identity(c, ident)
    s.dma("sp", gcol[:], P["mix_norm"][l].rearrange("(kc p) -> p kc", p=128), writes=["gcol"],
          allow_slow_non_contiguous=True)
    s.dma("act", bg[:], P["b_gate"][l:l + 1, :].partition_broadcast(128), writes=["bg"])
    n = 0
    dq = ("sp", "act", "pool")
    for kc in range(8):
        for hf in range(2):
            st, sn_ = stg[n % NSTG], f"stg{n % NSTG}"
            s.dma(dq[n % 3], st[:], P["w_in"][l, kc * 128:(kc + 1) * 128, 2056 + hf * 1536:2056 + (hf + 1) * 1536],
                  writes=[sn_])
            _cast(s, ("dve", "act")[n % 2], wg[:, kc, hf * 1536:(hf + 1) * 1536], st[:], [sn_, "gcol"],
                  [f"wg{kc}" if hf == 0 else f"wg{kc}b"], scalar=gcol[:, kc:kc + 1])
            n += 1
    for (wt, nm, nk) in ((wba, "w_br_attn", 4), (wbs, "w_br_ssm", 2), (wbp, "w_br_pool", 2), (wout, "w_out", 8)):
        for k0 in range(nk):
            st, sn_ = stg[n % NSTG], f"stg{n % NSTG}"
            s.dma(dq[n % 3], st[:, 0:D], P[nm][l, k0 * 128:(k0 + 1) * 128, :], writes=[sn_])
            _cast(s, ("dve", "act")[n % 2], wt[:, k0, :], st[:, 0:D], [sn_], [nm])
            n += 1

    NCH = NSEQ * NJ
    ptv = pt[:].bitcast(BF16)

    def load(n):
        q_, j = divmod(n, NJ)
        b = n % 2
        s.dma("sp", xt[n % 6][:], x_d[n * 128:(n + 1) * 128, :], writes=[f"xt{n % 6}"])

    def load_ya(n):
        q_, j = divmod(n, NJ)
        b = n % 2
        s.dma("sp", ya[b][:], scr["yattn"][q_, j * 128:(j + 1) * 128, :], writes=[f"ya{b}"])

    def load_ys(n):
        q_, j = divmod(n, NJ)
        b = n % 2
        s.dma("sp", ysT[b][:], scr["yssmT"][q_, :, j * 128:(j + 1) * 128].rearrange("(kc p) t -> p kc t", p=128),
              writes=[f"ysT{b}"])
        s.dma("sp", ypT[b][:], scr["ypoolT"][q_, :, j * 128:(j + 1) * 128].rearrange("(kc p) t -> p kc t", p=128),
              writes=[f"ypT{b}"])

    def s1(n):
        X, xb = xt[n % 6], f"xt{n % 6}"
        HT, htn = hT[n % 2], f"hT{n % 2}"
        rms_rstd(c, X[:], xb, junk[:], ss[:], rstd[:], D, "")
        yield
        TS(s, "dve", hb[:], X[:], rstd[:, 0:1], ALU.mult, [xb, "rstd"], ["hb"])
        yield
        for kc in range(8):
            TR(s, ptv[:, kc * 128:(kc + 1) * 128], hb[:, kc * 128:(kc + 1) * 128], ident[:], ["hb", "ident"], ["pt"])
        CP(s, "act", HT[:], ptv.rearrange("p (k t) -> p k t", k=8), ["pt"], [htn])
        yield

    def s2_gates(n):
        HT, htn = hT[n % 2], f"hT{n % 2}"
        gates = gates_b[n % 2]
        gp = f"g{n % 2}_"
        for gb in range(6):
            PG, pgn = pg[gb % 2], f"pg{gb % 2}"
            for kc in range(8):
                MM(s, PG[:], HT[:, kc, :], wg[:, kc, gb * 512:(gb + 1) * 512], [htn, f"wg{kc}" if gb < 3 else f"wg{kc}b"], [pgn],
                   start=(kc == 0), stop=(kc == 7))
                yield
            gsl = gates[:, gb * 512:(gb + 1) * 512]
            TT_(s, "dve", gsl, PG[:], bg[:, gb * 512:(gb + 1) * 512], ALU.add, [pgn, "bg"], [gp + f"gate{gb}"])
            yield
            ACTF(s, gsl, gsl, AF.Sigmoid, [gp + f"gate{gb}"], [gp + f"gate{gb}"])
            yield

    def s2_ya(n):
        b = n % 2
        for kc in range(4):
            TR(s, ptv[:, kc * 128:(kc + 1) * 128], ya[b][:, kc * 128:(kc + 1) * 128], ident[:], [f"ya{b}", "ident"],
               ["pt"])
        CP(s, "dve", yaT_b[b][:], ptv[:, 0:512].rearrange("p (k t) -> p k t", k=4), ["pt"], [f"yaT{b}"])
        yield

    def s2_branch(n, dbs):
        b = n % 2
        MR = mrg[b]
        gates = gates_b[n % 2]
        gp = f"g{n % 2}_"
        for db in dbs:
            cs = slice(db * 512, (db + 1) * 512)
            for kc in range(4):
                MM(s, pbr[0][:], yaT_b[b][:, kc, :], wba[:, kc, cs], [f"yaT{b}", "w_br_attn"], ["pbr0"], start=(kc == 0),
                   stop=(kc == 3))
                yield
            for kc in range(2):
                MM(s, pbr[1][:], ysT[b][:, kc, :], wbs[:, kc, cs], [f"ysT{b}", "w_br_ssm"], ["pbr1"], start=(kc == 0),
                   stop=(kc == 1))
                yield
            for kc in range(2):
                MM(s, pbr[2][:], ypT[b][:, kc, :], wbp[:, kc, cs], [f"ypT{b}", "w_br_pool"], ["pbr2"], start=(kc == 0),
                   stop=(kc == 1))
                yield
            MA, TM = macc[db], mtmp[db]
            TT_(s, "dve", MA[:], pbr[0][:], gates[:, db * 512:(db + 1) * 512], ALU.mult, ["pbr0", gp + f"gate{db}"],
                [f"macc{db}"])
            yield
            TT_(s, "dve", TM[:], pbr[1][:], gates[:, D + db * 512:D + (db + 1) * 512], ALU.mult,
                ["pbr1", gp + f"gate{2 + db}"], [f"mtmp{db}"])
            yield
            TT_(s, "pool", MA[:], MA[:], TM[:], ALU.add, [f"macc{db}", f"mtmp{db}"], [f"macc{db}"])
            yield
            TT_(s, "dve", TM[:], pbr[2][:], gates[:, 2 * D + db * 512:2 * D + (db + 1) * 512], ALU.mult,
                ["pbr2", gp + f"gate{4 + db}"], [f"mtmp{db}"])
            yield
            TT_(s, "pool", MR[:, cs], MA[:], TM[:], ALU.add, [f"macc{db}", f"mtmp{db}"], [f"mrg{b}_{db}"])
            yield

    def s3a(n):
        b = n % 2
        MR = mrg[b]
        for kc in range(8):
            TR(s, ptv[:, kc * 128:(kc + 1) * 128], MR[:, kc * 128:(kc + 1) * 128], ident[:],
               [f"mrg{b}_{kc // 4}", "ident"], ["pt"])
        CP(s, "act", mT_b[b][:], ptv.rearrange("p (k t) -> p k t", k=8), ["pt"], [f"mT{b}"])
        yield

    def s3b(n):
        b = n % 2
        X, xb = xt[n % 6], f"xt{n % 6}"
        for db in range(2):
            cs = slice(db * 512, (db + 1) * 512)
            for kc in range(8):
                MM(s, po[db][:], mT_b[b][:, kc, :], wout[:, kc, cs], [f"mT{b}", "w_out"], [f"po{db}"], start=(kc == 0),
                   stop=(kc == 7))
                yield
            TT_(s, "dve", X[:, cs], po[db][:], X[:, cs], ALU.add, [f"po{db}", xb], [xb])
            yield
        s.dma("pool", xout_d[n * 128:(n + 1) * 128, :], X[:], reads=[xb])
        yield

    def chain(*gs):
        for g in gs:
            yield from g

    def run(gens):
        gens = list(gens)
        while gens:
            for g in list(gens):
                try:
                    next(g)
                except StopIteration:
                    gens.remove(g)

    load(0)
    load_ya(0)
    load_ys(0)
    if NCH > 1:
        load(1)
    run([s1(0)])
    for t in range(NCH + 4):
        if t + 2 < NCH:
            load(t + 2)
        gens = []
        if t < NCH:
            gens.append(s2_gates(t))
        if t + 1 < NCH:
            gens.append(s1(t + 1))
        if t < NCH:
            gens.append(s2_ya(t))
        if 1 <= t <= NCH:
            gens.append(s2_branch(t - 1, [0, 1]))
        if 2 <= t <= NCH + 1:
            gens.append(s3a(t - 2))
        if 3 <= t <= NCH + 2:
            gens.append(s3b(t - 3))
        run(gens)
        if t + 1 < NCH:
            load_ya(t + 1)
            load_ys(t + 1)
    c.end()


def build_full(NSEQ, S, depth=DEPTH):
    T = NSEQ * S
    nc = bass.Bass("TRN2", target_bir_lowering=False)
    x = nc.dram_tensor("x", [T, D], F32, kind="ExternalInput").ap()
    y = nc.dram_tensor("y", [T, D], F32, kind="ExternalOutput").ap()
    P = declare_params(nc)
    scr = make_scratch(nc, NSEQ, S)
    xa = nc.dram_tensor("scr_xa", [T, D], F32).ap()
    with contextlib.ExitStack() as es:
        c = Ctx(nc, es)
        G = {"cumall": es.enter_context(nc.sbuf_tensor("cumall", [128, NSEQ, S // 128, 8], F32))}
        for l in range(depth):
            xin = x if l == 0 else xa
            phase_a(c, l, NSEQ, S, xin, P, G, scr)
            phase_a2(c, l, NSEQ, S, P, scr)
            phase_b(c, NSEQ, S, G, scr)
            phase_c(c, l, NSEQ, S, xin, xa, P, scr)
            phase_mlp(c, l, T, xa, P["w_up"], P["w_down"], P["mlp_norm"], xout_d=(y if l == depth - 1 else xa))
        print("ops", c.s.total, "waits", c.s.nwaits, flush=True)
    return nc


_NC_CACHE = {}


def kernel(**inputs):
    x = np.ascontiguousarray(np.asarray(inputs["x"], dtype=np.float32))
    B, S, _ = x.shape
    NSEQ = B // NCORES
    key = (NSEQ, S)
    if key not in _NC_CACHE:
        _NC_CACHE[key] = build_full(NSEQ, S)
    nc = _NC_CACHE[key]
    params = {k: np.ascontiguousarray(np.asarray(inputs[k], dtype=np.float32)) for k in PARAM_SHAPES}
    in_maps = []
    for cid in range(NCORES):
        m = {"x": x[cid * NSEQ:(cid + 1) * NSEQ].reshape(NSEQ * S, D)}
        m.update(params)
        in_maps.append(m)
    res = run_bass_kernel_spmd(nc, in_maps, core_ids=list(range(NCORES)))
    out = np.empty((B, S, D), dtype=np.float32)
    for cid in range(NCORES):
        out[cid * NSEQ:(cid + 1) * NSEQ] = np.asarray(res.results[cid]["y"]).reshape(NSEQ, S, D)
    return out
```

```python
import contextlib
import math
import numpy as np
import concourse.bass as bass
import concourse.mybir as mybir
from concourse.bass_utils import run_bass_kernel_spmd

F32 = mybir.dt.float32
BF16 = mybir.dt.bfloat16
ALU = mybir.AluOpType
AF = mybir.ActivationFunctionType
AX = mybir.AxisListType

D = 1024
DEPTH = 2
DFF = 4096
NH = 8
DH = 64
INC = 5128
EPS = 1e-6
NCORES = 8

ENGS = ("pe", "act", "dve", "pool", "sp")
N_DMA_SEMS = 12
SAME_ENGINE_SYNC = True


class Buf:
    __slots__ = ("name", "w", "r")

    def __init__(self, name):
        self.name = name
        self.w = None
        self.r = {}


class Op:
    __slots__ = ("eng", "idx", "fn", "deps", "dma", "needs_inc", "sem", "semval")

    def __init__(self, eng, idx, fn, deps, dma):
        self.eng, self.idx, self.fn, self.deps, self.dma = eng, idx, fn, deps, dma
        self.needs_inc = False
        self.sem = None
        self.semval = None


class Sched:
    def __init__(self, nc, es, same_engine_sync=SAME_ENGINE_SYNC):
        self.nc = nc
        self.q = {e: [] for e in ENGS}
        self.ndma = {e: 0 for e in ENGS}
        self.cnt = {e: 0 for e in ENGS}
        self.same_engine_sync = same_engine_sync
        self.bufs = {}
        self.esem = {e: es.enter_context(nc.semaphore("es_" + e)) for e in ENGS if e != "sp"}
        self.dsem = {}
        for e in ("sp", "act", "pool"):
            for j in range(N_DMA_SEMS):
                self.dsem[(e, j)] = es.enter_context(nc.semaphore(f"ds_{e}_{j}"))
        self.total = {e: 0 for e in ENGS}
        self.nwaits = {e: 0 for e in ENGS}

    def buf(self, name):
        b = self.bufs.get(name)
        if b is None:
            b = self.bufs[name] = Buf(name)
        return b

    def _b(self, x):
        return x if isinstance(x, Buf) else self.buf(x)

    def op(self, eng, fn, reads=(), writes=(), dma=False):
        reads = [self._b(x) for x in reads]
        writes = [self._b(x) for x in writes]
        deps = {}
        for b in reads:
            if b.w is not None:
                deps[id(b.w)] = b.w
        for b in writes:
            if b.w is not None:
                deps[id(b.w)] = b.w
            for o in b.r.values():
                deps[id(o)] = o
        o = Op(eng, len(self.q[eng]), fn, list(deps.values()), dma)
        if dma:
            i = self.ndma[eng]
            self.ndma[eng] += 1
            o.sem = (eng, i % N_DMA_SEMS)
            o.semval = 16 * (i // N_DMA_SEMS + 1)
        self.q[eng].append(o)
        for b in reads:
            key = ("dma", eng, o.idx) if dma else eng
            b.r[key] = o
        for b in writes:
            b.w = o
            b.r = {}
        for d in o.deps:
            if not d.dma:
                d.needs_inc = True
        return o

    def dma(self, eng, out, in_, reads=(), writes=(), **kw):
        return self.op(eng, lambda e: e.dma_start(out=out, in_=in_, **kw), reads, writes, dma=True)

    def emit(self):
        nc = self.nc
        for e in ENGS:
            c = self.cnt[e]
            for o in self.q[e]:
                if not o.dma and o.needs_inc:
                    c += 1
                    o.sem = e
                    o.semval = c
            self.cnt[e] = c

        def replay(e, engobj):
            known = {}
            for o in self.q[e]:
                waits = {}
                for d in o.deps:
                    if d.eng == e and not d.dma:
                        if e == "pe" or (not self.same_engine_sync and e != "pool"):
                            continue
                    k = d.sem
                    if waits.get(k, 0) < d.semval:
                        waits[k] = d.semval
                if o.dma and o.semval > 16:
                    k = o.sem
                    waits[k] = max(waits.get(k, 0), o.semval - 16)
                for k, v in waits.items():
                    if known.get(k, 0) < v:
                        known[k] = v
                        s = self.dsem[k] if isinstance(k, tuple) else self.esem[k]
                        engobj.wait_ge(s, v)
                        self.nwaits[e] += 1
                ins = o.fn(engobj)
                if o.dma:
                    ins.then_inc(self.dsem[o.sem], 16)
                elif o.needs_inc:
                    ins.then_inc(self.esem[e], 1)
            if self.ndma[e]:
                n = self.ndma[e]
                for j in range(min(N_DMA_SEMS, n)):
                    last = ((n - 1 - j) // N_DMA_SEMS) + 1
                    if known.get((e, j), 0) < 16 * last:
                        engobj.wait_ge(self.dsem[(e, j)], 16 * last)

        with nc.Block() as block:
            @block.sync
            def _(e):
                replay("sp", e)

            @block.tensor
            def _(e):
                replay("pe", e)

            @block.scalar
            def _(e):
                replay("act", e)

            @block.vector
            def _(e):
                replay("dve", e)

            @block.gpsimd
            def _(e):
                replay("pool", e)

        for e in ENGS:
            self.total[e] += len(self.q[e])
            self.q[e] = []
        for b in self.bufs.values():
            b.w = None
            b.r = {}


class Ctx:
    def __init__(self, nc, es):
        self.nc = nc
        self.es = es
        self.s = Sched(nc, es)
        self.pes = None
        self.uid = 0

    def begin(self):
        self.pes = contextlib.ExitStack()
        self.pes.__enter__()

    def end(self):
        self.s.emit()
        self.pes.close()
        self.pes = None

    def dbg(self, name, ap, reads):
        if not getattr(self, "debug", False):
            return
        d = self.nc.dram_tensor("dbg_" + name, list(ap.shape), ap.dtype, kind="ExternalOutput").ap()
        self.s.dma("sp", d, ap, reads=reads)

    def sb(self, name, shape, dtype):
        self.uid += 1
        return self.pes.enter_context(self.nc.sbuf_tensor(f"{name}_{self.uid}", list(shape), dtype))

    def ps(self, name, shape=(128, 512), dtype=F32):
        self.uid += 1
        return self.pes.enter_context(self.nc.psum_tensor(f"{name}_{self.uid}", list(shape), dtype))


def _cast_engine(i):
    return ("dve", "pool", "act")[i % 3]


def _cast(s, eng, out, in_, reads, writes, scalar=None):
    if scalar is None:
        if eng == "act":
            return s.op("act", lambda e: e.copy(out=out, in_=in_), reads, writes)
        return s.op(eng, lambda e: e.tensor_copy(out=out, in_=in_), reads, writes)
    if eng == "act":
        return s.op("act", lambda e: e.activation(out=out, in_=in_, func=AF.Copy, scale=scalar), reads, writes)
    return s.op(eng, lambda e: e.tensor_scalar(out=out, in0=in_, scalar1=scalar, scalar2=None, op0=ALU.mult),
                reads, writes)


def make_identity(c, ident):
    s = c.s
    s.op("pool", lambda e: e.memset(ident[:], 1.0), writes=["ident"])
    s.op("pool", lambda e: e.affine_select(out=ident[:], in_=ident[:], pattern=[[-1, 128]],
                                           compare_op=ALU.is_equal, fill=0.0, base=0, channel_multiplier=1),
         reads=["ident"], writes=["ident"])


def phase_mlp(c, l, T, x_d, w_up, w_down, mlp_norm, xout_d=None):
    nc, s = c.nc, c.s
    if xout_d is None:
        xout_d = x_d
    c.begin()
    TT = 256
    NT = T // TT
    wup = c.sb("wup", [128, 8, DFF], BF16)
    wdn = c.sb("wdn", [128, 32, D], BF16)
    NSTG = 4
    stg = [c.sb(f"stg{i}", [128, 1024], F32) for i in range(NSTG)]
    gcol = c.sb("gcol", [128, 8], F32)
    ident = c.sb("ident", [128, 128], BF16)
    xt = [c.sb(f"xt{i}", [128, 2, D], F32) for i in range(3)]
    hb = c.sb("hb", [128, 2, D], BF16)
    hT = [c.sb(f"hT{i}", [128, 8, TT], BF16) for i in range(2)]
    actT = c.sb("actT", [128, 32, TT], BF16)
    rl = [c.sb(f"rl{i}", [128, TT], F32) for i in range(2)]
    ss = c.sb("ss", [128, 2], F32)
    rstd = c.sb("rstd", [128, 2], F32)
    junk = c.sb("junk", [128, 2, D], BF16)
    pt = [c.ps(f"pt{i}") for i in range(2)]
    pu = [c.ps(f"pu{i}") for i in range(3)]
    pd = [c.ps(f"pd{i}") for i in range(3)]

    make_identity(c, ident)
    s.dma("sp", gcol[:], mlp_norm[l].rearrange("(kc p) -> p kc", p=128), writes=["gcol"],
          allow_slow_non_contiguous=True)
    n = 0
    dq = ("sp", "act", "pool")
    for kc in range(8):
        for qq in range(4):
            st, sn_ = stg[n % NSTG], f"stg{n % NSTG}"
            s.dma(dq[n % 3], st[:], w_up[l, kc * 128:(kc + 1) * 128, qq * 1024:(qq + 1) * 1024], writes=[sn_])
            _cast(s, ("dve", "act")[n % 2], wup[:, kc, qq * 1024:(qq + 1) * 1024], st[:],
                  [sn_, "gcol"], [f"wup{kc}_{qq}"], scalar=gcol[:, kc:kc + 1])
            n += 1
    for fc in range(32):
        st, sn_ = stg[n % NSTG], f"stg{n % NSTG}"
        s.dma(dq[n % 3], st[:], w_down[l, fc * 128:(fc + 1) * 128, :], writes=[sn_])
        _cast(s, ("dve", "act")[n % 2], wdn[:, fc, :], st[:], [sn_], [f"wdn{fc}"])
        n += 1

    def load(i):
        s.dma("sp", xt[i % 3][:], x_d[i * TT:(i + 1) * TT, :].rearrange("(a p) d -> p a d", p=128),
              writes=[f"xt{i % 3}"])

    def front_elem(i):
        X, xb = xt[i % 3], f"xt{i % 3}"
        for a in range(2):
            ACTF(s, junk[:, a, :], X[:, a, :], AF.Square, [xb], [f"junk{a}"])
        s.op("dve", lambda e: e.tensor_reduce(out=ss[:], in_=junk[:], axis=AX.X, op=ALU.add),
             reads=["junk0", "junk1"], writes=["ss"])
        ACTF(s, rstd[:], ss[:], AF.Sqrt, ["ss"], ["rstd"], scale=1.0 / D, bias=EPS)
        s.op("dve", lambda e: e.reciprocal(out=rstd[:], in_=rstd[:]), reads=["rstd"], writes=["rstd"])
        for a in range(2):
            TS(s, "dve", hb[:, a, :], X[:, a, :], rstd[:, a:a + 1], ALU.mult, [xb, "rstd"], [f"hb{a}"])
    def front_tr(i):
        HT, htn = hT[i % 2], f"hT{i % 2}"
        for a in range(2):
            pview = pt[a][:].bitcast(BF16)
            for kc in range(8):
                TR(s, pview[:, kc * 128:(kc + 1) * 128], hb[:, a, kc * 128:(kc + 1) * 128], ident[:],
                   [f"hb{a}", "ident"], [f"pt{a}"])
            CP(s, "dve" if a == 0 else "act", HT[:, :, a * 128:(a + 1) * 128],
               pview.rearrange("p (k t) -> p k t", k=8), [f"pt{a}"], [htn + f"_{a}"])

    def up(i):
        HT, htn = hT[i % 2], f"hT{i % 2}"
        for fc in range(32):
            Pb, pb = pu[fc % 3], f"pu{fc % 3}"
            for kc in range(8):
                MM(s, Pb[:, 0:TT], wup[:, kc, fc * 128:(fc + 1) * 128], HT[:, kc, :],
                   [f"wup{kc}_{fc // 8}", htn + "_0", htn + "_1"], [pb], start=(kc == 0), stop=(kc == 7))
            R, rb = rl[fc % 2], f"rl{fc % 2}"
            ACTF(s, R[:], Pb[:, 0:TT], AF.Relu, [pb], [rb])
            TT_(s, "pool", actT[:, fc, :], R[:], R[:], ALU.mult, [rb], ["actT"])

    def down(i):
        X, xb = xt[i % 3], f"xt{i % 3}"
        for a in range(2):
            for db in range(2):
                jj = a * 2 + db
                Pb, pb = pd[jj % 3], f"pd{jj % 3}"
                for fc in range(32):
                    MM(s, Pb[:], actT[:, fc, a * 128:(a + 1) * 128], wdn[:, fc, db * 512:(db + 1) * 512],
                       ["actT", f"wdn{fc}"], [pb], start=(fc == 0), stop=(fc == 31))
                TT_(s, "dve", X[:, a, db * 512:(db + 1) * 512], Pb[:], X[:, a, db * 512:(db + 1) * 512], ALU.add,
                    [pb, xb], [xb])
        s.dma("pool", xout_d[i * TT:(i + 1) * TT, :].rearrange("(a p) d -> p a d", p=128), X[:], reads=[xb])

    load(0)
    if NT > 1:
        load(1)
    front_elem(0)
    front_tr(0)
    for i in range(NT):
        if i + 2 < NT:
            load(i + 2)
        if i + 1 < NT:
            front_elem(i + 1)
        up(i)
        if i + 1 < NT:
            front_tr(i + 1)
        down(i)
    c.end()


def build_test_mlp(T):
    nc = bass.Bass("TRN2", target_bir_lowering=False)
    x = nc.dram_tensor("x", [T, D], F32, kind="ExternalInput").ap()
    w_up = nc.dram_tensor("w_up", [DEPTH, D, DFF], F32, kind="ExternalInput").ap()
    w_down = nc.dram_tensor("w_down", [DEPTH, DFF, D], F32, kind="ExternalInput").ap()
    mlp_norm = nc.dram_tensor("mlp_norm", [DEPTH, D], F32, kind="ExternalInput").ap()
    y = nc.dram_tensor("y", [T, D], F32, kind="ExternalOutput").ap()
    with contextlib.ExitStack() as es:
        c = Ctx(nc, es)
        c.debug = True
        phase_mlp(c, 0, T, x, w_up, w_down, mlp_norm, xout_d=y)
        print("ops", c.s.total, "waits", c.s.nwaits)
    return nc


def rms_rstd(c, X, xb, junk, ss, rstd, width, tag):
    s = c.s
    s.op("act", lambda e: e.activation(out=junk, in_=X, func=AF.Square), reads=[xb], writes=["junk" + tag])
    s.op("dve", lambda e: e.tensor_reduce(out=ss, in_=junk, axis=AX.X, op=ALU.add),
         reads=["junk" + tag], writes=["ss" + tag])
    s.op("act", lambda e: e.activation(out=rstd, in_=ss, func=AF.Sqrt, scale=1.0 / width, bias=EPS),
         reads=["ss" + tag], writes=["rstd" + tag])
    s.op("dve", lambda e: e.reciprocal(out=rstd, in_=rstd), reads=["rstd" + tag], writes=["rstd" + tag])


def make_tri(c, tri, name, dtype_one=1.0):
    s = c.s
    s.op("pool", lambda e: e.memset(tri[:], 1.0), writes=[name])
    s.op("pool", lambda e: e.affine_select(out=tri[:], in_=tri[:], pattern=[[1, 128]],
                                           compare_op=ALU.is_ge, fill=0.0, base=0, channel_multiplier=-1),
         reads=[name], writes=[name])


def phase_a(c, l, NSEQ, S, x_d, P, G, scr, do_ssm=True, do_pool=True):
    nc, s = c.nc, c.s
    NJ = S // 128
    c.begin()
    NA = 2056
    win = c.sb("win", [128, 8, NA], BF16)
    NSTG = 4
    stg = [c.sb(f"stg{i}", [128, NA // 2], F32) for i in range(NSTG)]
    gcol = c.sb("gcol", [128, 8], F32)
    ident = c.sb("ident", [128, 128], BF16)
    trif = c.sb("trif", [128, 128], F32)
    onesf = c.sb("onesf", [128, 128], F32)
    xt = [c.sb(f"xt{i}", [128, D], F32) for i in range(3)]
    junk = c.sb("junk", [128, D], F32)
    ss = c.sb("ss", [128, 1], F32)
    rstd = c.sb("rstd", [128, 1], F32)
    hb = c.sb("hb", [128, D], BF16)
    hT = [c.sb(f"hT{i}", [128, 8, 128], BF16) for i in range(2)]
    qkg = c.sb("qkg", [128, 2, 8, DH], F32)
    bfg = c.sb("bfg", [128, 8], F32)
    sq = [c.sb(f"sq{i}", [128, 512], F32) for i in range(2)]
    qe = [[c.sb(f"qe{i}_{b}", [128, 512], F32) for b in range(2)] for i in range(2)]
    ssq = [c.sb(f"ssq{i}", [128, 8], F32) for i in range(2)]
    rq = [c.sb(f"rq{i}", [128, 8], F32) for i in range(2)]
    qn = [c.sb(f"qn{i}", [128, 8, DH], F32) for i in range(2)]
    qa = [[c.sb(f"qa{i}_{b}", [128, 8, 70], BF16) for b in range(2)] for i in range(2)]
    r1 = c.sb("r1", [128, 8], F32)
    r2 = c.sb("r2", [128, 8], F32)
    qTs = [[c.sb(f"qTs{i}_{b}", [128, 8, 128], BF16) for b in range(2)] for i in range(2)]
    vst = [c.sb(f"vst{b}", [128, 8, 65], BF16) for b in range(2)]
    ust = [c.sb(f"ust{b}", [128, 512], F32) for b in range(2)]
    fls = [c.sb(f"fl{b}", [128, 8], F32) for b in range(2)]
    sp_ = c.sb("sp_", [128, 8], F32)
    carry = c.sb("carry", [128, 8], F32)
    cumall = G["cumall"]
    pt = c.ps("pt")
    pq = [c.ps(f"pq{i}") for i in range(2)]
    pv = c.ps("pv")
    pf = c.ps("pf")
    pp = c.ps("pp")
    ptq = [c.ps(f"ptq{i}") for i in range(2)]

    make_identity(c, ident)
    make_tri(c, trif, "trif")
    s.op("pool", lambda e: e.memset(onesf[:], 1.0), writes=["onesf"])
    for b in range(2):
        s.op("pool", lambda e, b=b: e.memset(vst[b][:], 1.0), writes=[f"vst{b}"])
    s.dma("sp", gcol[:], P["mix_norm"][l].rearrange("(kc p) -> p kc", p=128), writes=["gcol"],
          allow_slow_non_contiguous=True)
    s.dma("sp", qkg[:, 0, 0, :], P["q_norm"][l:l + 1, :].partition_broadcast(128), writes=["qkg"])
    s.dma("sp", qkg[:, 1, 0, :], P["k_norm"][l:l + 1, :].partition_broadcast(128), writes=["qkg"])
    s.dma("sp", bfg[:], P["b_forget"][l:l + 1, :].partition_broadcast(128), writes=["bfg"])
    s.op("dve", lambda e: e.tensor_scalar(out=qkg[:, 0, 0, :], in0=qkg[:, 0, 0, :], scalar1=DH ** -0.5, scalar2=None,
                                          op0=ALU.mult), reads=["qkg"], writes=["qkg"])
    for h in range(1, 8):
        s.op("dve", lambda e, h=h: e.tensor_copy(out=qkg[:, :, h, :], in_=qkg[:, :, 0, :]),
             reads=["qkg"], writes=["qkg"])
    nn = 0
    dq = ("sp", "act", "pool")
    HN = NA // 2
    for kc in range(8):
        for hf in range(2):
            st, sn_ = stg[nn % NSTG], f"stg{nn % NSTG}"
            s.dma(dq[nn % 3], st[:], P["w_in"][l, kc * 128:(kc + 1) * 128, hf * HN:(hf + 1) * HN], writes=[sn_])
            _cast(s, ("dve", "act")[nn % 2], win[:, kc, hf * HN:(hf + 1) * HN], st[:], [sn_, "gcol"], [f"win{kc}"],
                  scalar=gcol[:, kc:kc + 1])
            nn += 1

    def load(n):
        s.dma("sp", xt[n % 3][:], x_d[n * 128:(n + 1) * 128, :], writes=[f"xt{n % 3}"])

    NCH = NSEQ * NJ
    ptv = pt[:].bitcast(BF16)
    pc = pf[:, 272:288]
    for i in range(2):
        for b in range(2):
            s.op("pool", lambda e, i=i, b=b: e.memset(qa[i][b][:], 1.0), writes=[f"qg{i}_{b}", f"qaug{i}_{b}"])

    def front_elem(n):
        X, xb = xt[n % 3], f"xt{n % 3}"
        ACTF(s, junk[:], X[:], AF.Square, [xb], ["junk"])
        s.op("dve", lambda e: e.tensor_reduce(out=ss[:], in_=junk[:], axis=AX.X, op=ALU.add), reads=["junk"],
             writes=["ss"])
        ACTF(s, rstd[:], ss[:], AF.Ln, ["ss"], ["rstd"], scale=1.0 / D, bias=EPS)
        ACTF(s, rstd[:], rstd[:], AF.Exp, ["rstd"], ["rstd"], scale=-0.5)
        TS(s, "dve", hb[:], X[:], rstd[:, 0:1], ALU.mult, [xb, "rstd"], ["hb"])

    def front_tr(n):
        HT, htn = hT[n % 2], f"hT{n % 2}"
        for kc in range(8):
            TR(s, ptv[:, kc * 128:(kc + 1) * 128], hb[:, kc * 128:(kc + 1) * 128], ident[:], ["hb", "ident"], ["pt"])
        CP(s, "act", HT[:], ptv.rearrange("p (k t) -> p k t", k=8), ["pt"], [htn])

    def proj_all(n):
        HT, htn = hT[n % 2], f"hT{n % 2}"

        def proj(Pt, pname, c0, c1):
            for kc in range(8):
                MM(s, Pt, HT[:, kc, :], win[:, kc, c0:c1], [htn, f"win{kc}"], [pname], start=(kc == 0), stop=(kc == 7))
        proj(pq[0][:], "pq0", 0, 512)
        proj(pq[1][:], "pq1", 512, 1024)
        proj(pv[:], "pv", 1024, 1536)
        proj(pf[:, 0:264], "pf", 1536, 1800)
        proj(pp[:, 0:256], "pp", 1800, 2056)

    def evac(n):
        b = n % 2
        CP(s, "act", qe[0][b][:], pq[0][:], ["pq0"], [f"qe0_{b}"])
        CP(s, "dve", qe[1][b][:], pq[1][:], ["pq1"], [f"qe1_{b}"])
        VS = vst[b]
        CP(s, "act", VS[:, :, 0:64], pv[:].rearrange("p (h d) -> p h d", h=8), ["pv"], [f"vst{b}"])
        US = ust[b]
        CP(s, "dve", US[:, 0:256], pf[:, 8:264], ["pf"], [f"ust{b}"])
        TT_(s, "dve", fls[b][:], pf[:, 0:8], bfg[:], ALU.add, ["pf", "bfg"], [f"fl{b}"])
        CP(s, "act", US[:, 256:512], pp[:, 0:256], ["pp"], [f"ust{b}"])

    def forget(n):
        q_, j = divmod(n, NJ)
        b = n % 2
        fl = fls[b]
        ACTF(s, fl[:], fl[:], AF.Exp, [f"fl{b}"], [f"fl{b}"], scale=-1.0)
        ACTF(s, sp_[:], fl[:], AF.Ln, [f"fl{b}"], ["sp_"], scale=1.0, bias=1.0)
        if j == 0:
            s.op("pool", lambda e: e.memset(carry[:], 0.0), writes=["carry"])
        MM(s, pc[:, 0:8], trif[:], sp_[:], ["trif", "sp_"], ["pf"])
        MM(s, pc[:, 8:16], onesf[:], sp_[:], ["onesf", "sp_"], ["pf"])
        cum = cumall[:, q_, j, :]
        TT_(s, "dve", cum, carry[:], pc[:, 0:8], ALU.subtract, ["carry", "pf"], ["cumall"])
        TT_(s, "dve", carry[:], carry[:], pc[:, 8:16], ALU.subtract, ["carry", "pf"], ["carry"])

    def post_a(n):
        q_, j = divmod(n, NJ)
        b = n % 2
        VS, US = vst[b], ust[b]
        cum = cumall[:, q_, j, :]
        QA, KA = qa[0][b], qa[1][b]
        CP(s, "dve", QA[:, :, 67], cum, ["cumall"], [f"qaug0_{b}"])
        yield
        TT_(s, "dve", r1[:], cum, QA[:, :, 67], ALU.subtract, ["cumall", f"qaug0_{b}"], ["r1"])
        yield
        CP(s, "dve", QA[:, :, 68], r1[:], ["r1"], [f"qaug0_{b}"])
        yield
        TT_(s, "dve", r2[:], r1[:], QA[:, :, 68], ALU.subtract, ["r1", f"qaug0_{b}"], ["r2"])
        yield
        CP(s, "dve", QA[:, :, 69], r2[:], ["r2"], [f"qaug0_{b}"])
        yield
        TS(s, "dve", KA[:, :, 64:67], QA[:, :, 67:70], -1.0, ALU.mult, [f"qaug0_{b}"], [f"qaug1_{b}"])
        yield
        for i in range(2):
            PQ = qe[i][b]
            ACTF(s, sq[i][:], PQ[:], AF.Square, [f"qe{i}_{b}"], [f"sq{i}"])
            yield
            s.op("dve", lambda e, i=i: e.tensor_reduce(out=ssq[i][:], in_=sq[i][:].rearrange("p (h d) -> p h d", h=8),
                                                       axis=AX.X, op=ALU.add),
                 reads=[f"sq{i}"], writes=[f"ssq{i}"])
            yield
            ACTF(s, rq[i][:], ssq[i][:], AF.Ln, [f"ssq{i}"], [f"rq{i}"], scale=1.0 / DH, bias=EPS)
            yield
            ACTF(s, rq[i][:], rq[i][:], AF.Exp, [f"rq{i}"], [f"rq{i}"], scale=-0.5)
            yield
            TT_(s, "dve", qn[i][:], PQ[:].rearrange("p (h d) -> p h d", h=8),
                rq[i][:].unsqueeze(2).broadcast_to([128, 8, DH]), ALU.mult, [f"qe{i}_{b}", f"rq{i}"], [f"qn{i}"])
            yield
            TT_(s, "dve", qa[i][b][:, :, 0:64], qn[i][:], qkg[:, i, :, :], ALU.mult, [f"qn{i}", "qkg"],
                [f"qg{i}_{b}"])
            yield
        s.dma("pool", scr["v"][q_, j * 128:(j + 1) * 128, :, :], VS[:], reads=[f"vst{n % 2}"])
        yield
        s.dma("pool", scr["u"][q_, j * 128:(j + 1) * 128, :], US[:], reads=[f"ust{n % 2}"])
        yield

    def post_b(n):
        q_, j = divmod(n, NJ)
        b = n % 2
        for i in range(2):
            pvw = ptq[i][:].bitcast(BF16)
            for h in range(8):
                TR(s, pvw[0:70, h * 128:(h + 1) * 128], qa[i][b][:, h, :], ident[:],
                   [f"qg{i}_{b}", f"qaug{i}_{b}", "ident"], [f"ptq{i}"])
            QT = qTs[i][n % 2]
            CP(s, "act" if i == 0 else "dve", QT[0:70, :, :], pvw[0:70, :].rearrange("p (h t) -> p h t", h=8),
               [f"ptq{i}"], [f"qTs{i}_{n % 2}"])
            yield
            dst = scr["qT" if i == 0 else "kT"]
            s.dma("pool", dst[q_, :, :, j * 128:(j + 1) * 128].rearrange("h p t -> p h t"), QT[0:70, :, :],
                  reads=[f"qTs{i}_{n % 2}"])
            yield

    load(0)
    if NCH > 1:
        load(1)
    def main_stream(t):
        if t + 2 < NCH:
            load(t + 2)
        if t < NCH:
            front_elem(t)
        yield
        if 1 <= t <= NCH:
            HT, htn = hT[(t - 1) % 2], f"hT{(t - 1) % 2}"
            for (Pt, pname, c0, c1) in ((pq[0][:], "pq0", 0, 512), (pq[1][:], "pq1", 512, 1024),
                                        (pv[:], "pv", 1024, 1536), (pf[:, 0:264], "pf", 1536, 1800),
                                        (pp[:, 0:256], "pp", 1800, 2056)):
                for kc in range(8):
                    MM(s, Pt, HT[:, kc, :], win[:, kc, c0:c1], [htn, f"win{kc}"], [pname], start=(kc == 0),
                       stop=(kc == 7))
                    if kc % 2 == 1:
                        yield
        if t < NCH:
            front_tr(t)
        yield
        if 1 <= t <= NCH:
            evac(t - 1)
            forget(t - 1)
        yield

    for t in range(NCH + 3):
        gens = [main_stream(t)]
        if 2 <= t < NCH + 2:
            gens.append(post_a(t - 2))
        if t >= 3:
            gens.append(post_b(t - 3))
        while gens:
            for g in list(gens):
                try:
                    next(g)
                except StopIteration:
                    gens.remove(g)
    c.end()


def phase_b(c, NSEQ, S, G, scr):
    nc, s = c.nc, c.s
    NJ = S // 128
    NI = NJ // 4
    c.begin()
    tri = c.sb("tri", [128, 128], BF16)
    vall = [c.sb(f"vall{i}", [128, NJ, 8, 65], BF16) for i in range(2)]
    qT = [c.sb(f"qT{i}", [128, S], BF16) for i in range(2)]
    kT = [c.sb(f"kT{i}", [128, S], BF16) for i in range(2)]
    NPT = 4
    pT = [c.sb(f"pT{i}", [128, 512], BF16) for i in range(NPT)]
    ybuf = [c.sb(f"ybuf{i}", [128, NJ, 64], BF16) for i in range(2)]
    rc = [c.sb(f"rc{i}", [128, 1], F32) for i in range(4)]
    NPS = 4
    pS = [c.ps(f"pS{i}") for i in range(NPS)]
    pO = [c.ps(f"pO{i}") for i in range(4)]
    make_tri(c, tri, "tri")
    LOOK = 3
    blocks = []
    for q_ in range(NSEQ):
        for h in range(8):
            for i4 in range(NI):
                for j in range(4 * i4 + 4):
                    blocks.append((q_, h, i4, j))

    def stage1(bi):
        q_, h, i4, j = blocks[bi]
        g = q_ * 8 + h
        QT, KT = qT[g % 2], kT[g % 2]
        if h == 0 and i4 == 0 and j == 0 and q_ == 0:
            s.dma("sp", vall[q_ % 2][:], scr["v"][q_].rearrange("(j p) h d -> p j h d", p=128), writes=[f"vall{q_ % 2}"])
        if h == 4 and i4 == 0 and j == 0 and q_ + 1 < NSEQ:
            s.dma("act", vall[(q_ + 1) % 2][:], scr["v"][q_ + 1].rearrange("(j p) h d -> p j h d", p=128),
                  writes=[f"vall{(q_ + 1) % 2}"])
        if i4 == 0 and j == 0:
            if g == 0:
                s.dma("sp", QT[0:70, :], scr["qT"][q_, h], writes=[f"qT{g % 2}"])
                s.dma("act", KT[0:70, :], scr["kT"][q_, h], writes=[f"kT{g % 2}"])
            g2 = g + 1
            if g2 < NSEQ * 8:
                q2, h2 = divmod(g2, 8)
                s.dma("sp", qT[g2 % 2][0:70, :], scr["qT"][q2, h2], writes=[f"qT{g2 % 2}"])
                s.dma("sp", kT[g2 % 2][0:70, :], scr["kT"][q2, h2], writes=[f"kT{g2 % 2}"])
        jj = j - 4 * i4
        c0 = 128 * max(jj, 0)
        PS, psn = pS[bi % NPS], f"pS{bi % NPS}"
        PT, ptn = pT[bi % NPT], f"pT{bi % NPT}"
        MM(s, PS[:, c0:512], KT[0:70, j * 128:(j + 1) * 128], QT[0:70, i4 * 512 + c0:(i4 + 1) * 512],
           [f"qT{g % 2}", f"kT{g % 2}"], [psn])
        ACTF(s, PT[:, c0:512], PS[:, c0:512], AF.Exp, [psn], [ptn])
        if jj >= 0:
            TT_(s, "pool", PT[:, c0:c0 + 128], PT[:, c0:c0 + 128], tri[:], ALU.mult, [ptn, "tri"], [ptn])

    def stage2(bi):
        q_, h, i4, j = blocks[bi]
        g = q_ * 8 + h
        PT, ptn = pT[bi % NPT], f"pT{bi % NPT}"
        YB = ybuf[g % 2]
        jj = j - 4 * i4
        for tt in range(max(jj, 0), 4):
            last = (j == 4 * i4 + tt)
            MM(s, pO[tt][:, 0:65], PT[:, tt * 128:(tt + 1) * 128], vall[q_ % 2][:, j, h, :], [ptn, f"vall{q_ % 2}"],
               [f"pO{tt}"], start=(j == 0), stop=last)
            if last:
                s.op("dve", lambda e, tt=tt: e.reciprocal(out=rc[tt][:], in_=pO[tt][:, 64:65]), reads=[f"pO{tt}"],
                     writes=[f"rc{tt}"])
                TS(s, "dve", YB[:, 4 * i4 + tt, :], pO[tt][:, 0:64], rc[tt][:, 0:1], ALU.mult, [f"pO{tt}", f"rc{tt}"],
                   [f"ybuf{g % 2}"])
        if i4 == NI - 1 and j == NJ - 1:
            s.dma("pool", scr["yattn"][q_, :, h * 64:(h + 1) * 64].rearrange("(j p) c -> p j c", p=128), YB[:],
                  reads=[f"ybuf{g % 2}"])

    nb = len(blocks)
    for t in range(nb + LOOK):
        if t < nb:
            stage1(t)
        if t >= LOOK:
            stage2(t - LOOK)
    c.end()


def make_scratch(nc, NSEQ, S, debug_out=False):
    kind = {"kind": "ExternalOutput"} if debug_out else {}
    scr = {}
    scr["qT"] = nc.dram_tensor("scr_qT", [NSEQ, 8, 70, S], BF16).ap()
    scr["kT"] = nc.dram_tensor("scr_kT", [NSEQ, 8, 70, S], BF16).ap()
    scr["v"] = nc.dram_tensor("scr_v", [NSEQ, S, 8, 65], BF16).ap()
    scr["cend"] = nc.dram_tensor("scr_cend", [1, NSEQ, S // 128, 8], F32).ap()
    scr["u"] = nc.dram_tensor("scr_u", [NSEQ, S, 512], F32).ap()
    scr["yattn"] = nc.dram_tensor("scr_yattn", [NSEQ, S, 512], BF16, **kind).ap()
    scr["yssmT"] = nc.dram_tensor("scr_yssmT", [NSEQ, 256, S], BF16, **kind).ap()
    scr["ypoolT"] = nc.dram_tensor("scr_ypoolT", [NSEQ, 256, S], BF16, **kind).ap()
    return scr


PARAM_SHAPES = {
    "mix_norm": [DEPTH, D], "w_in": [DEPTH, D, INC], "b_forget": [DEPTH, NH], "q_norm": [DEPTH, DH],
    "k_norm": [DEPTH, DH], "ssm_a_re": [DEPTH, 16, 64], "ssm_a_im": [DEPTH, 16, 64], "ssm_log_dt": [DEPTH, 16],
    "ssm_b_re": [DEPTH, 16, 64, 16], "ssm_b_im": [DEPTH, 16, 64, 16], "ssm_c_re": [DEPTH, 16, 16, 64],
    "ssm_c_im": [DEPTH, 16, 16, 64], "ssm_d": [DEPTH, 256], "w_glu": [DEPTH, 256, 512],
    "pool_w": [DEPTH, 4, 64, 64], "pool_scale": [DEPTH, 256], "w_br_attn": [DEPTH, 512, D],
    "w_br_ssm": [DEPTH, 256, D], "w_br_pool": [DEPTH, 256, D], "b_gate": [DEPTH, 3 * D],
    "w_out": [DEPTH, D, D], "mlp_norm": [DEPTH, D], "w_up": [DEPTH, D, DFF], "w_down": [DEPTH, DFF, D],
}


def declare_params(nc):
    return {k: nc.dram_tensor(k, shp, F32, kind="ExternalInput").ap() for k, shp in PARAM_SHAPES.items()}


def build_test_ab(NSEQ, S, l=0):
    nc = bass.Bass("TRN2", target_bir_lowering=False)
    x = nc.dram_tensor("x", [NSEQ * S, D], F32, kind="ExternalInput").ap()
    P = declare_params(nc)
    scr = make_scratch(nc, NSEQ, S, debug_out=True)
    with contextlib.ExitStack() as es:
        c = Ctx(nc, es)
        G = {"cumall": es.enter_context(nc.sbuf_tensor("cumall", [128, NSEQ, S // 128, 8], F32))}
        phase_a(c, l, NSEQ, S, x, P, G, scr, do_ssm=False, do_pool=False)
        phase_b(c, NSEQ, S, G, scr)
        print("ops", c.s.total, "waits", c.s.nwaits)
    return nc


def TT_(s, eng, out, in0, in1, op, reads, writes):
    return s.op(eng, lambda e: e.tensor_tensor(out=out, in0=in0, in1=in1, op=op), reads, writes)


def TS(s, eng, out, in0, scalar1, op0, reads, writes, scalar2=None, op1=None):
    if op1 is None:
        return s.op(eng, lambda e: e.tensor_scalar(out=out, in0=in0, scalar1=scalar1, scalar2=None, op0=op0),
                    reads, writes)
    return s.op(eng, lambda e: e.tensor_scalar(out=out, in0=in0, scalar1=scalar1, scalar2=scalar2, op0=op0, op1=op1),
                reads, writes)


def ACTF(s, out, in_, func, reads, writes, scale=1.0, bias=None):
    if bias is None:
        return s.op("act", lambda e: e.activation(out=out, in_=in_, func=func, scale=scale), reads, writes)
    return s.op("act", lambda e: e.activation(out=out, in_=in_, func=func, scale=scale, bias=bias), reads, writes)


def CP(s, eng, out, in_, reads, writes):
    if eng == "act":
        return s.op("act", lambda e: e.copy(out=out, in_=in_), reads, writes)
    return s.op(eng, lambda e: e.tensor_copy(out=out, in_=in_), reads, writes)


def MM(s, out, lhsT, rhs, reads, writes, start=True, stop=True):
    return s.op("pe", lambda e: e.matmul(out, lhsT=lhsT, rhs=rhs, start=start, stop=stop), reads, writes)


def TR(s, out, in_, ident, reads, writes):
    return s.op("pe", lambda e: e.transpose(out=out, in_=in_, identity=ident), reads, writes)


TWO_PI = 2.0 * math.pi
MAGIC = 12582912.0


def sincos(s, eng, ang, o_sin, o_cos, t1, t2, rd, tag):
    n1, n2 = "sc1" + tag, "sc2" + tag
    for which, o in (("s", o_sin), ("c", o_cos)):
        if which == "s":
            TS(s, eng, t1, ang, 1.0 / TWO_PI, ALU.mult, rd, [n1])
        else:
            TS(s, eng, t1, ang, 1.0 / TWO_PI, ALU.mult, rd, [n1], scalar2=0.25, op1=ALU.add)
        TS(s, eng, t2, t1, MAGIC, ALU.add, [n1], [n2])
        TS(s, eng, t2, t2, MAGIC, ALU.subtract, [n2], [n2])
        TT_(s, eng, t2, t1, t2, ALU.subtract, [n1, n2], [n2])
        ACTF(s, o, t2, AF.Sin, [n2], [("sin" if which == "s" else "cos") + tag], scale=TWO_PI * (1 - 1e-6))


POOL_WINDOWS = (2, 4, 8, 16)


def phase_a2(c, l, NSEQ, S, P, scr):
    nc, s = c.nc, c.s
    NJ = S // 128
    c.begin()
    I32 = mybir.dt.int32
    ident = c.sb("ident", [128, 128], BF16)
    identf = c.sb("identf", [128, 128], F32)
    tribf = c.sb("tribf", [128, 128], BF16)
    iot_i = c.sb("iot_i", [128, 128], I32)
    iot = c.sb("iot", [128, 128], F32)
    pcol_i = c.sb("pcol_i", [128, 1], I32)
    pcol = c.sb("pcol", [128, 1], F32)
    npcol = c.sb("npcol", [128, 1], F32)
    are = c.sb("are", [128, 1024], F32)
    aim = c.sb("aim", [128, 1024], F32)
    ldt = c.sb("ldt", [128, 16], F32)
    dtr = c.sb("dtr", [128, 1024], F32)
    t1 = c.sb("t1", [128, 1024], F32)
    t2 = c.sb("t2", [128, 1024], F32)
    t3 = c.sb("t3", [128, 1024], F32)
    t4 = c.sb("t4", [128, 1024], F32)
    t5 = c.sb("t5", [128, 1024], F32)
    t6 = c.sb("t6", [128, 1024], F32)
    cfr = c.sb("cfr", [128, 1024], F32)
    cfi = c.sb("cfi", [128, 1024], F32)
    Tr = c.sb("Tr", [128, 1024], F32)
    Ti = c.sb("Ti", [128, 1024], F32)
    TAr = c.sb("TAr", [128, 8, 128], F32)
    TAi = c.sb("TAi", [128, 8, 128], F32)
    acol = c.sb("acol", [128, 3, 8], F32)
    BX = c.sb("BX", [128, 8, 2, 128], F32)
    BXb = c.sb("BXb", [128, 8, 2, 128], BF16)
    BT = c.sb("BT", [128, 8, 2, 128], BF16)
    CN = c.sb("CN", [32, 8, 2, 128], F32)
    CNb = c.sb("CNb", [32, 8, 2, 128], BF16)
    CX = c.sb("CX", [128, 8, 2, 32], BF16)
    drb = c.sb("drb", [128, 256], F32)
    wg_st = c.sb("wg_st", [128, 2, 512], F32)
    wglu = c.sb("wglu", [128, 2, 512], BF16)
    pw_st = c.sb("pw_st", [128, 2, 64], F32)
    PW = c.sb("PW", [128, 2, 64], BF16)
    pscol = c.sb("pscol", [128, 2], F32)
    MT = c.sb("MT", [128, 12, 128], BF16)
    mtmp = c.sb("mtmp", [128, 128], F32)
    mrat = c.sb("mrat", [128, 128], F32)
    ut = [c.sb(f"ut{i}", [128, 512], F32) for i in range(3)]
    ub = [c.sb(f"ub{i}", [128, 512], BF16) for i in range(3)]
    uT = c.sb("uT", [128, 2, 128], BF16)
    m1 = c.sb("m1", [128, 4, 128], F32)
    m2 = c.sb("m2", [128, 4, 128], F32)
    m3 = c.sb("m3", [128, 4, 128], F32)
    m4 = c.sb("m4", [128, 4, 128], F32)
    Wt = c.sb("Wt", [128, 8, 2, 128], BF16)
    Pr = c.sb("Pr", [128, 8, 128], F32)
    Pi = c.sb("Pi", [128, 8, 128], F32)
    Xt = c.sb("Xt", [128, 8, 2, 128], BF16)
    car = c.sb("car", [128, 2, 8], F32)
    sn = c.sb("sn", [128, 2, 4], F32)
    sm = c.sb("sm", [128, 4, 4], F32)
    du = c.sb("du", [128, 256], F32)
    yv = c.sb("yv", [128, 256], F32)
    y2 = c.sb("y2", [128, 256], F32)
    sg = c.sb("sg", [128, 256], F32)
    gy = c.sb("gy", [128, 256], BF16)
    gyT = c.sb("gyT", [128, 2, 128], BF16)
    sgb = c.sb("sgb", [128, 2, 128], F32)
    ysT = [c.sb(f"ysT{i}", [128, 2, 128], BF16) for i in range(2)]
    plT = c.sb("plT", [128, 2, 128], BF16)
    ypT = [c.sb(f"ypT{i}", [128, 2, 128], BF16) for i in range(2)]
    pbu = c.ps("pbu")
    ppf = c.ps("ppf", [128, 2048])
    pym = c.ps("pym")
    pglu = c.ps("pglu")
    ppl = c.ps("ppl")
    py = pym
    w1 = c.sb("w1", [128, 2, 128], F32)
    w2 = c.sb("w2", [128, 2, 128], F32)
    w3 = c.sb("w3", [128, 2, 128], F32)
    w4 = c.sb("w4", [128, 2, 128], F32)

    make_identity(c, ident)
    make_tri(c, tribf, "tribf")
    s.op("pool", lambda e: e.iota(iot_i[:], pattern=[[1, 128]], base=0, channel_multiplier=0), writes=["iot_i"])
    CP(s, "dve", iot[:], iot_i[:], ["iot_i"], ["iot"])
    s.op("pool", lambda e: e.iota(pcol_i[:], pattern=[[0, 1]], base=0, channel_multiplier=1), writes=["pcol_i"])
    CP(s, "dve", pcol[:], pcol_i[:], ["pcol_i"], ["pcol"])
    TS(s, "dve", npcol[:], pcol[:], -1.0, ALU.mult, ["pcol"], ["npcol"])
    CP(s, "dve", identf[:], ident[:], ["ident"], ["identf"])

    s.dma("sp", are[:], P["ssm_a_re"][l:l + 1].rearrange("o g n -> o (g n)").partition_broadcast(128), writes=["are"])
    s.dma("act", aim[:], P["ssm_a_im"][l:l + 1].rearrange("o g n -> o (g n)").partition_broadcast(128), writes=["aim"])
    s.dma("sp", ldt[:], P["ssm_log_dt"][l:l + 1, :].partition_broadcast(128), writes=["ldt"])
    s.dma("sp", acol[:, 0, :], P["ssm_a_re"][l].rearrange("(k gl) n -> (gl n) k", gl=2), writes=["acol"],
          allow_slow_non_contiguous=True)
    s.dma("sp", acol[:, 1, :], P["ssm_a_im"][l].rearrange("(k gl) n -> (gl n) k", gl=2), writes=["acol"],
          allow_slow_non_contiguous=True)
    for gl in range(2):
        s.dma("sp", acol[64 * gl:64 * gl + 64, 2, :],
              P["ssm_log_dt"][l:l + 1, :].rearrange("o (k gl) -> o k gl", gl=2)[:, :, gl].partition_broadcast(64),
              writes=["acol"], allow_slow_non_contiguous=True)
    s.dma("sp", drb[:], P["ssm_d"][l:l + 1, :].partition_broadcast(128), writes=["drb"])
    s.dma("act", wg_st[:], P["w_glu"][l].rearrange("(kc p) g -> p kc g", p=128), writes=["wg_st"])
    CP(s, "pool", wglu[:], wg_st[:], ["wg_st"], ["wglu"])
    for g in range(4):
        s.dma("sp", pw_st[64 * (g % 2):64 * (g % 2) + 64, g // 2, :], P["pool_w"][l, g], writes=["pw_st"])
    CP(s, "pool", PW[:], pw_st[:], ["pw_st"], ["PW"])
    s.dma("sp", pscol[:], P["pool_scale"][l].rearrange("(kc p) -> p kc", p=128), writes=["pscol"],
          allow_slow_non_contiguous=True)
    s.op("pool", lambda e: e.memset(BX[:], 0.0), writes=["BX"])
    s.op("pool", lambda e: e.memset(CN[:], 0.0), writes=["CN"])
    nd = 0
    for k in range(8):
        for gl in range(2):
            g = 2 * k + gl
            c0 = 32 * (k % 4) + 16 * gl
            for part, nm in ((0, "ssm_b_re"), (1, "ssm_b_im")):
                s.dma("sp" if nd % 2 == 0 else "act", BX[64 * gl:64 * gl + 64, k, part, c0:c0 + 16], P[nm][l, g],
                      reads=[], writes=["BX"])
                nd += 1
            for part, nm in ((0, "ssm_c_re"), (1, "ssm_c_im")):
                s.dma("sp" if nd % 2 == 0 else "act", CN[16 * gl:16 * gl + 16, k, part, 64 * gl:64 * gl + 64],
                      P[nm][l, g], reads=[], writes=["CN"])
                nd += 1
    CP(s, "dve", BXb[:], BX[:], ["BX"], ["BXb"])
    CP(s, "dve", CNb[:], CN[:], ["CN"], ["CNb"])
    pmv = ppf[:, 0:512].bitcast(BF16)
    for k in range(8):
        for part in range(2):
            i = (k * 2 + part) % 8
            TR(s, pmv[:, i * 128:(i + 1) * 128], BXb[:, k, part, :], ident[:], ["BXb", "ident"], ["ppf"])
        if k % 4 == 3:
            k0 = k - 3
            CP(s, "dve", BT[:, k0:k0 + 4, :, :], pmv.rearrange("p (k a m) -> p k a m", k=4, a=2), ["ppf"], ["BT"])
    for k in range(8):
        for part in range(2):
            i = k * 2 + part
            TR(s, pmv[:, i * 32:(i + 1) * 32], CNb[:, k, part, :], ident[0:32, 0:32], ["CNb", "ident"], ["ppf"])
    cxv = pmv[:, 0:512].rearrange("p (k a m) -> p k a m", k=8, a=2)
    CP(s, "dve", CX[:, :, 0, :], cxv[:, :, 0, :], ["ppf"], ["CX"])
    TS(s, "dve", CX[:, :, 1, :], cxv[:, :, 1, :], -1.0, ALU.mult, ["ppf"], ["CX"])

    ACTF(s, ldt[:], ldt[:], AF.Exp, ["ldt"], ["ldt"])
    CP(s, "dve", dtr[:].rearrange("p (g n) -> p g n", g=16), ldt[:].unsqueeze(2).broadcast_to([128, 16, 64]),
       ["ldt"], ["dtr"])
    ardt, wr = t5, t6
    TT_(s, "dve", ardt[:], are[:], dtr[:], ALU.mult, ["are", "dtr"], ["ardt"])
    TT_(s, "dve", wr[:], aim[:], dtr[:], ALU.mult, ["aim", "dtr"], ["wr"])
    sincos(s, "dve", wr[:], t3[:], t4[:], t1[:], t2[:], ["wr"], "0")
    ACTF(s, t1[:], ardt[:], AF.Exp, ["ardt", "sc10"], ["mag1"])
    TT_(s, "dve", t4[:], t4[:], t1[:], ALU.mult, ["cos0", "mag1"], ["cos0"])
    TT_(s, "dve", t3[:], t3[:], t1[:], ALU.mult, ["sin0", "mag1"], ["sin0"])
    TS(s, "dve", t4[:], t4[:], -1.0, ALU.add, ["cos0"], ["cos0"])
    TT_(s, "dve", t1[:], are[:], are[:], ALU.mult, ["are", "mag1", "sin0"], ["mag1"])
    TT_(s, "dve", t2[:], aim[:], aim[:], ALU.mult, ["aim", "sc20"], ["sc20"])
    TT_(s, "dve", t1[:], t1[:], t2[:], ALU.add, ["mag1", "sc20"], ["mag1"])
    s.op("dve", lambda e: e.reciprocal(out=t1[:], in_=t1[:]), reads=["mag1"], writes=["mag1"])
    TT_(s, "dve", cfr[:], t4[:], are[:], ALU.mult, ["cos0", "are"], ["cfr"])
    TT_(s, "dve", t2[:], t3[:], aim[:], ALU.mult, ["sin0", "aim", "sc20"], ["sc20"])
    TT_(s, "dve", cfr[:], cfr[:], t2[:], ALU.add, ["cfr", "sc20"], ["cfr"])
    TT_(s, "dve", cfr[:], cfr[:], t1[:], ALU.mult, ["cfr", "mag1"], ["cfr"])
    TT_(s, "dve", cfi[:], t3[:], are[:], ALU.mult, ["sin0", "are"], ["cfi"])
    TT_(s, "dve", t2[:], t4[:], aim[:], ALU.mult, ["cos0", "aim", "sc20", "cfr"], ["sc20"])
    TT_(s, "dve", cfi[:], cfi[:], t2[:], ALU.subtract, ["cfi", "sc20"], ["cfi"])
    TT_(s, "dve", cfi[:], cfi[:], t1[:], ALU.mult, ["cfi", "mag1"], ["cfi"])
    TS(s, "dve", dtr[:], wr[:], pcol[:, 0:1], ALU.mult, ["wr", "pcol", "dtr"], ["ang"])
    sincos(s, "dve", dtr[:], t3[:], t4[:], t1[:], t2[:], ["ang", "cfi", "cfr"], "1")
    s.op("act", lambda e: e.activation(out=t1[:], in_=ardt[:], func=AF.Exp, scale=npcol[:, 0:1]),
         reads=["ardt", "npcol", "sc11", "cos1"], writes=["mag2"])
    TT_(s, "dve", t4[:], t4[:], t1[:], ALU.mult, ["cos1", "mag2"], ["cos1"])
    TT_(s, "dve", t3[:], t3[:], t1[:], ALU.mult, ["sin1", "mag2"], ["sin1"])
    TT_(s, "dve", Tr[:], t4[:], cfr[:], ALU.mult, ["cos1", "cfr"], ["Tr"])
    TT_(s, "dve", t2[:], t3[:], cfi[:], ALU.mult, ["sin1", "cfi", "sc21"], ["sc21"])
    TT_(s, "dve", Tr[:], Tr[:], t2[:], ALU.add, ["Tr", "sc21"], ["Tr"])
    TT_(s, "dve", Ti[:], t4[:], cfi[:], ALU.mult, ["cos1", "cfi"], ["Ti"])
    TT_(s, "dve", t2[:], t3[:], cfr[:], ALU.mult, ["sin1", "cfr", "Tr"], ["sc21"])
    TT_(s, "dve", Ti[:], Ti[:], t2[:], ALU.subtract, ["Ti", "sc21"], ["Ti"])
    ACTF(s, acol[:, 2, :], acol[:, 2, :], AF.Exp, ["acol"], ["acol"])
    TT_(s, "dve", acol[:, 0, :], acol[:, 0, :], acol[:, 2, :], ALU.mult, ["acol"], ["acol"])
    TT_(s, "dve", acol[:, 1, :], acol[:, 1, :], acol[:, 2, :], ALU.mult, ["acol"], ["acol"])
    angf = t5[:].rearrange("p (k t) -> p k t", k=8)
    magf = t6[:].rearrange("p (k t) -> p k t", k=8)
    for k in range(8):
        TS(s, "dve", angf[:, k, :], iot[:], acol[:, 1, k:k + 1], ALU.mult, ["iot", "acol", "ardt", "Tr", "Ti"], ["angf"])
        s.op("act", lambda e, k=k: e.activation(out=magf[:, k, :], in_=iot[:], func=AF.Exp, scale=acol[:, 0, k:k + 1]),
             reads=["iot", "acol", "wr", "ang", "Tr", "Ti"], writes=["magf"])
    sincos(s, "dve", t5[:], t3[:], t4[:], t1[:], t2[:], ["angf", "Tr", "Ti"], "2")
    TT_(s, "dve", TAr[:].rearrange("p k t -> p (k t)"), t4[:], t6[:], ALU.mult, ["cos2", "magf"], ["TAr"])
    TT_(s, "dve", TAi[:].rearrange("p k t -> p (k t)"), t3[:], t6[:], ALU.mult, ["sin2", "magf"], ["TAi"])

    for g, w in enumerate(POOL_WINDOWS):
        s.op("pool", lambda e, w=w: e.memset(mtmp[:], 1.0 / w), reads=["mtmp", "mrat", "MT"], writes=["mtmp"])
        s.op("pool", lambda e: e.affine_select(out=mtmp[:], in_=mtmp[:], pattern=[[1, 128]], compare_op=ALU.is_ge,
                                               fill=0.0, base=0, channel_multiplier=-1),
             reads=["mtmp"], writes=["mtmp"])
        s.op("pool", lambda e, w=w: e.affine_select(out=mtmp[:], in_=mtmp[:], pattern=[[-1, 128]],
                                                    compare_op=ALU.is_ge, fill=0.0, base=w - 1, channel_multiplier=1),
             reads=["mtmp"], writes=["mtmp"])
        TT_(s, "pool", MT[:, g * 3 + 0, :], mtmp[:], identf[:], ALU.subtract, ["mtmp", "identf"], ["MT"])
        TS(s, "dve", mrat[:], iot[:], 1.0, ALU.add, ["iot"], ["mrat"])
        s.op("dve", lambda e: e.reciprocal(out=mrat[:], in_=mrat[:]), reads=["mrat"], writes=["mrat"])
        TS(s, "dve", mrat[:], mrat[:], float(w), ALU.mult, ["mrat"], ["mrat"], scalar2=1.0, op1=ALU.max)
        TT_(s, "dve", mrat[:], mrat[:], mtmp[:], ALU.mult, ["mrat", "mtmp"], ["mrat"])
        TT_(s, "dve", MT[:, g * 3 + 2, :], mrat[:], identf[:], ALU.subtract, ["mrat", "identf"], ["MT"])
        s.op("pool", lambda e, w=w: e.memset(mtmp[:], 1.0 / w), reads=["mtmp", "mrat", "MT"], writes=["mtmp"])
        s.op("pool", lambda e, w=w: e.affine_select(out=mtmp[:], in_=mtmp[:], pattern=[[-1, 128]],
                                                    compare_op=ALU.is_ge, fill=0.0, base=-(129 - w),
                                                    channel_multiplier=1),
             reads=["mtmp"], writes=["mtmp"])
        CP(s, "pool", MT[:, g * 3 + 1, :], mtmp[:], ["mtmp"], ["MT"])

    NCH = NSEQ * NJ
    pmq = pym[:, 256:512].bitcast(BF16)

    def load(n):
        q_, j = divmod(n, NJ)
        s.dma("sp", ut[n % 3][:], scr["u"][q_, j * 128:(j + 1) * 128, :], writes=[f"ut{n % 3}"])

    def S1(n):
        U, UB = ut[n % 3], ub[n % 3]
        un, ubn = f"ut{n % 3}", f"ub{n % 3}"
        CP(s, "act", UB[:], U[:], [un], [ubn])
        for kc in range(2):
            TR(s, pmq[:, kc * 128:(kc + 1) * 128], UB[:, kc * 128:(kc + 1) * 128], ident[:], [ubn, "ident"], ["pym"])
        CP(s, "dve", uT[:], pmq[:, 0:256].rearrange("p (k t) -> p k t", k=2), ["pym"], ["uT"])
        for qt in range(4):
            k0 = qt * 2
            for kk in range(2):
                k = k0 + kk
                MM(s, pbu[:, kk * 256:(kk + 1) * 256], uT[:, k // 4, :], BT[:, k, :, :].rearrange("p a m -> p (a m)"),
                   ["uT", "BT"], ["pbu"])
            buv = pbu[:].rearrange("p (k a m) -> p k a m", k=2, a=2)
            trv = Tr[:, k0 * 128:(k0 + 2) * 128].rearrange("p (k m) -> p k m", k=2)
            tiv = Ti[:, k0 * 128:(k0 + 2) * 128].rearrange("p (k m) -> p k m", k=2)
            yield
            TT_(s, "dve", w1[:], buv[:, :, 0, :], trv, ALU.mult, ["pbu", "Tr"], ["w1"])
            TT_(s, "dve", w2[:], buv[:, :, 1, :], tiv, ALU.mult, ["pbu", "Ti"], ["w2"])
            yield
            TT_(s, "pool", Wt[:, k0:k0 + 2, 0, :], w1[:], w2[:], ALU.subtract, ["w1", "w2"], [f"Wt{qt}"])
            TT_(s, "dve", w3[:], buv[:, :, 1, :], trv, ALU.mult, ["pbu", "Tr"], ["w3"])
            TT_(s, "dve", w4[:], buv[:, :, 0, :], tiv, ALU.mult, ["pbu", "Ti"], ["w4"])
            yield
            TT_(s, "pool", Wt[:, k0:k0 + 2, 1, :], w3[:], w4[:], ALU.add, ["w3", "w4"], [f"Wt{qt}"])
            yield
            for kk in range(2):
                for part in range(2):
                    i = (k0 + kk) * 2 + part
                    MM(s, ppf[:, i * 128:(i + 1) * 128], Wt[:, k0 + kk, part, :], tribf[:], [f"Wt{qt}", "tribf"],
                       [f"ppf{qt // 2}"])

    def XP(n):
        q_, j = divmod(n, NJ)
        if j == 0:
            s.op("pool", lambda e: e.memset(car[:], 0.0), writes=["car"])
        for hf in range(2):
            k0 = hf * 4
            pfv = ppf[:, hf * 1024:(hf + 1) * 1024].rearrange("p (k a t) -> p k a t", k=4, a=2)
            pfn = f"ppf{hf}"
            TT_(s, "dve", Pr[:, k0:k0 + 4, :], pfv[:, :, 0, :],
                car[:, 0, k0:k0 + 4].unsqueeze(2).broadcast_to([128, 4, 128]), ALU.add, [pfn, "car"], [f"Pr{hf}"])
            TT_(s, "dve", Pi[:, k0:k0 + 4, :], pfv[:, :, 1, :],
                car[:, 1, k0:k0 + 4].unsqueeze(2).broadcast_to([128, 4, 128]), ALU.add, [pfn, "car"], [f"Pi{hf}"])
        yield
        for hf in range(2):
            k0 = hf * 4
            prn, pin = f"Pr{hf}", f"Pi{hf}"
            PR, PI = Pr[:, k0:k0 + 4, :], Pi[:, k0:k0 + 4, :]
            tar, tai = TAr[:, k0:k0 + 4, :], TAi[:, k0:k0 + 4, :]
            TT_(s, "dve", m1[:], tar, PR, ALU.mult, ["TAr", prn], ["m1"])
            TT_(s, "pool", m2[:], tai, PI, ALU.mult, ["TAi", pin], ["m2"])
            yield
            TT_(s, "dve", m3[:], tar, PI, ALU.mult, ["TAr", pin], ["m3"])
            TT_(s, "pool", m4[:], tai, PR, ALU.mult, ["TAi", prn], ["m4"])
            yield
            TT_(s, "pool", Xt[:, k0:k0 + 4, 0, :], m1[:], m2[:], ALU.subtract, ["m1", "m2"], [f"Xt{hf}"])
            TT_(s, "dve", sn[:, 0, :], m1[:, :, 127], m2[:, :, 127], ALU.subtract, ["m1", "m2"], ["sn"])
            yield
            TT_(s, "dve", Xt[:, k0:k0 + 4, 1, :], m3[:], m4[:], ALU.add, ["m3", "m4"], [f"Xt{hf}"])
            TT_(s, "dve", sn[:, 1, :], m3[:, :, 127], m4[:, :, 127], ALU.add, ["m3", "m4"], ["sn"])
            yield
            a1r, a1i = TAr[:, k0:k0 + 4, 1], TAi[:, k0:k0 + 4, 1]
            TT_(s, "dve", sm[:, 0, :], a1r, sn[:, 0, :], ALU.mult, ["TAr", "sn"], ["sm"])
            TT_(s, "dve", sm[:, 1, :], a1i, sn[:, 1, :], ALU.mult, ["TAi", "sn"], ["sm"])
            TT_(s, "dve", sm[:, 2, :], a1r, sn[:, 1, :], ALU.mult, ["TAr", "sn"], ["sm"])
            TT_(s, "dve", sm[:, 3, :], a1i, sn[:, 0, :], ALU.mult, ["TAi", "sn"], ["sm"])
            yield
            TT_(s, "dve", car[:, 0, k0:k0 + 4], sm[:, 0, :], sm[:, 1, :], ALU.subtract, ["sm"], ["car"])
            TT_(s, "dve", car[:, 1, k0:k0 + 4], sm[:, 2, :], sm[:, 3, :], ALU.add, ["sm"], ["car"])
            yield

    def REST(n):
        q_, j = divmod(n, NJ)
        U, UB = ut[n % 3], ub[n % 3]
        un, ubn = f"ut{n % 3}", f"ub{n % 3}"
        UBP, ubpn = ub[(n - 1) % 3], f"ub{(n - 1) % 3}"
        for k in range(8):
            for part in range(2):
                MM(s, py[:, 32 * k:32 * k + 32], Xt[:, k, part, :], CX[:, k, part, :], [f"Xt{k // 4}", "CX"], ["pym"],
                   start=(part == 0), stop=(part == 1))
        yield
        TT_(s, "pool", du[:], U[:, 0:256], drb[:], ALU.mult, [un, "drb"], ["du"])
        yield
        TT_(s, "dve", yv[:], py[:, 0:256], du[:], ALU.add, ["pym", "du"], ["yv"])
        yield
        TT_(s, "pool", y2[:], yv[:], yv[:], ALU.mult, ["yv"], ["y2"])
        yield
        TS(s, "pool", y2[:], y2[:], 0.044715, ALU.mult, ["y2"], ["y2"], scalar2=1.0, op1=ALU.add)
        yield
        TT_(s, "pool", y2[:], y2[:], yv[:], ALU.mult, ["y2", "yv"], ["y2"])
        ACTF(s, sg[:], y2[:], AF.Sigmoid, ["y2"], ["sg"], scale=1.5957691216057308)
        yield
        TT_(s, "dve", gy[:], yv[:], sg[:], ALU.mult, ["yv", "sg"], ["gy"])
        yield
        for kc in range(2):
            TR(s, pmq[:, 256 + kc * 128:256 + (kc + 1) * 128], gy[:, kc * 128:(kc + 1) * 128], ident[:],
               ["gy", "ident"], ["pym"])
        CP(s, "act", gyT[:], pmq[:, 256:512].rearrange("p (k t) -> p k t", k=2), ["pym"], ["gyT"])
        for gc in range(4):
            for kc in range(2):
                MM(s, pglu[:, gc * 128:(gc + 1) * 128], wglu[:, kc, gc * 128:(gc + 1) * 128], gyT[:, kc, :],
                   ["wglu", "gyT"], ["pglu"], start=(kc == 0), stop=(kc == 1))
        yield
        ACTF(s, sgb[:], pglu[:, 256:512].rearrange("p (k t) -> p k t", k=2), AF.Sigmoid, ["pglu"], ["sgb"])
        yield
        YS = ysT[n % 2]
        TT_(s, "dve", YS[:], pglu[:, 0:256].rearrange("p (k t) -> p k t", k=2), sgb[:], ALU.mult, ["pglu", "sgb"],
            [f"ysT{n % 2}"])
        s.dma("sp", scr["yssmT"][q_, :, j * 128:(j + 1) * 128].rearrange("(kc p) t -> p kc t", p=128), YS[:],
              reads=[f"ysT{n % 2}"])
        yield
        for g in range(4):
            o = ppl[64 * (g % 2):64 * (g % 2) + 64, (g // 2) * 128:(g // 2 + 1) * 128]
            if j == 0:
                MM(s, o, UB[:, 256 + 64 * g:256 + 64 * g + 64], MT[:, g * 3 + 2, :], [ubn, "MT"], ["ppl"])
            else:
                MM(s, o, UB[:, 256 + 64 * g:256 + 64 * g + 64], MT[:, g * 3 + 0, :], [ubn, "MT"], ["ppl"],
                   start=True, stop=False)
                MM(s, o, UBP[:, 256 + 64 * g:256 + 64 * g + 64], MT[:, g * 3 + 1, :], [ubpn, "MT"], ["ppl"],
                   start=False, stop=True)
        CP(s, "act", plT[:], ppl[:, 0:256].rearrange("p (k t) -> p k t", k=2), ["ppl"], ["plT"])
        for g in range(4):
            pb = 64 * (g % 2)
            MM(s, ppl[pb:pb + 64, 256 + (g // 2) * 128:256 + (g // 2 + 1) * 128], PW[pb:pb + 64, g // 2, :],
               plT[pb:pb + 64, g // 2, :], ["PW", "plT"], ["ppl"])
        yield
        YP = ypT[n % 2]
        for kc in range(2):
            TS(s, "dve", YP[:, kc, :], ppl[:, 256 + kc * 128:256 + (kc + 1) * 128], pscol[:, kc:kc + 1], ALU.mult,
               ["ppl", "pscol"], [f"ypT{n % 2}"])
        s.dma("sp", scr["ypoolT"][q_, :, j * 128:(j + 1) * 128].rearrange("(kc p) t -> p kc t", p=128), YP[:],
              reads=[f"ypT{n % 2}"])

    load(0)
    if NCH > 1:
        load(1)
    def chain(*gs):
        for g in gs:
            yield from g

    for t in range(-1, NCH):
        if 0 <= t + 2 < NCH and t + 2 >= 2:
            load(t + 2)
        gens = []
        if t >= 0:
            gens.append(chain(XP(t), REST(t)))
        if t + 1 < NCH:
            gens.append(S1(t + 1))
        while gens:
            for g in list(gens):
                try:
                    next(g)
                except StopIteration:
                    gens.remove(g)
    c.end()


def build_test_a2(NSEQ, S, l=0):
    nc = bass.Bass("TRN2", target_bir_lowering=False)
    u = nc.dram_tensor("u_in", [NSEQ, S, 512], F32, kind="ExternalInput").ap()
    P = declare_params(nc)
    scr = make_scratch(nc, NSEQ, S, debug_out=True)
    scr["u"] = u
    with contextlib.ExitStack() as es:
        c = Ctx(nc, es)
        phase_a2(c, l, NSEQ, S, P, scr)
        print("ops", c.s.total, "waits", c.s.nwaits)
    return nc


def phase_c(c, l, NSEQ, S, x_d, xout_d, P, scr):
    nc, s = c.nc, c.s
    NJ = S // 128
    c.begin()
    wg = c.sb("wg", [128, 8, 3 * D], BF16)
    wba = c.sb("wba", [128, 4, D], BF16)
    wbs = c.sb("wbs", [128, 2, D], BF16)
    wbp = c.sb("wbp", [128, 2, D], BF16)
    wout = c.sb("wout", [128, 8, D], BF16)
    bg = c.sb("bg", [128, 3 * D], F32)
    NSTG = 4
    stg = [c.sb(f"stg{i}", [128, 1536], F32) for i in range(NSTG)]
    gcol = c.sb("gcol", [128, 8], F32)
    ident = c.sb("ident", [128, 128], BF16)
    xt = [c.sb(f"xt{i}", [128, D], F32) for i in range(6)]
    junk = c.sb("junk", [128, D], F32)
    ss = c.sb("ss", [128, 1], F32)
    rstd = c.sb("rstd", [128, 1], F32)
    hb = c.sb("hb", [128, D], BF16)
    hT = [c.sb(f"hT{i}", [128, 8, 128], BF16) for i in range(2)]
    gates_b = [c.sb(f"gates{i}", [128, 3 * D], F32) for i in range(2)]
    ya = [c.sb(f"ya{i}", [128, 512], BF16) for i in range(2)]
    yaT_b = [c.sb(f"yaT{i}", [128, 4, 128], BF16) for i in range(2)]
    ysT = [c.sb(f"ysT{i}", [128, 2, 128], BF16) for i in range(2)]
    ypT = [c.sb(f"ypT{i}", [128, 2, 128], BF16) for i in range(2)]
    macc = [c.sb(f"macc{i}", [128, 512], F32) for i in range(2)]
    mtmp = [c.sb(f"mtmp{i}", [128, 512], F32) for i in range(2)]
    mrg = [c.sb(f"mrg{i}", [128, D], BF16) for i in range(2)]
    mT_b = [c.sb(f"mT{i}", [128, 8, 128], BF16) for i in range(2)]
    pt = c.ps("pt")
    pg = [c.ps(f"pg{i}") for i in range(2)]
    pbr = [c.ps(f"pbr{i}") for i in range(3)]
    po = [c.ps(f"po{i}") for i in range(2)]

    make_identity(c, ident)
    s.dma("sp", gcol[:], P["mix_norm"][l].rearrange("(kc p) -> p kc", p=128), writes=["gcol"],
          allow_slow_non_contiguous=True)
    s.dma("act", bg[:], P["b_gate"][l:l + 1, :].partition_broadcast(128), writes=["bg"])
    n = 0
    dq = ("sp", "act", "pool")
    for kc in range(8):
        for hf in range(2):
            st, sn_ = stg[n % NSTG], f"stg{n % NSTG}"
            s.dma(dq[n % 3], st[:], P["w_in"][l, kc * 128:(kc + 1) * 128, 2056 + hf * 1536:2056 + (hf + 1) * 1536],
                  writes=[sn_])
            _cast(s, ("dve", "act")[n % 2], wg[:, kc, hf * 1536:(hf + 1) * 1536], st[:], [sn_, "gcol"],
                  [f"wg{kc}" if hf == 0 else f"wg{kc}b"], scalar=gcol[:, kc:kc + 1])
            n += 1
    for (wt, nm, nk) in ((wba, "w_br_attn", 4), (wbs, "w_br_ssm", 2), (wbp, "w_br_pool", 2), (wout, "w_out", 8)):
        for k0 in range(nk):
            st, sn_ = stg[n % NSTG], f"stg{n % NSTG}"
            s.dma(dq[n % 3], st[:, 0:D], P[nm][l, k0 * 128:(k0 + 1) * 128, :], writes=[sn_])
            _cast(s, ("dve", "act")[n % 2], wt[:, k0, :], st[:, 0:D], [sn_], [nm])
            n += 1

    NCH = NSEQ * NJ
    ptv = pt[:].bitcast(BF16)

    def load(n):
        q_, j = divmod(n, NJ)
        b = n % 2
        s.dma("sp", xt[n % 6][:], x_d[n * 128:(n + 1) * 128, :], writes=[f"xt{n % 6}"])

    def load_ya(n):
        q_, j = divmod(n, NJ)
        b = n % 2
        s.dma("sp", ya[b][:], scr["yattn"][q_, j * 128:(j + 1) * 128, :], writes=[f"ya{b}"])

    def load_ys(n):
        q_, j = divmod(n, NJ)
        b = n % 2
        s.dma("sp", ysT[b][:], scr["yssmT"][q_, :, j * 128:(j + 1) * 128].rearrange("(kc p) t -> p kc t", p=128),
              writes=[f"ysT{b}"])
        s.dma("sp", ypT[b][:], scr["ypoolT"][q_, :, j * 128:(j + 1) * 128].rearrange("(kc p) t -> p kc t", p=128),
              writes=[f"ypT{b}"])

    def s1(n):
        X, xb = xt[n % 6], f"xt{n % 6}"
        HT, htn = hT[n % 2], f"hT{n % 2}"
        rms_rstd(c, X[:], xb, junk[:], ss[:], rstd[:], D, "")
        yield
        TS(s, "dve", hb[:], X[:], rstd[:, 0:1], ALU.mult, [xb, "rstd"], ["hb"])
        yield
        for kc in range(8):
            TR(s, ptv[:, kc * 128:(kc + 1) * 128], hb[:, kc * 128:(kc + 1) * 128], ident[:], ["hb", "ident"], ["pt"])
        CP(s, "act", HT[:], ptv.rearrange("p (k t) -> p k t", k=8), ["pt"], [htn])
        yield

    def s2_gates(n):
        HT, htn = hT[n % 2], f"hT{n % 2}"
        gates = gates_b[n % 2]
        gp = f"g{n % 2}_"
        for gb in range(6):
            PG, pgn = pg[gb % 2], f"pg{gb % 2}"
            for kc in range(8):
                MM(s, PG[:], HT[:, kc, :], wg[:, kc, gb * 512:(gb + 1) * 512], [htn, f"wg{kc}" if gb < 3 else f"wg{kc}b"], [pgn],
                   start=(kc == 0), stop=(kc == 7))
                yield
            gsl = gates[:, gb * 512:(gb + 1) * 512]
            TT_(s, "dve", gsl, PG[:], bg[:, gb * 512:(gb + 1) * 512], ALU.add, [pgn, "bg"], [gp + f"gate{gb}"])
            yield
            ACTF(s, gsl, gsl, AF.Sigmoid, [gp + f"gate{gb}"], [gp + f"gate{gb}"])
            yield

    def s2_ya(n):
        b = n % 2
        for kc in range(4):
            TR(s, ptv[:, kc * 128:(kc + 1) * 128], ya[b][:, kc * 128:(kc + 1) * 128], ident[:], [f"ya{b}", "ident"],
               ["pt"])
        CP(s, "dve", yaT_b[b][:], ptv[:, 0:512].rearrange("p (k t) -> p k t", k=4), ["pt"], [f"yaT{b}"])
        yield

    def s2_branch(n, dbs):
        b = n % 2
        MR = mrg[b]
        gates = gates_b[n % 2]
        gp = f"g{n % 2}_"
        for db in dbs:
            cs = slice(db * 512, (db + 1) * 512)
            for kc in range(4):
                MM(s, pbr[0][:], yaT_b[b][:, kc, :], wba[:, kc, cs], [f"yaT{b}", "w_br_attn"], ["pbr0"], start=(kc == 0),
                   stop=(kc == 3))
                yield
            for kc in range(2):
                MM(s, pbr[1][:], ysT[b][:, kc, :], wbs[:, kc, cs], [f"ysT{b}", "w_br_ssm"], ["pbr1"], start=(kc == 0),
                   stop=(kc == 1))
                yield
            for kc in range(2):
                MM(s, pbr[2][:], ypT[b][:, kc, :], wbp[:, kc, cs], [f"ypT{b}", "w_br_pool"], ["pbr2"], start=(kc == 0),
                   stop=(kc == 1))
                yield
            MA, TM = macc[db], mtmp[db]
            TT_(s, "dve", MA[:], pbr[0][:], gates[:, db * 512:(db + 1) * 512], ALU.mult, ["pbr0", gp + f"gate{db}"],
                [f"macc{db}"])
            yield
            TT_(s, "dve", TM[:], pbr[1][:], gates[:, D + db * 512:D + (db + 1) * 512], ALU.mult,
                ["pbr1", gp + f"gate{2 + db}"], [f"mtmp{db}"])
            yield
            TT_(s, "pool", MA[:], MA[:], TM[:], ALU.add, [f"macc{db}", f"mtmp{db}"], [f"macc{db}"])
            yield
            TT_(s, "dve", TM[:], pbr[2][:], gates[:, 2 * D + db * 512:2 * D + (db + 1) * 512], ALU.mult,
                ["pbr2", gp + f"gate{4 + db}"], [f"mtmp{db}"])
            yield
            TT_(s, "pool", MR[:, cs], MA[:], TM[:], ALU.add, [f"macc{db}", f"mtmp{db}"], [f"mrg{b}_{db}"])
            yield

    def s3a(n):
        b = n % 2
        MR = mrg[b]
        for kc in range(8):
            TR(s, ptv[:, kc * 128:(kc + 1) * 128], MR[:, kc * 128:(kc + 1) * 128], ident[:],
               [f"mrg{b}_{kc // 4}", "ident"], ["pt"])
        CP(s, "act", mT_b[b][:], ptv.rearrange("p (k t) -> p k t", k=8), ["pt"], [f"mT{b}"])
        yield

    def s3b(n):
        b = n % 2
        X, xb = xt[n % 6], f"xt{n % 6}"
        for db in range(2):
            cs = slice(db * 512, (db + 1) * 512)
            for kc in range(8):
                MM(s, po[db][:], mT_b[b][:, kc, :], wout[:, kc, cs], [f"mT{b}", "w_out"], [f"po{db}"], start=(kc == 0),
                   stop=(kc == 7))
                yield
            TT_(s, "dve", X[:, cs], po[db][:], X[:, cs], ALU.add, [f"po{db}", xb], [xb])
            yield
        s.dma("pool", xout_d[n * 128:(n + 1) * 128, :], X[:], reads=[xb])
        yield

    def chain(*gs):
        for g in gs:
            yield from g

    def run(gens):
        gens = list(gens)
        while gens:
            for g in list(gens):
                try:
                    next(g)
                except StopIteration:
                    gens.remove(g)

    load(0)
    load_ya(0)
    load_ys(0)
    if NCH > 1:
        load(1)
    run([s1(0)])
    for t in range(NCH + 4):
        if t + 2 < NCH:
            load(t + 2)
        gens = []
        if t < NCH:
            gens.append(s2_gates(t))
        if t + 1 < NCH:
            gens.append(s1(t + 1))
        if t < NCH:
            gens.append(s2_ya(t))
        if 1 <= t <= NCH:
            gens.append(s2_branch(t - 1, [0, 1]))
        if 2 <= t <= NCH + 1:
            gens.append(s3a(t - 2))
        if 3 <= t <= NCH + 2:
            gens.append(s3b(t - 3))
        run(gens)
        if t + 1 < NCH:
            load_ya(t + 1)
            load_ys(t + 1)
    c.end()


def build_full(NSEQ, S, depth=DEPTH):
    T = NSEQ * S
    nc = bass.Bass("TRN2", target_bir_lowering=False)
    x = nc.dram_tensor("x", [T, D], F32, kind="ExternalInput").ap()
    y = nc.dram_tensor("y", [T, D], F32, kind="ExternalOutput").ap()
    P = declare_params(nc)
    scr = make_scratch(nc, NSEQ, S)
    xa = nc.dram_tensor("scr_xa", [T, D], F32).ap()
    with contextlib.ExitStack() as es:
        c = Ctx(nc, es)
        G = {"cumall": es.enter_context(nc.sbuf_tensor("cumall", [128, NSEQ, S // 128, 8], F32))}
        for l in range(depth):
            xin = x if l == 0 else xa
            phase_a(c, l, NSEQ, S, xin, P, G, scr)
            phase_a2(c, l, NSEQ, S, P, scr)
            phase_b(c, NSEQ, S, G, scr)
            phase_c(c, l, NSEQ, S, xin, xa, P, scr)
            phase_mlp(c, l, T, xa, P["w_up"], P["w_down"], P["mlp_norm"], xout_d=(y if l == depth - 1 else xa))
        print("ops", c.s.total, "waits", c.s.nwaits, flush=True)
    return nc


_NC_CACHE = {}


def kernel(**inputs):
    x = np.ascontiguousarray(np.asarray(inputs["x"], dtype=np.float32))
    B, S, _ = x.shape
    NSEQ = B // NCORES
    key = (NSEQ, S)
    if key not in _NC_CACHE:
        _NC_CACHE[key] = build_full(NSEQ, S)
    nc = _NC_CACHE[key]
    params = {k: np.ascontiguousarray(np.asarray(inputs[k], dtype=np.float32)) for k in PARAM_SHAPES}
    in_maps = []
    for cid in range(NCORES):
        m = {"x": x[cid * NSEQ:(cid + 1) * NSEQ].reshape(NSEQ * S, D)}
        m.update(params)
        in_maps.append(m)
    res = run_bass_kernel_spmd(nc, in_maps, core_ids=list(range(NCORES)))
    out = np.empty((B, S, D), dtype=np.float32)
    for cid in range(NCORES):
        out[cid * NSEQ:(cid + 1) * NSEQ] = np.asarray(res.results[cid]["y"]).reshape(NSEQ, S, D)
    return out
```

```python
import contextlib
import math
import numpy as np
import concourse.bass as bass
import concourse.mybir as mybir
from concourse.bass_utils import run_bass_kernel_spmd

F32 = mybir.dt.float32
BF16 = mybir.dt.bfloat16
ALU = mybir.AluOpType
AF = mybir.ActivationFunctionType
AX = mybir.AxisListType

D = 1024
DEPTH = 2
DFF = 4096
NH = 8
DH = 64
INC = 5128
EPS = 1e-6
NCORES = 8

ENGS = ("pe", "act", "dve", "pool", "sp")
N_DMA_SEMS = 12
SAME_ENGINE_SYNC = True


class Buf:
    __slots__ = ("name", "w", "r")

    def __init__(self, name):
        self.name = name
        self.w = None
        self.r = {}


class Op:
    __slots__ = ("eng", "idx", "fn", "deps", "dma", "needs_inc", "sem", "semval")

    def __init__(self, eng, idx, fn, deps, dma):
        self.eng, self.idx, self.fn, self.deps, self.dma = eng, idx, fn, deps, dma
        self.needs_inc = False
        self.sem = None
        self.semval = None


class Sched:
    def __init__(self, nc, es, same_engine_sync=SAME_ENGINE_SYNC):
        self.nc = nc
        self.q = {e: [] for e in ENGS}
        self.ndma = {e: 0 for e in ENGS}
        self.cnt = {e: 0 for e in ENGS}
        self.same_engine_sync = same_engine_sync
        self.bufs = {}
        self.esem = {e: es.enter_context(nc.semaphore("es_" + e)) for e in ENGS if e != "sp"}
        self.dsem = {}
        for e in ("sp", "act", "pool"):
            for j in range(N_DMA_SEMS):
                self.dsem[(e, j)] = es.enter_context(nc.semaphore(f"ds_{e}_{j}"))
        self.total = {e: 0 for e in ENGS}
        self.nwaits = {e: 0 for e in ENGS}

    def buf(self, name):
        b = self.bufs.get(name)
        if b is None:
            b = self.bufs[name] = Buf(name)
        return b

    def _b(self, x):
        return x if isinstance(x, Buf) else self.buf(x)

    def op(self, eng, fn, reads=(), writes=(), dma=False):
        reads = [self._b(x) for x in reads]
        writes = [self._b(x) for x in writes]
        deps = {}
        for b in reads:
            if b.w is not None:
                deps[id(b.w)] = b.w
        for b in writes:
            if b.w is not None:
                deps[id(b.w)] = b.w
            for o in b.r.values():
                deps[id(o)] = o
        o = Op(eng, len(self.q[eng]), fn, list(deps.values()), dma)
        if dma:
            i = self.ndma[eng]
            self.ndma[eng] += 1
            o.sem = (eng, i % N_DMA_SEMS)
            o.semval = 16 * (i // N_DMA_SEMS + 1)
        self.q[eng].append(o)
        for b in reads:
            key = ("dma", eng, o.idx) if dma else eng
            b.r[key] = o
        for b in writes:
            b.w = o
            b.r = {}
        for d in o.deps:
            if not d.dma:
                d.needs_inc = True
        return o

    def dma(self, eng, out, in_, reads=(), writes=(), **kw):
        return self.op(eng, lambda e: e.dma_start(out=out, in_=in_, **kw), reads, writes, dma=True)

    def emit(self):
        nc = self.nc
        for e in ENGS:
            c = self.cnt[e]
            for o in self.q[e]:
                if not o.dma and o.needs_inc:
                    c += 1
                    o.sem = e
                    o.semval = c
            self.cnt[e] = c

        def replay(e, engobj):
            known = {}
            for o in self.q[e]:
                waits = {}
                for d in o.deps:
                    if d.eng == e and not d.dma:
                        if e == "pe" or (not self.same_engine_sync and e != "pool"):
                            continue
                    k = d.sem
                    if waits.get(k, 0) < d.semval:
                        waits[k] = d.semval
                if o.dma and o.semval > 16:
                    k = o.sem
                    waits[k] = max(waits.get(k, 0), o.semval - 16)
                for k, v in waits.items():
                    if known.get(k, 0) < v:
                        known[k] = v
                        s = self.dsem[k] if isinstance(k, tuple) else self.esem[k]
                        engobj.wait_ge(s, v)
                        self.nwaits[e] += 1
                ins = o.fn(engobj)
                if o.dma:
                    ins.then_inc(self.dsem[o.sem], 16)
                elif o.needs_inc:
                    ins.then_inc(self.esem[e], 1)
            if self.ndma[e]:
                n = self.ndma[e]
                for j in range(min(N_DMA_SEMS, n)):
                    last = ((n - 1 - j) // N_DMA_SEMS) + 1
                    if known.get((e, j), 0) < 16 * last:
                        engobj.wait_ge(self.dsem[(e, j)], 16 * last)

        with nc.Block() as block:
            @block.sync
            def _(e):
                replay("sp", e)

            @block.tensor
            def _(e):
                replay("pe", e)

            @block.scalar
            def _(e):
                replay("act", e)

            @block.vector
            def _(e):
                replay("dve", e)

            @block.gpsimd
            def _(e):
                replay("pool", e)

        for e in ENGS:
            self.total[e] += len(self.q[e])
            self.q[e] = []
        for b in self.bufs.values():
            b.w = None
            b.r = {}


class Ctx:
    def __init__(self, nc, es):
        self.nc = nc
        self.es = es
        self.s = Sched(nc, es)
        self.pes = None
        self.uid = 0

    def begin(self):
        self.pes = contextlib.ExitStack()
        self.pes.__enter__()

    def end(self):
        self.s.emit()
        self.pes.close()
        self.pes = None

    def dbg(self, name, ap, reads):
        if not getattr(self, "debug", False):
            return
        d = self.nc.dram_tensor("dbg_" + name, list(ap.shape), ap.dtype, kind="ExternalOutput").ap()
        self.s.dma("sp", d, ap, reads=reads)

    def sb(self, name, shape, dtype):
        self.uid += 1
        return self.pes.enter_context(self.nc.sbuf_tensor(f"{name}_{self.uid}", list(shape), dtype))

    def ps(self, name, shape=(128, 512), dtype=F32):
        self.uid += 1
        return self.pes.enter_context(self.nc.psum_tensor(f"{name}_{self.uid}", list(shape), dtype))


def _cast_engine(i):
    return ("dve", "pool", "act")[i % 3]


def _cast(s, eng, out, in_, reads, writes, scalar=None):
    if scalar is None:
        if eng == "act":
            return s.op("act", lambda e: e.copy(out=out, in_=in_), reads, writes)
        return s.op(eng, lambda e: e.tensor_copy(out=out, in_=in_), reads, writes)
    if eng == "act":
        return s.op("act", lambda e: e.activation(out=out, in_=in_, func=AF.Copy, scale=scalar), reads, writes)
    return s.op(eng, lambda e: e.tensor_scalar(out=out, in0=in_, scalar1=scalar, scalar2=None, op0=ALU.mult),
                reads, writes)


def make_identity(c, ident):
    s = c.s
    s.op("pool", lambda e: e.memset(ident[:], 1.0), writes=["ident"])
    s.op("pool", lambda e: e.affine_select(out=ident[:], in_=ident[:], pattern=[[-1, 128]],
                                           compare_op=ALU.is_equal, fill=0.0, base=0, channel_multiplier=1),
         reads=["ident"], writes=["ident"])


def phase_mlp(c, l, T, x_d, w_up, w_down, mlp_norm, xout_d=None):
    nc, s = c.nc, c.s
    if xout_d is None:
        xout_d = x_d
    c.begin()
    TT = 256
    NT = T // TT
    wup = c.sb("wup", [128, 8, DFF], BF16)
    wdn = c.sb("wdn", [128, 32, D], BF16)
    NSTG = 4
    stg = [c.sb(f"stg{i}", [128, 1024], F32) for i in range(NSTG)]
    gcol = c.sb("gcol", [128, 8], F32)
    ident = c.sb("ident", [128, 128], BF16)
    xt = [c.sb(f"xt{i}", [128, 2, D], F32) for i in range(3)]
    hb = c.sb("hb", [128, 2, D], BF16)
    hT = [c.sb(f"hT{i}", [128, 8, TT], BF16) for i in range(2)]
    actT = c.sb("actT", [128, 32, TT], BF16)
    rl = [c.sb(f"rl{i}", [128, TT], F32) for i in range(2)]
    ss = c.sb("ss", [128, 2], F32)
    rstd = c.sb("rstd", [128, 2], F32)
    junk = c.sb("junk", [128, 2, D], BF16)
    pt = [c.ps(f"pt{i}") for i in range(2)]
    pu = [c.ps(f"pu{i}") for i in range(3)]
    pd = [c.ps(f"pd{i}") for i in range(3)]

    make_identity(c, ident)
    s.dma("sp", gcol[:], mlp_norm[l].rearrange("(kc p) -> p kc", p=128), writes=["gcol"],
          allow_slow_non_contiguous=True)
    n = 0
    dq = ("sp", "act", "pool")
    for kc in range(8):
        for qq in range(4):
            st, sn_ = stg[n % NSTG], f"stg{n % NSTG}"
            s.dma(dq[n % 3], st[:], w_up[l, kc * 128:(kc + 1) * 128, qq * 1024:(qq + 1) * 1024], writes=[sn_])
            _cast(s, ("dve", "act")[n % 2], wup[:, kc, qq * 1024:(qq + 1) * 1024], st[:],
                  [sn_, "gcol"], [f"wup{kc}_{qq}"], scalar=gcol[:, kc:kc + 1])
            n += 1
    for fc in range(32):
        st, sn_ = stg[n % NSTG], f"stg{n % NSTG}"
        s.dma(dq[n % 3], st[:], w_down[l, fc * 128:(fc + 1) * 128, :], writes=[sn_])
        _cast(s, ("dve", "act")[n % 2], wdn[:, fc, :], st[:], [sn_], [f"wdn{fc}"])
        n += 1

    def load(i):
        s.dma("sp", xt[i % 3][:], x_d[i * TT:(i + 1) * TT, :].rearrange("(a p) d -> p a d", p=128),
              writes=[f"xt{i % 3}"])

    def front_elem(i):
        X, xb = xt[i % 3], f"xt{i % 3}"
        for a in range(2):
            ACTF(s, junk[:, a, :], X[:, a, :], AF.Square, [xb], [f"junk{a}"])
        s.op("dve", lambda e: e.tensor_reduce(out=ss[:], in_=junk[:], axis=AX.X, op=ALU.add),
             reads=["junk0", "junk1"], writes=["ss"])
        ACTF(s, rstd[:], ss[:], AF.Ln, ["ss"], ["rstd"], scale=1.0 / D, bias=EPS)
        ACTF(s, rstd[:], rstd[:], AF.Exp, ["rstd"], ["rstd"], scale=-0.5)
        for a in range(2):
            TS(s, "dve", hb[:, a, :], X[:, a, :], rstd[:, a:a + 1], ALU.mult, [xb, "rstd"], [f"hb{a}"])
    def front_tr(i):
        HT, htn = hT[i % 2], f"hT{i % 2}"
        for a in range(2):
            pview = pt[a][:].bitcast(BF16)
            for kc in range(8):
                TR(s, pview[:, kc * 128:(kc + 1) * 128], hb[:, a, kc * 128:(kc + 1) * 128], ident[:],
                   [f"hb{a}", "ident"], [f"pt{a}"])
            CP(s, "dve" if a == 0 else "act", HT[:, :, a * 128:(a + 1) * 128],
               pview.rearrange("p (k t) -> p k t", k=8), [f"pt{a}"], [htn + f"_{a}"])

    def up(i):
        HT, htn = hT[i % 2], f"hT{i % 2}"
        for fc in range(32):
            Pb, pb = pu[fc % 3], f"pu{fc % 3}"
            for kc in range(8):
                MM(s, Pb[:, 0:TT], wup[:, kc, fc * 128:(fc + 1) * 128], HT[:, kc, :],
                   [f"wup{kc}_{fc // 8}", htn + "_0", htn + "_1"], [pb], start=(kc == 0), stop=(kc == 7))
            R, rb = rl[fc % 2], f"rl{fc % 2}"
            ACTF(s, R[:], Pb[:, 0:TT], AF.Relu, [pb], [rb])
            TT_(s, "pool", actT[:, fc, :], R[:], R[:], ALU.mult, [rb], ["actT"])

    def down(i):
        X, xb = xt[i % 3], f"xt{i % 3}"
        for a in range(2):
            for db in range(2):
                jj = a * 2 + db
                Pb, pb = pd[jj % 3], f"pd{jj % 3}"
                for fc in range(32):
                    MM(s, Pb[:], actT[:, fc, a * 128:(a + 1) * 128], wdn[:, fc, db * 512:(db + 1) * 512],
                       ["actT", f"wdn{fc}"], [pb], start=(fc == 0), stop=(fc == 31))
                TT_(s, "dve", X[:, a, db * 512:(db + 1) * 512], Pb[:], X[:, a, db * 512:(db + 1) * 512], ALU.add,
                    [pb, xb], [xb])
        s.dma("pool", xout_d[i * TT:(i + 1) * TT, :].rearrange("(a p) d -> p a d", p=128), X[:], reads=[xb])

    load(0)
    if NT > 1:
        load(1)
    front_elem(0)
    front_tr(0)
    for i in range(NT):
        if i + 2 < NT:
            load(i + 2)
        if i + 1 < NT:
            front_elem(i + 1)
        up(i)
        if i + 1 < NT:
            front_tr(i + 1)
        down(i)
    c.end()


def build_test_mlp(T):
    nc = bass.Bass("TRN2", target_bir_lowering=False)
    x = nc.dram_tensor("x", [T, D], F32, kind="ExternalInput").ap()
    w_up = nc.dram_tensor("w_up", [DEPTH, D, DFF], F32, kind="ExternalInput").ap()
    w_down = nc.dram_tensor("w_down", [DEPTH, DFF, D], F32, kind="ExternalInput").ap()
    mlp_norm = nc.dram_tensor("mlp_norm", [DEPTH, D], F32, kind="ExternalInput").ap()
    y = nc.dram_tensor("y", [T, D], F32, kind="ExternalOutput").ap()
    with contextlib.ExitStack() as es:
        c = Ctx(nc, es)
        c.debug = True
        phase_mlp(c, 0, T, x, w_up, w_down, mlp_norm, xout_d=y)
        print("ops", c.s.total, "waits", c.s.nwaits)
    return nc


def rms_rstd(c, X, xb, junk, ss, rstd, width, tag):
    s = c.s
    s.op("act", lambda e: e.activation(out=junk, in_=X, func=AF.Square), reads=[xb], writes=["junk" + tag])
    s.op("dve", lambda e: e.tensor_reduce(out=ss, in_=junk, axis=AX.X, op=ALU.add),
         reads=["junk" + tag], writes=["ss" + tag])
    s.op("act", lambda e: e.activation(out=rstd, in_=ss, func=AF.Ln, scale=1.0 / width, bias=EPS),
         reads=["ss" + tag], writes=["rstd" + tag])
    s.op("act", lambda e: e.activation(out=rstd, in_=rstd, func=AF.Exp, scale=-0.5),
         reads=["rstd" + tag], writes=["rstd" + tag])


def make_tri(c, tri, name, dtype_one=1.0):
    s = c.s
    s.op("pool", lambda e: e.memset(tri[:], 1.0), writes=[name])
    s.op("pool", lambda e: e.affine_select(out=tri[:], in_=tri[:], pattern=[[1, 128]],
                                           compare_op=ALU.is_ge, fill=0.0, base=0, channel_multiplier=-1),
         reads=[name], writes=[name])


def phase_a(c, l, NSEQ, S, x_d, P, G, scr, do_ssm=True, do_pool=True):
    nc, s = c.nc, c.s
    NJ = S // 128
    c.begin()
    NA = 2056
    win = c.sb("win", [128, 8, NA], BF16)
    NSTG = 4
    stg = [c.sb(f"stg{i}", [128, NA // 2], F32) for i in range(NSTG)]
    gcol = c.sb("gcol", [128, 8], F32)
    ident = c.sb("ident", [128, 128], BF16)
    trif = c.sb("trif", [128, 128], F32)
    onesf = c.sb("onesf", [128, 128], F32)
    xt = [c.sb(f"xt{i}", [128, D], F32) for i in range(3)]
    junk = c.sb("junk", [128, D], F32)
    ss = c.sb("ss", [128, 1], F32)
    rstd = c.sb("rstd", [128, 1], F32)
    hb = c.sb("hb", [128, D], BF16)
    hT = [c.sb(f"hT{i}", [128, 8, 128], BF16) for i in range(2)]
    qkg = c.sb("qkg", [128, 2, 8, DH], F32)
    bfg = c.sb("bfg", [128, 8], F32)
    sq = [c.sb(f"sq{i}", [128, 512], F32) for i in range(2)]
    qe = [[c.sb(f"qe{i}_{b}", [128, 512], F32) for b in range(2)] for i in range(2)]
    ssq = [c.sb(f"ssq{i}", [128, 8], F32) for i in range(2)]
    rq = [c.sb(f"rq{i}", [128, 8], F32) for i in range(2)]
    qn = [c.sb(f"qn{i}", [128, 8, DH], F32) for i in range(2)]
    qa = [[c.sb(f"qa{i}_{b}", [128, 8, 70], BF16) for b in range(2)] for i in range(2)]
    r1 = c.sb("r1", [128, 8], F32)
    r2 = c.sb("r2", [128, 8], F32)
    qTs = [[c.sb(f"qTs{i}_{b}", [128, 8, 128], BF16) for b in range(2)] for i in range(2)]
    vst = [c.sb(f"vst{b}", [128, 8, 65], BF16) for b in range(2)]
    ust = [c.sb(f"ust{b}", [128, 512], F32) for b in range(2)]
    fls = [c.sb(f"fl{b}", [128, 8], F32) for b in range(2)]
    sp_ = c.sb("sp_", [128, 8], F32)
    carry = c.sb("carry", [128, 8], F32)
    cumall = G["cumall"]
    pt = c.ps("pt")
    pq = [c.ps(f"pq{i}") for i in range(2)]
    pv = c.ps("pv")
    pf = c.ps("pf")
    pp = c.ps("pp")
    ptq = [c.ps(f"ptq{i}") for i in range(2)]

    make_identity(c, ident)
    make_tri(c, trif, "trif")
    s.op("pool", lambda e: e.memset(onesf[:], 1.0), writes=["onesf"])
    for b in range(2):
        s.op("pool", lambda e, b=b: e.memset(vst[b][:], 1.0), writes=[f"vst{b}"])
    s.dma("sp", gcol[:], P["mix_norm"][l].rearrange("(kc p) -> p kc", p=128), writes=["gcol"],
          allow_slow_non_contiguous=True)
    s.dma("sp", qkg[:, 0, 0, :], P["q_norm"][l:l + 1, :].partition_broadcast(128), writes=["qkg"])
    s.dma("sp", qkg[:, 1, 0, :], P["k_norm"][l:l + 1, :].partition_broadcast(128), writes=["qkg"])
    s.dma("sp", bfg[:], P["b_forget"][l:l + 1, :].partition_broadcast(128), writes=["bfg"])
    s.op("dve", lambda e: e.tensor_scalar(out=qkg[:, 0, 0, :], in0=qkg[:, 0, 0, :], scalar1=DH ** -0.5, scalar2=None,
                                          op0=ALU.mult), reads=["qkg"], writes=["qkg"])
    for h in range(1, 8):
        s.op("dve", lambda e, h=h: e.tensor_copy(out=qkg[:, :, h, :], in_=qkg[:, :, 0, :]),
             reads=["qkg"], writes=["qkg"])
    nn = 0
    dq = ("sp", "act", "pool")
    HN = NA // 2
    for kc in range(8):
        for hf in range(2):
            st, sn_ = stg[nn % NSTG], f"stg{nn % NSTG}"
            s.dma(dq[nn % 3], st[:], P["w_in"][l, kc * 128:(kc + 1) * 128, hf * HN:(hf + 1) * HN], writes=[sn_])
            _cast(s, ("dve", "act")[nn % 2], win[:, kc, hf * HN:(hf + 1) * HN], st[:], [sn_, "gcol"], [f"win{kc}"],
                  scalar=gcol[:, kc:kc + 1])
            nn += 1

    def load(n):
        s.dma("sp", xt[n % 3][:], x_d[n * 128:(n + 1) * 128, :], writes=[f"xt{n % 3}"])

    NCH = NSEQ * NJ
    ptv = pt[:].bitcast(BF16)
    pc = pf[:, 272:288]
    for i in range(2):
        for b in range(2):
            s.op("pool", lambda e, i=i, b=b: e.memset(qa[i][b][:], 1.0), writes=[f"qg{i}_{b}", f"qaug{i}_{b}"])

    def front_elem(n):
        X, xb = xt[n % 3], f"xt{n % 3}"
        ACTF(s, junk[:], X[:], AF.Square, [xb], ["junk"])
        s.op("dve", lambda e: e.tensor_reduce(out=ss[:], in_=junk[:], axis=AX.X, op=ALU.add), reads=["junk"],
             writes=["ss"])
        ACTF(s, rstd[:], ss[:], AF.Ln, ["ss"], ["rstd"], scale=1.0 / D, bias=EPS)
        ACTF(s, rstd[:], rstd[:], AF.Exp, ["rstd"], ["rstd"], scale=-0.5)
        TS(s, "dve", hb[:], X[:], rstd[:, 0:1], ALU.mult, [xb, "rstd"], ["hb"])

    def front_tr(n):
        HT, htn = hT[n % 2], f"hT{n % 2}"
        for kc in range(8):
            TR(s, ptv[:, kc * 128:(kc + 1) * 128], hb[:, kc * 128:(kc + 1) * 128], ident[:], ["hb", "ident"], ["pt"])
        CP(s, "act", HT[:], ptv.rearrange("p (k t) -> p k t", k=8), ["pt"], [htn])

    def proj_all(n):
        HT, htn = hT[n % 2], f"hT{n % 2}"

        def proj(Pt, pname, c0, c1):
            for kc in range(8):
                MM(s, Pt, HT[:, kc, :], win[:, kc, c0:c1], [htn, f"win{kc}"], [pname], start=(kc == 0), stop=(kc == 7))
        proj(pq[0][:], "pq0", 0, 512)
        proj(pq[1][:], "pq1", 512, 1024)
        proj(pv[:], "pv", 1024, 1536)
        proj(pf[:, 0:264], "pf", 1536, 1800)
        proj(pp[:, 0:256], "pp", 1800, 2056)

    def evac(n):
        b = n % 2
        CP(s, "act", qe[0][b][:], pq[0][:], ["pq0"], [f"qe0_{b}"])
        CP(s, "dve", qe[1][b][:], pq[1][:], ["pq1"], [f"qe1_{b}"])
        VS = vst[b]
        CP(s, "act", VS[:, :, 0:64], pv[:].rearrange("p (h d) -> p h d", h=8), ["pv"], [f"vst{b}"])
        US = ust[b]
        CP(s, "dve", US[:, 0:256], pf[:, 8:264], ["pf"], [f"ust{b}"])
        TT_(s, "dve", fls[b][:], pf[:, 0:8], bfg[:], ALU.add, ["pf", "bfg"], [f"fl{b}"])
        CP(s, "act", US[:, 256:512], pp[:, 0:256], ["pp"], [f"ust{b}"])

    def forget(n):
        q_, j = divmod(n, NJ)
        b = n % 2
        fl = fls[b]
        ACTF(s, fl[:], fl[:], AF.Exp, [f"fl{b}"], [f"fl{b}"], scale=-1.0)
        ACTF(s, sp_[:], fl[:], AF.Ln, [f"fl{b}"], ["sp_"], scale=1.0, bias=1.0)
        if j == 0:
            s.op("pool", lambda e: e.memset(carry[:], 0.0), writes=["carry"])
        MM(s, pc[:, 0:8], trif[:], sp_[:], ["trif", "sp_"], ["pf"])
        MM(s, pc[:, 8:16], onesf[:], sp_[:], ["onesf", "sp_"], ["pf"])
        cum = cumall[:, q_, j, :]
        TT_(s, "dve", cum, carry[:], pc[:, 0:8], ALU.subtract, ["carry", "pf"], ["cumall"])
        TT_(s, "dve", carry[:], carry[:], pc[:, 8:16], ALU.subtract, ["carry", "pf"], ["carry"])

    def post_a(n):
        q_, j = divmod(n, NJ)
        b = n % 2
        VS, US = vst[b], ust[b]
        cum = cumall[:, q_, j, :]
        QA, KA = qa[0][b], qa[1][b]
        CP(s, "dve", QA[:, :, 67], cum, ["cumall"], [f"qaug0_{b}"])
        yield
        TT_(s, "dve", r1[:], cum, QA[:, :, 67], ALU.subtract, ["cumall", f"qaug0_{b}"], ["r1"])
        yield
        CP(s, "dve", QA[:, :, 68], r1[:], ["r1"], [f"qaug0_{b}"])
        yield
        TT_(s, "dve", r2[:], r1[:], QA[:, :, 68], ALU.subtract, ["r1", f"qaug0_{b}"], ["r2"])
        yield
        CP(s, "dve", QA[:, :, 69], r2[:], ["r2"], [f"qaug0_{b}"])
        yield
        TS(s, "dve", KA[:, :, 64:67], QA[:, :, 67:70], -1.0, ALU.mult, [f"qaug0_{b}"], [f"qaug1_{b}"])
        yield
        for i in range(2):
            PQ = qe[i][b]
            ACTF(s, sq[i][:], PQ[:], AF.Square, [f"qe{i}_{b}"], [f"sq{i}"])
            yield
            s.op("dve", lambda e, i=i: e.tensor_reduce(out=ssq[i][:], in_=sq[i][:].rearrange("p (h d) -> p h d", h=8),
                                                       axis=AX.X, op=ALU.add),
                 reads=[f"sq{i}"], writes=[f"ssq{i}"])
            yield
            ACTF(s, rq[i][:], ssq[i][:], AF.Ln, [f"ssq{i}"], [f"rq{i}"], scale=1.0 / DH, bias=EPS)
            yield
            ACTF(s, rq[i][:], rq[i][:], AF.Exp, [f"rq{i}"], [f"rq{i}"], scale=-0.5)
            yield
            TT_(s, "dve", qn[i][:], PQ[:].rearrange("p (h d) -> p h d", h=8),
                rq[i][:].unsqueeze(2).broadcast_to([128, 8, DH]), ALU.mult, [f"qe{i}_{b}", f"rq{i}"], [f"qn{i}"])
            yield
            TT_(s, "dve", qa[i][b][:, :, 0:64], qn[i][:], qkg[:, i, :, :], ALU.mult, [f"qn{i}", "qkg"],
                [f"qg{i}_{b}"])
            yield
        s.dma("pool", scr["v"][q_, j * 128:(j + 1) * 128, :, :], VS[:], reads=[f"vst{n % 2}"])
        yield
        s.dma("pool", scr["u"][q_, j * 128:(j + 1) * 128, :], US[:], reads=[f"ust{n % 2}"])
        yield

    def post_b(n):
        q_, j = divmod(n, NJ)
        b = n % 2
        for i in range(2):
            pvw = ptq[i][:].bitcast(BF16)
            for h in range(8):
                TR(s, pvw[0:70, h * 128:(h + 1) * 128], qa[i][b][:, h, :], ident[:],
                   [f"qg{i}_{b}", f"qaug{i}_{b}", "ident"], [f"ptq{i}"])
            QT = qTs[i][n % 2]
            CP(s, "act" if i == 0 else "dve", QT[0:70, :, :], pvw[0:70, :].rearrange("p (h t) -> p h t", h=8),
               [f"ptq{i}"], [f"qTs{i}_{n % 2}"])
            yield
            dst = scr["qT" if i == 0 else "kT"]
            s.dma("pool", dst[q_, :, :, j * 128:(j + 1) * 128].rearrange("h p t -> p h t"), QT[0:70, :, :],
                  reads=[f"qTs{i}_{n % 2}"])
            yield

    load(0)
    if NCH > 1:
        load(1)
    def main_stream(t):
        if t + 2 < NCH:
            load(t + 2)
        if t < NCH:
            front_elem(t)
        yield
        if 1 <= t <= NCH:
            HT, htn = hT[(t - 1) % 2], f"hT{(t - 1) % 2}"
            for (Pt, pname, c0, c1) in ((pq[0][:], "pq0", 0, 512), (pq[1][:], "pq1", 512, 1024),
                                        (pv[:], "pv", 1024, 1536), (pf[:, 0:264], "pf", 1536, 1800),
                                        (pp[:, 0:256], "pp", 1800, 2056)):
                for kc in range(8):
                    MM(s, Pt, HT[:, kc, :], win[:, kc, c0:c1], [htn, f"win{kc}"], [pname], start=(kc == 0),
                       stop=(kc == 7))
                    if kc % 2 == 1:
                        yield
        if t < NCH:
            front_tr(t)
        yield
        if 1 <= t <= NCH:
            evac(t - 1)
            forget(t - 1)
        yield

    for t in range(NCH + 3):
        gens = [main_stream(t)]
        if 2 <= t < NCH + 2:
            gens.append(post_a(t - 2))
        if t >= 3:
            gens.append(post_b(t - 3))
        while gens:
            for g in list(gens):
                try:
                    next(g)
                except StopIteration:
                    gens.remove(g)
    c.end()


def phase_b(c, NSEQ, S, G, scr):
    nc, s = c.nc, c.s
    NJ = S // 128
    NI = NJ // 4
    c.begin()
    tri = c.sb("tri", [128, 128], BF16)
    vall = [c.sb(f"vall{i}", [128, NJ, 8, 65], BF16) for i in range(2)]
    qT = [c.sb(f"qT{i}", [128, S], BF16) for i in range(2)]
    kT = [c.sb(f"kT{i}", [128, S], BF16) for i in range(2)]
    NPT = 4
    pT = [c.sb(f"pT{i}", [128, 512], BF16) for i in range(NPT)]
    ybuf = [c.sb(f"ybuf{i}", [128, NJ, 64], BF16) for i in range(2)]
    rc = [c.sb(f"rc{i}", [128, 1], F32) for i in range(4)]
    NPS = 4
    pS = [c.ps(f"pS{i}") for i in range(NPS)]
    pO = [c.ps(f"pO{i}") for i in range(4)]
    make_tri(c, tri, "tri")
    LOOK = 3
    blocks = []
    for q_ in range(NSEQ):
        for h in range(8):
            for i4 in range(NI):
                for j in range(4 * i4 + 4):
                    blocks.append((q_, h, i4, j))

    def stage1(bi):
        q_, h, i4, j = blocks[bi]
        g = q_ * 8 + h
        QT, KT = qT[g % 2], kT[g % 2]
        if h == 0 and i4 == 0 and j == 0 and q_ == 0:
            s.dma("sp", vall[q_ % 2][:], scr["v"][q_].rearrange("(j p) h d -> p j h d", p=128), writes=[f"vall{q_ % 2}"])
        if h == 4 and i4 == 0 and j == 0 and q_ + 1 < NSEQ:
            s.dma("act", vall[(q_ + 1) % 2][:], scr["v"][q_ + 1].rearrange("(j p) h d -> p j h d", p=128),
                  writes=[f"vall{(q_ + 1) % 2}"])
        if i4 == 0 and j == 0:
            if g == 0:
                s.dma("sp", QT[0:70, :], scr["qT"][q_, h], writes=[f"qT{g % 2}"])
                s.dma("act", KT[0:70, :], scr["kT"][q_, h], writes=[f"kT{g % 2}"])
            g2 = g + 1
            if g2 < NSEQ * 8:
                q2, h2 = divmod(g2, 8)
                s.dma("sp", qT[g2 % 2][0:70, :], scr["qT"][q2, h2], writes=[f"qT{g2 % 2}"])
                s.dma("sp", kT[g2 % 2][0:70, :], scr["kT"][q2, h2], writes=[f"kT{g2 % 2}"])
        jj = j - 4 * i4
        c0 = 128 * max(jj, 0)
        PS, psn = pS[bi % NPS], f"pS{bi % NPS}"
        PT, ptn = pT[bi % NPT], f"pT{bi % NPT}"
        MM(s, PS[:, c0:512], KT[0:70, j * 128:(j + 1) * 128], QT[0:70, i4 * 512 + c0:(i4 + 1) * 512],
           [f"qT{g % 2}", f"kT{g % 2}"], [psn])
        ACTF(s, PT[:, c0:512], PS[:, c0:512], AF.Exp, [psn], [ptn])
        if jj >= 0:
            TT_(s, "pool", PT[:, c0:c0 + 128], PT[:, c0:c0 + 128], tri[:], ALU.mult, [ptn, "tri"], [ptn])

    def stage2(bi):
        q_, h, i4, j = blocks[bi]
        g = q_ * 8 + h
        PT, ptn = pT[bi % NPT], f"pT{bi % NPT}"
        YB = ybuf[g % 2]
        jj = j - 4 * i4
        for tt in range(max(jj, 0), 4):
            last = (j == 4 * i4 + tt)
            MM(s, pO[tt][:, 0:65], PT[:, tt * 128:(tt + 1) * 128], vall[q_ % 2][:, j, h, :], [ptn, f"vall{q_ % 2}"],
               [f"pO{tt}"], start=(j == 0), stop=last)
            if last:
                s.op("dve", lambda e, tt=tt: e.reciprocal(out=rc[tt][:], in_=pO[tt][:, 64:65]), reads=[f"pO{tt}"],
                     writes=[f"rc{tt}"])
                TS(s, "dve", YB[:, 4 * i4 + tt, :], pO[tt][:, 0:64], rc[tt][:, 0:1], ALU.mult, [f"pO{tt}", f"rc{tt}"],
                   [f"ybuf{g % 2}"])
        if i4 == NI - 1 and j == NJ - 1:
            s.dma("pool", scr["yattn"][q_, :, h * 64:(h + 1) * 64].rearrange("(j p) c -> p j c", p=128), YB[:],
                  reads=[f"ybuf{g % 2}"])

    nb = len(blocks)
    for t in range(nb + LOOK):
        if t < nb:
            stage1(t)
        if t >= LOOK:
            stage2(t - LOOK)
    c.end()


def make_scratch(nc, NSEQ, S, debug_out=False):
    kind = {"kind": "ExternalOutput"} if debug_out else {}
    scr = {}
    scr["qT"] = nc.dram_tensor("scr_qT", [NSEQ, 8, 70, S], BF16).ap()
    scr["kT"] = nc.dram_tensor("scr_kT", [NSEQ, 8, 70, S], BF16).ap()
    scr["v"] = nc.dram_tensor("scr_v", [NSEQ, S, 8, 65], BF16).ap()
    scr["cend"] = nc.dram_tensor("scr_cend", [1, NSEQ, S // 128, 8], F32).ap()
    scr["u"] = nc.dram_tensor("scr_u", [NSEQ, S, 512], F32).ap()
    scr["yattn"] = nc.dram_tensor("scr_yattn", [NSEQ, S, 512], BF16, **kind).ap()
    scr["yssmT"] = nc.dram_tensor("scr_yssmT", [NSEQ, 256, S], BF16, **kind).ap()
    scr["ypoolT"] = nc.dram_tensor("scr_ypoolT", [NSEQ, 256, S], BF16, **kind).ap()
    return scr


PARAM_SHAPES = {
    "mix_norm": [DEPTH, D], "w_in": [DEPTH, D, INC], "b_forget": [DEPTH, NH], "q_norm": [DEPTH, DH],
    "k_norm": [DEPTH, DH], "ssm_a_re": [DEPTH, 16, 64], "ssm_a_im": [DEPTH, 16, 64], "ssm_log_dt": [DEPTH, 16],
    "ssm_b_re": [DEPTH, 16, 64, 16], "ssm_b_im": [DEPTH, 16, 64, 16], "ssm_c_re": [DEPTH, 16, 16, 64],
    "ssm_c_im": [DEPTH, 16, 16, 64], "ssm_d": [DEPTH, 256], "w_glu": [DEPTH, 256, 512],
    "pool_w": [DEPTH, 4, 64, 64], "pool_scale": [DEPTH, 256], "w_br_attn": [DEPTH, 512, D],
    "w_br_ssm": [DEPTH, 256, D], "w_br_pool": [DEPTH, 256, D], "b_gate": [DEPTH, 3 * D],
    "w_out": [DEPTH, D, D], "mlp_norm": [DEPTH, D], "w_up": [DEPTH, D, DFF], "w_down": [DEPTH, DFF, D],
}


def declare_params(nc):
    return {k: nc.dram_tensor(k, shp, F32, kind="ExternalInput").ap() for k, shp in PARAM_SHAPES.items()}


def build_test_ab(NSEQ, S, l=0):
    nc = bass.Bass("TRN2", target_bir_lowering=False)
    x = nc.dram_tensor("x", [NSEQ * S, D], F32, kind="ExternalInput").ap()
    P = declare_params(nc)
    scr = make_scratch(nc, NSEQ, S, debug_out=True)
    with contextlib.ExitStack() as es:
        c = Ctx(nc, es)
        G = {"cumall": es.enter_context(nc.sbuf_tensor("cumall", [128, NSEQ, S // 128, 8], F32))}
        phase_a(c, l, NSEQ, S, x, P, G, scr, do_ssm=False, do_pool=False)
        phase_b(c, NSEQ, S, G, scr)
        print("ops", c.s.total, "waits", c.s.nwaits)
    return nc


def TT_(s, eng, out, in0, in1, op, reads, writes):
    return s.op(eng, lambda e: e.tensor_tensor(out=out, in0=in0, in1=in1, op=op), reads, writes)


def TS(s, eng, out, in0, scalar1, op0, reads, writes, scalar2=None, op1=None):
    if op1 is None:
        return s.op(eng, lambda e: e.tensor_scalar(out=out, in0=in0, scalar1=scalar1, scalar2=None, op0=op0),
                    reads, writes)
    return s.op(eng, lambda e: e.tensor_scalar(out=out, in0=in0, scalar1=scalar1, scalar2=scalar2, op0=op0, op1=op1),
                reads, writes)


def ACTF(s, out, in_, func, reads, writes, scale=1.0, bias=None):
    if bias is None:
        return s.op("act", lambda e: e.activation(out=out, in_=in_, func=func, scale=scale), reads, writes)
    return s.op("act", lambda e: e.activation(out=out, in_=in_, func=func, scale=scale, bias=bias), reads, writes)


def CP(s, eng, out, in_, reads, writes):
    if eng == "act":
        return s.op("act", lambda e: e.copy(out=out, in_=in_), reads, writes)
    return s.op(eng, lambda e: e.tensor_copy(out=out, in_=in_), reads, writes)


def MM(s, out, lhsT, rhs, reads, writes, start=True, stop=True):
    return s.op("pe", lambda e: e.matmul(out, lhsT=lhsT, rhs=rhs, start=start, stop=stop), reads, writes)


def TR(s, out, in_, ident, reads, writes):
    return s.op("pe", lambda e: e.transpose(out=out, in_=in_, identity=ident), reads, writes)


TWO_PI = 2.0 * math.pi
MAGIC = 12582912.0


def sincos(s, eng, ang, o_sin, o_cos, t1, t2, rd, tag):
    n1, n2 = "sc1" + tag, "sc2" + tag
    for which, o in (("s", o_sin), ("c", o_cos)):
        if which == "s":
            TS(s, eng, t1, ang, 1.0 / TWO_PI, ALU.mult, rd, [n1])
        else:
            TS(s, eng, t1, ang, 1.0 / TWO_PI, ALU.mult, rd, [n1], scalar2=0.25, op1=ALU.add)
        TS(s, eng, t2, t1, MAGIC, ALU.add, [n1], [n2])
        TS(s, eng, t2, t2, MAGIC, ALU.subtract, [n2], [n2])
        TT_(s, eng, t2, t1, t2, ALU.subtract, [n1, n2], [n2])
        ACTF(s, o, t2, AF.Sin, [n2], [("sin" if which == "s" else "cos") + tag], scale=TWO_PI * (1 - 1e-6))


POOL_WINDOWS = (2, 4, 8, 16)


def phase_a2(c, l, NSEQ, S, P, scr):
    nc, s = c.nc, c.s
    NJ = S // 128
    c.begin()
    I32 = mybir.dt.int32
    ident = c.sb("ident", [128, 128], BF16)
    identf = c.sb("identf", [128, 128], F32)
    tribf = c.sb("tribf", [128, 128], BF16)
    iot_i = c.sb("iot_i", [128, 128], I32)
    iot = c.sb("iot", [128, 128], F32)
    pcol_i = c.sb("pcol_i", [128, 1], I32)
    pcol = c.sb("pcol", [128, 1], F32)
    npcol = c.sb("npcol", [128, 1], F32)
    are = c.sb("are", [128, 1024], F32)
    aim = c.sb("aim", [128, 1024], F32)
    ldt = c.sb("ldt", [128, 16], F32)
    dtr = c.sb("dtr", [128, 1024], F32)
    t1 = c.sb("t1", [128, 1024], F32)
    t2 = c.sb("t2", [128, 1024], F32)
    t3 = c.sb("t3", [128, 1024], F32)
    t4 = c.sb("t4", [128, 1024], F32)
    t5 = c.sb("t5", [128, 1024], F32)
    t6 = c.sb("t6", [128, 1024], F32)
    cfr = c.sb("cfr", [128, 1024], F32)
    cfi = c.sb("cfi", [128, 1024], F32)
    Tr = c.sb("Tr", [128, 1024], F32)
    Ti = c.sb("Ti", [128, 1024], F32)
    TAr = c.sb("TAr", [128, 8, 128], F32)
    TAi = c.sb("TAi", [128, 8, 128], F32)
    acol = c.sb("acol", [128, 3, 8], F32)
    BX = c.sb("BX", [128, 8, 2, 128], F32)
    BXb = c.sb("BXb", [128, 8, 2, 128], BF16)
    BT = c.sb("BT", [128, 8, 2, 128], BF16)
    CN = c.sb("CN", [32, 8, 2, 128], F32)
    CNb = c.sb("CNb", [32, 8, 2, 128], BF16)
    CX = c.sb("CX", [128, 8, 2, 32], BF16)
    drb = c.sb("drb", [128, 256], F32)
    wg_st = c.sb("wg_st", [128, 2, 512], F32)
    wglu = c.sb("wglu", [128, 2, 512], BF16)
    pw_st = c.sb("pw_st", [128, 2, 64], F32)
    PW = c.sb("PW", [128, 2, 64], BF16)
    pscol = c.sb("pscol", [128, 2], F32)
    MT = c.sb("MT", [128, 12, 128], BF16)
    mtmp = c.sb("mtmp", [128, 128], F32)
    mrat = c.sb("mrat", [128, 128], F32)
    ut = [c.sb(f"ut{i}", [128, 512], F32) for i in range(3)]
    ub = [c.sb(f"ub{i}", [128, 512], BF16) for i in range(3)]
    uT = c.sb("uT", [128, 2, 128], BF16)
    m1 = c.sb("m1", [128, 4, 128], F32)
    m2 = c.sb("m2", [128, 4, 128], F32)
    m3 = c.sb("m3", [128, 4, 128], F32)
    m4 = c.sb("m4", [128, 4, 128], F32)
    Wt = c.sb("Wt", [128, 8, 2, 128], BF16)
    Pr = c.sb("Pr", [128, 8, 128], F32)
    Pi = c.sb("Pi", [128, 8, 128], F32)
    Xt = c.sb("Xt", [128, 8, 2, 128], BF16)
    car = c.sb("car", [128, 2, 8], F32)
    sn = c.sb("sn", [128, 2, 4], F32)
    sm = c.sb("sm", [128, 4, 4], F32)
    du = c.sb("du", [128, 256], F32)
    yv = c.sb("yv", [128, 256], F32)
    y2 = c.sb("y2", [128, 256], F32)
    sg = c.sb("sg", [128, 256], F32)
    gy = c.sb("gy", [128, 256], BF16)
    gyT = c.sb("gyT", [128, 2, 128], BF16)
    sgb = c.sb("sgb", [128, 2, 128], F32)
    ysT = [c.sb(f"ysT{i}", [128, 2, 128], BF16) for i in range(2)]
    plT = c.sb("plT", [128, 2, 128], BF16)
    ypT = [c.sb(f"ypT{i}", [128, 2, 128], BF16) for i in range(2)]
    pbu = c.ps("pbu")
    ppf = c.ps("ppf", [128, 2048])
    pym = c.ps("pym")
    pglu = c.ps("pglu")
    ppl = c.ps("ppl")
    py = pym
    w1 = c.sb("w1", [128, 2, 128], F32)
    w2 = c.sb("w2", [128, 2, 128], F32)
    w3 = c.sb("w3", [128, 2, 128], F32)
    w4 = c.sb("w4", [128, 2, 128], F32)

    make_identity(c, ident)
    make_tri(c, tribf, "tribf")
    s.op("pool", lambda e: e.iota(iot_i[:], pattern=[[1, 128]], base=0, channel_multiplier=0), writes=["iot_i"])
    CP(s, "dve", iot[:], iot_i[:], ["iot_i"], ["iot"])
    s.op("pool", lambda e: e.iota(pcol_i[:], pattern=[[0, 1]], base=0, channel_multiplier=1), writes=["pcol_i"])
    CP(s, "dve", pcol[:], pcol_i[:], ["pcol_i"], ["pcol"])
    TS(s, "dve", npcol[:], pcol[:], -1.0, ALU.mult, ["pcol"], ["npcol"])
    CP(s, "dve", identf[:], ident[:], ["ident"], ["identf"])

    s.dma("sp", are[:], P["ssm_a_re"][l:l + 1].rearrange("o g n -> o (g n)").partition_broadcast(128), writes=["are"])
    s.dma("act", aim[:], P["ssm_a_im"][l:l + 1].rearrange("o g n -> o (g n)").partition_broadcast(128), writes=["aim"])
    s.dma("sp", ldt[:], P["ssm_log_dt"][l:l + 1, :].partition_broadcast(128), writes=["ldt"])
    s.dma("sp", acol[:, 0, :], P["ssm_a_re"][l].rearrange("(k gl) n -> (gl n) k", gl=2), writes=["acol"],
          allow_slow_non_contiguous=True)
    s.dma("sp", acol[:, 1, :], P["ssm_a_im"][l].rearrange("(k gl) n -> (gl n) k", gl=2), writes=["acol"],
          allow_slow_non_contiguous=True)
    for gl in range(2):
        s.dma("sp", acol[64 * gl:64 * gl + 64, 2, :],
              P["ssm_log_dt"][l:l + 1, :].rearrange("o (k gl) -> o k gl", gl=2)[:, :, gl].partition_broadcast(64),
              writes=["acol"], allow_slow_non_contiguous=True)
    s.dma("sp", drb[:], P["ssm_d"][l:l + 1, :].partition_broadcast(128), writes=["drb"])
    s.dma("act", wg_st[:], P["w_glu"][l].rearrange("(kc p) g -> p kc g", p=128), writes=["wg_st"])
    CP(s, "pool", wglu[:], wg_st[:], ["wg_st"], ["wglu"])
    for g in range(4):
        s.dma("sp", pw_st[64 * (g % 2):64 * (g % 2) + 64, g // 2, :], P["pool_w"][l, g], writes=["pw_st"])
    CP(s, "pool", PW[:], pw_st[:], ["pw_st"], ["PW"])
    s.dma("sp", pscol[:], P["pool_scale"][l].rearrange("(kc p) -> p kc", p=128), writes=["pscol"],
          allow_slow_non_contiguous=True)
    s.op("pool", lambda e: e.memset(BX[:], 0.0), writes=["BX"])
    s.op("pool", lambda e: e.memset(CN[:], 0.0), writes=["CN"])
    nd = 0
    for k in range(8):
        for gl in range(2):
            g = 2 * k + gl
            c0 = 32 * (k % 4) + 16 * gl
            for part, nm in ((0, "ssm_b_re"), (1, "ssm_b_im")):
                s.dma("sp" if nd % 2 == 0 else "act", BX[64 * gl:64 * gl + 64, k, part, c0:c0 + 16], P[nm][l, g],
                      reads=[], writes=["BX"])
                nd += 1
            for part, nm in ((0, "ssm_c_re"), (1, "ssm_c_im")):
                s.dma("sp" if nd % 2 == 0 else "act", CN[16 * gl:16 * gl + 16, k, part, 64 * gl:64 * gl + 64],
                      P[nm][l, g], reads=[], writes=["CN"])
                nd += 1
    CP(s, "dve", BXb[:], BX[:], ["BX"], ["BXb"])
    CP(s, "dve", CNb[:], CN[:], ["CN"], ["CNb"])
    pmv = ppf[:, 0:512].bitcast(BF16)
    for k in range(8):
        for part in range(2):
            i = (k * 2 + part) % 8
            TR(s, pmv[:, i * 128:(i + 1) * 128], BXb[:, k, part, :], ident[:], ["BXb", "ident"], ["ppf"])
        if k % 4 == 3:
            k0 = k - 3
            CP(s, "dve", BT[:, k0:k0 + 4, :, :], pmv.rearrange("p (k a m) -> p k a m", k=4, a=2), ["ppf"], ["BT"])
    for k in range(8):
        for part in range(2):
            i = k * 2 + part
            TR(s, pmv[:, i * 32:(i + 1) * 32], CNb[:, k, part, :], ident[0:32, 0:32], ["CNb", "ident"], ["ppf"])
    cxv = pmv[:, 0:512].rearrange("p (k a m) -> p k a m", k=8, a=2)
    CP(s, "dve", CX[:, :, 0, :], cxv[:, :, 0, :], ["ppf"], ["CX"])
    TS(s, "dve", CX[:, :, 1, :], cxv[:, :, 1, :], -1.0, ALU.mult, ["ppf"], ["CX"])

    ACTF(s, ldt[:], ldt[:], AF.Exp, ["ldt"], ["ldt"])
    CP(s, "dve", dtr[:].rearrange("p (g n) -> p g n", g=16), ldt[:].unsqueeze(2).broadcast_to([128, 16, 64]),
       ["ldt"], ["dtr"])
    ardt, wr = t5, t6
    TT_(s, "dve", ardt[:], are[:], dtr[:], ALU.mult, ["are", "dtr"], ["ardt"])
    TT_(s, "dve", wr[:], aim[:], dtr[:], ALU.mult, ["aim", "dtr"], ["wr"])
    sincos(s, "dve", wr[:], t3[:], t4[:], t1[:], t2[:], ["wr"], "0")
    ACTF(s, t1[:], ardt[:], AF.Exp, ["ardt", "sc10"], ["mag1"])
    TT_(s, "dve", t4[:], t4[:], t1[:], ALU.mult, ["cos0", "mag1"], ["cos0"])
    TT_(s, "dve", t3[:], t3[:], t1[:], ALU.mult, ["sin0", "mag1"], ["sin0"])
    TS(s, "dve", t4[:], t4[:], -1.0, ALU.add, ["cos0"], ["cos0"])
    TT_(s, "dve", t1[:], are[:], are[:], ALU.mult, ["are", "mag1", "sin0"], ["mag1"])
    TT_(s, "dve", t2[:], aim[:], aim[:], ALU.mult, ["aim", "sc20"], ["sc20"])
    TT_(s, "dve", t1[:], t1[:], t2[:], ALU.add, ["mag1", "sc20"], ["mag1"])
    s.op("dve", lambda e: e.reciprocal(out=t1[:], in_=t1[:]), reads=["mag1"], writes=["mag1"])
    TT_(s, "dve", cfr[:], t4[:], are[:], ALU.mult, ["cos0", "are"], ["cfr"])
    TT_(s, "dve", t2[:], t3[:], aim[:], ALU.mult, ["sin0", "aim", "sc20"], ["sc20"])
    TT_(s, "dve", cfr[:], cfr[:], t2[:], ALU.add, ["cfr", "sc20"], ["cfr"])
    TT_(s, "dve", cfr[:], cfr[:], t1[:], ALU.mult, ["cfr", "mag1"], ["cfr"])
    TT_(s, "dve", cfi[:], t3[:], are[:], ALU.mult, ["sin0", "are"], ["cfi"])
    TT_(s, "dve", t2[:], t4[:], aim[:], ALU.mult, ["cos0", "aim", "sc20", "cfr"], ["sc20"])
    TT_(s, "dve", cfi[:], cfi[:], t2[:], ALU.subtract, ["cfi", "sc20"], ["cfi"])
    TT_(s, "dve", cfi[:], cfi[:], t1[:], ALU.mult, ["cfi", "mag1"], ["cfi"])
    TS(s, "dve", dtr[:], wr[:], pcol[:, 0:1], ALU.mult, ["wr", "pcol", "dtr"], ["ang"])
    sincos(s, "dve", dtr[:], t3[:], t4[:], t1[:], t2[:], ["ang", "cfi", "cfr"], "1")
    s.op("act", lambda e: e.activation(out=t1[:], in_=ardt[:], func=AF.Exp, scale=npcol[:, 0:1]),
         reads=["ardt", "npcol", "sc11", "cos1"], writes=["mag2"])
    TT_(s, "dve", t4[:], t4[:], t1[:], ALU.mult, ["cos1", "mag2"], ["cos1"])
    TT_(s, "dve", t3[:], t3[:], t1[:], ALU.mult, ["sin1", "mag2"], ["sin1"])
    TT_(s, "dve", Tr[:], t4[:], cfr[:], ALU.mult, ["cos1", "cfr"], ["Tr"])
    TT_(s, "dve", t2[:], t3[:], cfi[:], ALU.mult, ["sin1", "cfi", "sc21"], ["sc21"])
    TT_(s, "dve", Tr[:], Tr[:], t2[:], ALU.add, ["Tr", "sc21"], ["Tr"])
    TT_(s, "dve", Ti[:], t4[:], cfi[:], ALU.mult, ["cos1", "cfi"], ["Ti"])
    TT_(s, "dve", t2[:], t3[:], cfr[:], ALU.mult, ["sin1", "cfr", "Tr"], ["sc21"])
    TT_(s, "dve", Ti[:], Ti[:], t2[:], ALU.subtract, ["Ti", "sc21"], ["Ti"])
    ACTF(s, acol[:, 2, :], acol[:, 2, :], AF.Exp, ["acol"], ["acol"])
    TT_(s, "dve", acol[:, 0, :], acol[:, 0, :], acol[:, 2, :], ALU.mult, ["acol"], ["acol"])
    TT_(s, "dve", acol[:, 1, :], acol[:, 1, :], acol[:, 2, :], ALU.mult, ["acol"], ["acol"])
    angf = t5[:].rearrange("p (k t) -> p k t", k=8)
    magf = t6[:].rearrange("p (k t) -> p k t", k=8)
    for k in range(8):
        TS(s, "dve", angf[:, k, :], iot[:], acol[:, 1, k:k + 1], ALU.mult, ["iot", "acol", "ardt", "Tr", "Ti"], ["angf"])
        s.op("act", lambda e, k=k: e.activation(out=magf[:, k, :], in_=iot[:], func=AF.Exp, scale=acol[:, 0, k:k + 1]),
             reads=["iot", "acol", "wr", "ang", "Tr", "Ti"], writes=["magf"])
    sincos(s, "dve", t5[:], t3[:], t4[:], t1[:], t2[:], ["angf", "Tr", "Ti"], "2")
    TT_(s, "dve", TAr[:].rearrange("p k t -> p (k t)"), t4[:], t6[:], ALU.mult, ["cos2", "magf"], ["TAr"])
    TT_(s, "dve", TAi[:].rearrange("p k t -> p (k t)"), t3[:], t6[:], ALU.mult, ["sin2", "magf"], ["TAi"])

    for g, w in enumerate(POOL_WINDOWS):
        s.op("pool", lambda e, w=w: e.memset(mtmp[:], 1.0 / w), reads=["mtmp", "mrat", "MT"], writes=["mtmp"])
        s.op("pool", lambda e: e.affine_select(out=mtmp[:], in_=mtmp[:], pattern=[[1, 128]], compare_op=ALU.is_ge,
                                               fill=0.0, base=0, channel_multiplier=-1),
             reads=["mtmp"], writes=["mtmp"])
        s.op("pool", lambda e, w=w: e.affine_select(out=mtmp[:], in_=mtmp[:], pattern=[[-1, 128]],
                                                    compare_op=ALU.is_ge, fill=0.0, base=w - 1, channel_multiplier=1),
             reads=["mtmp"], writes=["mtmp"])
        TT_(s, "pool", MT[:, g * 3 + 0, :], mtmp[:], identf[:], ALU.subtract, ["mtmp", "identf"], ["MT"])
        TS(s, "dve", mrat[:], iot[:], 1.0, ALU.add, ["iot"], ["mrat"])
        s.op("dve", lambda e: e.reciprocal(out=mrat[:], in_=mrat[:]), reads=["mrat"], writes=["mrat"])
        TS(s, "dve", mrat[:], mrat[:], float(w), ALU.mult, ["mrat"], ["mrat"], scalar2=1.0, op1=ALU.max)
        TT_(s, "dve", mrat[:], mrat[:], mtmp[:], ALU.mult, ["mrat", "mtmp"], ["mrat"])
        TT_(s, "dve", MT[:, g * 3 + 2, :], mrat[:], identf[:], ALU.subtract, ["mrat", "identf"], ["MT"])
        s.op("pool", lambda e, w=w: e.memset(mtmp[:], 1.0 / w), reads=["mtmp", "mrat", "MT"], writes=["mtmp"])
        s.op("pool", lambda e, w=w: e.affine_select(out=mtmp[:], in_=mtmp[:], pattern=[[-1, 128]],
                                                    compare_op=ALU.is_ge, fill=0.0, base=-(129 - w),
                                                    channel_multiplier=1),
             reads=["mtmp"], writes=["mtmp"])
        CP(s, "pool", MT[:, g * 3 + 1, :], mtmp[:], ["mtmp"], ["MT"])

    NCH = NSEQ * NJ
    pmq = pym[:, 256:512].bitcast(BF16)

    def load(n):
        q_, j = divmod(n, NJ)
        s.dma("sp", ut[n % 3][:], scr["u"][q_, j * 128:(j + 1) * 128, :], writes=[f"ut{n % 3}"])

    def S1(n):
        U, UB = ut[n % 3], ub[n % 3]
        un, ubn = f"ut{n % 3}", f"ub{n % 3}"
        CP(s, "act", UB[:], U[:], [un], [ubn])
        for kc in range(2):
            TR(s, pmq[:, kc * 128:(kc + 1) * 128], UB[:, kc * 128:(kc + 1) * 128], ident[:], [ubn, "ident"], ["pym"])
        CP(s, "dve", uT[:], pmq[:, 0:256].rearrange("p (k t) -> p k t", k=2), ["pym"], ["uT"])
        for qt in range(4):
            k0 = qt * 2
            for kk in range(2):
                k = k0 + kk
                MM(s, pbu[:, kk * 256:(kk + 1) * 256], uT[:, k // 4, :], BT[:, k, :, :].rearrange("p a m -> p (a m)"),
                   ["uT", "BT"], ["pbu"])
            buv = pbu[:].rearrange("p (k a m) -> p k a m", k=2, a=2)
            trv = Tr[:, k0 * 128:(k0 + 2) * 128].rearrange("p (k m) -> p k m", k=2)
            tiv = Ti[:, k0 * 128:(k0 + 2) * 128].rearrange("p (k m) -> p k m", k=2)
            yield
            TT_(s, "dve", w1[:], buv[:, :, 0, :], trv, ALU.mult, ["pbu", "Tr"], ["w1"])
            TT_(s, "dve", w2[:], buv[:, :, 1, :], tiv, ALU.mult, ["pbu", "Ti"], ["w2"])
            yield
            TT_(s, "pool", Wt[:, k0:k0 + 2, 0, :], w1[:], w2[:], ALU.subtract, ["w1", "w2"], [f"Wt{qt}"])
            TT_(s, "dve", w3[:], buv[:, :, 1, :], trv, ALU.mult, ["pbu", "Tr"], ["w3"])
            TT_(s, "dve", w4[:], buv[:, :, 0, :], tiv, ALU.mult, ["pbu", "Ti"], ["w4"])
            yield
            TT_(s, "pool", Wt[:, k0:k0 + 2, 1, :], w3[:], w4[:], ALU.add, ["w3", "w4"], [f"Wt{qt}"])
            yield
            for kk in range(2):
                for part in range(2):
                    i = (k0 + kk) * 2 + part
                    MM(s, ppf[:, i * 128:(i + 1) * 128], Wt[:, k0 + kk, part, :], tribf[:], [f"Wt{qt}", "tribf"],
                       [f"ppf{qt // 2}"])

    def XP(n):
        q_, j = divmod(n, NJ)
        if j == 0:
            s.op("pool", lambda e: e.memset(car[:], 0.0), writes=["car"])
        for hf in range(2):
            k0 = hf * 4
            pfv = ppf[:, hf * 1024:(hf + 1) * 1024].rearrange("p (k a t) -> p k a t", k=4, a=2)
            pfn = f"ppf{hf}"
            TT_(s, "dve", Pr[:, k0:k0 + 4, :], pfv[:, :, 0, :],
                car[:, 0, k0:k0 + 4].unsqueeze(2).broadcast_to([128, 4, 128]), ALU.add, [pfn, "car"], [f"Pr{hf}"])
            TT_(s, "dve", Pi[:, k0:k0 + 4, :], pfv[:, :, 1, :],
                car[:, 1, k0:k0 + 4].unsqueeze(2).broadcast_to([128, 4, 128]), ALU.add, [pfn, "car"], [f"Pi{hf}"])
        yield
        for hf in range(2):
            k0 = hf * 4
            prn, pin = f"Pr{hf}", f"Pi{hf}"
            PR, PI = Pr[:, k0:k0 + 4, :], Pi[:, k0:k0 + 4, :]
            tar, tai = TAr[:, k0:k0 + 4, :], TAi[:, k0:k0 + 4, :]
            TT_(s, "dve", m1[:], tar, PR, ALU.mult, ["TAr", prn], ["m1"])
            TT_(s, "pool", m2[:], tai, PI, ALU.mult, ["TAi", pin], ["m2"])
            yield
            TT_(s, "dve", m3[:], tar, PI, ALU.mult, ["TAr", pin], ["m3"])
            TT_(s, "pool", m4[:], tai, PR, ALU.mult, ["TAi", prn], ["m4"])
            yield
            TT_(s, "pool", Xt[:, k0:k0 + 4, 0, :], m1[:], m2[:], ALU.subtract, ["m1", "m2"], [f"Xt{hf}"])
            TT_(s, "dve", sn[:, 0, :], m1[:, :, 127], m2[:, :, 127], ALU.subtract, ["m1", "m2"], ["sn"])
            yield
            TT_(s, "dve", Xt[:, k0:k0 + 4, 1, :], m3[:], m4[:], ALU.add, ["m3", "m4"], [f"Xt{hf}"])
            TT_(s, "dve", sn[:, 1, :], m3[:, :, 127], m4[:, :, 127], ALU.add, ["m3", "m4"], ["sn"])
            yield
            a1r, a1i = TAr[:, k0:k0 + 4, 1], TAi[:, k0:k0 + 4, 1]
            TT_(s, "dve", sm[:, 0, :], a1r, sn[:, 0, :], ALU.mult, ["TAr", "sn"], ["sm"])
            TT_(s, "dve", sm[:, 1, :], a1i, sn[:, 1, :], ALU.mult, ["TAi", "sn"], ["sm"])
            TT_(s, "dve", sm[:, 2, :], a1r, sn[:, 1, :], ALU.mult, ["TAr", "sn"], ["sm"])
            TT_(s, "dve", sm[:, 3, :], a1i, sn[:, 0, :], ALU.mult, ["TAi", "sn"], ["sm"])
            yield
            TT_(s, "dve", car[:, 0, k0:k0 + 4], sm[:, 0, :], sm[:, 1, :], ALU.subtract, ["sm"], ["car"])
            TT_(s, "dve", car[:, 1, k0:k0 + 4], sm[:, 2, :], sm[:, 3, :], ALU.add, ["sm"], ["car"])
            yield

    def REST(n):
        q_, j = divmod(n, NJ)
        U, UB = ut[n % 3], ub[n % 3]
        un, ubn = f"ut{n % 3}", f"ub{n % 3}"
        UBP, ubpn = ub[(n - 1) % 3], f"ub{(n - 1) % 3}"
        for k in range(8):
            for part in range(2):
                MM(s, py[:, 32 * k:32 * k + 32], Xt[:, k, part, :], CX[:, k, part, :], [f"Xt{k // 4}", "CX"], ["pym"],
                   start=(part == 0), stop=(part == 1))
        yield
        TT_(s, "pool", du[:], U[:, 0:256], drb[:], ALU.mult, [un, "drb"], ["du"])
        yield
        TT_(s, "dve", yv[:], py[:, 0:256], du[:], ALU.add, ["pym", "du"], ["yv"])
        yield
        TT_(s, "pool", y2[:], yv[:], yv[:], ALU.mult, ["yv"], ["y2"])
        yield
        TS(s, "pool", y2[:], y2[:], 0.044715, ALU.mult, ["y2"], ["y2"], scalar2=1.0, op1=ALU.add)
        yield
        TT_(s, "pool", y2[:], y2[:], yv[:], ALU.mult, ["y2", "yv"], ["y2"])
        ACTF(s, sg[:], y2[:], AF.Sigmoid, ["y2"], ["sg"], scale=1.5957691216057308)
        yield
        TT_(s, "dve", gy[:], yv[:], sg[:], ALU.mult, ["yv", "sg"], ["gy"])
        yield
        for kc in range(2):
            TR(s, pmq[:, 256 + kc * 128:256 + (kc + 1) * 128], gy[:, kc * 128:(kc + 1) * 128], ident[:],
               ["gy", "ident"], ["pym"])
        CP(s, "act", gyT[:], pmq[:, 256:512].rearrange("p (k t) -> p k t", k=2), ["pym"], ["gyT"])
        for gc in range(4):
            for kc in range(2):
                MM(s, pglu[:, gc * 128:(gc + 1) * 128], wglu[:, kc, gc * 128:(gc + 1) * 128], gyT[:, kc, :],
                   ["wglu", "gyT"], ["pglu"], start=(kc == 0), stop=(kc == 1))
        yield
        ACTF(s, sgb[:], pglu[:, 256:512].rearrange("p (k t) -> p k t", k=2), AF.Sigmoid, ["pglu"], ["sgb"])
        yield
        YS = ysT[n % 2]
        TT_(s, "dve", YS[:], pglu[:, 0:256].rearrange("p (k t) -> p k t", k=2), sgb[:], ALU.mult, ["pglu", "sgb"],
            [f"ysT{n % 2}"])
        s.dma("sp", scr["yssmT"][q_, :, j * 128:(j + 1) * 128].rearrange("(kc p) t -> p kc t", p=128), YS[:],
              reads=[f"ysT{n % 2}"])
        yield
        for g in range(4):
            o = ppl[64 * (g % 2):64 * (g % 2) + 64, (g // 2) * 128:(g // 2 + 1) * 128]
            if j == 0:
                MM(s, o, UB[:, 256 + 64 * g:256 + 64 * g + 64], MT[:, g * 3 + 2, :], [ubn, "MT"], ["ppl"])
            else:
                MM(s, o, UB[:, 256 + 64 * g:256 + 64 * g + 64], MT[:, g * 3 + 0, :], [ubn, "MT"], ["ppl"],
                   start=True, stop=False)
                MM(s, o, UBP[:, 256 + 64 * g:256 + 64 * g + 64], MT[:, g * 3 + 1, :], [ubpn, "MT"], ["ppl"],
                   start=False, stop=True)
        CP(s, "act", plT[:], ppl[:, 0:256].rearrange("p (k t) -> p k t", k=2), ["ppl"], ["plT"])
        for g in range(4):
            pb = 64 * (g % 2)
            MM(s, ppl[pb:pb + 64, 256 + (g // 2) * 128:256 + (g // 2 + 1) * 128], PW[pb:pb + 64, g // 2, :],
               plT[pb:pb + 64, g // 2, :], ["PW", "plT"], ["ppl"])
        yield
        YP = ypT[n % 2]
        for kc in range(2):
            TS(s, "dve", YP[:, kc, :], ppl[:, 256 + kc * 128:256 + (kc + 1) * 128], pscol[:, kc:kc + 1], ALU.mult,
               ["ppl", "pscol"], [f"ypT{n % 2}"])
        s.dma("sp", scr["ypoolT"][q_, :, j * 128:(j + 1) * 128].rearrange("(kc p) t -> p kc t", p=128), YP[:],
              reads=[f"ypT{n % 2}"])

    load(0)
    if NCH > 1:
        load(1)
    def chain(*gs):
        for g in gs:
            yield from g

    for t in range(-1, NCH):
        if 0 <= t + 2 < NCH and t + 2 >= 2:
            load(t + 2)
        gens = []
        if t >= 0:
            gens.append(chain(XP(t), REST(t)))
        if t + 1 < NCH:
            gens.append(S1(t + 1))
        while gens:
            for g in list(gens):
                try:
                    next(g)
                except StopIteration:
                    gens.remove(g)
    c.end()


def build_test_a2(NSEQ, S, l=0):
    nc = bass.Bass("TRN2", target_bir_lowering=False)
    u = nc.dram_tensor("u_in", [NSEQ, S, 512], F32, kind="ExternalInput").ap()
    P = declare_params(nc)
    scr = make_scratch(nc, NSEQ, S, debug_out=True)
    scr["u"] = u
    with contextlib.ExitStack() as es:
        c = Ctx(nc, es)
        phase_a2(c, l, NSEQ, S, P, scr)
        print("ops", c.s.total, "waits", c.s.nwaits)
    return nc


def phase_c(c, l, NSEQ, S, x_d, xout_d, P, scr):
    nc, s = c.nc, c.s
    NJ = S // 128
    c.begin()
    wg = c.sb("wg", [128, 8, 3 * D], BF16)
    wba = c.sb("wba", [128, 4, D], BF16)
    wbs = c.sb("wbs", [128, 2, D], BF16)
    wbp = c.sb("wbp", [128, 2, D], BF16)
    wout = c.sb("wout", [128, 8, D], BF16)
    bg = c.sb("bg", [128, 3 * D], F32)
    NSTG = 4
    stg = [c.sb(f"stg{i}", [128, 1536], F32) for i in range(NSTG)]
    gcol = c.sb("gcol", [128, 8], F32)
    ident = c.sb("ident", [128, 128], BF16)
    xt = [c.sb(f"xt{i}", [128, D], F32) for i in range(6)]
    junk = c.sb("junk", [128, D], F32)
    ss = c.sb("ss", [128, 1], F32)
    rstd = c.sb("rstd", [128, 1], F32)
    hb = c.sb("hb", [128, D], BF16)
    hT = [c.sb(f"hT{i}", [128, 8, 128], BF16) for i in range(2)]
    gates_b = [c.sb(f"gates{i}", [128, 3 * D], F32) for i in range(2)]
    ya = [c.sb(f"ya{i}", [128, 512], BF16) for i in range(2)]
    yaT_b = [c.sb(f"yaT{i}", [128, 4, 128], BF16) for i in range(2)]
    ysT = [c.sb(f"ysT{i}", [128, 2, 128], BF16) for i in range(2)]
    ypT = [c.sb(f"ypT{i}", [128, 2, 128], BF16) for i in range(2)]
    macc = [c.sb(f"macc{i}", [128, 512], F32) for i in range(2)]
    mtmp = [c.sb(f"mtmp{i}", [128, 512], F32) for i in range(2)]
    mrg = [c.sb(f"mrg{i}", [128, D], BF16) for i in range(2)]
    mT_b = [c.sb(f"mT{i}", [128, 8, 128], BF16) for i in range(2)]
    pt = c.ps("pt")
    pg = [c.ps(f"pg{i}") for i in range(2)]
    pbr = [c.ps(f"pbr{i}") for i in range(3)]
    po = [c.ps(f"po{i}") for i in range(2)]

    make_identity(c, ident)
    s.dma("sp", gcol[:], P["mix_norm"][l].rearrange("(kc p) -> p kc", p=128), writes=["gcol"],
          allow_slow_non_contiguous=True)
    s.dma("act", bg[:], P["b_gate"][l:l + 1, :].partition_broadcast(128), writes=["bg"])
    n = 0
    dq = ("sp", "act", "pool")
    for kc in range(8):
        for hf in range(2):
            st, sn_ = stg[n % NSTG], f"stg{n % NSTG}"
            s.dma(dq[n % 3], st[:], P["w_in"][l, kc * 128:(kc + 1) * 128, 2056 + hf * 1536:2056 + (hf + 1) * 1536],
                  writes=[sn_])
            _cast(s, ("dve", "act")[n % 2], wg[:, kc, hf * 1536:(hf + 1) * 1536], st[:], [sn_, "gcol"],
                  [f"wg{kc}" if hf == 0 else f"wg{kc}b"], scalar=gcol[:, kc:kc + 1])
            n += 1
    for (wt, nm, nk) in ((wba, "w_br_attn", 4), (wbs, "w_br_ssm", 2), (wbp, "w_br_pool", 2), (wout, "w_out", 8)):
        for k0 in range(nk):
            st, sn_ = stg[n % NSTG], f"stg{n % NSTG}"
            s.dma(dq[n % 3], st[:, 0:D], P[nm][l, k0 * 128:(k0 + 1) * 128, :], writes=[sn_])
            _cast(s, ("dve", "act")[n % 2], wt[:, k0, :], st[:, 0:D], [sn_], [nm])
            n += 1

    NCH = NSEQ * NJ
    ptv = pt[:].bitcast(BF16)

    def load(n):
        q_, j = divmod(n, NJ)
        b = n % 2
        s.dma("sp", xt[n % 6][:], x_d[n * 128:(n + 1) * 128, :], writes=[f"xt{n % 6}"])

    def load_ya(n):
        q_, j = divmod(n, NJ)
        b = n % 2
        s.dma("sp", ya[b][:], scr["yattn"][q_, j * 128:(j + 1) * 128, :], writes=[f"ya{b}"])

    def load_ys(n):
        q_, j = divmod(n, NJ)
        b = n % 2
        s.dma("sp", ysT[b][:], scr["yssmT"][q_, :, j * 128:(j + 1) * 128].rearrange("(kc p) t -> p kc t", p=128),
              writes=[f"ysT{b}"])
        s.dma("sp", ypT[b][:], scr["ypoolT"][q_, :, j * 128:(j + 1) * 128].rearrange("(kc p) t -> p kc t", p=128),
              writes=[f"ypT{b}"])

    def s1(n):
        X, xb = xt[n % 6], f"xt{n % 6}"
        HT, htn = hT[n % 2], f"hT{n % 2}"
        rms_rstd(c, X[:], xb, junk[:], ss[:], rstd[:], D, "")
        yield
        TS(s, "dve", hb[:], X[:], rstd[:, 0:1], ALU.mult, [xb, "rstd"], ["hb"])
        yield
        for kc in range(8):
            TR(s, ptv[:, kc * 128:(kc + 1) * 128], hb[:, kc * 128:(kc + 1) * 128], ident[:], ["hb", "ident"], ["pt"])
        CP(s, "act", HT[:], ptv.rearrange("p (k t) -> p k t", k=8), ["pt"], [htn])
        yield

    def s2_gates(n):
        HT, htn = hT[n % 2], f"hT{n % 2}"
        gates = gates_b[n % 2]
        gp = f"g{n % 2}_"
        for gb in range(6):
            PG, pgn = pg[gb % 2], f"pg{gb % 2}"
            for kc in range(8):
                MM(s, PG[:], HT[:, kc, :], wg[:, kc, gb * 512:(gb + 1) * 512], [htn, f"wg{kc}" if gb < 3 else f"wg{kc}b"], [pgn],
                   start=(kc == 0), stop=(kc == 7))
                yield
            gsl = gates[:, gb * 512:(gb + 1) * 512]
            TT_(s, "dve", gsl, PG[:], bg[:, gb * 512:(gb + 1) * 512], ALU.add, [pgn, "bg"], [gp + f"gate{gb}"])
            yield
            ACTF(s, gsl, gsl, AF.Sigmoid, [gp + f"gate{gb}"], [gp + f"gate{gb}"])
            yield

    def s2_ya(n):
        b = n % 2
        for kc in range(4):
            TR(s, ptv[:, kc * 128:(kc + 1) * 128], ya[b][:, kc * 128:(kc + 1) * 128], ident[:], [f"ya{b}", "ident"],
               ["pt"])
        CP(s, "dve", yaT_b[b][:], ptv[:, 0:512].rearrange("p (k t) -> p k t", k=4), ["pt"], [f"yaT{b}"])
        yield

    def s2_branch(n, dbs):
        b = n % 2
        MR = mrg[b]
        gates = gates_b[n % 2]
        gp = f"g{n % 2}_"
        for db in dbs:
            cs = slice(db * 512, (db + 1) * 512)
            for kc in range(4):
                MM(s, pbr[0][:], yaT_b[b][:, kc, :], wba[:, kc, cs], [f"yaT{b}", "w_br_attn"], ["pbr0"], start=(kc == 0),
                   stop=(kc == 3))
                yield
            for kc in range(2):
                MM(s, pbr[1][:], ysT[b][:, kc, :], wbs[:, kc, cs], [f"ysT{b}", "w_br_ssm"], ["pbr1"], start=(kc == 0),
                   stop=(kc == 1))
                yield
            for kc in range(2):
                MM(s, pbr[2][:], ypT[b][:, kc, :], wbp[:, kc, cs], [f"ypT{b}", "w_br_pool"], ["pbr2"], start=(kc == 0),
                   stop=(kc == 1))
                yield
            MA, TM = macc[db], mtmp[db]
            TT_(s, "dve", MA[:], pbr[0][:], gates[:, db * 512:(db + 1) * 512], ALU.mult, ["pbr0", gp + f"gate{db}"],
                [f"macc{db}"])
            yield
            TT_(s, "dve", TM[:], pbr[1][:], gates[:, D + db * 512:D + (db + 1) * 512], ALU.mult,
                ["pbr1", gp + f"gate{2 + db}"], [f"mtmp{db}"])
            yield
            TT_(s, "pool", MA[:], MA[:], TM[:], ALU.add, [f"macc{db}", f"mtmp{db}"], [f"macc{db}"])
            yield
            TT_(s, "dve", TM[:], pbr[2][:], gates[:, 2 * D + db * 512:2 * D + (db + 1) * 512], ALU.mult,
                ["pbr2", gp + f"gate{4 + db}"], [f"mtmp{db}"])
            yield
            TT_(s, "pool", MR[:, cs], MA[:], TM[:], ALU.add, [f"macc{db}", f"mtmp{db}"], [f"mrg{b}_{db}"])
            yield

    def s3a(n):
        b = n % 2
        MR = mrg[b]
        for kc in range(8):
            TR(s, ptv[:, kc * 128:(kc + 1) * 128], MR[:, kc * 128:(kc + 1) * 128], ident[:],
               [f"mrg{b}_{kc // 4}", "ident"], ["pt"])
        CP(s, "act", mT_b[b][:], ptv.rearrange("p (k t) -> p k t", k=8), ["pt"], [f"mT{b}"])
        yield

    def s3b(n):
        b = n % 2
        X, xb = xt[n % 6], f"xt{n % 6}"
        for db in range(2):
            cs = slice(db * 512, (db + 1) * 512)
            for kc in range(8):
                MM(s, po[db][:], mT_b[b][:, kc, :], wout[:, kc, cs], [f"mT{b}", "w_out"], [f"po{db}"], start=(kc == 0),
                   stop=(kc == 7))
                yield
            TT_(s, "dve", X[:, cs], po[db][:], X[:, cs], ALU.add, [f"po{db}", xb], [xb])
            yield
        s.dma("pool", xout_d[n * 128:(n + 1) * 128, :], X[:], reads=[xb])
        yield

    def chain(*gs):
        for g in gs:
            yield from g

    def run(gens):
        gens = list(gens)
        while gens:
            for g in list(gens):
                try:
                    next(g)
                except StopIteration:
                    gens.remove(g)

    load(0)
    load_ya(0)
    load_ys(0)
    if NCH > 1:
        load(1)
    run([s1(0)])
    for t in range(NCH + 4):
        if t + 2 < NCH:
            load(t + 2)
        gens = []
        if t < NCH:
            gens.append(s2_gates(t))
        if t + 1 < NCH:
            gens.append(s1(t + 1))
        if t < NCH:
            gens.append(s2_ya(t))
        if 1 <= t <= NCH:
            gens.append(s2_branch(t - 1, [0, 1]))
        if 2 <= t <= NCH + 1:
            gens.append(s3a(t - 2))
        if 3 <= t <= NCH + 2:
            gens.append(s3b(t - 3))
        run(gens)
        if t + 1 < NCH:
            load_ya(t + 1)
            load_ys(t + 1)
    c.end()


def build_full(NSEQ, S, depth=DEPTH):
    T = NSEQ * S
    nc = bass.Bass("TRN2", target_bir_lowering=False)
    x = nc.dram_tensor("x", [T, D], F32, kind="ExternalInput").ap()
    y = nc.dram_tensor("y", [T, D], F32, kind="ExternalOutput").ap()
    P = declare_params(nc)
    scr = make_scratch(nc, NSEQ, S)
    xa = nc.dram_tensor("scr_xa", [T, D], F32).ap()
    with contextlib.ExitStack() as es:
        c = Ctx(nc, es)
        G = {"cumall": es.enter_context(nc.sbuf_tensor("cumall", [128, NSEQ, S // 128, 8], F32))}
        for l in range(depth):
            xin = x if l == 0 else xa
            phase_a(c, l, NSEQ, S, xin, P, G, scr)
            phase_a2(c, l, NSEQ, S, P, scr)
            phase_b(c, NSEQ, S, G, scr)
            phase_c(c, l, NSEQ, S, xin, xa, P, scr)
            phase_mlp(c, l, T, xa, P["w_up"], P["w_down"], P["mlp_norm"], xout_d=(y if l == depth - 1 else xa))
        print("ops", c.s.total, "waits", c.s.nwaits, flush=True)
    return nc


_NC_CACHE = {}


def kernel(**inputs):
    x = np.ascontiguousarray(np.asarray(inputs["x"], dtype=np.float32))
    B, S, _ = x.shape
    NSEQ = B // NCORES
    key = (NSEQ, S)
    if key not in _NC_CACHE:
        _NC_CACHE[key] = build_full(NSEQ, S)
    nc = _NC_CACHE[key]
    params = {k: np.ascontiguousarray(np.asarray(inputs[k], dtype=np.float32)) for k in PARAM_SHAPES}
    in_maps = []
    for cid in range(NCORES):
        m = {"x": x[cid * NSEQ:(cid + 1) * NSEQ].reshape(NSEQ * S, D)}
        m.update(params)
        in_maps.append(m)
    res = run_bass_kernel_spmd(nc, in_maps, core_ids=list(range(NCORES)))
    out = np.empty((B, S, D), dtype=np.float32)
    for cid in range(NCORES):
        out[cid * NSEQ:(cid + 1) * NSEQ] = np.asarray(res.results[cid]["y"]).reshape(NSEQ, S, D)
    return out
```
